# Optimizing a Trainium2 kernel written in Bass

```python
import math
import jax, jax.numpy as jnp
from jax import lax
import numpy as np

D_MODEL = 1024
BATCH = 4
SEQ = 8192
DEPTH = 1

GDN_HEAD_K = 128
GDN_HEAD_V = 128
GDN_HEADS = D_MODEL // GDN_HEAD_V
GDN_CONV = 4
GDN_CHUNK = 64
GDN_QK_DIM = GDN_HEADS * GDN_HEAD_K
GDN_V_DIM = GDN_HEADS * GDN_HEAD_V
GDN_CONV_DIM = 2 * GDN_QK_DIM + GDN_V_DIM

ATT_HEAD_DIM = 128
ATT_HEADS = D_MODEL // ATT_HEAD_DIM
ATT_KV_HEADS = 2
ATT_Q_DIM = ATT_HEADS * ATT_HEAD_DIM
ATT_KV_DIM = ATT_KV_HEADS * ATT_HEAD_DIM
IDX_HEADS = 8
IDX_DIM = 64
TOPK_MAX = 256
Q_BLOCK = 128

ROPE_THETA = 500000.0
ROPE_FRACTION = 4
D_FF = 4 * D_MODEL
EPS = 1e-6

IN_SPLITS = (GDN_CONV_DIM, GDN_V_DIM, GDN_HEADS, GDN_HEADS,
             ATT_Q_DIM, ATT_KV_DIM, ATT_KV_DIM,
             IDX_HEADS * IDX_DIM, IDX_DIM, IDX_HEADS,
             D_MODEL, D_MODEL)
IN_PROJ_DIM = (GDN_CONV_DIM + GDN_V_DIM + 2 * GDN_HEADS + ATT_Q_DIM + 2 * ATT_KV_DIM
               + IDX_HEADS * IDX_DIM + IDX_DIM + IDX_HEADS + 2 * D_MODEL)

kernel_name = "hybrid_gdn_dsa_gated_merge"


def rms_norm(x, g):
    xf = x.astype(jnp.float32)
    y = xf * lax.rsqrt(jnp.mean(xf * xf, axis=-1, keepdims=True) + EPS)
    return (y * g.astype(jnp.float32)).astype(x.dtype)


def l2_normalize(x):
    xf = x.astype(jnp.float32)
    return xf * lax.rsqrt(jnp.sum(xf * xf, axis=-1, keepdims=True) + EPS)


def rope_tables(seq, rot_dim):
    inv_freq = ROPE_THETA ** (-jnp.arange(0, rot_dim, 2, dtype=jnp.float32) / rot_dim)
    ang = jnp.arange(seq, dtype=jnp.float32)[:, None] * inv_freq[None, :]
    return jnp.cos(ang), jnp.sin(ang)


def apply_partial_rope(t, cos, sin):
    half = cos.shape[-1]
    shape = (t.shape[1],) + (1,) * (t.ndim - 3) + (half,)
    c, s = cos.reshape(shape), sin.reshape(shape)
    tf = t.astype(jnp.float32)
    x1, x2 = tf[..., :half], tf[..., half:2 * half]
    out = jnp.concatenate([x1 * c - x2 * s, x2 * c + x1 * s, tf[..., 2 * half:]], axis=-1)
    return out.astype(t.dtype)


def causal_short_conv(u, w):
    k = w.shape[0]
    s = u.shape[1]
    up = jnp.pad(u, ((0, 0), (k - 1, 0), (0, 0)))
    y = up[:, 0:s] * w[0]
    for j in range(1, k):
        y = y + up[:, j:j + s] * w[j]
    return y


def gated_delta_rule_chunked(q, k, v, g, beta):
    B, S, H, dk = q.shape
    dv = v.shape[-1]
    C = GDN_CHUNK
    N = S // C
    f32 = jnp.float32

    def chunks(t):
        t = jnp.swapaxes(t.astype(f32), 1, 2)
        return t.reshape((B, H, N, C) + t.shape[3:])

    q, k, v, g, beta = chunks(q), chunks(k), chunks(v), chunks(g), chunks(beta)
    g_cum = jnp.cumsum(g, axis=-1)
    pos = jnp.arange(C)
    incl = pos[:, None] >= pos[None, :]
    strict = pos[:, None] > pos[None, :]
    diff = g_cum[..., :, None] - g_cum[..., None, :]
    decay = jnp.where(incl, jnp.exp(jnp.where(incl, diff, 0.0)), 0.0)
    k_beta = k * beta[..., None]
    a_mat = jnp.where(strict, jnp.einsum('bhnid,bhnjd->bhnij', k_beta, k) * decay, 0.0)
    lhs = a_mat + jnp.eye(C, dtype=f32)
    rhs = jnp.concatenate([v * beta[..., None], k_beta * jnp.exp(g_cum)[..., None]], axis=-1)
    sol = lax.linalg.triangular_solve(lhs, rhs, left_side=True, lower=True, unit_diagonal=True)
    u, w = sol[..., :dv], sol[..., dv:]
    qk = jnp.where(incl, jnp.einsum('bhnid,bhnjd->bhnij', q, k) * decay, 0.0)
    q_dec = q * jnp.exp(g_cum)[..., None]
    g_last = g_cum[..., -1]
    k_dec = k * jnp.exp(g_last[..., None] - g_cum)[..., None]

    def step(state, xs):
        u_n, w_n, qk_n, q_n, k_n, gl_n = xs
        v_new = u_n - jnp.einsum('bhcd,bhde->bhce', w_n, state)
        o_n = (jnp.einsum('bhcd,bhde->bhce', q_n, state)
               + jnp.einsum('bhij,bhje->bhie', qk_n, v_new))
        state = state * jnp.exp(gl_n)[..., None, None] + jnp.einsum('bhcd,bhce->bhde', k_n, v_new)
        return state, o_n

    xs = tuple(jnp.moveaxis(t, 2, 0) for t in (u, w, qk, q_dec, k_dec, g_last))
    state0 = jnp.zeros((B, H, dk, dv), f32)
    _, o = lax.scan(step, state0, xs)
    o = jnp.moveaxis(o, 0, 2).reshape(B, H, S, dv)
    return jnp.swapaxes(o, 1, 2)


def dsa_sparse_attention(q, k, v, q_idx, k_idx, w_idx):
    B, S, H, hd = q.shape
    kvh = k.shape[2]
    grp = H // kvh
    n_blk = S // Q_BLOCK
    k_sel = min(TOPK_MAX, S // 4)
    scale = hd ** -0.5
    key_pos = jnp.arange(S, dtype=jnp.int32)
    k_idx_f = k_idx.astype(jnp.float32)

    def to_blocks(t):
        return jnp.moveaxis(t.reshape((B, n_blk, Q_BLOCK) + t.shape[2:]), 1, 0)

    def block(args):
        blk, qb, qib, wb = args
        t = blk * Q_BLOCK + jnp.arange(Q_BLOCK, dtype=jnp.int32)
        logits = jnp.einsum('bqhd,bsd->bqhs', qib.astype(jnp.float32), k_idx_f)
        score = jnp.einsum('bqh,bqhs->bqs', wb.astype(jnp.float32), jax.nn.relu(logits))
        causal = key_pos[None, :] <= t[:, None]
        score = jnp.where(causal[None], score, -jnp.inf)
        _, sel = lax.top_k(score, k_sel)
        valid = sel <= t[None, :, None]
        k_g = jax.vmap(lambda kb, ib: kb[ib])(k, sel)
        v_g = jax.vmap(lambda vb, ib: vb[ib])(v, sel)
        qg = qb.reshape(B, Q_BLOCK, kvh, grp, hd)
        s = jnp.einsum('bqcgd,bqncd->bqcgn', qg, k_g).astype(jnp.float32) * scale
        s = jnp.where(valid[:, :, None, None, :], s, -jnp.inf)
        p = jax.nn.softmax(s, axis=-1).astype(v_g.dtype)
        o = jnp.einsum('bqcgn,bqncd->bqcgd', p, v_g)
        return o.reshape(B, Q_BLOCK, H * hd)

    o = lax.map(block, (jnp.arange(n_blk, dtype=jnp.int32), to_blocks(q),
                        to_blocks(q_idx), to_blocks(w_idx)))
    return jnp.moveaxis(o, 0, 1).reshape(B, S, H * hd)


def hybrid_layer(x, norm_mix_g, w_in, conv_w, a_log, dt_bias, gdn_norm_g,
                 q_norm_g, k_norm_g, w_out, norm_mlp_g, w_mlp_up, w_mlp_down):
    B, S, _ = x.shape
    h = rms_norm(x, norm_mix_g)
    proj = h @ w_in
    points = np.cumsum(np.array(IN_SPLITS))[:-1].tolist()
    (g_qkv, g_z, g_a, g_b, a_q, a_k, a_v,
     i_q, i_k, i_w, gate_a, gate_b) = jnp.split(proj, points, axis=-1)

    qkv = jax.nn.silu(causal_short_conv(g_qkv, conv_w))
    gq, gk, gv = jnp.split(qkv, [GDN_QK_DIM, 2 * GDN_QK_DIM], axis=-1)
    gq = l2_normalize(gq.reshape(B, S, GDN_HEADS, GDN_HEAD_K)) * (GDN_HEAD_K ** -0.5)
    gk = l2_normalize(gk.reshape(B, S, GDN_HEADS, GDN_HEAD_K))
    gv = gv.reshape(B, S, GDN_HEADS, GDN_HEAD_V)
    beta = jax.nn.sigmoid(g_b.astype(jnp.float32))
    g = -jnp.exp(a_log.astype(jnp.float32)) * jax.nn.softplus(
        g_a.astype(jnp.float32) + dt_bias.astype(jnp.float32))
    o_gdn = gated_delta_rule_chunked(gq, gk, gv, g, beta)
    z = jax.nn.silu(g_z.reshape(B, S, GDN_HEADS, GDN_HEAD_V).astype(jnp.float32))
    o_gdn = (rms_norm(o_gdn, gdn_norm_g) * z).reshape(B, S, GDN_V_DIM).astype(x.dtype)

    cos, sin = rope_tables(S, ATT_HEAD_DIM // ROPE_FRACTION)
    a_q = apply_partial_rope(rms_norm(a_q.reshape(B, S, ATT_HEADS, ATT_HEAD_DIM), q_norm_g), cos, sin)
    a_k = apply_partial_rope(rms_norm(a_k.reshape(B, S, ATT_KV_HEADS, ATT_HEAD_DIM), k_norm_g), cos, sin)
    a_v = a_v.reshape(B, S, ATT_KV_HEADS, ATT_HEAD_DIM)
    cos_i, sin_i = rope_tables(S, IDX_DIM // ROPE_FRACTION)
    i_q = apply_partial_rope(i_q.reshape(B, S, IDX_HEADS, IDX_DIM), cos_i, sin_i)
    i_k = apply_partial_rope(i_k, cos_i, sin_i)
    i_w = i_w * (IDX_HEADS ** -0.5 * IDX_DIM ** -0.5)
    o_att = dsa_sparse_attention(a_q, a_k, a_v, i_q, i_k, i_w).astype(x.dtype)

    merged = jax.nn.sigmoid(gate_a) * o_gdn + jax.nn.sigmoid(gate_b) * o_att
    x = x + merged @ w_out

    h2 = rms_norm(x, norm_mlp_g)
    x = x + jnp.square(jax.nn.relu(h2 @ w_mlp_up)) @ w_mlp_down
    return x


def setup_inputs(seed: int = 0) -> dict:
    key = jax.random.key(seed)
    ks = jax.random.split(key, 14)
    f32 = jnp.float32

    def gain(k, n):
        return 1.0 + 0.02 * jax.random.normal(k, (DEPTH, n), f32)

    x = jax.random.normal(ks[0], (BATCH, SEQ, D_MODEL), f32)
    norm_mix_g = gain(ks[1], D_MODEL)
    w_in = jax.random.normal(ks[2], (DEPTH, D_MODEL, IN_PROJ_DIM), f32) * D_MODEL ** -0.5
    conv_w = jax.random.normal(ks[3], (DEPTH, GDN_CONV, GDN_CONV_DIM), f32) * GDN_CONV ** -0.5
    a_log = jnp.log(jax.random.uniform(ks[4], (DEPTH, GDN_HEADS), f32, 1.0, 16.0))
    dt = jnp.exp(jax.random.uniform(ks[5], (DEPTH, GDN_HEADS), f32, math.log(1e-3), math.log(1e-1)))
    dt_bias = dt + jnp.log(-jnp.expm1(-dt))
    gdn_norm_g = gain(ks[6], GDN_HEAD_V)
    q_norm_g = gain(ks[7], ATT_HEAD_DIM)
    k_norm_g = gain(ks[8], ATT_HEAD_DIM)
    w_out = jax.random.normal(ks[9], (DEPTH, D_MODEL, D_MODEL), f32) * D_MODEL ** -0.5
    norm_mlp_g = gain(ks[10], D_MODEL)
    w_mlp_up = jax.random.normal(ks[11], (DEPTH, D_MODEL, D_FF), f32) * D_MODEL ** -0.5
    w_mlp_down = jax.random.normal(ks[12], (DEPTH, D_FF, D_MODEL), f32) * D_FF ** -0.5
    return {"x": x, "norm_mix_g": norm_mix_g, "w_in": w_in, "conv_w": conv_w,
            "a_log": a_log, "dt_bias": dt_bias, "gdn_norm_g": gdn_norm_g,
            "q_norm_g": q_norm_g, "k_norm_g": k_norm_g, "w_out": w_out,
            "norm_mlp_g": norm_mlp_g, "w_mlp_up": w_mlp_up, "w_mlp_down": w_mlp_down}


def reference(x, norm_mix_g, w_in, conv_w, a_log, dt_bias, gdn_norm_g, q_norm_g, k_norm_g,
              w_out, norm_mlp_g, w_mlp_up, w_mlp_down):
    for layer in range(DEPTH):
        x = hybrid_layer(x, norm_mix_g[layer], w_in[layer], conv_w[layer], a_log[layer],
                         dt_bias[layer], gdn_norm_g[layer], q_norm_g[layer], k_norm_g[layer],
                         w_out[layer], norm_mlp_g[layer], w_mlp_up[layer], w_mlp_down[layer])
    return x
```

```python
from contextlib import ExitStack
import numpy as np
import ml_dtypes
import concourse.bass as bass
import concourse.mybir as mybir
from concourse.bass_utils import run_bass_kernel_spmd

F32 = mybir.dt.float32
BF16 = mybir.dt.bfloat16
AF = mybir.ActivationFunctionType
ALU = mybir.AluOpType
AX = mybir.AxisListType

D = 1024
EPS = 1e-6
NEG = -1.0e30
SAME_ENGINE_SYNC = True


class Tok:
    __slots__ = ("name", "w", "r")

    def __init__(self, name):
        self.name = name
        self.w = None
        self.r = {}


class Sched:
    ENG = ("pe", "act", "dve", "pool", "sp")

    def __init__(self, nc, es, n_dma_sems=24):
        self.nc = nc
        self.sems = {}
        for e in ("pe", "act", "dve", "pool"):
            self.sems[e] = es.enter_context(nc.semaphore("s_" + e))
        self.ndma = n_dma_sems
        for k in range(n_dma_sems):
            self.sems[("d", k)] = es.enter_context(nc.semaphore("s_d%d" % k))
        self.cnt = {e: 0 for e in ("pe", "act", "dve", "pool")}
        self.dtot = [0] * n_dma_sems
        self.rr = 0
        self.waited = {e: {} for e in self.ENG}
        self.ops = {e: [] for e in self.ENG}
        self.nops = 0

    def _deps(self, eng, reads, writes):
        deps = {}

        def add(d):
            if d is None:
                return
            sid, val = d
            if sid == eng and (eng == "pe" or not SAME_ENGINE_SYNC):
                return
            if deps.get(sid, 0) < val:
                deps[sid] = val

        for t in reads:
            add(t.w)
        for t in writes:
            add(t.w)
            for sid, val in t.r.items():
                add((sid, val))
        out = []
        wd = self.waited[eng]
        for sid, val in deps.items():
            if wd.get(sid, 0) < val:
                wd[sid] = val
                out.append((sid, val))
        return out

    def _mark(self, me, reads, writes):
        for t in reads:
            if t.r.get(me[0], 0) < me[1]:
                t.r[me[0]] = me[1]
        for t in writes:
            t.w = me
            t.r = {}

    def op(self, eng, fn, reads=(), writes=()):
        waits = self._deps(eng, reads, writes)
        self.cnt[eng] += 1
        me = (eng, self.cnt[eng])
        self._mark(me, reads, writes)
        self.ops[eng].append((waits, fn, eng, 1))
        self.nops += 1

    def dma(self, fn, reads=(), writes=()):
        k = self.rr
        self.rr = (self.rr + 1) % self.ndma
        sid = ("d", k)
        waits = self._deps("sp", reads, writes)
        wd = self.waited["sp"]
        if wd.get(sid, 0) < self.dtot[k]:
            wd[sid] = self.dtot[k]
            waits.append((sid, self.dtot[k]))
        self.dtot[k] += 16
        me = (sid, self.dtot[k])
        self._mark(me, reads, writes)
        self.ops["sp"].append((waits, fn, sid, 16))
        self.nops += 1

    def barrier(self):
        for e in self.ENG:
            waits = []
            wd = self.waited[e]
            for c in ("pe", "act", "dve", "pool"):
                if c != e and wd.get(c, 0) < self.cnt[c]:
                    wd[c] = self.cnt[c]
                    waits.append((c, self.cnt[c]))
            for k in range(self.ndma):
                sid = ("d", k)
                if wd.get(sid, 0) < self.dtot[k]:
                    wd[sid] = self.dtot[k]
                    waits.append((sid, self.dtot[k]))
            if waits:
                self.ops[e].append((waits, None, None, 0))

    def emit(self):
        nc = self.nc
        ops = self.ops
        self.ops = {e: [] for e in self.ENG}
        sems = self.sems

        def run(engh, lst):
            for waits, fn, sid, inc in lst:
                for s, v in waits:
                    engh.wait_ge(sems[s], v)
                if fn is not None:
                    fn(engh).then_inc(sems[sid], inc)

        with nc.Block() as block:
            @block.tensor
            def _(e):
                run(e, ops["pe"])

            @block.scalar
            def _(e):
                run(e, ops["act"])

            @block.vector
            def _(e):
                run(e, ops["dve"])

            @block.gpsimd
            def _(e):
                run(e, ops["pool"])

            @block.sync
            def _(e):
                run(e, ops["sp"])


class Ctx:
    pass


def bc(ap, shape):
    return ap.to_broadcast(list(shape))


class TL:
    def __init__(self, t, name):
        self.t = t
        self.k = Tok(name)

    def __getitem__(self, key):
        return self.t[key]


C1B = 5192
C1A = 3088


class Builder:
    def __init__(self, S, debug=False):
        self.Sq = S
        self.P = S // 128
        self.NO = self.P // 2
        self.debug = debug
        self.nc = bass.Bass("TRN2", target_bir_lowering=False)

    def dram_in(self, name, shape, dt=F32):
        return self.nc.dram_tensor(name, list(shape), dt, kind="ExternalInput").ap()

    def dram_scr(self, name, shape, dt):
        kind = "ExternalOutput" if self.debug else "Internal"
        return self.nc.dram_tensor(name, list(shape), dt, kind=kind).ap()

    def sb(self, es, name, shape, dt):
        return TL(es.enter_context(self.nc.sbuf_tensor("t_" + name, list(shape), dt)), name)

    def load(self, dst, dst_ap, src_ap):
        self.S.dma(lambda e: e.dma_start(out=dst_ap, in_=src_ap), writes=[dst.k])

    def rsqrt_small(self, v, n):
        S = self.S
        S.op("act", lambda e: e.activation(out=v[:, 0:n], in_=v[:, 0:n], func=AF.Ln), reads=[v.k], writes=[v.k])
        S.op("act", lambda e: e.activation(out=v[:, 0:n], in_=v[:, 0:n], func=AF.Exp, scale=-0.5), reads=[v.k], writes=[v.k])

    def load_weight_bf16(self, es, wb, w_ap, ncols, stg):
        S = self.S
        nk = w_ap.shape[0] // 128
        CH = stg[0].t.shape[1]
        i = 0
        for k in range(nk):
            for c0 in range(0, ncols, CH):
                cw = min(CH, ncols - c0)
                st = stg[i % len(stg)]
                S.dma(lambda e, st=st, k=k, c0=c0, cw=cw: e.dma_start(out=st[:, 0:cw], in_=w_ap[k * 128:(k + 1) * 128, c0:c0 + cw]), writes=[st.k])
                eng = ("pool", "dve", "act")[i % 3] if False else "pool"
                S.op(eng, lambda e, st=st, k=k, c0=c0, cw=cw: e.tensor_copy(out=wb[:, k, c0:c0 + cw], in_=st[:, 0:cw]), reads=[st.k], writes=[wb.k])
                i += 1

    def rms_hT(self, xb, ss, xnb, hT, gcol, psb):
        S = self.S
        junk = self.junk
        idb = self.identb
        S.op("act", lambda e: e.activation(out=junk[:, 0:1024], in_=xb[:], func=AF.Square, accum_out=ss[:, 0:1]), reads=[xb.k], writes=[junk.k, ss.k])
        S.op("dve", lambda e: e.tensor_scalar(out=ss[:, 0:1], in0=ss[:, 0:1], scalar1=1.0 / 1024, scalar2=EPS, op0=ALU.mult, op1=ALU.add), reads=[ss.k], writes=[ss.k])
        self.rsqrt_small(ss, 1)
        S.op("dve", lambda e: e.tensor_scalar(out=xnb[:], in0=xb[:], scalar1=ss[:, 0:1], scalar2=None, op0=ALU.mult), reads=[ss.k, xb.k], writes=[xnb.k])

        def tr(e):
            for kc in range(8):
                i = e.transpose(out=psb[:, kc * 128:(kc + 1) * 128], in_=xnb[:, kc * 128:(kc + 1) * 128], identity=idb[:])
            return i
        S.op("pe", tr, reads=[xnb.k, idb.k], writes=[psb.k])
        S.op("dve", lambda e: e.tensor_tensor(out=hT[:], in0=psb[:].rearrange("p (k t) -> p k t", k=8), in1=bc(gcol[:].unsqueeze(2), [128, 8, 128]), op=ALU.mult), reads=[psb.k, gcol.k], writes=[hT.k])

    def proj(self, ps, ncol, hT, wb, c0, pcol=0, start=True):
        def f(e):
            for kc in range(8):
                i = e.matmul(ps[:, pcol:pcol + ncol], lhsT=hT[:, kc, :], rhs=wb[:, kc, c0:c0 + ncol], start=(kc == 0), stop=(kc == 7))
            return i
        self.S.op("pe", f, reads=[hT.k, wb.k], writes=[ps.k])

    def setup(self, es):
        nc = self.nc
        S_, P, NO = self.Sq, self.P, self.NO
        self.S = Sched(nc, es)
        d = self.dram_in
        self.i_x = d("xseq", [S_, D])
        self.i_w1a = d("w1a", [D, C1A])
        self.i_w1b = d("w1b", [D, C1B])
        self.i_wout = d("wout", [D, D])
        self.i_wup = d("wup", [D, 4 * D])
        self.i_wdn = d("wdn", [4 * D, D])
        self.i_cosA = d("cosA", [S_, 16]); self.i_sinA = d("sinA", [S_, 16])
        self.i_cosI = d("cosI", [S_, 8]); self.i_sinI = d("sinI", [S_, 8])
        self.i_blk0 = d("blk0", [128, 128])
        self.i_identf = d("identf", [128, 128]); self.i_identb = d("identb", [128, 128], BF16)
        self.i_uinc = d("uinc", [128, 128]); self.i_causal = d("causal", [128, 128])
        self.i_mskU = d("mskU", [128, 128], BF16); self.i_mskL = d("mskL", [128, 128], BF16)
        self.i_gmix = d("gmix", [128, 8]); self.i_gmlp = d("gmlp", [128, 8])
        self.i_gq = d("gq", [128, 128]); self.i_gk = d("gk", [128, 128]); self.i_ggdn = d("ggdn", [128, 128])
        self.i_alog = d("alog", [128, 8]); self.i_dtb = d("dtb", [128, 8])
        self.i_convw = d("convw", [128, 24 * 4])
        self.o_out = nc.dram_tensor("out", [NO * 128, D], F32, kind="ExternalOutput").ap()
        s = self.dram_scr
        self.s_akT = s("s_akT", [128, 2 * S_], BF16)
        self.s_v = s("s_v", [S_, 258], BF16)
        self.s_ikT = s("s_ikT", [128, S_], F32)
        self.s_aqT = s("s_aqT", [NO, 128, 1024], BF16)
        self.s_iqT = s("s_iqT", [NO, 128, 512], F32)
        self.s_iw = s("s_iw", [NO, 128, 8], F32)
        self.s_sgb = s("s_sgb", [NO, 128, 1024], BF16)
        self.s_mg = s("s_mg", [NO, 128, 1024], BF16)
        self.s_zg = s("s_zg", [NO, 128, 1024], BF16)
        self.s_x1 = s("s_x1", [NO, 128, 1024], F32)
        self.k_akT = Tok("akT_d"); self.k_v = Tok("v_d"); self.k_ikT = Tok("ikT_d")
        self.k_own = [{n: Tok(n + str(j)) for n in ("aqT", "iqT", "iw", "sgb", "mg", "x1", "zg")} for j in range(NO)]
        self.ps = [TL(es.enter_context(nc.psum_tensor("ps%d" % i, [128, 512], F32)), "ps%d" % i) for i in range(7)]
        self.psb = TL(es.enter_context(nc.psum_tensor("psb", [128, 1024], BF16)), "psb")
        sb = lambda n, sh, dt: self.sb(es, n, sh, dt)
        self.identf = sb("identf", [128, 128], F32); self.identb = sb("identb", [128, 128], BF16)
        self.junk = sb("junk", [128, 1024], F32)
        for t, src in ((self.identf, self.i_identf), (self.identb, self.i_identb)):
            self.load(t, t[:], src)

    def norm_rope(self, H, src_ap, src_tok, gain, extra, out, ocol, cs, sn, tmp):
        S = self.S
        tf, sq, ssh, ra, rb = tmp
        n = H * 128
        S.op("act", lambda e: e.activation(out=tf[:, 0:n], in_=src_ap, func=AF.Copy), reads=[src_tok], writes=[tf.k])
        S.op("dve", lambda e: e.tensor_tensor(out=sq[:, 0:n], in0=tf[:, 0:n], in1=tf[:, 0:n], op=ALU.mult), reads=[tf.k], writes=[sq.k])
        S.op("dve", lambda e: e.tensor_reduce(out=ssh[:, 0:H], in_=sq[:, 0:n].rearrange("p (h d) -> p h d", h=H), axis=AX.X, op=ALU.add), reads=[sq.k], writes=[ssh.k])
        S.op("dve", lambda e: e.tensor_scalar(out=ssh[:, 0:H], in0=ssh[:, 0:H], scalar1=1.0 / 128, scalar2=EPS, op0=ALU.mult, op1=ALU.add), reads=[ssh.k], writes=[ssh.k])
        self.rsqrt_small(ssh, H)
        if extra != 1.0:
            S.op("dve", lambda e: e.tensor_scalar(out=ssh[:, 0:H], in0=ssh[:, 0:H], scalar1=extra, scalar2=None, op0=ALU.mult), reads=[ssh.k], writes=[ssh.k])
        tf3 = tf[:, 0:n].rearrange("p (h d) -> p h d", h=H)
        S.op("dve", lambda e: e.tensor_tensor(out=tf3, in0=tf3, in1=bc(ssh[:, 0:H].unsqueeze(2), [128, H, 128]), op=ALU.mult), reads=[tf.k, ssh.k], writes=[tf.k])
        S.op("pool", lambda e: e.tensor_tensor(out=tf3, in0=tf3, in1=bc(gain[:].unsqueeze(1), [128, H, 128]), op=ALU.mult), reads=[tf.k, gain.k], writes=[tf.k])
        o3 = out[:, ocol:ocol + n].rearrange("p (h d) -> p h d", h=H)
        S.op("act", lambda e: e.activation(out=out[:, ocol:ocol + n], in_=tf[:, 0:n], func=AF.Copy), reads=[tf.k], writes=[out.k])
        self.rope(tf3, tf.k, o3, out.k, H, 16, cs, sn, ra, rb)

    def rope(self, t3, ttok, o3, otok, H, hf, cs, sn, ra, rb):
        S = self.S
        x1 = t3[:, :, 0:hf]; x2 = t3[:, :, hf:2 * hf]
        c = bc(cs[:].unsqueeze(1), [128, H, hf]); s_ = bc(sn[:].unsqueeze(1), [128, H, hf])
        a3 = ra[:, 0:H * hf].rearrange("p (h d) -> p h d", h=H)
        b3 = rb[:, 0:H * hf].rearrange("p (h d) -> p h d", h=H)
        S.op("pool", lambda e: e.tensor_tensor(out=a3, in0=x1, in1=c, op=ALU.mult), reads=[ttok, cs.k], writes=[ra.k])
        S.op("pool", lambda e: e.tensor_tensor(out=b3, in0=x2, in1=s_, op=ALU.mult), reads=[ttok, sn.k], writes=[rb.k])
        S.op("dve", lambda e: e.tensor_tensor(out=o3[:, :, 0:hf], in0=a3, in1=b3, op=ALU.subtract), reads=[ra.k, rb.k], writes=[otok])
        S.op("pool", lambda e: e.tensor_tensor(out=a3, in0=x2, in1=c, op=ALU.mult), reads=[ttok, cs.k], writes=[ra.k])
        S.op("pool", lambda e: e.tensor_tensor(out=b3, in0=x1, in1=s_, op=ALU.mult), reads=[ttok, sn.k], writes=[rb.k])
        S.op("dve", lambda e: e.tensor_tensor(out=o3[:, :, hf:2 * hf], in0=a3, in1=b3, op=ALU.add), reads=[ra.k, rb.k], writes=[otok])

    def phase1b(self):
        S = self.S
        P = self.P
        ps, psb = self.ps, self.psb
        with ExitStack() as es:
            sb = lambda n, sh, dt: self.sb(es, n, sh, dt)
            wb = sb("w1b", [128, 8, C1B], BF16)
            stg = [sb("stg%d" % i, [128, 1024], F32) for i in range(2)]
            gmix = sb("gmix", [128, 8], F32); gq = sb("gq", [128, 128], F32); gk = sb("gk", [128, 128], F32)
            self.load(gmix, gmix[:], self.i_gmix); self.load(gq, gq[:], self.i_gq); self.load(gk, gk[:], self.i_gk)
            self.load_weight_bf16(es, wb, self.i_w1b, C1B, stg)
            xb = [sb("xb%d" % i, [128, 1024], F32) for i in range(2)]
            ss = sb("ss", [128, 1], F32); xnb = sb("xnb", [128, 1024], BF16); hT = sb("hT", [128, 8, 128], BF16)
            cA = sb("cA", [128, 16], F32); sA = sb("sA", [128, 16], F32); cI = sb("cI", [128, 8], F32); sI = sb("sI", [128, 8], F32)
            tmp = (sb("tf", [128, 512], F32), sb("sq", [128, 512], F32), sb("ssh", [128, 8], F32), sb("ra", [128, 64], F32), sb("rb", [128, 64], F32))
            ra, rb = tmp[3], tmp[4]
            akb = sb("akb", [128, 256], BF16); akT = sb("akTb", [128, 256], BF16)
            vaug = sb("vaug", [128, 258], BF16)
            ikf = sb("ikf", [128, 72], F32); ik2 = sb("ik2", [128, 128], F32); ikT = sb("ikTb", [128, 128], F32)
            iw = sb("iwb", [128, 8], F32)
            aqb = sb("aqb", [128, 1024], BF16); aqT = sb("aqTb", [128, 1024], BF16)
            iqf = sb("iqf", [128, 512], F32); iqo = sb("iqo", [128, 512], F32); iqT = sb("iqTb", [128, 512], F32)
            sgb = sb("sgbb", [128, 1024], BF16)
            zs = sb("zs", [128, 1024], F32); sga = sb("sga", [128, 1024], F32); zg = sb("zgb", [128, 1024], BF16)
            S.op("pool", lambda e: e.memset(vaug[:], 1.0), writes=[vaug.k])
            for p in range(P):
                own = (p % 2 == 1)
                j = p // 2
                x = xb[p % 2]
                r0 = p * 128
                self.load(x, x[:], self.i_x[r0:r0 + 128, :])
                self.load(cA, cA[:], self.i_cosA[r0:r0 + 128, :]); self.load(sA, sA[:], self.i_sinA[r0:r0 + 128, :])
                self.load(cI, cI[:], self.i_cosI[r0:r0 + 128, :]); self.load(sI, sI[:], self.i_sinI[r0:r0 + 128, :])
                self.rms_hT(x, ss, xnb, hT, gmix, psb)
                self.proj(ps[0], 512, hT, wb, 0)
                self.norm_rope(2, ps[0][:, 0:256], ps[0].k, gk, 1.0, akb, 0, cA, sA, tmp)
                S.op("act", lambda e: e.activation(out=vaug[:].rearrange("p (c d) -> p c d", c=2)[:, :, 0:128], in_=ps[0][:, 256:512].rearrange("p (c d) -> p c d", c=2), func=AF.Copy), reads=[ps[0].k], writes=[vaug.k])
                S.dma(lambda e, r0=r0: e.dma_start(out=self.s_v[r0:r0 + 128, :], in_=vaug[:]), reads=[vaug.k], writes=[self.k_v])

                def trk(e):
                    for c in range(2):
                        i = e.transpose(out=psb[:, c * 128:(c + 1) * 128], in_=akb[:, c * 128:(c + 1) * 128], identity=self.identb[:])
                    return i
                S.op("pe", trk, reads=[akb.k, self.identb.k], writes=[psb.k])
                S.op("act", lambda e: e.activation(out=akT[:], in_=psb[:, 0:256], func=AF.Copy), reads=[psb.k], writes=[akT.k])
                S.dma(lambda e, r0=r0: e.dma_start(out=self.s_akT.rearrange("h (c s) -> h c s", c=2)[:, :, r0:r0 + 128], in_=akT[:].rearrange("h (c s) -> h c s", c=2)), reads=[akT.k], writes=[self.k_akT])
                self.proj(ps[1], 72, hT, wb, 512)
                S.op("act", lambda e: e.activation(out=ikf[:], in_=ps[1][:, 0:72], func=AF.Copy), reads=[ps[1].k], writes=[ikf.k])
                S.op("dve", lambda e: e.tensor_copy(out=ik2[:, 0:64], in_=ikf[:, 0:64]), reads=[ikf.k], writes=[ik2.k])
                self.rope(ikf[:, 0:64].rearrange("p (h d) -> p h d", h=1), ikf.k, ik2[:, 0:64].rearrange("p (h d) -> p h d", h=1), ik2.k, 1, 8, cI, sI, ra, rb)
                S.op("dve", lambda e: e.tensor_copy(out=ik2[:, 64:128], in_=ik2[:, 0:64]), reads=[ik2.k], writes=[ik2.k])
                S.op("pe", lambda e: e.transpose(out=ps[2][:, 0:128], in_=ik2[:], identity=self.identf[:]), reads=[ik2.k, self.identf.k], writes=[ps[2].k])
                S.op("act", lambda e: e.activation(out=ikT[:], in_=ps[2][:, 0:128], func=AF.Copy), reads=[ps[2].k], writes=[ikT.k])
                S.dma(lambda e, r0=r0: e.dma_start(out=self.s_ikT[:, r0:r0 + 128], in_=ikT[:]), reads=[ikT.k], writes=[self.k_ikT])
                if not own:
                    continue
                ko = self.k_own[j]
                S.op("dve", lambda e: e.tensor_scalar(out=iw[:], in0=ikf[:, 64:72], scalar1=float(512.0 ** -0.5), scalar2=None, op0=ALU.mult), reads=[ikf.k], writes=[iw.k])
                S.dma(lambda e, j=j: e.dma_start(out=self.s_iw[j], in_=iw[:]), reads=[iw.k], writes=[ko["iw"]])
                for half in range(2):
                    pq = ps[3 + half]
                    self.proj(pq, 512, hT, wb, 584 + half * 512)
                    self.norm_rope(4, pq[:, 0:512], pq.k, gq, float(128.0 ** -0.5), aqb, half * 512, cA, sA, tmp)

                def trq(e):
                    for h in range(8):
                        i = e.transpose(out=psb[:, h * 128:(h + 1) * 128], in_=aqb[:, h * 128:(h + 1) * 128], identity=self.identb[:])
                    return i
                S.op("pe", trq, reads=[aqb.k, self.identb.k], writes=[psb.k])
                S.op("act", lambda e: e.activation(out=aqT[:], in_=psb[:], func=AF.Copy), reads=[psb.k], writes=[aqT.k])
                S.dma(lambda e, j=j: e.dma_start(out=self.s_aqT[j], in_=aqT[:]), reads=[aqT.k], writes=[ko["aqT"]])
                self.proj(ps[5], 512, hT, wb, 1608)
                S.op("act", lambda e: e.activation(out=iqf[:], in_=ps[5][:], func=AF.Copy), reads=[ps[5].k], writes=[iqf.k])
                S.op("dve", lambda e: e.tensor_copy(out=iqo[:], in_=iqf[:]), reads=[iqf.k], writes=[iqo.k])
                self.rope(iqf[:].rearrange("p (h d) -> p h d", h=8), iqf.k, iqo[:].rearrange("p (h d) -> p h d", h=8), iqo.k, 8, 8, cI, sI, ra, rb)

                def tri(e):
                    for g in range(4):
                        i = e.transpose(out=ps[6][:, g * 128:(g + 1) * 128], in_=iqo[:, g * 128:(g + 1) * 128], identity=self.identf[:])
                    return i
                S.op("pe", tri, reads=[iqo.k, self.identf.k], writes=[ps[6].k])
                S.op("act", lambda e: e.activation(out=iqT[:], in_=ps[6][:], func=AF.Copy), reads=[ps[6].k], writes=[iqT.k])
                S.dma(lambda e, j=j: e.dma_start(out=self.s_iqT[j], in_=iqT[:]), reads=[iqT.k], writes=[ko["iqT"]])
                for half in range(2):
                    pq = ps[half]
                    self.proj(pq, 512, hT, wb, 2120 + half * 512)
                    S.op("act", lambda e, pq=pq, half=half: e.activation(out=sgb[:, half * 512:(half + 1) * 512], in_=pq[:], func=AF.Sigmoid), reads=[pq.k], writes=[sgb.k])
                S.dma(lambda e, j=j: e.dma_start(out=self.s_sgb[j], in_=sgb[:]), reads=[sgb.k], writes=[ko["sgb"]])
                for half in range(2):
                    pq = ps[2 + half]; pg = ps[4 + half]
                    self.proj(pq, 512, hT, wb, 3144 + half * 512)
                    self.proj(pg, 512, hT, wb, 4168 + half * 512)
                    S.op("act", lambda e, pq=pq, half=half: e.activation(out=zs[:, half * 512:(half + 1) * 512], in_=pq[:], func=AF.Silu), reads=[pq.k], writes=[zs.k])
                    S.op("act", lambda e, pg=pg, half=half: e.activation(out=sga[:, half * 512:(half + 1) * 512], in_=pg[:], func=AF.Sigmoid), reads=[pg.k], writes=[sga.k])
                S.op("pool", lambda e: e.tensor_tensor(out=zg[:], in0=zs[:], in1=sga[:], op=ALU.mult), reads=[zs.k, sga.k], writes=[zg.k])
                S.dma(lambda e, j=j: e.dma_start(out=self.s_zg[j], in_=zg[:]), reads=[zg.k], writes=[ko["zg"]])
            S.barrier()
            S.emit()

    def phase2(self, NBIS=24):
        S = self.S
        P, NO, S_ = self.P, self.NO, self.Sq
        ps, psb = self.ps, self.psb
        with ExitStack() as es:
            sb = lambda n, sh, dt: self.sb(es, n, sh, dt)
            akT = sb("akT", [128, 2, S_], BF16)
            vall = sb("vall", [128, P, 258], BF16)
            ikT = sb("ikT", [128, S_], F32)
            woutb = sb("woutb", [128, 8, 1024], BF16)
            stg = [sb("stg2_%d" % i, [128, 512], F32) for i in range(2)]
            score = sb("score", [128, S_], F32)
            msk = sb("msk", [128, S_], BF16)
            gq = sb("gq2", [128, 128], F32); gk = sb("gk2", [128, 128], F32)
            negM = sb("negM", [128, 2], F32)
            causal = sb("causal", [128, 128], F32); blk0 = sb("blk0", [128, 128], F32)
            self.load(gq, gq[:], self.i_gq); self.load(gk, gk[:], self.i_gk)
            self.load(causal, causal[:], self.i_causal); self.load(blk0, blk0[:], self.i_blk0)
            S.dma(lambda e: e.dma_start(out=akT[:], in_=self.s_akT.rearrange("h (c s) -> h c s", c=2)), reads=[self.k_akT], writes=[akT.k])
            S.dma(lambda e: e.dma_start(out=vall[:], in_=self.s_v.rearrange("(p t) n -> t p n", t=128)), reads=[self.k_v], writes=[vall.k])
            S.dma(lambda e: e.dma_start(out=ikT[:], in_=self.s_ikT), reads=[self.k_ikT], writes=[ikT.k])
            self.load_weight_bf16(es, woutb, self.i_wout, 1024, stg)
            S.op("dve", lambda e: e.tensor_reduce(out=negM[:, 0:1], in_=gq[:], axis=AX.X, op=ALU.max, apply_absolute_value=True), reads=[gq.k], writes=[negM.k])
            S.op("dve", lambda e: e.tensor_reduce(out=negM[:, 1:2], in_=gk[:], axis=AX.X, op=ALU.max, apply_absolute_value=True), reads=[gk.k], writes=[negM.k])
            S.op("dve", lambda e: e.scalar_tensor_tensor(out=negM[:, 0:1], in0=negM[:, 0:1], scalar=float(-(128.0 ** 0.5)), in1=negM[:, 1:2], op0=ALU.mult, op1=ALU.mult), reads=[negM.k], writes=[negM.k])
            aqT = sb("aqT", [128, 1024], BF16); iqT = sb("iqT", [128, 512], F32); iw = sb("iw", [128, 8], F32)
            sgb = sb("sgb", [128, 1024], BF16); mg = sb("mg", [128, 1024], BF16); xb = sb("xb2", [128, 1024], F32)
            rl = [sb("rl%d" % i, [128, 512], F32) for i in range(2)]
            pT = [sb("pT%d" % i, [128, 512], BF16) for i in range(2)]
            mT = [sb("mT%d" % i, [128, 1024], BF16) for i in range(2)]
            st = sb("bst", [128, 8], F32)
            rec = sb("rec", [128, 8], F32)
            oatt = sb("oatt", [128, 1024], F32); mrg = sb("mrg", [128, 1024], BF16); mT2 = sb("mT2", [128, 8, 128], BF16)
            x1 = sb("x1", [128, 1024], F32)
            accb = [ps[4], ps[5], ps[6]]
            hb = [(0, 0), (0, 1), (0, 2), (1, 0), (1, 1), (1, 2), (2, 0), (2, 1)]
            for j in range(NO):
                p = 2 * j + 1
                nk = p + 1
                nkeys = nk * 128
                ko = self.k_own[j]
                S.dma(lambda e, j=j: e.dma_start(out=aqT[:], in_=self.s_aqT[j]), reads=[ko["aqT"]], writes=[aqT.k])
                S.dma(lambda e, j=j: e.dma_start(out=iqT[:], in_=self.s_iqT[j]), reads=[ko["iqT"]], writes=[iqT.k])
                S.dma(lambda e, j=j: e.dma_start(out=iw[:], in_=self.s_iw[j]), reads=[ko["iw"]], writes=[iw.k])
                S.dma(lambda e, j=j: e.dma_start(out=sgb[:], in_=self.s_sgb[j]), reads=[ko["sgb"]], writes=[sgb.k])
                S.dma(lambda e, j=j: e.dma_start(out=mg[:], in_=self.s_mg[j]), reads=[ko["mg"]], writes=[mg.k])
                S.dma(lambda e, p=p: e.dma_start(out=xb[:], in_=self.i_x[p * 128:(p + 1) * 128, :]), writes=[xb.k])
                it = 0
                for kg in range((nkeys + 511) // 512):
                    k0 = kg * 512
                    w = min(512, nkeys - k0)
                    for h in range(8):
                        pb = ps[it % 2]; r = rl[it % 2]; it += 1
                        b0 = (h % 2) * 64
                        S.op("pe", lambda e, pb=pb, h=h, b0=b0, k0=k0, w=w: e.matmul(pb[:, 0:w], lhsT=iqT[b0:b0 + 64, (h // 2) * 128:(h // 2 + 1) * 128], rhs=ikT[b0:b0 + 64, k0:k0 + w], start=True, stop=True), reads=[iqT.k, ikT.k], writes=[pb.k])
                        S.op("act", lambda e, pb=pb, r=r, w=w: e.activation(out=r[:, 0:w], in_=pb[:, 0:w], func=AF.Relu), reads=[pb.k], writes=[r.k])
                        if h == 0:
                            S.op("dve", lambda e, r=r, k0=k0, w=w: e.tensor_scalar(out=score[:, k0:k0 + w], in0=r[:, 0:w], scalar1=iw[:, 0:1], scalar2=None, op0=ALU.mult), reads=[r.k, iw.k], writes=[score.k])
                        else:
                            S.op("dve", lambda e, r=r, k0=k0, w=w, h=h: e.scalar_tensor_tensor(out=score[:, k0:k0 + w], in0=r[:, 0:w], scalar=iw[:, h:h + 1], in1=score[:, k0:k0 + w], op0=ALU.mult, op1=ALU.add), reads=[r.k, iw.k, score.k], writes=[score.k])
                S.op("dve", lambda e, nkeys=nkeys: e.tensor_reduce(out=st[:, 5:6], in_=score[:, 0:nkeys], axis=AX.X, op=ALU.max), reads=[score.k], writes=[st.k])
                S.op("dve", lambda e, nkeys=nkeys: e.tensor_reduce(out=st[:, 6:7], in_=score[:, 0:nkeys], axis=AX.X, op=ALU.min), reads=[score.k], writes=[st.k])
                S.op("dve", lambda e: e.tensor_scalar(out=st[:, 0:1], in0=st[:, 6:7], scalar1=-1.0, scalar2=None, op0=ALU.add), reads=[st.k], writes=[st.k])
                S.op("dve", lambda e: e.tensor_tensor(out=st[:, 1:2], in0=st[:, 5:6], in1=st[:, 0:1], op=ALU.subtract), reads=[st.k], writes=[st.k])
                S.op("dve", lambda e, nkeys=nkeys: e.tensor_tensor(out=score[:, nkeys - 128:nkeys], in0=score[:, nkeys - 128:nkeys], in1=causal[:], op=ALU.add), reads=[score.k, causal.k], writes=[score.k])
                S.op("dve", lambda e: e.tensor_tensor(out=score[:, 0:128], in0=score[:, 0:128], in1=blk0[:], op=ALU.add), reads=[score.k, blk0.k], writes=[score.k])
                for it_ in range(1, NBIS + 1):
                    f = float(2.0 ** -it_)
                    S.op("dve", lambda e, f=f: e.scalar_tensor_tensor(out=st[:, 2:3], in0=st[:, 1:2], scalar=f, in1=st[:, 0:1], op0=ALU.mult, op1=ALU.add), reads=[st.k], writes=[st.k])
                    S.op("dve", lambda e, nkeys=nkeys: e.tensor_scalar(out=msk[:, 0:nkeys], in0=score[:, 0:nkeys], scalar1=st[:, 2:3], scalar2=0.0, op0=ALU.is_gt, op1=ALU.add, accum_out=st[:, 3:4]), reads=[score.k, st.k], writes=[msk.k, st.k])
                    S.op("dve", lambda e, f=f: e.tensor_scalar(out=st[:, 4:5], in0=st[:, 3:4], scalar1=float(min(256, S_ // 4)) - 0.5, scalar2=f, op0=ALU.is_gt, op1=ALU.mult), reads=[st.k], writes=[st.k])
                    S.op("dve", lambda e: e.scalar_tensor_tensor(out=st[:, 0:1], in0=st[:, 4:5], scalar=st[:, 1:2], in1=st[:, 0:1], op0=ALU.mult, op1=ALU.add), reads=[st.k], writes=[st.k])
                S.op("dve", lambda e, nkeys=nkeys: e.tensor_scalar(out=msk[:, 0:nkeys], in0=score[:, 0:nkeys], scalar1=st[:, 0:1], scalar2=None, op0=ALU.is_gt), reads=[score.k, st.k], writes=[msk.k])
                sti = 0
                for kb in range(nk):
                    if kb % 8 == 0:
                        m = mT[(kb // 8) % 2]
                        nb = min(8, nk - kb)

                        def trm(e, kb=kb, nb=nb):
                            for i_ in range(nb):
                                ins = e.transpose(out=psb[:, i_ * 128:(i_ + 1) * 128], in_=msk[:, (kb + i_) * 128:(kb + i_ + 1) * 128], identity=self.identb[:])
                            return ins
                        S.op("pe", trm, reads=[msk.k, self.identb.k], writes=[psb.k])
                        S.op("act", lambda e, m=m, nb=nb: e.activation(out=m[:, 0:nb * 128], in_=psb[:, 0:nb * 128], func=AF.Copy), reads=[psb.k], writes=[m.k])
                    for c in range(2):
                        pss = ps[2 + sti % 2]; pt = pT[sti % 2]; sti += 1
                        S.op("pe", lambda e, pss=pss, c=c, kb=kb: e.matmul(pss[:], lhsT=akT[:, c, kb * 128:(kb + 1) * 128], rhs=aqT[:, c * 512:(c + 1) * 512], start=True, stop=True), reads=[akT.k, aqT.k], writes=[pss.k])
                        S.op("act", lambda e, pss=pss, pt=pt: e.activation(out=pt[:], in_=pss[:], func=AF.Exp, bias=negM[:, 0:1], scale=1.0), reads=[pss.k, negM.k], writes=[pt.k])
                        mi = (kb % 8) * 128
                        S.op("pool", lambda e, pt=pt, m=m, mi=mi: e.tensor_tensor(out=pt[:].rearrange("p (h q) -> p h q", h=4), in0=pt[:].rearrange("p (h q) -> p h q", h=4), in1=bc(m[:, mi:mi + 128].unsqueeze(1), [128, 4, 128]), op=ALU.mult), reads=[pt.k, m.k], writes=[pt.k])

                        def pv(e, pt=pt, c=c, kb=kb, nk=nk):
                            for hh in range(4):
                                h = c * 4 + hh
                                bk, sl = hb[h]
                                ins = e.matmul(accb[bk][:, sl * 129:(sl + 1) * 129], lhsT=pt[:, hh * 128:(hh + 1) * 128], rhs=vall[:, kb, c * 129:(c + 1) * 129], start=(kb == 0 and sl == 0), stop=(kb == nk - 1), skip_group_check=True)
                            return ins
                        S.op("pe", pv, reads=[pt.k, vall.k], writes=[accb[hb[c * 4][0]].k, accb[hb[c * 4 + 3][0]].k])
                for bk, (h0, n) in enumerate(((0, 3), (3, 3), (6, 2))):
                    a3 = accb[bk][:, 0:n * 129].rearrange("p (h d) -> p h d", h=n)
                    S.op("dve", lambda e, a3=a3, h0=h0, n=n: e.reciprocal(out=rec[:, h0:h0 + n], in_=a3[:, :, 128]), reads=[accb[bk].k], writes=[rec.k])
                    S.op("dve", lambda e, a3=a3, h0=h0, n=n: e.tensor_tensor(out=oatt[:, h0 * 128:(h0 + n) * 128].rearrange("p (h d) -> p h d", h=n), in0=a3[:, :, 0:128], in1=bc(rec[:, h0:h0 + n].unsqueeze(2), [128, n, 128]), op=ALU.mult), reads=[accb[bk].k, rec.k], writes=[oatt.k])
                S.op("pool", lambda e: e.tensor_tensor(out=oatt[:], in0=oatt[:], in1=sgb[:], op=ALU.mult), reads=[oatt.k, sgb.k], writes=[oatt.k])
                S.op("pool", lambda e: e.tensor_tensor(out=mrg[:], in0=oatt[:], in1=mg[:], op=ALU.add), reads=[oatt.k, mg.k], writes=[mrg.k])

                def trg(e):
                    for kc in range(8):
                        ins = e.transpose(out=psb[:, kc * 128:(kc + 1) * 128], in_=mrg[:, kc * 128:(kc + 1) * 128], identity=self.identb[:])
                    return ins
                S.op("pe", trg, reads=[mrg.k, self.identb.k], writes=[psb.k])
                S.op("act", lambda e: e.activation(out=mT2[:].rearrange("p k t -> p (k t)"), in_=psb[:], func=AF.Copy), reads=[psb.k], writes=[mT2.k])
                for half in range(2):
                    pq = ps[half]
                    self.proj(pq, 512, mT2, woutb, half * 512)
                    S.op("dve", lambda e, pq=pq, half=half: e.tensor_tensor(out=x1[:, half * 512:(half + 1) * 512], in0=xb[:, half * 512:(half + 1) * 512], in1=pq[:], op=ALU.add), reads=[xb.k, pq.k], writes=[x1.k])
                S.dma(lambda e, j=j: e.dma_start(out=self.s_x1[j], in_=x1[:]), reads=[x1.k], writes=[ko["x1"]])
            S.barrier()
            S.emit()

    def phase3(self, G=4):
        S = self.S
        NO = self.NO
        ps, psb = self.ps, self.psb
        G = min(G, NO)
        with ExitStack() as es:
            sb = lambda n, sh, dt: self.sb(es, n, sh, dt)
            wup = sb("wupb", [128, 8, 4096], BF16); wdn = sb("wdnb", [128, 32, 1024], BF16)
            stg = [sb("stg3_%d" % i, [128, 512], F32) for i in range(2)]
            gmlp = sb("gmlp", [128, 8], F32)
            self.load(gmlp, gmlp[:], self.i_gmlp)
            self.load_weight_bf16(es, wup, self.i_wup, 4096, stg)
            self.load_weight_bf16(es, wdn, self.i_wdn, 1024, stg)
            x1 = sb("x1g", [128, G, 1024], F32)
            ss = sb("ss3", [128, 1], F32); xnb = sb("xnb3", [128, 1024], BF16)
            h2T = sb("h2T", [128, 8, G * 128], BF16)
            hTb = sb("hTb3", [128, 8, 128], BF16)
            hid = sb("hidT", [128, 32, G * 128], BF16)
            rl = [sb("rl3_%d" % i, [128, G * 128], F32) for i in range(2)]
            ot = [sb("ot%d" % i, [128, 1024], F32) for i in range(1)]
            for g0 in range(0, NO, G):
                xv = [TL(None, "x1v%d" % b) for b in range(G)]
                for b in range(G):
                    j = g0 + b
                    xt = TL(x1.t, "x1g%d" % b)
                    S.dma(lambda e, j=j, b=b: e.dma_start(out=x1[:, b, :], in_=self.s_x1[j]), reads=[self.k_own[j]["x1"]], writes=[x1.k])
                for b in range(G):
                    xbv = TL(x1.t[:, b, :], "x")
                    xbv.k = x1.k
                    self.rms_hT(xbv, ss, xnb, hTb, gmlp, psb)
                    S.op("pool", lambda e, b=b: e.tensor_copy(out=h2T[:, :, b * 128:(b + 1) * 128], in_=hTb[:]), reads=[hTb.k], writes=[h2T.k])
                for f in range(32):
                    pq = ps[f % 2]; r = rl[f % 2]

                    def up(e, pq=pq, f=f):
                        for kc in range(8):
                            ins = e.matmul(pq[:, 0:G * 128], lhsT=wup[:, kc, f * 128:(f + 1) * 128], rhs=h2T[:, kc, :], start=(kc == 0), stop=(kc == 7))
                        return ins
                    S.op("pe", up, reads=[wup.k, h2T.k], writes=[pq.k])
                    S.op("act", lambda e, pq=pq, r=r: e.activation(out=r[:], in_=pq[:, 0:G * 128], func=AF.Relu), reads=[pq.k], writes=[r.k])
                    S.op("dve" if f % 2 else "pool", lambda e, r=r, f=f: e.tensor_tensor(out=hid[:, f, :], in0=r[:], in1=r[:], op=ALU.mult), reads=[r.k], writes=[hid.k])
                for b in range(G):
                    j = g0 + b
                    o = ot[0]
                    for half in range(2):
                        pq = ps[2 + half]

                        def dn(e, pq=pq, b=b, half=half):
                            for f in range(32):
                                ins = e.matmul(pq[:], lhsT=hid[:, f, b * 128:(b + 1) * 128], rhs=wdn[:, f, half * 512:(half + 1) * 512], start=(f == 0), stop=(f == 31))
                            return ins
                        S.op("pe", dn, reads=[hid.k, wdn.k], writes=[pq.k])
                        S.op("dve", lambda e, pq=pq, o=o, b=b, half=half: e.tensor_tensor(out=o[:, half * 512:(half + 1) * 512], in0=x1[:, b, half * 512:(half + 1) * 512], in1=pq[:], op=ALU.add), reads=[x1.k, pq.k], writes=[o.k])
                    S.dma(lambda e, j=j, o=o: e.dma_start(out=self.o_out[j * 128:(j + 1) * 128, :], in_=o[:]), reads=[o.k], writes=[])
            S.barrier()
            S.emit()


    def phase1a(self):
        S = self.S
        P = self.P
        ps, psb = self.ps, self.psb
        with ExitStack() as es:
            sb = lambda n, sh, dt: self.sb(es, n, sh, dt)
            wb = sb("w1a", [128, 8, C1A], BF16)
            stg = [sb("stga%d" % i, [128, 1024], F32) for i in range(2)]
            gmix = sb("gmixa", [128, 8], F32); ggdn = sb("ggdn", [128, 128], F32)
            aexp = sb("aexp", [128, 8], F32); dtb = sb("dtb", [128, 8], F32)
            convw = sb("convw", [128, 24, 4], F32)
            uinc = sb("uinc", [128, 128], F32); onesf = sb("onesf", [128, 128], F32); onesb = sb("onesb", [128, 128], BF16)
            mskL4 = sb("mskL4", [128, 4, 128], BF16); mskU4 = sb("mskU4", [128, 4, 128], BF16)
            self.load(gmix, gmix[:], self.i_gmix); self.load(ggdn, ggdn[:], self.i_ggdn)
            self.load(aexp, aexp[:], self.i_alog); self.load(dtb, dtb[:], self.i_dtb)
            self.load(convw, convw[:].rearrange("p t j -> p (t j)"), self.i_convw)
            self.load(uinc, uinc[:], self.i_uinc)
            for q4 in range(4):
                self.load(mskL4, mskL4[:, q4, :], self.i_mskL); self.load(mskU4, mskU4[:, q4, :], self.i_mskU)
            S.op("pool", lambda e: e.memset(onesf[:], 1.0), writes=[onesf.k])
            S.op("pool", lambda e: e.memset(onesb[:], 1.0), writes=[onesb.k])
            S.op("act", lambda e: e.activation(out=aexp[:], in_=aexp[:], func=AF.Exp), reads=[aexp.k], writes=[aexp.k])
            self.load_weight_bf16(es, wb, self.i_w1a, C1A, stg)
            xb = sb("xba", [128, 1024], F32); ss = sb("ssa", [128, 1], F32); xnb = sb("xnba", [128, 1024], BF16); hT = sb("hTa", [128, 8, 128], BF16)
            U = sb("U", [128, 24, 131], F32)
            Yg = sb("Yg", [128, 8, 128], F32); Ct = sb("Ct", [128, 8, 128], F32)
            sq = sb("sqa", [128, 1024], BF16); rn = sb("rn", [128, 1024], F32)
            qT = sb("qT", [128, 8, 128], BF16); kT = sb("kT", [128, 8, 128], BF16); vTb = sb("vTb", [128, 8, 128], BF16)
            ktok = sb("ktok", [128, 8, 128], F32)
            kbg = sb("kbg", [128, 8, 128], BF16); kdec = sb("kdec", [128, 8, 128], BF16); vb = sb("vb", [128, 8, 128], BF16)
            sm = sb("sm", [128, 96], F32)
            sm2 = sb("sm2", [128, 16], F32)
            Rall = sb("Rall", [128, 8, 128], F32)
            Dls = sb("Dls", [128, 8, 128], F32); DT = sb("DTm", [128, 8, 128], F32); egr = sb("egr", [128, 8, 128], F32)
            qdT = sb("qdT", [128, 8, 128], BF16); QKT = sb("QKT", [128, 8, 128], BF16)
            Am = [sb("Am%d" % i, [128, 8, 128], F32) for i in range(2)]
            Bm = [sb("Bm%d" % i, [128, 8, 128], F32) for i in range(2)]
            Pm = sb("Pm", [128, 8, 128], F32); TTb = sb("TTb", [128, 8, 128], BF16)
            u = sb("u", [128, 8, 128], F32); wT = sb("wT", [128, 8, 128], BF16); vn = sb("vn", [128, 8, 128], BF16)
            S32 = sb("S32", [128, 8, 128], F32); Stmp = sb("Stmp", [128, 8, 128], F32); Sbf = sb("Sbf", [128, 8, 128], BF16)
            o = sb("o", [128, 8, 128], F32); osq = sb("osq", [128, 8, 128], F32)
            zg = sb("zga", [128, 1024], BF16); mgt = sb("mgt", [128, 1024], BF16)
            S.op("pool", lambda e: e.memset(U[:], 0.0), writes=[U.k])
            S.op("pool", lambda e: e.memset(S32[:], 0.0), writes=[S32.k])
            S.op("pool", lambda e: e.memset(Sbf[:], 0.0), writes=[Sbf.k])
            f4 = lambda t, g: t[:, g * 4:(g + 1) * 4, :].rearrange("p h d -> p (h d)")

            def batch_mm(banks, fn_h):
                for g in range(2):
                    def f(e, g=g):
                        for hh in range(4):
                            h = g * 4 + hh
                            ins = fn_h(e, banks[g][:, hh * 128:(hh + 1) * 128], h)
                        return ins
                    yield g, f

            for p in range(P):
                own = (p % 2 == 1)
                j = p // 2
                r0 = p * 128
                self.load(xb, xb[:], self.i_x[r0:r0 + 128, :])
                if own:
                    S.dma(lambda e, j=j: e.dma_start(out=zg[:], in_=self.s_zg[j]), reads=[self.k_own[j]["zg"]], writes=[zg.k])
                self.rms_hT(xb, ss, xnb, hT, gmix, psb)
                for grp in range(6):
                    pq = ps[grp % 4]

                    def fm(e, pq=pq, grp=grp):
                        for t4 in range(4):
                            ct = grp * 4 + t4
                            for kc in range(8):
                                ins = e.matmul(pq[:, t4 * 128:(t4 + 1) * 128], lhsT=wb[:, kc, ct * 128:(ct + 1) * 128], rhs=hT[:, kc, :], start=(kc == 0), stop=(kc == 7), skip_group_check=True)
                        return ins
                    S.op("pe", fm, reads=[wb.k, hT.k], writes=[pq.k])
                    S.op("act", lambda e, pq=pq, grp=grp: e.activation(out=U[:, grp * 4:(grp + 1) * 4, 3:131], in_=pq[:].rearrange("p (t n) -> p t n", t=4), func=AF.Copy), reads=[pq.k], writes=[U.k])
                self.proj(ps[6], 16, hT, wb, 3072)
                S.op("act", lambda e: e.activation(out=sm[:, 0:16], in_=ps[6][:, 0:16], func=AF.Copy), reads=[ps[6].k], writes=[sm.k])
                S.op("act", lambda e: e.activation(out=sm[:, 16:24], in_=sm[:, 8:16], func=AF.Sigmoid), reads=[sm.k], writes=[sm.k])
                S.op("dve", lambda e: e.tensor_tensor(out=sm[:, 24:32], in0=sm[:, 0:8], in1=dtb[:], op=ALU.add), reads=[sm.k, dtb.k], writes=[sm.k])
                S.op("act", lambda e: e.activation(out=sm[:, 32:40], in_=sm[:, 24:32], func=AF.Abs), reads=[sm.k], writes=[sm.k])
                S.op("act", lambda e: e.activation(out=sm[:, 32:40], in_=sm[:, 32:40], func=AF.Exp, scale=-1.0), reads=[sm.k], writes=[sm.k])
                S.op("dve", lambda e: e.tensor_scalar(out=sm[:, 32:40], in0=sm[:, 32:40], scalar1=1.0, scalar2=None, op0=ALU.add), reads=[sm.k], writes=[sm.k])
                S.op("act", lambda e: e.activation(out=sm[:, 32:40], in_=sm[:, 32:40], func=AF.Ln), reads=[sm.k], writes=[sm.k])
                S.op("dve", lambda e: e.scalar_tensor_tensor(out=sm[:, 40:48], in0=sm[:, 24:32], scalar=0.0, in1=sm[:, 32:40], op0=ALU.max, op1=ALU.add), reads=[sm.k], writes=[sm.k])
                S.op("dve", lambda e: e.scalar_tensor_tensor(out=sm[:, 48:56], in0=sm[:, 40:48], scalar=-1.0, in1=aexp[:], op0=ALU.mult, op1=ALU.mult), reads=[sm.k, aexp.k], writes=[sm.k])

                def cs(e):
                    e.matmul(ps[6][:, 16:24], lhsT=uinc[:], rhs=sm[:, 48:56], start=True, stop=True, skip_group_check=True)
                    return e.matmul(ps[6][:, 24:32], lhsT=onesf[:], rhs=sm[:, 48:56], start=True, stop=True, skip_group_check=True)
                S.op("pe", cs, reads=[uinc.k, onesf.k, sm.k], writes=[ps[6].k])
                S.op("act", lambda e: e.activation(out=sm[:, 56:72], in_=ps[6][:, 16:32], func=AF.Copy), reads=[ps[6].k], writes=[sm.k])
                S.op("act", lambda e: e.activation(out=sm[:, 72:80], in_=sm[:, 56:64], func=AF.Exp), reads=[sm.k], writes=[sm.k])
                S.op("dve", lambda e: e.tensor_tensor(out=sm[:, 80:88], in0=sm[:, 64:72], in1=sm[:, 56:64], op=ALU.subtract), reads=[sm.k], writes=[sm.k])
                S.op("act", lambda e: e.activation(out=sm[:, 80:96], in_=sm[:, 80:96] if False else sm[:, 80:88], func=AF.Exp) if False else e.activation(out=sm[:, 80:88], in_=sm[:, 80:88], func=AF.Exp), reads=[sm.k], writes=[sm.k])
                S.op("act", lambda e: e.activation(out=sm[:, 88:96], in_=sm[:, 64:72], func=AF.Exp), reads=[sm.k], writes=[sm.k])
                S.op("dve", lambda e: e.tensor_tensor(out=sm2[:, 0:8], in0=sm[:, 16:24], in1=sm[:, 72:80], op=ALU.mult), reads=[sm.k], writes=[sm2.k])
                for grp, dst in ((0, qT), (1, kT), (2, vTb)):
                    if grp == 0 and not own:
                        continue
                    Ug = lambda jj, grp=grp: U[:, grp * 8:(grp + 1) * 8, jj:jj + 128]
                    cw = lambda jj, grp=grp: bc(convw[:, grp * 8:(grp + 1) * 8, jj:jj + 1], [128, 8, 128])
                    S.op("dve", lambda e, Ug=Ug, cw=cw: e.tensor_tensor(out=Yg[:], in0=Ug(0), in1=cw(0), op=ALU.mult), reads=[U.k, convw.k], writes=[Yg.k])
                    for jj in range(1, 4):
                        S.op("pool", lambda e, Ug=Ug, cw=cw, jj=jj: e.tensor_tensor(out=Ct[:], in0=Ug(jj), in1=cw(jj), op=ALU.mult), reads=[U.k, convw.k], writes=[Ct.k])
                        S.op("dve", lambda e: e.tensor_tensor(out=Yg[:], in0=Yg[:], in1=Ct[:], op=ALU.add), reads=[Yg.k, Ct.k], writes=[Yg.k])
                    Yf = Yg[:].rearrange("p h d -> p (h d)")
                    if grp == 2:
                        S.op("act", lambda e, Yf=Yf: e.activation(out=vTb[:].rearrange("p h d -> p (h d)"), in_=Yf, func=AF.Silu), reads=[Yg.k], writes=[vTb.k])
                        continue
                    S.op("act", lambda e, Yf=Yf: e.activation(out=Yf, in_=Yf, func=AF.Silu), reads=[Yg.k], writes=[Yg.k])
                    S.op("pool", lambda e, Yf=Yf: e.tensor_tensor(out=sq[:], in0=Yf, in1=Yf, op=ALU.mult), reads=[Yg.k], writes=[sq.k])
                    for g in range(2):
                        pq = ps[g]
                        S.op("pe", lambda e, pq=pq, g=g: e.matmul(pq[:], lhsT=onesb[:], rhs=sq[:, g * 512:(g + 1) * 512], start=True, stop=True), reads=[onesb.k, sq.k], writes=[pq.k])
                        S.op("dve", lambda e, pq=pq, g=g: e.tensor_scalar(out=rn[:, g * 512:(g + 1) * 512], in0=pq[:], scalar1=EPS, scalar2=None, op0=ALU.add), reads=[pq.k], writes=[rn.k])
                    S.op("act", lambda e: e.activation(out=rn[:], in_=rn[:], func=AF.Ln), reads=[rn.k], writes=[rn.k])
                    S.op("act", lambda e: e.activation(out=rn[:], in_=rn[:], func=AF.Exp, scale=-0.5), reads=[rn.k], writes=[rn.k])
                    sc = float(128.0 ** -0.5) if grp == 0 else 1.0
                    S.op("dve", lambda e, Yf=Yf, dst=dst, sc=sc: e.scalar_tensor_tensor(out=dst[:].rearrange("p h d -> p (h d)"), in0=Yf, scalar=sc, in1=rn[:], op0=ALU.mult, op1=ALU.mult), reads=[Yg.k, rn.k], writes=[dst.k])
                S.op("pool", lambda e: e.tensor_copy(out=U[:, :, 0:3], in_=U[:, :, 128:131]), reads=[U.k], writes=[U.k])
                def trk(e):
                    for h in range(8):
                        ins = e.transpose(out=psb[:, h * 128:(h + 1) * 128], in_=kT[:, h, :], identity=self.identb[:])
                    return ins
                S.op("pe", trk, reads=[kT.k, self.identb.k], writes=[psb.k])
                S.op("act", lambda e: e.activation(out=ktok[:].rearrange("p h d -> p (h d)"), in_=psb[:], func=AF.Copy), reads=[psb.k], writes=[ktok.k])
                S.op("dve", lambda e: e.tensor_tensor(out=kbg[:], in0=ktok[:], in1=bc(sm2[:, 0:8].unsqueeze(2), [128, 8, 128]), op=ALU.mult), reads=[ktok.k, sm2.k], writes=[kbg.k])
                S.op("pool", lambda e: e.tensor_tensor(out=kdec[:], in0=ktok[:], in1=bc(sm[:, 80:88].unsqueeze(2), [128, 8, 128]), op=ALU.mult), reads=[ktok.k, sm.k], writes=[kdec.k])

                def trv(e):
                    for h in range(8):
                        ins = e.transpose(out=psb[:, h * 128:(h + 1) * 128], in_=vTb[:, h, :], identity=self.identb[:])
                    return ins
                S.op("pe", trv, reads=[vTb.k, self.identb.k], writes=[psb.k])
                S.op("dve", lambda e: e.tensor_tensor(out=vb[:], in0=psb[:].rearrange("p (h d) -> p h d", h=8), in1=bc(sm[:, 16:24].unsqueeze(2), [128, 8, 128]), op=ALU.mult), reads=[psb.k, sm.k], writes=[vb.k])
                S.op("dve", lambda e: e.tensor_tensor(out=Rall[:], in0=bc(uinc[:].unsqueeze(1), [128, 8, 128]), in1=bc(sm[:, 48:56].unsqueeze(2), [128, 8, 128]), op=ALU.mult), reads=[uinc.k, sm.k], writes=[Rall.k])
                for g in range(2):
                    pq = ps[4 + g]

                    def grl(e, pq=pq, g=g):
                        e.matmul(pq[:], lhsT=onesf[:], rhs=f4(Rall, g), start=True, stop=False)
                        return e.matmul(pq[:], lhsT=self.identb[:], rhs=mskL4[:].rearrange("p h d -> p (h d)"), start=False, stop=True)
                    S.op("pe", grl, reads=[onesf.k, Rall.k, self.identb.k, mskL4.k], writes=[pq.k])
                    for hh in range(4):
                        h = g * 4 + hh
                        S.op("act", lambda e, pq=pq, h=h, hh=hh: e.activation(out=Dls[:, h, :], in_=pq[:, hh * 128:(hh + 1) * 128], func=AF.Exp, scale=-1.0, bias=sm[:, 56 + h:57 + h]), reads=[pq.k, sm.k], writes=[Dls.k])
                if own:
                    for g in range(2):
                        pq = ps[2 + g]
                        S.op("pe", lambda e, pq=pq, g=g: e.matmul(pq[:], lhsT=onesf[:], rhs=f4(Rall, g), start=True, stop=False), reads=[onesf.k, Rall.k], writes=[pq.k])
                        S.op("act", lambda e, pq=pq, g=g: e.activation(out=f4(egr, g), in_=pq[:], func=AF.Exp), reads=[pq.k], writes=[egr.k])
                        S.op("pe", lambda e, pq=pq: e.matmul(pq[:], lhsT=self.identb[:], rhs=mskU4[:].rearrange("p h d -> p (h d)"), start=False, stop=True), reads=[self.identb.k, mskU4.k], writes=[pq.k])
                        S.op("pool", lambda e, g=g: e.tensor_scalar(out=sm2[:, 8:16], in0=sm[:, 56:64], scalar1=-1.0, scalar2=None, op0=ALU.mult), reads=[sm.k], writes=[sm2.k])
                        for hh in range(4):
                            h = g * 4 + hh
                            S.op("act", lambda e, pq=pq, h=h, hh=hh: e.activation(out=DT[:, h, :], in_=pq[:, hh * 128:(hh + 1) * 128], func=AF.Exp, scale=1.0, bias=sm2[:, 8 + h:9 + h]), reads=[pq.k, sm2.k], writes=[DT.k])
                    S.op("pool", lambda e: e.tensor_tensor(out=qdT[:], in0=qT[:], in1=egr[:], op=ALU.mult), reads=[qT.k, egr.k], writes=[qdT.k])
                    for g, f in batch_mm((ps[0], ps[1]), lambda e, out, h: e.matmul(out, lhsT=kT[:, h, :], rhs=qT[:, h, :], start=True, stop=True, skip_group_check=True)):
                        S.op("pe", f, reads=[kT.k, qT.k], writes=[ps[g].k])
                        S.op("dve", lambda e, g=g: e.tensor_tensor(out=f4(QKT, g), in0=ps[g][:], in1=f4(DT, g), op=ALU.mult), reads=[ps[g].k, DT.k], writes=[QKT.k])
                A0, B0 = Am[0], Bm[0]
                for g, f in batch_mm((ps[0], ps[1]), lambda e, out, h: e.matmul(out, lhsT=kT[:, h, :], rhs=kT[:, h, :], start=True, stop=True, skip_group_check=True)):
                    S.op("pe", f, reads=[kT.k], writes=[ps[g].k])
                    for hh in range(4):
                        h = g * 4 + hh
                        S.op("dve", lambda e, g=g, h=h, hh=hh: e.scalar_tensor_tensor(out=A0[:, h, :], in0=ps[g][:, hh * 128:(hh + 1) * 128], scalar=sm[:, 16 + h:17 + h], in1=Dls[:, h, :], op0=ALU.mult, op1=ALU.mult), reads=[ps[g].k, sm.k, Dls.k], writes=[A0.k])
                for g, f in batch_mm((ps[2], ps[3]), lambda e, out, h: e.transpose(out=out, in_=A0[:, h, :], identity=self.identf[:])):
                    S.op("pe", f, reads=[A0.k, self.identf.k], writes=[ps[2 + g].k])
                    S.op("act", lambda e, g=g: e.activation(out=f4(B0, g), in_=ps[2 + g][:], func=AF.Copy), reads=[ps[2 + g].k], writes=[B0.k])
                S.op("pool", lambda e: e.tensor_tensor(out=Pm[:], in0=bc(self.identf[:].unsqueeze(1), [128, 8, 128]), in1=B0[:], op=ALU.subtract), reads=[self.identf.k, B0.k], writes=[Pm.k])
                cur = 0
                for lv in range(6):
                    Ac, Bc = Am[cur], Bm[cur]
                    An, Bn_ = Am[1 - cur], Bm[1 - cur]
                    last = (lv == 5)
                    for g, f in batch_mm((ps[0], ps[1]), lambda e, out, h, Ac=Ac, Bc=Bc: e.matmul(out, lhsT=Bc[:, h, :], rhs=Ac[:, h, :], start=True, stop=True, skip_group_check=True)):
                        S.op("pe", f, reads=[Ac.k, Bc.k], writes=[ps[g].k])
                        S.op("act", lambda e, g=g, An=An: e.activation(out=f4(An, g), in_=ps[g][:], func=AF.Copy), reads=[ps[g].k], writes=[An.k])
                    if not last:
                        for g, f in batch_mm((ps[2], ps[3]), lambda e, out, h, Ac=Ac, Bc=Bc: e.matmul(out, lhsT=Ac[:, h, :], rhs=Bc[:, h, :], start=True, stop=True, skip_group_check=True)):
                            S.op("pe", f, reads=[Ac.k, Bc.k], writes=[ps[2 + g].k])
                            S.op("dve", lambda e, g=g, Bn_=Bn_: e.tensor_copy(out=f4(Bn_, g), in_=ps[2 + g][:]), reads=[ps[2 + g].k], writes=[Bn_.k])
                    for g, f in batch_mm((ps[4], ps[5]), lambda e, out, h, An=An: e.matmul(out, lhsT=An[:, h, :], rhs=Pm[:, h, :], start=True, stop=True, skip_group_check=True)):
                        S.op("pe", f, reads=[An.k, Pm.k], writes=[ps[4 + g].k])
                        S.op("dve", lambda e, g=g: e.tensor_tensor(out=f4(Pm, g), in0=f4(Pm, g), in1=ps[4 + g][:], op=ALU.add), reads=[Pm.k, ps[4 + g].k], writes=[Pm.k])
                    cur = 1 - cur
                S.op("pool", lambda e: e.tensor_copy(out=TTb[:], in_=Pm[:]), reads=[Pm.k], writes=[TTb.k])
                for g, f in batch_mm((ps[0], ps[1]), lambda e, out, h: e.matmul(out, lhsT=TTb[:, h, :], rhs=vb[:, h, :], start=True, stop=True, skip_group_check=True)):
                    S.op("pe", f, reads=[TTb.k, vb.k], writes=[ps[g].k])
                    S.op("act", lambda e, g=g: e.activation(out=f4(u, g), in_=ps[g][:], func=AF.Copy), reads=[ps[g].k], writes=[u.k])
                for g, f in batch_mm((ps[2], ps[3]), lambda e, out, h: e.matmul(out, lhsT=kbg[:, h, :], rhs=TTb[:, h, :], start=True, stop=True, skip_group_check=True)):
                    S.op("pe", f, reads=[TTb.k, kbg.k], writes=[ps[2 + g].k])
                    S.op("act", lambda e, g=g: e.activation(out=f4(wT, g), in_=ps[2 + g][:], func=AF.Copy), reads=[ps[2 + g].k], writes=[wT.k])
                for g, f in batch_mm((ps[4], ps[5]), lambda e, out, h: e.matmul(out, lhsT=wT[:, h, :], rhs=Sbf[:, h, :], start=True, stop=True, skip_group_check=True)):
                    S.op("pe", f, reads=[wT.k, Sbf.k], writes=[ps[4 + g].k])
                    S.op("dve", lambda e, g=g: e.tensor_tensor(out=f4(vn, g), in0=f4(u, g), in1=ps[4 + g][:], op=ALU.subtract), reads=[u.k, ps[4 + g].k], writes=[vn.k])
                if own:
                    for g in range(2):
                        def fo(e, g=g):
                            for hh in range(4):
                                h = g * 4 + hh
                                e.matmul(ps[g][:, hh * 128:(hh + 1) * 128], lhsT=qdT[:, h, :], rhs=Sbf[:, h, :], start=True, stop=False, skip_group_check=True)
                                ins = e.matmul(ps[g][:, hh * 128:(hh + 1) * 128], lhsT=QKT[:, h, :], rhs=vn[:, h, :], start=False, stop=True, skip_group_check=True)
                            return ins
                        S.op("pe", fo, reads=[qdT.k, Sbf.k, QKT.k, vn.k], writes=[ps[g].k])
                        S.op("act", lambda e, g=g: e.activation(out=f4(o, g), in_=ps[g][:], func=AF.Copy), reads=[ps[g].k], writes=[o.k])
                S.op("pool", lambda e: e.tensor_tensor(out=Stmp[:], in0=S32[:], in1=bc(sm[:, 88:96].unsqueeze(2), [128, 8, 128]), op=ALU.mult), reads=[S32.k, sm.k], writes=[Stmp.k])
                for g, f in batch_mm((ps[2], ps[3]), lambda e, out, h: e.matmul(out, lhsT=kdec[:, h, :], rhs=vn[:, h, :], start=True, stop=True, skip_group_check=True)):
                    S.op("pe", f, reads=[kdec.k, vn.k], writes=[ps[2 + g].k])
                    S.op("dve", lambda e, g=g: e.tensor_tensor(out=f4(S32, g), in0=f4(Stmp, g), in1=ps[2 + g][:], op=ALU.add), reads=[Stmp.k, ps[2 + g].k], writes=[S32.k])
                S.op("act", lambda e: e.activation(out=Sbf[:].rearrange("p h d -> p (h d)"), in_=S32[:].rearrange("p h d -> p (h d)"), func=AF.Copy), reads=[S32.k], writes=[Sbf.k])
                if own:
                    S.op("pool", lambda e: e.tensor_tensor(out=osq[:], in0=o[:], in1=o[:], op=ALU.mult), reads=[o.k], writes=[osq.k])
                    S.op("dve", lambda e: e.tensor_reduce(out=sm2[:, 8:16], in_=osq[:], axis=AX.X, op=ALU.add), reads=[osq.k], writes=[sm2.k])
                    S.op("dve", lambda e: e.tensor_scalar(out=sm2[:, 8:16], in0=sm2[:, 8:16], scalar1=1.0 / 128, scalar2=EPS, op0=ALU.mult, op1=ALU.add), reads=[sm2.k], writes=[sm2.k])
                    S.op("act", lambda e: e.activation(out=sm2[:, 8:16], in_=sm2[:, 8:16], func=AF.Ln), reads=[sm2.k], writes=[sm2.k])
                    S.op("act", lambda e: e.activation(out=sm2[:, 8:16], in_=sm2[:, 8:16], func=AF.Exp, scale=-0.5), reads=[sm2.k], writes=[sm2.k])
                    S.op("dve", lambda e: e.tensor_tensor(out=o[:], in0=o[:], in1=bc(sm2[:, 8:16].unsqueeze(2), [128, 8, 128]), op=ALU.mult), reads=[o.k, sm2.k], writes=[o.k])
                    S.op("pool", lambda e: e.tensor_tensor(out=o[:], in0=o[:], in1=bc(ggdn[:].unsqueeze(1), [128, 8, 128]), op=ALU.mult), reads=[o.k, ggdn.k], writes=[o.k])
                    S.op("dve", lambda e: e.tensor_tensor(out=mgt[:], in0=o[:].rearrange("p h d -> p (h d)"), in1=zg[:], op=ALU.mult), reads=[o.k, zg.k], writes=[mgt.k])
                    S.dma(lambda e, j=j: e.dma_start(out=self.s_mg[j], in_=mgt[:]), reads=[mgt.k], writes=[self.k_own[j]["mg"]])
            S.barrier()
            S.emit()

    def phase1a_stub(self):
        S = self.S
        with ExitStack() as es:
            z = self.sb(es, "zmg", [128, 1024], BF16)
            S.op("pool", lambda e: e.memset(z[:], 0.0), writes=[z.k])
            for j in range(self.NO):
                S.dma(lambda e, j=j: e.dma_start(out=self.s_mg[j], in_=z[:]), reads=[z.k], writes=[self.k_own[j]["mg"]])
            S.barrier()
            S.emit()


def build(S, debug=False, gdn=True):
    B = Builder(S, debug)
    with ExitStack() as es:
        B.setup(es)
        B.phase1b()
        if gdn:
            B.phase1a()
        else:
            B.phase1a_stub()
        B.phase2()
        B.phase3()
    return B.nc


IN_SPLITS = (3072, 1024, 8, 8, 1024, 256, 256, 512, 64, 8, 1024, 1024)


def host_consts():
    bf = ml_dtypes.bfloat16
    i = np.arange(128)
    c = {}
    c["identf"] = np.eye(128, dtype=np.float32)
    c["identb"] = np.eye(128).astype(bf)
    c["uinc"] = (i[:, None] <= i[None, :]).astype(np.float32)
    c["causal"] = np.where(i[None, :] <= i[:, None], 0.0, NEG).astype(np.float32)
    c["mskU"] = np.where(i[None, :] >= i[:, None], 0.0, -30000.0).astype(bf)
    c["mskL"] = np.where(i[None, :] < i[:, None], 0.0, 30000.0).astype(bf)
    return c


def rope_tab(pos, rot):
    inv = (np.float32(500000.0) ** (-np.arange(0, rot, 2, dtype=np.float32) / np.float32(rot))).astype(np.float32)
    ang = pos.astype(np.float32)[:, None] * inv[None, :]
    return np.cos(ang).astype(np.float32), np.sin(ang).astype(np.float32)


def make_in_maps(S, x, norm_mix_g, w_in, conv_w, a_log, dt_bias, gdn_norm_g, q_norm_g, k_norm_g,
                 w_out, norm_mlp_g, w_mlp_up, w_mlp_down):
    Bn = x.shape[0]
    f = np.float32
    w_in = np.asarray(w_in[0], f)
    pts = np.cumsum((0,) + IN_SPLITS)
    seg = {n: w_in[:, pts[i]:pts[i + 1]] for i, n in enumerate(
        ("qkv", "z", "ga", "gb", "aq", "ak", "av", "iq", "ik", "iw", "gatea", "gateb"))}
    w1a = np.ascontiguousarray(np.concatenate([seg["qkv"], seg["ga"], seg["gb"]], 1))
    w1b = np.ascontiguousarray(np.concatenate([seg["ak"], seg["av"], seg["ik"], seg["iw"], seg["aq"], seg["iq"], seg["gateb"], seg["z"], seg["gatea"]], 1))
    col = lambda v: np.ascontiguousarray(np.asarray(v, f).reshape(8, 128).T)
    rep = lambda v: np.ascontiguousarray(np.broadcast_to(np.asarray(v, f)[None, :], (128, len(v))))
    common = host_consts()
    common.update(dict(
        w1a=w1a, w1b=w1b, wout=np.ascontiguousarray(w_out[0], f), wup=np.ascontiguousarray(w_mlp_up[0], f),
        wdn=np.ascontiguousarray(w_mlp_down[0], f),
        gmix=col(norm_mix_g[0]), gmlp=col(norm_mlp_g[0]), gq=rep(q_norm_g[0]), gk=rep(k_norm_g[0]),
        ggdn=rep(gdn_norm_g[0]), alog=rep(a_log[0]), dtb=rep(dt_bias[0]),
        convw=np.ascontiguousarray(np.asarray(conv_w[0], f).reshape(4, 24, 128).transpose(2, 1, 0).reshape(128, 96)),
    ))
    maps = []
    for b in range(Bn):
        for c in range(2):
            m = dict(common)
            xb = np.asarray(x[b], f)
            if c == 0:
                xs = np.concatenate([np.zeros((128, D), f), xb[:S - 128]], 0)
            else:
                xs = xb
            pos = np.maximum(np.arange(S) + (c - 1) * 128, 0)
            m["xseq"] = np.ascontiguousarray(xs)
            m["cosA"], m["sinA"] = rope_tab(pos, 32)
            m["cosI"], m["sinI"] = rope_tab(pos, 16)
            m["blk0"] = np.full((128, 128), NEG if c == 0 else 0.0, f)
            maps.append(m)
    return maps


def assemble(S, results, Bn):
    out = np.zeros((Bn, S, D), np.float32)
    ov = out.reshape(Bn, S // 256, 2, 128, D)
    for b in range(Bn):
        for c in range(2):
            r = results[b * 2 + c]["out"].reshape(S // 256, 128, D)
            ov[b, :, c] = r
    return out


_NC_CACHE = {}


def kernel(**inputs):
    x = np.asarray(inputs["x"])
    Bn, S, _ = x.shape
    if S not in _NC_CACHE:
        _NC_CACHE[S] = build(S)
    nc = _NC_CACHE[S]
    maps = make_in_maps(S, **{k: np.asarray(v) for k, v in inputs.items()})
    res = run_bass_kernel_spmd(nc, maps, core_ids=list(range(2 * Bn)))
    return assemble(S, res.results, Bn)
```

```python
from contextlib import ExitStack
import numpy as np
import ml_dtypes
import concourse.bass as bass
import concourse.mybir as mybir
from concourse.bass_utils import run_bass_kernel_spmd

F32 = mybir.dt.float32
BF16 = mybir.dt.bfloat16
AF = mybir.ActivationFunctionType
ALU = mybir.AluOpType
AX = mybir.AxisListType

D = 1024
EPS = 1e-6
NEG = -1.0e30
SAME_ENGINE_SYNC = True


class Tok:
    __slots__ = ("name", "w", "r")

    def __init__(self, name):
        self.name = name
        self.w = None
        self.r = {}


class Sched:
    ENG = ("pe", "act", "dve", "pool", "sp")

    def __init__(self, nc, es, n_dma_sems=24):
        self.nc = nc
        self.sems = {}
        for e in ("pe", "act", "dve", "pool"):
            self.sems[e] = es.enter_context(nc.semaphore("s_" + e))
        self.ndma = n_dma_sems
        for k in range(n_dma_sems):
            self.sems[("d", k)] = es.enter_context(nc.semaphore("s_d%d" % k))
        self.cnt = {e: 0 for e in ("pe", "act", "dve", "pool")}
        self.dtot = [0] * n_dma_sems
        self.rr = 0
        self.waited = {e: {} for e in self.ENG}
        self.ops = {e: [] for e in self.ENG}
        self.nops = 0

    def _deps(self, eng, reads, writes):
        deps = {}

        def add(d):
            if d is None:
                return
            sid, val = d
            if sid == eng and (eng == "pe" or not SAME_ENGINE_SYNC):
                return
            if deps.get(sid, 0) < val:
                deps[sid] = val

        for t in reads:
            add(t.w)
        for t in writes:
            add(t.w)
            for sid, val in t.r.items():
                add((sid, val))
        out = []
        wd = self.waited[eng]
        for sid, val in deps.items():
            if wd.get(sid, 0) < val:
                wd[sid] = val
                out.append((sid, val))
        return out

    def _mark(self, me, reads, writes):
        for t in reads:
            if t.r.get(me[0], 0) < me[1]:
                t.r[me[0]] = me[1]
        for t in writes:
            t.w = me
            t.r = {}

    def op(self, eng, fn, reads=(), writes=()):
        waits = self._deps(eng, reads, writes)
        self.cnt[eng] += 1
        me = (eng, self.cnt[eng])
        self._mark(me, reads, writes)
        self.ops[eng].append((waits, fn, eng, 1))
        self.nops += 1

    def dma(self, fn, reads=(), writes=()):
        k = self.rr
        self.rr = (self.rr + 1) % self.ndma
        sid = ("d", k)
        waits = self._deps("sp", reads, writes)
        wd = self.waited["sp"]
        if wd.get(sid, 0) < self.dtot[k]:
            wd[sid] = self.dtot[k]
            waits.append((sid, self.dtot[k]))
        self.dtot[k] += 16
        me = (sid, self.dtot[k])
        self._mark(me, reads, writes)
        self.ops["sp"].append((waits, fn, sid, 16))
        self.nops += 1

    def barrier(self):
        for e in self.ENG:
            waits = []
            wd = self.waited[e]
            for c in ("pe", "act", "dve", "pool"):
                if c != e and wd.get(c, 0) < self.cnt[c]:
                    wd[c] = self.cnt[c]
                    waits.append((c, self.cnt[c]))
            for k in range(self.ndma):
                sid = ("d", k)
                if wd.get(sid, 0) < self.dtot[k]:
                    wd[sid] = self.dtot[k]
                    waits.append((sid, self.dtot[k]))
            if waits:
                self.ops[e].append((waits, None, None, 0))

    def emit(self):
        nc = self.nc
        ops = self.ops
        self.ops = {e: [] for e in self.ENG}
        sems = self.sems

        def run(engh, lst):
            for waits, fn, sid, inc in lst:
                for s, v in waits:
                    engh.wait_ge(sems[s], v)
                if fn is not None:
                    fn(engh).then_inc(sems[sid], inc)

        with nc.Block() as block:
            @block.tensor
            def _(e):
                run(e, ops["pe"])

            @block.scalar
            def _(e):
                run(e, ops["act"])

            @block.vector
            def _(e):
                run(e, ops["dve"])

            @block.gpsimd
            def _(e):
                run(e, ops["pool"])

            @block.sync
            def _(e):
                run(e, ops["sp"])


class Ctx:
    pass


def bc(ap, shape):
    return ap.to_broadcast(list(shape))


class TL:
    def __init__(self, t, name):
        self.t = t
        self.k = Tok(name)

    def __getitem__(self, key):
        return self.t[key]


C1B = 5192
C1A = 3088


class Builder:
    def __init__(self, S, debug=False):
        self.Sq = S
        self.P = S // 128
        self.NO = self.P // 2
        self.debug = debug
        self.nc = bass.Bass("TRN2", target_bir_lowering=False)

    def dram_in(self, name, shape, dt=F32):
        return self.nc.dram_tensor(name, list(shape), dt, kind="ExternalInput").ap()

    def dram_scr(self, name, shape, dt):
        kind = "ExternalOutput" if self.debug else "Internal"
        return self.nc.dram_tensor(name, list(shape), dt, kind=kind).ap()

    def sb(self, es, name, shape, dt):
        return TL(es.enter_context(self.nc.sbuf_tensor("t_" + name, list(shape), dt)), name)

    def load(self, dst, dst_ap, src_ap):
        self.S.dma(lambda e: e.dma_start(out=dst_ap, in_=src_ap), writes=[dst.k])

    def rsqrt_small(self, v, n):
        S = self.S
        S.op("act", lambda e: e.activation(out=v[:, 0:n], in_=v[:, 0:n], func=AF.Ln), reads=[v.k], writes=[v.k])
        S.op("act", lambda e: e.activation(out=v[:, 0:n], in_=v[:, 0:n], func=AF.Exp, scale=-0.5), reads=[v.k], writes=[v.k])

    def load_weight_bf16(self, es, wb, w_ap, ncols, stg):
        S = self.S
        nk = w_ap.shape[0] // 128
        CH = stg[0].t.shape[1]
        i = 0
        for k in range(nk):
            for c0 in range(0, ncols, CH):
                cw = min(CH, ncols - c0)
                st = stg[i % len(stg)]
                S.dma(lambda e, st=st, k=k, c0=c0, cw=cw: e.dma_start(out=st[:, 0:cw], in_=w_ap[k * 128:(k + 1) * 128, c0:c0 + cw]), writes=[st.k])
                eng = ("pool", "dve", "act")[i % 3] if False else "pool"
                S.op(eng, lambda e, st=st, k=k, c0=c0, cw=cw: e.tensor_copy(out=wb[:, k, c0:c0 + cw], in_=st[:, 0:cw]), reads=[st.k], writes=[wb.k])
                i += 1

    def rms_hT(self, xb, ss, xnb, hT, gcol, psb):
        S = self.S
        junk = self.junk
        idb = self.identb
        S.op("act", lambda e: e.activation(out=junk[:, 0:1024], in_=xb[:], func=AF.Square, accum_out=ss[:, 0:1]), reads=[xb.k], writes=[junk.k, ss.k])
        S.op("dve", lambda e: e.tensor_scalar(out=ss[:, 0:1], in0=ss[:, 0:1], scalar1=1.0 / 1024, scalar2=EPS, op0=ALU.mult, op1=ALU.add), reads=[ss.k], writes=[ss.k])
        self.rsqrt_small(ss, 1)
        S.op("dve", lambda e: e.tensor_scalar(out=xnb[:], in0=xb[:], scalar1=ss[:, 0:1], scalar2=None, op0=ALU.mult), reads=[ss.k, xb.k], writes=[xnb.k])

        def tr(e):
            for kc in range(8):
                i = e.transpose(out=psb[:, kc * 128:(kc + 1) * 128], in_=xnb[:, kc * 128:(kc + 1) * 128], identity=idb[:])
            return i
        S.op("pe", tr, reads=[xnb.k, idb.k], writes=[psb.k])
        S.op("dve", lambda e: e.tensor_tensor(out=hT[:], in0=psb[:].rearrange("p (k t) -> p k t", k=8), in1=bc(gcol[:].unsqueeze(2), [128, 8, 128]), op=ALU.mult), reads=[psb.k, gcol.k], writes=[hT.k])

    def proj(self, ps, ncol, hT, wb, c0, pcol=0, start=True):
        def f(e):
            for kc in range(8):
                i = e.matmul(ps[:, pcol:pcol + ncol], lhsT=hT[:, kc, :], rhs=wb[:, kc, c0:c0 + ncol], start=(kc == 0), stop=(kc == 7))
            return i
        self.S.op("pe", f, reads=[hT.k, wb.k], writes=[ps.k])

    def setup(self, es):
        nc = self.nc
        S_, P, NO = self.Sq, self.P, self.NO
        self.S = Sched(nc, es)
        d = self.dram_in
        self.i_x = d("xseq", [S_, D])
        self.i_w1a = d("w1a", [D, C1A])
        self.i_w1b = d("w1b", [D, C1B])
        self.i_wout = d("wout", [D, D])
        self.i_wup = d("wup", [D, 4 * D])
        self.i_wdn = d("wdn", [4 * D, D])
        self.i_cosA = d("cosA", [S_, 16]); self.i_sinA = d("sinA", [S_, 16])
        self.i_cosI = d("cosI", [S_, 8]); self.i_sinI = d("sinI", [S_, 8])
        self.i_blk0 = d("blk0", [128, 128])
        self.i_identf = d("identf", [128, 128]); self.i_identb = d("identb", [128, 128], BF16)
        self.i_uinc = d("uinc", [128, 128]); self.i_causal = d("causal", [128, 128])
        self.i_mskU = d("mskU", [128, 128], BF16); self.i_mskL = d("mskL", [128, 128], BF16)
        self.i_gmix = d("gmix", [128, 8]); self.i_gmlp = d("gmlp", [128, 8])
        self.i_gq = d("gq", [128, 128]); self.i_gk = d("gk", [128, 128]); self.i_ggdn = d("ggdn", [128, 128])
        self.i_alog = d("alog", [128, 8]); self.i_dtb = d("dtb", [128, 8])
        self.i_convw = d("convw", [128, 24 * 4])
        self.o_out = nc.dram_tensor("out", [NO * 128, D], F32, kind="ExternalOutput").ap()
        s = self.dram_scr
        self.s_akT = s("s_akT", [128, 2 * S_], BF16)
        self.s_v = s("s_v", [S_, 258], BF16)
        self.s_ikT = s("s_ikT", [128, S_], F32)
        self.s_aqT = s("s_aqT", [NO, 128, 1024], BF16)
        self.s_iqT = s("s_iqT", [NO, 128, 512], F32)
        self.s_iw = s("s_iw", [NO, 128, 8], F32)
        self.s_sgb = s("s_sgb", [NO, 128, 1024], BF16)
        self.s_mg = s("s_mg", [NO, 128, 1024], BF16)
        self.s_zg = s("s_zg", [NO, 128, 1024], BF16)
        self.s_mrg = s("s_mrg", [NO, 128, 1024], BF16)
        self.k_akT = Tok("akT_d"); self.k_v = Tok("v_d"); self.k_ikT = Tok("ikT_d")
        self.k_own = [{n: Tok(n + str(j)) for n in ("aqT", "iqT", "iw", "sgb", "mg", "x1", "zg")} for j in range(NO)]
        self.ps = [TL(es.enter_context(nc.psum_tensor("ps%d" % i, [128, 512], F32)), "ps%d" % i) for i in range(7)]
        self.psb = TL(es.enter_context(nc.psum_tensor("psb", [128, 1024], BF16)), "psb")
        sb = lambda n, sh, dt: self.sb(es, n, sh, dt)
        self.identf = sb("identf", [128, 128], F32); self.identb = sb("identb", [128, 128], BF16)
        self.junk = sb("junk", [128, 1024], F32)
        for t, src in ((self.identf, self.i_identf), (self.identb, self.i_identb)):
            self.load(t, t[:], src)

    def norm_rope(self, H, src_ap, src_tok, gain, extra, out, ocol, cs, sn, tmp):
        S = self.S
        tf, sq, ssh, ra, rb = tmp
        n = H * 128
        S.op("act", lambda e: e.activation(out=tf[:, 0:n], in_=src_ap, func=AF.Copy), reads=[src_tok], writes=[tf.k])
        S.op("dve", lambda e: e.tensor_tensor(out=sq[:, 0:n], in0=tf[:, 0:n], in1=tf[:, 0:n], op=ALU.mult), reads=[tf.k], writes=[sq.k])
        S.op("dve", lambda e: e.tensor_reduce(out=ssh[:, 0:H], in_=sq[:, 0:n].rearrange("p (h d) -> p h d", h=H), axis=AX.X, op=ALU.add), reads=[sq.k], writes=[ssh.k])
        S.op("dve", lambda e: e.tensor_scalar(out=ssh[:, 0:H], in0=ssh[:, 0:H], scalar1=1.0 / 128, scalar2=EPS, op0=ALU.mult, op1=ALU.add), reads=[ssh.k], writes=[ssh.k])
        self.rsqrt_small(ssh, H)
        if extra != 1.0:
            S.op("dve", lambda e: e.tensor_scalar(out=ssh[:, 0:H], in0=ssh[:, 0:H], scalar1=extra, scalar2=None, op0=ALU.mult), reads=[ssh.k], writes=[ssh.k])
        tf3 = tf[:, 0:n].rearrange("p (h d) -> p h d", h=H)
        S.op("dve", lambda e: e.tensor_tensor(out=tf3, in0=tf3, in1=bc(ssh[:, 0:H].unsqueeze(2), [128, H, 128]), op=ALU.mult), reads=[tf.k, ssh.k], writes=[tf.k])
        S.op("pool", lambda e: e.tensor_tensor(out=tf3, in0=tf3, in1=bc(gain[:].unsqueeze(1), [128, H, 128]), op=ALU.mult), reads=[tf.k, gain.k], writes=[tf.k])
        o3 = out[:, ocol:ocol + n].rearrange("p (h d) -> p h d", h=H)
        S.op("act", lambda e: e.activation(out=out[:, ocol:ocol + n], in_=tf[:, 0:n], func=AF.Copy), reads=[tf.k], writes=[out.k])
        self.rope(tf3, tf.k, o3, out.k, H, 16, cs, sn, ra, rb)

    def rope(self, t3, ttok, o3, otok, H, hf, cs, sn, ra, rb):
        S = self.S
        x1 = t3[:, :, 0:hf]; x2 = t3[:, :, hf:2 * hf]
        c = bc(cs[:].unsqueeze(1), [128, H, hf]); s_ = bc(sn[:].unsqueeze(1), [128, H, hf])
        a3 = ra[:, 0:H * hf].rearrange("p (h d) -> p h d", h=H)
        b3 = rb[:, 0:H * hf].rearrange("p (h d) -> p h d", h=H)
        S.op("pool", lambda e: e.tensor_tensor(out=a3, in0=x1, in1=c, op=ALU.mult), reads=[ttok, cs.k], writes=[ra.k])
        S.op("pool", lambda e: e.tensor_tensor(out=b3, in0=x2, in1=s_, op=ALU.mult), reads=[ttok, sn.k], writes=[rb.k])
        S.op("dve", lambda e: e.tensor_tensor(out=o3[:, :, 0:hf], in0=a3, in1=b3, op=ALU.subtract), reads=[ra.k, rb.k], writes=[otok])
        S.op("pool", lambda e: e.tensor_tensor(out=a3, in0=x2, in1=c, op=ALU.mult), reads=[ttok, cs.k], writes=[ra.k])
        S.op("pool", lambda e: e.tensor_tensor(out=b3, in0=x1, in1=s_, op=ALU.mult), reads=[ttok, sn.k], writes=[rb.k])
        S.op("dve", lambda e: e.tensor_tensor(out=o3[:, :, hf:2 * hf], in0=a3, in1=b3, op=ALU.add), reads=[ra.k, rb.k], writes=[otok])

    def phase1b(self):
        S = self.S
        P = self.P
        ps, psb = self.ps, self.psb
        with ExitStack() as es:
            sb = lambda n, sh, dt: self.sb(es, n, sh, dt)
            wb = sb("w1b", [128, 8, C1B], BF16)
            stg = [sb("stg%d" % i, [128, 1024], F32) for i in range(2)]
            gmix = sb("gmix", [128, 8], F32); gq = sb("gq", [128, 128], F32); gk = sb("gk", [128, 128], F32)
            self.load(gmix, gmix[:], self.i_gmix); self.load(gq, gq[:], self.i_gq); self.load(gk, gk[:], self.i_gk)
            self.load_weight_bf16(es, wb, self.i_w1b, C1B, stg)
            xb = [sb("xb%d" % i, [128, 1024], F32) for i in range(2)]
            ss = sb("ss", [128, 1], F32); xnb = sb("xnb", [128, 1024], BF16); hT = sb("hT", [128, 8, 128], BF16)
            cA = sb("cA", [128, 16], F32); sA = sb("sA", [128, 16], F32); cI = sb("cI", [128, 8], F32); sI = sb("sI", [128, 8], F32)
            tmp = (sb("tf", [128, 512], F32), sb("sq", [128, 512], F32), sb("ssh", [128, 8], F32), sb("ra", [128, 64], F32), sb("rb", [128, 64], F32))
            ra, rb = tmp[3], tmp[4]
            akb = sb("akb", [128, 256], BF16); akT = sb("akTb", [128, 256], BF16)
            vaug = sb("vaug", [128, 258], BF16)
            ikf = sb("ikf", [128, 72], F32); ik2 = sb("ik2", [128, 128], F32); ikT = sb("ikTb", [128, 128], F32)
            iw = sb("iwb", [128, 8], F32)
            aqb = sb("aqb", [128, 1024], BF16); aqT = sb("aqTb", [128, 1024], BF16)
            iqf = sb("iqf", [128, 512], F32); iqo = sb("iqo", [128, 512], F32); iqT = sb("iqTb", [128, 512], F32)
            sgb = sb("sgbb", [128, 1024], BF16)
            zs = sb("zs", [128, 1024], F32); sga = sb("sga", [128, 1024], F32); zg = sb("zgb", [128, 1024], BF16)
            S.op("pool", lambda e: e.memset(vaug[:], 1.0), writes=[vaug.k])
            for p in range(P):
                own = (p % 2 == 1)
                j = p // 2
                x = xb[p % 2]
                r0 = p * 128
                self.load(x, x[:], self.i_x[r0:r0 + 128, :])
                self.load(cA, cA[:], self.i_cosA[r0:r0 + 128, :]); self.load(sA, sA[:], self.i_sinA[r0:r0 + 128, :])
                self.load(cI, cI[:], self.i_cosI[r0:r0 + 128, :]); self.load(sI, sI[:], self.i_sinI[r0:r0 + 128, :])
                self.rms_hT(x, ss, xnb, hT, gmix, psb)
                self.proj(ps[0], 512, hT, wb, 0)
                self.norm_rope(2, ps[0][:, 0:256], ps[0].k, gk, 1.0, akb, 0, cA, sA, tmp)
                S.op("act", lambda e: e.activation(out=vaug[:].rearrange("p (c d) -> p c d", c=2)[:, :, 0:128], in_=ps[0][:, 256:512].rearrange("p (c d) -> p c d", c=2), func=AF.Copy), reads=[ps[0].k], writes=[vaug.k])
                S.dma(lambda e, r0=r0: e.dma_start(out=self.s_v[r0:r0 + 128, :], in_=vaug[:]), reads=[vaug.k], writes=[self.k_v])

                def trk(e):
                    for c in range(2):
                        i = e.transpose(out=psb[:, c * 128:(c + 1) * 128], in_=akb[:, c * 128:(c + 1) * 128], identity=self.identb[:])
                    return i
                S.op("pe", trk, reads=[akb.k, self.identb.k], writes=[psb.k])
                S.op("act", lambda e: e.activation(out=akT[:], in_=psb[:, 0:256], func=AF.Copy), reads=[psb.k], writes=[akT.k])
                S.dma(lambda e, r0=r0: e.dma_start(out=self.s_akT.rearrange("h (c s) -> h c s", c=2)[:, :, r0:r0 + 128], in_=akT[:].rearrange("h (c s) -> h c s", c=2)), reads=[akT.k], writes=[self.k_akT])
                self.proj(ps[1], 72, hT, wb, 512)
                S.op("act", lambda e: e.activation(out=ikf[:], in_=ps[1][:, 0:72], func=AF.Copy), reads=[ps[1].k], writes=[ikf.k])
                S.op("dve", lambda e: e.tensor_copy(out=ik2[:, 0:64], in_=ikf[:, 0:64]), reads=[ikf.k], writes=[ik2.k])
                self.rope(ikf[:, 0:64].rearrange("p (h d) -> p h d", h=1), ikf.k, ik2[:, 0:64].rearrange("p (h d) -> p h d", h=1), ik2.k, 1, 8, cI, sI, ra, rb)
                S.op("dve", lambda e: e.tensor_copy(out=ik2[:, 64:128], in_=ik2[:, 0:64]), reads=[ik2.k], writes=[ik2.k])
                S.op("pe", lambda e: e.transpose(out=ps[2][:, 0:128], in_=ik2[:], identity=self.identf[:]), reads=[ik2.k, self.identf.k], writes=[ps[2].k])
                S.op("act", lambda e: e.activation(out=ikT[:], in_=ps[2][:, 0:128], func=AF.Copy), reads=[ps[2].k], writes=[ikT.k])
                S.dma(lambda e, r0=r0: e.dma_start(out=self.s_ikT[:, r0:r0 + 128], in_=ikT[:]), reads=[ikT.k], writes=[self.k_ikT])
                if not own:
                    continue
                ko = self.k_own[j]
                S.op("dve", lambda e: e.tensor_scalar(out=iw[:], in0=ikf[:, 64:72], scalar1=float(512.0 ** -0.5), scalar2=None, op0=ALU.mult), reads=[ikf.k], writes=[iw.k])
                S.dma(lambda e, j=j: e.dma_start(out=self.s_iw[j], in_=iw[:]), reads=[iw.k], writes=[ko["iw"]])
                for half in range(2):
                    pq = ps[3 + half]
                    self.proj(pq, 512, hT, wb, 584 + half * 512)
                    self.norm_rope(4, pq[:, 0:512], pq.k, gq, float(128.0 ** -0.5), aqb, half * 512, cA, sA, tmp)

                def trq(e):
                    for h in range(8):
                        i = e.transpose(out=psb[:, h * 128:(h + 1) * 128], in_=aqb[:, h * 128:(h + 1) * 128], identity=self.identb[:])
                    return i
                S.op("pe", trq, reads=[aqb.k, self.identb.k], writes=[psb.k])
                S.op("act", lambda e: e.activation(out=aqT[:], in_=psb[:], func=AF.Copy), reads=[psb.k], writes=[aqT.k])
                S.dma(lambda e, j=j: e.dma_start(out=self.s_aqT[j], in_=aqT[:]), reads=[aqT.k], writes=[ko["aqT"]])
                self.proj(ps[5], 512, hT, wb, 1608)
                S.op("act", lambda e: e.activation(out=iqf[:], in_=ps[5][:], func=AF.Copy), reads=[ps[5].k], writes=[iqf.k])
                S.op("dve", lambda e: e.tensor_copy(out=iqo[:], in_=iqf[:]), reads=[iqf.k], writes=[iqo.k])
                self.rope(iqf[:].rearrange("p (h d) -> p h d", h=8), iqf.k, iqo[:].rearrange("p (h d) -> p h d", h=8), iqo.k, 8, 8, cI, sI, ra, rb)

                def tri(e):
                    for g in range(4):
                        i = e.transpose(out=ps[6][:, g * 128:(g + 1) * 128], in_=iqo[:, g * 128:(g + 1) * 128], identity=self.identf[:])
                    return i
                S.op("pe", tri, reads=[iqo.k, self.identf.k], writes=[ps[6].k])
                S.op("act", lambda e: e.activation(out=iqT[:], in_=ps[6][:], func=AF.Copy), reads=[ps[6].k], writes=[iqT.k])
                S.dma(lambda e, j=j: e.dma_start(out=self.s_iqT[j], in_=iqT[:]), reads=[iqT.k], writes=[ko["iqT"]])
                for half in range(2):
                    pq = ps[half]
                    self.proj(pq, 512, hT, wb, 2120 + half * 512)
                    S.op("act", lambda e, pq=pq, half=half: e.activation(out=sgb[:, half * 512:(half + 1) * 512], in_=pq[:], func=AF.Sigmoid), reads=[pq.k], writes=[sgb.k])
                S.dma(lambda e, j=j: e.dma_start(out=self.s_sgb[j], in_=sgb[:]), reads=[sgb.k], writes=[ko["sgb"]])
                for half in range(2):
                    pq = ps[2 + half]; pg = ps[4 + half]
                    self.proj(pq, 512, hT, wb, 3144 + half * 512)
                    self.proj(pg, 512, hT, wb, 4168 + half * 512)
                    S.op("act", lambda e, pq=pq, half=half: e.activation(out=zs[:, half * 512:(half + 1) * 512], in_=pq[:], func=AF.Silu), reads=[pq.k], writes=[zs.k])
                    S.op("act", lambda e, pg=pg, half=half: e.activation(out=sga[:, half * 512:(half + 1) * 512], in_=pg[:], func=AF.Sigmoid), reads=[pg.k], writes=[sga.k])
                S.op("pool", lambda e: e.tensor_tensor(out=zg[:], in0=zs[:], in1=sga[:], op=ALU.mult), reads=[zs.k, sga.k], writes=[zg.k])
                S.dma(lambda e, j=j: e.dma_start(out=self.s_zg[j], in_=zg[:]), reads=[zg.k], writes=[ko["zg"]])
            S.barrier()
            S.emit()

    def phase2(self, NBIS=24):
        S = self.S
        P, NO, S_ = self.P, self.NO, self.Sq
        ps, psb = self.ps, self.psb
        KSEL = float(min(256, S_ // 4)) - 0.5
        with ExitStack() as es:
            sb = lambda n, sh, dt: self.sb(es, n, sh, dt)
            akT = sb("akT", [128, 2, S_], BF16)
            vall = sb("vall", [128, P, 258], BF16)
            H2 = S_ // 2
            assert H2 % 512 == 0
            ikT = sb("ikT", [128, H2], F32)
            scores = [sb("score%d" % i, [128, S_], F32) for i in range(2)]
            msk = sb("msk", [128, S_], BF16)
            jk = sb("jk", [128, S_], mybir.dt.uint8)
            gq = sb("gq2", [128, 128], F32); gk = sb("gk2", [128, 128], F32)
            negM = sb("negM", [128, 2], F32)
            causal = sb("causal", [128, 128], F32); blk0 = sb("blk0", [128, 128], F32)
            self.load(gq, gq[:], self.i_gq); self.load(gk, gk[:], self.i_gk)
            self.load(causal, causal[:], self.i_causal); self.load(blk0, blk0[:], self.i_blk0)
            S.dma(lambda e: e.dma_start(out=akT[:], in_=self.s_akT.rearrange("h (c s) -> h c s", c=2)), reads=[self.k_akT], writes=[akT.k])
            S.dma(lambda e: e.dma_start(out=vall[:], in_=self.s_v.rearrange("(p t) n -> t p n", t=128)), reads=[self.k_v], writes=[vall.k])
            S.dma(lambda e: e.dma_start(out=ikT[0:64, :], in_=self.s_ikT[0:64, 0:H2]), reads=[self.k_ikT], writes=[ikT.k])
            S.dma(lambda e: e.dma_start(out=ikT[64:128, :], in_=self.s_ikT[64:128, H2:S_]), reads=[self.k_ikT], writes=[ikT.k])
            S.op("dve", lambda e: e.tensor_reduce(out=negM[:, 0:1], in_=gq[:], axis=AX.X, op=ALU.max, apply_absolute_value=True), reads=[gq.k], writes=[negM.k])
            S.op("dve", lambda e: e.tensor_reduce(out=negM[:, 1:2], in_=gk[:], axis=AX.X, op=ALU.max, apply_absolute_value=True), reads=[gk.k], writes=[negM.k])
            S.op("dve", lambda e: e.scalar_tensor_tensor(out=negM[:, 0:1], in0=negM[:, 0:1], scalar=float(-(128.0 ** 0.5)), in1=negM[:, 1:2], op0=ALU.mult, op1=ALU.mult), reads=[negM.k], writes=[negM.k])
            aqT = sb("aqT", [128, 1024], BF16); iqT = sb("iqT", [128, 512], F32); iw = sb("iw", [128, 8], F32)
            iqT2 = sb("iqT2", [128, 512], F32)
            sgb = sb("sgb", [128, 1024], BF16); mg = sb("mg", [128, 1024], BF16)
            rl = [sb("rl%d" % i, [128, 512], F32) for i in range(2)]
            pT = [sb("pT%d" % i, [128, 512], BF16) for i in range(2)]
            mT = [sb("mT%d" % i, [128, 1024], BF16) for i in range(2)]
            sts = [sb("bst%d" % i, [128, 8], F32) for i in range(3)]
            rec = sb("rec", [128, 8], F32)
            oatt = self.junk
            mrg = sb("mrg", [128, 1024], BF16)
            accb = [ps[4], ps[5], ps[6]]
            hb = [(0, 0), (0, 1), (0, 2), (1, 0), (1, 1), (1, 2), (2, 0), (2, 1)]

            def gI(j):
                p = 2 * j + 1
                nkeys = (p + 1) * 128
                ko = self.k_own[j]
                score = scores[j % 2]; st = sts[j % 3]
                S.dma(lambda e: e.dma_start(out=iqT[:], in_=self.s_iqT[j]), reads=[ko["iqT"]], writes=[iqT.k])
                S.dma(lambda e: e.dma_start(out=iqT2[0:64, :], in_=self.s_iqT[j][64:128, :]), reads=[ko["iqT"]], writes=[iqT2.k])
                S.dma(lambda e: e.dma_start(out=iqT2[64:128, :], in_=self.s_iqT[j][0:64, :]), reads=[ko["iqT"]], writes=[iqT2.k])
                S.dma(lambda e: e.dma_start(out=iw[:], in_=self.s_iw[j]), reads=[ko["iw"]], writes=[iw.k])
                it = 0
                for kg in range((nkeys + 511) // 512):
                    k0 = kg * 512
                    w = min(512, nkeys - k0)
                    for h in range(8):
                        pb = ps[it % 2]; r = rl[it % 2]; it += 1
                        b0 = 64 if k0 >= H2 else 0
                        kk0 = k0 - (H2 if k0 >= H2 else 0)
                        qsrc = iqT if (h % 2) * 64 == b0 else iqT2
                        S.op("pe", lambda e, pb=pb, h=h, b0=b0, kk0=kk0, w=w, qsrc=qsrc: e.matmul(pb[:, 0:w], lhsT=qsrc[b0:b0 + 64, (h // 2) * 128:(h // 2 + 1) * 128], rhs=ikT[b0:b0 + 64, kk0:kk0 + w], start=True, stop=True), reads=[iqT.k, iqT2.k, ikT.k], writes=[pb.k])
                        S.op("act", lambda e, pb=pb, r=r, w=w: e.activation(out=r[:, 0:w], in_=pb[:, 0:w], func=AF.Relu), reads=[pb.k], writes=[r.k])
                        if h == 0:
                            S.op("dve", lambda e, r=r, k0=k0, w=w: e.tensor_scalar(out=score[:, k0:k0 + w], in0=r[:, 0:w], scalar1=iw[:, 0:1], scalar2=None, op0=ALU.mult), reads=[r.k, iw.k], writes=[score.k])
                        else:
                            S.op("dve", lambda e, r=r, k0=k0, w=w, h=h: e.scalar_tensor_tensor(out=score[:, k0:k0 + w], in0=r[:, 0:w], scalar=iw[:, h:h + 1], in1=score[:, k0:k0 + w], op0=ALU.mult, op1=ALU.add), reads=[r.k, iw.k, score.k], writes=[score.k])
                        yield
                S.op("dve", lambda e: e.tensor_reduce(out=st[:, 5:6], in_=score[:, 0:nkeys], axis=AX.X, op=ALU.max), reads=[score.k], writes=[st.k])
                S.op("dve", lambda e: e.tensor_reduce(out=st[:, 6:7], in_=score[:, 0:nkeys], axis=AX.X, op=ALU.min), reads=[score.k], writes=[st.k])
                S.op("dve", lambda e: e.tensor_scalar(out=st[:, 0:1], in0=st[:, 6:7], scalar1=-1.0, scalar2=None, op0=ALU.add), reads=[st.k], writes=[st.k])
                S.op("dve", lambda e: e.tensor_tensor(out=st[:, 1:2], in0=st[:, 5:6], in1=st[:, 0:1], op=ALU.subtract), reads=[st.k], writes=[st.k])
                S.op("dve", lambda e: e.tensor_tensor(out=score[:, nkeys - 128:nkeys], in0=score[:, nkeys - 128:nkeys], in1=causal[:], op=ALU.add), reads=[score.k, causal.k], writes=[score.k])
                S.op("dve", lambda e: e.tensor_tensor(out=score[:, 0:128], in0=score[:, 0:128], in1=blk0[:], op=ALU.add), reads=[score.k, blk0.k], writes=[score.k])
                yield

            def gB(j):
                nkeys = (2 * j + 2) * 128
                score = scores[j % 2]; st = sts[j % 3]
                for it_ in range(1, NBIS + 1):
                    f = float(2.0 ** -it_)
                    S.op("dve", lambda e, f=f: e.scalar_tensor_tensor(out=st[:, 2:3], in0=st[:, 1:2], scalar=f, in1=st[:, 0:1], op0=ALU.mult, op1=ALU.add), reads=[st.k], writes=[st.k])
                    S.op("dve", lambda e: e.tensor_scalar(out=jk[:, 0:nkeys], in0=score[:, 0:nkeys], scalar1=st[:, 2:3], scalar2=0.0, op0=ALU.is_gt, op1=ALU.add, accum_out=st[:, 3:4]), reads=[score.k, st.k], writes=[jk.k, st.k])
                    S.op("dve", lambda e, f=f: e.tensor_scalar(out=st[:, 4:5], in0=st[:, 3:4], scalar1=KSEL, scalar2=f, op0=ALU.is_gt, op1=ALU.mult), reads=[st.k], writes=[st.k])
                    S.op("dve", lambda e: e.scalar_tensor_tensor(out=st[:, 0:1], in0=st[:, 4:5], scalar=st[:, 1:2], in1=st[:, 0:1], op0=ALU.mult, op1=ALU.add), reads=[st.k], writes=[st.k])
                    yield

            def gA(j):
                p = 2 * j + 1
                nk = p + 1
                nkeys = nk * 128
                ko = self.k_own[j]
                score = scores[j % 2]; st = sts[j % 3]
                S.op("dve", lambda e: e.tensor_scalar(out=msk[:, 0:nkeys], in0=score[:, 0:nkeys], scalar1=st[:, 0:1], scalar2=None, op0=ALU.is_gt), reads=[score.k, st.k], writes=[msk.k])
                S.dma(lambda e: e.dma_start(out=aqT[:], in_=self.s_aqT[j]), reads=[ko["aqT"]], writes=[aqT.k])
                S.dma(lambda e: e.dma_start(out=sgb[:], in_=self.s_sgb[j]), reads=[ko["sgb"]], writes=[sgb.k])
                S.dma(lambda e: e.dma_start(out=mg[:], in_=self.s_mg[j]), reads=[ko["mg"]], writes=[mg.k])
                yield
                sti = 0
                for kb in range(nk):
                    if kb % 8 == 0:
                        m = mT[(kb // 8) % 2]
                        nb = min(8, nk - kb)

                        def trm(e, kb=kb, nb=nb):
                            for i_ in range(nb):
                                ins = e.transpose(out=psb[:, i_ * 128:(i_ + 1) * 128], in_=msk[:, (kb + i_) * 128:(kb + i_ + 1) * 128], identity=self.identb[:])
                            return ins
                        S.op("pe", trm, reads=[msk.k, self.identb.k], writes=[psb.k])
                        S.op("act", lambda e, m=m, nb=nb: e.activation(out=m[:, 0:nb * 128], in_=psb[:, 0:nb * 128], func=AF.Copy), reads=[psb.k], writes=[m.k])
                    for c in range(2):
                        pss = ps[2 + sti % 2]; pt = pT[sti % 2]; sti += 1
                        S.op("pe", lambda e, pss=pss, c=c, kb=kb: e.matmul(pss[:], lhsT=akT[:, c, kb * 128:(kb + 1) * 128], rhs=aqT[:, c * 512:(c + 1) * 512], start=True, stop=True), reads=[akT.k, aqT.k], writes=[pss.k])
                        S.op("act", lambda e, pss=pss, pt=pt: e.activation(out=pt[:], in_=pss[:], func=AF.Exp, bias=negM[:, 0:1], scale=1.0), reads=[pss.k, negM.k], writes=[pt.k])
                        mi = (kb % 8) * 128
                        S.op("pool", lambda e, pt=pt, m=m, mi=mi: e.tensor_tensor(out=pt[:].rearrange("p (h q) -> p h q", h=4), in0=pt[:].rearrange("p (h q) -> p h q", h=4), in1=bc(m[:, mi:mi + 128].unsqueeze(1), [128, 4, 128]), op=ALU.mult), reads=[pt.k, m.k], writes=[pt.k])

                        def pv(e, pt=pt, c=c, kb=kb, nk=nk):
                            for hh in range(4):
                                h = c * 4 + hh
                                bk, sl = hb[h]
                                ins = e.matmul(accb[bk][:, sl * 129:(sl + 1) * 129], lhsT=pt[:, hh * 128:(hh + 1) * 128], rhs=vall[:, kb, c * 129:(c + 1) * 129], start=(kb == 0 and sl == 0), stop=(kb == nk - 1), skip_group_check=True)
                            return ins
                        S.op("pe", pv, reads=[pt.k, vall.k], writes=[accb[hb[c * 4][0]].k, accb[hb[c * 4 + 3][0]].k])
                        yield
                for bk, (h0, n) in enumerate(((0, 3), (3, 3), (6, 2))):
                    a3 = accb[bk][:, 0:n * 129].rearrange("p (h d) -> p h d", h=n)
                    S.op("dve", lambda e, a3=a3, h0=h0, n=n: e.reciprocal(out=rec[:, h0:h0 + n], in_=a3[:, :, 128]), reads=[accb[bk].k], writes=[rec.k])
                    S.op("dve", lambda e, a3=a3, h0=h0, n=n: e.tensor_tensor(out=oatt[:, h0 * 128:(h0 + n) * 128].rearrange("p (h d) -> p h d", h=n), in0=a3[:, :, 0:128], in1=bc(rec[:, h0:h0 + n].unsqueeze(2), [128, n, 128]), op=ALU.mult), reads=[accb[bk].k, rec.k], writes=[oatt.k])
                S.op("pool", lambda e: e.tensor_tensor(out=oatt[:], in0=oatt[:], in1=sgb[:], op=ALU.mult), reads=[oatt.k, sgb.k], writes=[oatt.k])
                S.op("pool", lambda e: e.tensor_tensor(out=mrg[:], in0=oatt[:], in1=mg[:], op=ALU.add), reads=[oatt.k, mg.k], writes=[mrg.k])
                S.dma(lambda e: e.dma_start(out=self.s_mrg[j], in_=mrg[:]), reads=[mrg.k], writes=[ko["x1"]])
                yield

            def count(g):
                n = 0
                for _ in g:
                    n += 1
                return n

            for t in range(NO + 2):
                gens = []
                for mk, jj in ((gA, t - 2), (gB, t - 1), (gI, t)):
                    if 0 <= jj < NO:
                        gens.append(mk(jj))
                sizes = []
                for mk, jj in ((gA, t - 2), (gB, t - 1), (gI, t)):
                    if 0 <= jj < NO:
                        nk = 2 * jj + 2
                        if mk is gA:
                            sizes.append(2 + 2 * nk)
                        elif mk is gB:
                            sizes.append(NBIS)
                        else:
                            sizes.append(((nk * 128 + 511) // 512) * 8 + 1)
                nmax = max(sizes)
                done = [0] * len(gens)
                if 0 <= t - 2 < NO:
                    next(gens[0], None)
                    done[0] = 1
                for k in range(nmax):
                    for gi, g in enumerate(gens):
                        tgt = ((k + 1) * sizes[gi]) // nmax
                        while done[gi] < tgt:
                            next(g, None)
                            done[gi] += 1
                for g in gens:
                    for _ in g:
                        pass
            S.barrier()
            S.emit()

    def phase3(self, G=2):
        S = self.S
        NO = self.NO
        ps, psb = self.ps, self.psb
        G = min(G, NO)
        with ExitStack() as es:
            sb = lambda n, sh, dt: self.sb(es, n, sh, dt)
            wup = sb("wupb", [128, 8, 4096], BF16); wdn = sb("wdnb", [128, 32, 1024], BF16)
            woutb = sb("woutb", [128, 8, 1024], BF16)
            stg = [sb("stg3_%d" % i, [128, 512], F32) for i in range(2)]
            gmlp = sb("gmlp", [128, 8], F32)
            self.load(gmlp, gmlp[:], self.i_gmlp)
            self.load_weight_bf16(es, woutb, self.i_wout, 1024, stg)
            self.load_weight_bf16(es, wup, self.i_wup, 4096, stg)
            self.load_weight_bf16(es, wdn, self.i_wdn, 1024, stg)
            x1 = sb("x1g", [128, G, 1024], F32)
            xb = sb("xb3", [128, 1024], F32); mrg = sb("mrg3", [128, 1024], BF16); mT2 = sb("mT2", [128, 8, 128], BF16)
            ss = sb("ss3", [128, 1], F32); xnb = sb("xnb3", [128, 1024], BF16)
            h2T = sb("h2T", [128, 8, G * 128], BF16)
            hTb = sb("hTb3", [128, 8, 128], BF16)
            hid = sb("hidT", [128, 32, G * 128], BF16)
            rl = [sb("rl3_%d" % i, [128, G * 128], F32) for i in range(2)]
            ot = [sb("ot%d" % i, [128, 1024], F32) for i in range(2)]
            for g0 in range(0, NO, G):
                for b in range(G):
                    j = g0 + b
                    p = 2 * j + 1
                    S.dma(lambda e, j=j: e.dma_start(out=mrg[:], in_=self.s_mrg[j]), reads=[self.k_own[j]["x1"]], writes=[mrg.k])
                    S.dma(lambda e, p=p: e.dma_start(out=xb[:], in_=self.i_x[p * 128:(p + 1) * 128, :]), writes=[xb.k])

                    def trg(e):
                        for kc in range(8):
                            ins = e.transpose(out=psb[:, kc * 128:(kc + 1) * 128], in_=mrg[:, kc * 128:(kc + 1) * 128], identity=self.identb[:])
                        return ins
                    S.op("pe", trg, reads=[mrg.k, self.identb.k], writes=[psb.k])
                    S.op("act", lambda e: e.activation(out=mT2[:].rearrange("p k t -> p (k t)"), in_=psb[:], func=AF.Copy), reads=[psb.k], writes=[mT2.k])
                    for half in range(2):
                        pq = ps[4 + half]
                        self.proj(pq, 512, mT2, woutb, half * 512)
                        S.op("dve", lambda e, pq=pq, half=half, b=b: e.tensor_tensor(out=x1[:, b, half * 512:(half + 1) * 512], in0=xb[:, half * 512:(half + 1) * 512], in1=pq[:], op=ALU.add), reads=[xb.k, pq.k], writes=[x1.k])
                    xbv = TL(x1.t[:, b, :], "x")
                    xbv.k = x1.k
                    self.rms_hT(xbv, ss, xnb, hTb, gmlp, psb)
                    S.op("pool", lambda e, b=b: e.tensor_copy(out=h2T[:, :, b * 128:(b + 1) * 128], in_=hTb[:]), reads=[hTb.k], writes=[h2T.k])
                for f in range(32):
                    pq = ps[f % 2]; r = rl[f % 2]

                    def up(e, pq=pq, f=f):
                        for kc in range(8):
                            ins = e.matmul(pq[:, 0:G * 128], lhsT=wup[:, kc, f * 128:(f + 1) * 128], rhs=h2T[:, kc, :], start=(kc == 0), stop=(kc == 7))
                        return ins
                    S.op("pe", up, reads=[wup.k, h2T.k], writes=[pq.k])
                    S.op("act", lambda e, pq=pq, r=r: e.activation(out=r[:], in_=pq[:, 0:G * 128], func=AF.Relu), reads=[pq.k], writes=[r.k])
                    S.op("dve" if f % 2 else "pool", lambda e, r=r, f=f: e.tensor_tensor(out=hid[:, f, :], in0=r[:], in1=r[:], op=ALU.mult), reads=[r.k], writes=[hid.k])
                for b in range(G):
                    j = g0 + b
                    o = ot[b % 2]
                    for half in range(2):
                        pq = ps[2 + half]

                        def dn(e, pq=pq, b=b, half=half):
                            for f in range(32):
                                ins = e.matmul(pq[:], lhsT=hid[:, f, b * 128:(b + 1) * 128], rhs=wdn[:, f, half * 512:(half + 1) * 512], start=(f == 0), stop=(f == 31))
                            return ins
                        S.op("pe", dn, reads=[hid.k, wdn.k], writes=[pq.k])
                        S.op("dve", lambda e, pq=pq, o=o, b=b, half=half: e.tensor_tensor(out=o[:, half * 512:(half + 1) * 512], in0=x1[:, b, half * 512:(half + 1) * 512], in1=pq[:], op=ALU.add), reads=[x1.k, pq.k], writes=[o.k])
                    S.dma(lambda e, j=j, o=o: e.dma_start(out=self.o_out[j * 128:(j + 1) * 128, :], in_=o[:]), reads=[o.k], writes=[])
            S.barrier()
            S.emit()

    def phase1a(self):
        S = self.S
        P = self.P
        ps, psb = self.ps, self.psb
        with ExitStack() as es:
            sb = lambda n, sh, dt: self.sb(es, n, sh, dt)
            wb = sb("w1a", [128, 8, C1A], BF16)
            stg = [sb("stga%d" % i, [128, 1024], F32) for i in range(2)]
            gmix = sb("gmixa", [128, 8], F32); ggdn = sb("ggdn", [128, 128], F32)
            aexp = sb("aexp", [128, 8], F32); dtb = sb("dtb", [128, 8], F32)
            convw = sb("convw", [128, 24, 4], F32)
            uinc = sb("uinc", [128, 128], F32); onesf = sb("onesf", [128, 128], F32); onesb = sb("onesb", [128, 128], BF16)
            mskL4 = sb("mskL4", [128, 4, 128], BF16); mskU4 = sb("mskU4", [128, 4, 128], BF16)
            self.load(gmix, gmix[:], self.i_gmix); self.load(ggdn, ggdn[:], self.i_ggdn)
            self.load(aexp, aexp[:], self.i_alog); self.load(dtb, dtb[:], self.i_dtb)
            self.load(convw, convw[:].rearrange("p t j -> p (t j)"), self.i_convw)
            self.load(uinc, uinc[:], self.i_uinc)
            for q4 in range(4):
                self.load(mskL4, mskL4[:, q4, :], self.i_mskL); self.load(mskU4, mskU4[:, q4, :], self.i_mskU)
            S.op("pool", lambda e: e.memset(onesf[:], 1.0), writes=[onesf.k])
            S.op("pool", lambda e: e.memset(onesb[:], 1.0), writes=[onesb.k])
            S.op("act", lambda e: e.activation(out=aexp[:], in_=aexp[:], func=AF.Exp), reads=[aexp.k], writes=[aexp.k])
            self.load_weight_bf16(es, wb, self.i_w1a, C1A, stg)
            xb = sb("xba", [128, 1024], F32); ss = sb("ssa", [128, 1], F32); xnb = sb("xnba", [128, 1024], BF16); hT = sb("hTa", [128, 8, 128], BF16)
            U = sb("U", [128, 24, 131], F32)
            Yg = sb("Yg", [128, 8, 128], F32); Ct = sb("Ct", [128, 8, 128], F32)
            sq = sb("sqa", [128, 1024], BF16); rn = sb("rn", [128, 1024], F32)
            qT = sb("qT", [128, 8, 128], BF16); kT = sb("kT", [128, 8, 128], BF16); vTb = sb("vTb", [128, 8, 128], BF16)
            ktok = sb("ktok", [128, 8, 128], F32)
            kbg = sb("kbg", [128, 8, 128], BF16); kdec = sb("kdec", [128, 8, 128], BF16); vb = sb("vb", [128, 8, 128], BF16)
            sm = sb("sm", [128, 96], F32)
            sm2 = sb("sm2", [128, 16], F32)
            Rall = sb("Rall", [128, 8, 128], F32)
            Dls = sb("Dls", [128, 8, 128], F32); DT = sb("DTm", [128, 8, 128], F32); egr = sb("egr", [128, 8, 128], F32)
            qdT = sb("qdT", [128, 8, 128], BF16); QKT = sb("QKT", [128, 8, 128], BF16)
            Am = [sb("Am%d" % i, [128, 8, 128], F32) for i in range(2)]
            Bm = [sb("Bm%d" % i, [128, 8, 128], F32) for i in range(2)]
            Pm = sb("Pm", [128, 8, 128], F32); TTb = sb("TTb", [128, 8, 128], BF16)
            u = sb("u", [128, 8, 128], F32); wT = sb("wT", [128, 8, 128], BF16); vn = sb("vn", [128, 8, 128], BF16)
            S32 = sb("S32", [128, 8, 128], F32); Stmp = sb("Stmp", [128, 8, 128], F32); Sbf = sb("Sbf", [128, 8, 128], BF16)
            o = sb("o", [128, 8, 128], F32); osq = sb("osq", [128, 8, 128], F32)
            zg = sb("zga", [128, 1024], BF16); mgt = sb("mgt", [128, 1024], BF16)
            S.op("pool", lambda e: e.memset(U[:], 0.0), writes=[U.k])
            S.op("pool", lambda e: e.memset(S32[:], 0.0), writes=[S32.k])
            S.op("pool", lambda e: e.memset(Sbf[:], 0.0), writes=[Sbf.k])
            f4 = lambda t, g: t[:, g * 4:(g + 1) * 4, :].rearrange("p h d -> p (h d)")

            def batch_mm(banks, fn_h):
                for g in range(2):
                    def f(e, g=g):
                        for hh in range(4):
                            h = g * 4 + hh
                            ins = fn_h(e, banks[g][:, hh * 128:(hh + 1) * 128], h)
                        return ins
                    yield g, f

            for p in range(P):
                own = (p % 2 == 1)
                j = p // 2
                r0 = p * 128
                self.load(xb, xb[:], self.i_x[r0:r0 + 128, :])
                if own:
                    S.dma(lambda e, j=j: e.dma_start(out=zg[:], in_=self.s_zg[j]), reads=[self.k_own[j]["zg"]], writes=[zg.k])
                self.rms_hT(xb, ss, xnb, hT, gmix, psb)
                for grp in range(6):
                    pq = ps[grp % 4]

                    def fm(e, pq=pq, grp=grp):
                        for t4 in range(4):
                            ct = grp * 4 + t4
                            for kc in range(8):
                                ins = e.matmul(pq[:, t4 * 128:(t4 + 1) * 128], lhsT=wb[:, kc, ct * 128:(ct + 1) * 128], rhs=hT[:, kc, :], start=(kc == 0), stop=(kc == 7), skip_group_check=True)
                        return ins
                    S.op("pe", fm, reads=[wb.k, hT.k], writes=[pq.k])
                    S.op("act", lambda e, pq=pq, grp=grp: e.activation(out=U[:, grp * 4:(grp + 1) * 4, 3:131], in_=pq[:].rearrange("p (t n) -> p t n", t=4), func=AF.Copy), reads=[pq.k], writes=[U.k])
                self.proj(ps[6], 16, hT, wb, 3072)
                S.op("act", lambda e: e.activation(out=sm[:, 0:16], in_=ps[6][:, 0:16], func=AF.Copy), reads=[ps[6].k], writes=[sm.k])
                S.op("act", lambda e: e.activation(out=sm[:, 16:24], in_=sm[:, 8:16], func=AF.Sigmoid), reads=[sm.k], writes=[sm.k])
                S.op("dve", lambda e: e.tensor_tensor(out=sm[:, 24:32], in0=sm[:, 0:8], in1=dtb[:], op=ALU.add), reads=[sm.k, dtb.k], writes=[sm.k])
                S.op("act", lambda e: e.activation(out=sm[:, 32:40], in_=sm[:, 24:32], func=AF.Abs), reads=[sm.k], writes=[sm.k])
                S.op("act", lambda e: e.activation(out=sm[:, 32:40], in_=sm[:, 32:40], func=AF.Exp, scale=-1.0), reads=[sm.k], writes=[sm.k])
                S.op("dve", lambda e: e.tensor_scalar(out=sm[:, 32:40], in0=sm[:, 32:40], scalar1=1.0, scalar2=None, op0=ALU.add), reads=[sm.k], writes=[sm.k])
                S.op("act", lambda e: e.activation(out=sm[:, 32:40], in_=sm[:, 32:40], func=AF.Ln), reads=[sm.k], writes=[sm.k])
                S.op("dve", lambda e: e.scalar_tensor_tensor(out=sm[:, 40:48], in0=sm[:, 24:32], scalar=0.0, in1=sm[:, 32:40], op0=ALU.max, op1=ALU.add), reads=[sm.k], writes=[sm.k])
                S.op("dve", lambda e: e.scalar_tensor_tensor(out=sm[:, 48:56], in0=sm[:, 40:48], scalar=-1.0, in1=aexp[:], op0=ALU.mult, op1=ALU.mult), reads=[sm.k, aexp.k], writes=[sm.k])

                def cs(e):
                    e.matmul(ps[6][:, 16:24], lhsT=uinc[:], rhs=sm[:, 48:56], start=True, stop=True, skip_group_check=True)
                    return e.matmul(ps[6][:, 24:32], lhsT=onesf[:], rhs=sm[:, 48:56], start=True, stop=True, skip_group_check=True)
                S.op("pe", cs, reads=[uinc.k, onesf.k, sm.k], writes=[ps[6].k])
                S.op("act", lambda e: e.activation(out=sm[:, 56:72], in_=ps[6][:, 16:32], func=AF.Copy), reads=[ps[6].k], writes=[sm.k])
                S.op("act", lambda e: e.activation(out=sm[:, 72:80], in_=sm[:, 56:64], func=AF.Exp), reads=[sm.k], writes=[sm.k])
                S.op("dve", lambda e: e.tensor_tensor(out=sm[:, 80:88], in0=sm[:, 64:72], in1=sm[:, 56:64], op=ALU.subtract), reads=[sm.k], writes=[sm.k])
                S.op("act", lambda e: e.activation(out=sm[:, 80:96], in_=sm[:, 80:96] if False else sm[:, 80:88], func=AF.Exp) if False else e.activation(out=sm[:, 80:88], in_=sm[:, 80:88], func=AF.Exp), reads=[sm.k], writes=[sm.k])
                S.op("act", lambda e: e.activation(out=sm[:, 88:96], in_=sm[:, 64:72], func=AF.Exp), reads=[sm.k], writes=[sm.k])
                S.op("dve", lambda e: e.tensor_tensor(out=sm2[:, 0:8], in0=sm[:, 16:24], in1=sm[:, 72:80], op=ALU.mult), reads=[sm.k], writes=[sm2.k])
                for grp, dst in ((0, qT), (1, kT), (2, vTb)):
                    if grp == 0 and not own:
                        continue
                    Ug = lambda jj, grp=grp: U[:, grp * 8:(grp + 1) * 8, jj:jj + 128]
                    cw = lambda jj, grp=grp: bc(convw[:, grp * 8:(grp + 1) * 8, jj:jj + 1], [128, 8, 128])
                    S.op("dve", lambda e, Ug=Ug, cw=cw: e.tensor_tensor(out=Yg[:], in0=Ug(0), in1=cw(0), op=ALU.mult), reads=[U.k, convw.k], writes=[Yg.k])
                    for jj in range(1, 4):
                        S.op("pool", lambda e, Ug=Ug, cw=cw, jj=jj: e.tensor_tensor(out=Ct[:], in0=Ug(jj), in1=cw(jj), op=ALU.mult), reads=[U.k, convw.k], writes=[Ct.k])
                        S.op("dve", lambda e: e.tensor_tensor(out=Yg[:], in0=Yg[:], in1=Ct[:], op=ALU.add), reads=[Yg.k, Ct.k], writes=[Yg.k])
                    Yf = Yg[:].rearrange("p h d -> p (h d)")
                    if grp == 2:
                        S.op("act", lambda e, Yf=Yf: e.activation(out=vTb[:].rearrange("p h d -> p (h d)"), in_=Yf, func=AF.Silu), reads=[Yg.k], writes=[vTb.k])
                        continue
                    S.op("act", lambda e, Yf=Yf: e.activation(out=Yf, in_=Yf, func=AF.Silu), reads=[Yg.k], writes=[Yg.k])
                    S.op("pool", lambda e, Yf=Yf: e.tensor_tensor(out=sq[:], in0=Yf, in1=Yf, op=ALU.mult), reads=[Yg.k], writes=[sq.k])
                    for g in range(2):
                        pq = ps[g]
                        S.op("pe", lambda e, pq=pq, g=g: e.matmul(pq[:], lhsT=onesb[:], rhs=sq[:, g * 512:(g + 1) * 512], start=True, stop=True), reads=[onesb.k, sq.k], writes=[pq.k])
                        S.op("dve", lambda e, pq=pq, g=g: e.tensor_scalar(out=rn[:, g * 512:(g + 1) * 512], in0=pq[:], scalar1=EPS, scalar2=None, op0=ALU.add), reads=[pq.k], writes=[rn.k])
                    S.op("act", lambda e: e.activation(out=rn[:], in_=rn[:], func=AF.Ln), reads=[rn.k], writes=[rn.k])
                    S.op("act", lambda e: e.activation(out=rn[:], in_=rn[:], func=AF.Exp, scale=-0.5), reads=[rn.k], writes=[rn.k])
                    sc = float(128.0 ** -0.5) if grp == 0 else 1.0
                    S.op("dve", lambda e, Yf=Yf, dst=dst, sc=sc: e.scalar_tensor_tensor(out=dst[:].rearrange("p h d -> p (h d)"), in0=Yf, scalar=sc, in1=rn[:], op0=ALU.mult, op1=ALU.mult), reads=[Yg.k, rn.k], writes=[dst.k])
                S.op("pool", lambda e: e.tensor_copy(out=U[:, :, 0:3], in_=U[:, :, 128:131]), reads=[U.k], writes=[U.k])
                def trk(e):
                    for h in range(8):
                        ins = e.transpose(out=psb[:, h * 128:(h + 1) * 128], in_=kT[:, h, :], identity=self.identb[:])
                    return ins
                S.op("pe", trk, reads=[kT.k, self.identb.k], writes=[psb.k])
                S.op("act", lambda e: e.activation(out=ktok[:].rearrange("p h d -> p (h d)"), in_=psb[:], func=AF.Copy), reads=[psb.k], writes=[ktok.k])
                S.op("dve", lambda e: e.tensor_tensor(out=kbg[:], in0=ktok[:], in1=bc(sm2[:, 0:8].unsqueeze(2), [128, 8, 128]), op=ALU.mult), reads=[ktok.k, sm2.k], writes=[kbg.k])
                S.op("pool", lambda e: e.tensor_tensor(out=kdec[:], in0=ktok[:], in1=bc(sm[:, 80:88].unsqueeze(2), [128, 8, 128]), op=ALU.mult), reads=[ktok.k, sm.k], writes=[kdec.k])

                def trv(e):
                    for h in range(8):
                        ins = e.transpose(out=psb[:, h * 128:(h + 1) * 128], in_=vTb[:, h, :], identity=self.identb[:])
                    return ins
                S.op("pe", trv, reads=[vTb.k, self.identb.k], writes=[psb.k])
                S.op("dve", lambda e: e.tensor_tensor(out=vb[:], in0=psb[:].rearrange("p (h d) -> p h d", h=8), in1=bc(sm[:, 16:24].unsqueeze(2), [128, 8, 128]), op=ALU.mult), reads=[psb.k, sm.k], writes=[vb.k])
                S.op("dve", lambda e: e.tensor_tensor(out=Rall[:], in0=bc(uinc[:].unsqueeze(1), [128, 8, 128]), in1=bc(sm[:, 48:56].unsqueeze(2), [128, 8, 128]), op=ALU.mult), reads=[uinc.k, sm.k], writes=[Rall.k])
                for g in range(2):
                    pq = ps[4 + g]

                    def grl(e, pq=pq, g=g):
                        e.matmul(pq[:], lhsT=onesf[:], rhs=f4(Rall, g), start=True, stop=False)
                        return e.matmul(pq[:], lhsT=self.identb[:], rhs=mskL4[:].rearrange("p h d -> p (h d)"), start=False, stop=True)
                    S.op("pe", grl, reads=[onesf.k, Rall.k, self.identb.k, mskL4.k], writes=[pq.k])
                    for hh in range(4):
                        h = g * 4 + hh
                        S.op("act", lambda e, pq=pq, h=h, hh=hh: e.activation(out=Dls[:, h, :], in_=pq[:, hh * 128:(hh + 1) * 128], func=AF.Exp, scale=-1.0, bias=sm[:, 56 + h:57 + h]), reads=[pq.k, sm.k], writes=[Dls.k])
                if own:
                    for g in range(2):
                        pq = ps[2 + g]
                        S.op("pe", lambda e, pq=pq, g=g: e.matmul(pq[:], lhsT=onesf[:], rhs=f4(Rall, g), start=True, stop=False), reads=[onesf.k, Rall.k], writes=[pq.k])
                        S.op("act", lambda e, pq=pq, g=g: e.activation(out=f4(egr, g), in_=pq[:], func=AF.Exp), reads=[pq.k], writes=[egr.k])
                        S.op("pe", lambda e, pq=pq: e.matmul(pq[:], lhsT=self.identb[:], rhs=mskU4[:].rearrange("p h d -> p (h d)"), start=False, stop=True), reads=[self.identb.k, mskU4.k], writes=[pq.k])
                        S.op("pool", lambda e, g=g: e.tensor_scalar(out=sm2[:, 8:16], in0=sm[:, 56:64], scalar1=-1.0, scalar2=None, op0=ALU.mult), reads=[sm.k], writes=[sm2.k])
                        for hh in range(4):
                            h = g * 4 + hh
                            S.op("act", lambda e, pq=pq, h=h, hh=hh: e.activation(out=DT[:, h, :], in_=pq[:, hh * 128:(hh + 1) * 128], func=AF.Exp, scale=1.0, bias=sm2[:, 8 + h:9 + h]), reads=[pq.k, sm2.k], writes=[DT.k])
                    S.op("pool", lambda e: e.tensor_tensor(out=qdT[:], in0=qT[:], in1=egr[:], op=ALU.mult), reads=[qT.k, egr.k], writes=[qdT.k])
                    for g, f in batch_mm((ps[0], ps[1]), lambda e, out, h: e.matmul(out, lhsT=kT[:, h, :], rhs=qT[:, h, :], start=True, stop=True, skip_group_check=True)):
                        S.op("pe", f, reads=[kT.k, qT.k], writes=[ps[g].k])
                        S.op("dve", lambda e, g=g: e.tensor_tensor(out=f4(QKT, g), in0=ps[g][:], in1=f4(DT, g), op=ALU.mult), reads=[ps[g].k, DT.k], writes=[QKT.k])
                A0, B0 = Am[0], Bm[0]
                for g, f in batch_mm((ps[0], ps[1]), lambda e, out, h: e.matmul(out, lhsT=kT[:, h, :], rhs=kT[:, h, :], start=True, stop=True, skip_group_check=True)):
                    S.op("pe", f, reads=[kT.k], writes=[ps[g].k])
                    for hh in range(4):
                        h = g * 4 + hh
                        S.op("dve", lambda e, g=g, h=h, hh=hh: e.scalar_tensor_tensor(out=A0[:, h, :], in0=ps[g][:, hh * 128:(hh + 1) * 128], scalar=sm[:, 16 + h:17 + h], in1=Dls[:, h, :], op0=ALU.mult, op1=ALU.mult), reads=[ps[g].k, sm.k, Dls.k], writes=[A0.k])
                for g, f in batch_mm((ps[2], ps[3]), lambda e, out, h: e.transpose(out=out, in_=A0[:, h, :], identity=self.identf[:])):
                    S.op("pe", f, reads=[A0.k, self.identf.k], writes=[ps[2 + g].k])
                    S.op("act", lambda e, g=g: e.activation(out=f4(B0, g), in_=ps[2 + g][:], func=AF.Copy), reads=[ps[2 + g].k], writes=[B0.k])
                S.op("pool", lambda e: e.tensor_tensor(out=Pm[:], in0=bc(self.identf[:].unsqueeze(1), [128, 8, 128]), in1=B0[:], op=ALU.subtract), reads=[self.identf.k, B0.k], writes=[Pm.k])
                cur = 0
                for lv in range(6):
                    Ac, Bc = Am[cur], Bm[cur]
                    An, Bn_ = Am[1 - cur], Bm[1 - cur]
                    last = (lv == 5)
                    for g, f in batch_mm((ps[0], ps[1]), lambda e, out, h, Ac=Ac, Bc=Bc: e.matmul(out, lhsT=Bc[:, h, :], rhs=Ac[:, h, :], start=True, stop=True, skip_group_check=True)):
                        S.op("pe", f, reads=[Ac.k, Bc.k], writes=[ps[g].k])
                        S.op("act", lambda e, g=g, An=An: e.activation(out=f4(An, g), in_=ps[g][:], func=AF.Copy), reads=[ps[g].k], writes=[An.k])
                    if not last:
                        for g, f in batch_mm((ps[2], ps[3]), lambda e, out, h, Ac=Ac, Bc=Bc: e.matmul(out, lhsT=Ac[:, h, :], rhs=Bc[:, h, :], start=True, stop=True, skip_group_check=True)):
                            S.op("pe", f, reads=[Ac.k, Bc.k], writes=[ps[2 + g].k])
                            S.op("dve", lambda e, g=g, Bn_=Bn_: e.tensor_copy(out=f4(Bn_, g), in_=ps[2 + g][:]), reads=[ps[2 + g].k], writes=[Bn_.k])
                    for g, f in batch_mm((ps[4], ps[5]), lambda e, out, h, An=An: e.matmul(out, lhsT=An[:, h, :], rhs=Pm[:, h, :], start=True, stop=True, skip_group_check=True)):
                        S.op("pe", f, reads=[An.k, Pm.k], writes=[ps[4 + g].k])
                        S.op("dve", lambda e, g=g: e.tensor_tensor(out=f4(Pm, g), in0=f4(Pm, g), in1=ps[4 + g][:], op=ALU.add), reads=[Pm.k, ps[4 + g].k], writes=[Pm.k])
                    cur = 1 - cur
                S.op("pool", lambda e: e.tensor_copy(out=TTb[:], in_=Pm[:]), reads=[Pm.k], writes=[TTb.k])
                for g, f in batch_mm((ps[0], ps[1]), lambda e, out, h: e.matmul(out, lhsT=TTb[:, h, :], rhs=vb[:, h, :], start=True, stop=True, skip_group_check=True)):
                    S.op("pe", f, reads=[TTb.k, vb.k], writes=[ps[g].k])
                    S.op("act", lambda e, g=g: e.activation(out=f4(u, g), in_=ps[g][:], func=AF.Copy), reads=[ps[g].k], writes=[u.k])
                for g, f in batch_mm((ps[2], ps[3]), lambda e, out, h: e.matmul(out, lhsT=kbg[:, h, :], rhs=TTb[:, h, :], start=True, stop=True, skip_group_check=True)):
                    S.op("pe", f, reads=[TTb.k, kbg.k], writes=[ps[2 + g].k])
                    S.op("act", lambda e, g=g: e.activation(out=f4(wT, g), in_=ps[2 + g][:], func=AF.Copy), reads=[ps[2 + g].k], writes=[wT.k])
                for g, f in batch_mm((ps[4], ps[5]), lambda e, out, h: e.matmul(out, lhsT=wT[:, h, :], rhs=Sbf[:, h, :], start=True, stop=True, skip_group_check=True)):
                    S.op("pe", f, reads=[wT.k, Sbf.k], writes=[ps[4 + g].k])
                    S.op("dve", lambda e, g=g: e.tensor_tensor(out=f4(vn, g), in0=f4(u, g), in1=ps[4 + g][:], op=ALU.subtract), reads=[u.k, ps[4 + g].k], writes=[vn.k])
                if own:
                    for g in range(2):
                        def fo(e, g=g):
                            for hh in range(4):
                                h = g * 4 + hh
                                e.matmul(ps[g][:, hh * 128:(hh + 1) * 128], lhsT=qdT[:, h, :], rhs=Sbf[:, h, :], start=True, stop=False, skip_group_check=True)
                                ins = e.matmul(ps[g][:, hh * 128:(hh + 1) * 128], lhsT=QKT[:, h, :], rhs=vn[:, h, :], start=False, stop=True, skip_group_check=True)
                            return ins
                        S.op("pe", fo, reads=[qdT.k, Sbf.k, QKT.k, vn.k], writes=[ps[g].k])
                        S.op("act", lambda e, g=g: e.activation(out=f4(o, g), in_=ps[g][:], func=AF.Copy), reads=[ps[g].k], writes=[o.k])
                S.op("pool", lambda e: e.tensor_tensor(out=Stmp[:], in0=S32[:], in1=bc(sm[:, 88:96].unsqueeze(2), [128, 8, 128]), op=ALU.mult), reads=[S32.k, sm.k], writes=[Stmp.k])
                for g, f in batch_mm((ps[2], ps[3]), lambda e, out, h: e.matmul(out, lhsT=kdec[:, h, :], rhs=vn[:, h, :], start=True, stop=True, skip_group_check=True)):
                    S.op("pe", f, reads=[kdec.k, vn.k], writes=[ps[2 + g].k])
                    S.op("dve", lambda e, g=g: e.tensor_tensor(out=f4(S32, g), in0=f4(Stmp, g), in1=ps[2 + g][:], op=ALU.add), reads=[Stmp.k, ps[2 + g].k], writes=[S32.k])
                S.op("act", lambda e: e.activation(out=Sbf[:].rearrange("p h d -> p (h d)"), in_=S32[:].rearrange("p h d -> p (h d)"), func=AF.Copy), reads=[S32.k], writes=[Sbf.k])
                if own:
                    S.op("pool", lambda e: e.tensor_tensor(out=osq[:], in0=o[:], in1=o[:], op=ALU.mult), reads=[o.k], writes=[osq.k])
                    S.op("dve", lambda e: e.tensor_reduce(out=sm2[:, 8:16], in_=osq[:], axis=AX.X, op=ALU.add), reads=[osq.k], writes=[sm2.k])
                    S.op("dve", lambda e: e.tensor_scalar(out=sm2[:, 8:16], in0=sm2[:, 8:16], scalar1=1.0 / 128, scalar2=EPS, op0=ALU.mult, op1=ALU.add), reads=[sm2.k], writes=[sm2.k])
                    S.op("act", lambda e: e.activation(out=sm2[:, 8:16], in_=sm2[:, 8:16], func=AF.Ln), reads=[sm2.k], writes=[sm2.k])
                    S.op("act", lambda e: e.activation(out=sm2[:, 8:16], in_=sm2[:, 8:16], func=AF.Exp, scale=-0.5), reads=[sm2.k], writes=[sm2.k])
                    S.op("dve", lambda e: e.tensor_tensor(out=o[:], in0=o[:], in1=bc(sm2[:, 8:16].unsqueeze(2), [128, 8, 128]), op=ALU.mult), reads=[o.k, sm2.k], writes=[o.k])
                    S.op("pool", lambda e: e.tensor_tensor(out=o[:], in0=o[:], in1=bc(ggdn[:].unsqueeze(1), [128, 8, 128]), op=ALU.mult), reads=[o.k, ggdn.k], writes=[o.k])
                    S.op("dve", lambda e: e.tensor_tensor(out=mgt[:], in0=o[:].rearrange("p h d -> p (h d)"), in1=zg[:], op=ALU.mult), reads=[o.k, zg.k], writes=[mgt.k])
                    S.dma(lambda e, j=j: e.dma_start(out=self.s_mg[j], in_=mgt[:]), reads=[mgt.k], writes=[self.k_own[j]["mg"]])
            S.barrier()
            S.emit()

    def phase1a_stub(self):
        S = self.S
        with ExitStack() as es:
            z = self.sb(es, "zmg", [128, 1024], BF16)
            S.op("pool", lambda e: e.memset(z[:], 0.0), writes=[z.k])
            for j in range(self.NO):
                S.dma(lambda e, j=j: e.dma_start(out=self.s_mg[j], in_=z[:]), reads=[z.k], writes=[self.k_own[j]["mg"]])
            S.barrier()
            S.emit()


def build(S, debug=False, gdn=True):
    B = Builder(S, debug)
    with ExitStack() as es:
        B.setup(es)
        B.phase1b()
        if gdn:
            B.phase1a()
        else:
            B.phase1a_stub()
        B.phase2()
        B.phase3()
    return B.nc


IN_SPLITS = (3072, 1024, 8, 8, 1024, 256, 256, 512, 64, 8, 1024, 1024)


def host_consts():
    bf = ml_dtypes.bfloat16
    i = np.arange(128)
    c = {}
    c["identf"] = np.eye(128, dtype=np.float32)
    c["identb"] = np.eye(128).astype(bf)
    c["uinc"] = (i[:, None] <= i[None, :]).astype(np.float32)
    c["causal"] = np.where(i[None, :] <= i[:, None], 0.0, NEG).astype(np.float32)
    c["mskU"] = np.where(i[None, :] >= i[:, None], 0.0, -30000.0).astype(bf)
    c["mskL"] = np.where(i[None, :] < i[:, None], 0.0, 30000.0).astype(bf)
    return c


def rope_tab(pos, rot):
    inv = (np.float32(500000.0) ** (-np.arange(0, rot, 2, dtype=np.float32) / np.float32(rot))).astype(np.float32)
    ang = pos.astype(np.float32)[:, None] * inv[None, :]
    return np.cos(ang).astype(np.float32), np.sin(ang).astype(np.float32)


def make_in_maps(S, x, norm_mix_g, w_in, conv_w, a_log, dt_bias, gdn_norm_g, q_norm_g, k_norm_g,
                 w_out, norm_mlp_g, w_mlp_up, w_mlp_down):
    Bn = x.shape[0]
    f = np.float32
    w_in = np.asarray(w_in[0], f)
    pts = np.cumsum((0,) + IN_SPLITS)
    seg = {n: w_in[:, pts[i]:pts[i + 1]] for i, n in enumerate(
        ("qkv", "z", "ga", "gb", "aq", "ak", "av", "iq", "ik", "iw", "gatea", "gateb"))}
    w1a = np.ascontiguousarray(np.concatenate([seg["qkv"], seg["ga"], seg["gb"]], 1))
    w1b = np.ascontiguousarray(np.concatenate([seg["ak"], seg["av"], seg["ik"], seg["iw"], seg["aq"], seg["iq"], seg["gateb"], seg["z"], seg["gatea"]], 1))
    col = lambda v: np.ascontiguousarray(np.asarray(v, f).reshape(8, 128).T)
    rep = lambda v: np.ascontiguousarray(np.broadcast_to(np.asarray(v, f)[None, :], (128, len(v))))
    common = host_consts()
    common.update(dict(
        w1a=w1a, w1b=w1b, wout=np.ascontiguousarray(w_out[0], f), wup=np.ascontiguousarray(w_mlp_up[0], f),
        wdn=np.ascontiguousarray(w_mlp_down[0], f),
        gmix=col(norm_mix_g[0]), gmlp=col(norm_mlp_g[0]), gq=rep(q_norm_g[0]), gk=rep(k_norm_g[0]),
        ggdn=rep(gdn_norm_g[0]), alog=rep(a_log[0]), dtb=rep(dt_bias[0]),
        convw=np.ascontiguousarray(np.asarray(conv_w[0], f).reshape(4, 24, 128).transpose(2, 1, 0).reshape(128, 96)),
    ))
    maps = []
    for b in range(Bn):
        for c in range(2):
            m = dict(common)
            xb = np.asarray(x[b], f)
            if c == 0:
                xs = np.concatenate([np.zeros((128, D), f), xb[:S - 128]], 0)
            else:
                xs = xb
            pos = np.maximum(np.arange(S) + (c - 1) * 128, 0)
            m["xseq"] = np.ascontiguousarray(xs)
            m["cosA"], m["sinA"] = rope_tab(pos, 32)
            m["cosI"], m["sinI"] = rope_tab(pos, 16)
            m["blk0"] = np.full((128, 128), NEG if c == 0 else 0.0, f)
            maps.append(m)
    return maps


def assemble(S, results, Bn):
    out = np.zeros((Bn, S, D), np.float32)
    ov = out.reshape(Bn, S // 256, 2, 128, D)
    for b in range(Bn):
        for c in range(2):
            r = results[b * 2 + c]["out"].reshape(S // 256, 128, D)
            ov[b, :, c] = r
    return out


_NC_CACHE = {}


def kernel(**inputs):
    x = np.asarray(inputs["x"])
    Bn, S, _ = x.shape
    if S not in _NC_CACHE:
        _NC_CACHE[S] = build(S)
    nc = _NC_CACHE[S]
    maps = make_in_maps(S, **{k: np.asarray(v) for k, v in inputs.items()})
    res = run_bass_kernel_spmd(nc, maps, core_ids=list(range(2 * Bn)))
    return assemble(S, res.results, Bn)
```

```python
from contextlib import ExitStack
import numpy as np
import ml_dtypes
import concourse.bass as bass
import concourse.mybir as mybir
from concourse.bass_utils import run_bass_kernel_spmd

F32 = mybir.dt.float32
BF16 = mybir.dt.bfloat16
AF = mybir.ActivationFunctionType
ALU = mybir.AluOpType
AX = mybir.AxisListType

D = 1024
EPS = 1e-6
NEG = -1.0e30
SAME_ENGINE_SYNC = True
IDX_F32R = False


class Tok:
    __slots__ = ("name", "w", "r")

    def __init__(self, name):
        self.name = name
        self.w = None
        self.r = {}


class Sched:
    ENG = ("pe", "act", "dve", "pool", "sp")

    def __init__(self, nc, es, n_dma_sems=24):
        self.nc = nc
        self.sems = {}
        for e in ("pe", "act", "dve", "pool"):
            self.sems[e] = es.enter_context(nc.semaphore("s_" + e))
        self.ndma = n_dma_sems
        for k in range(n_dma_sems):
            self.sems[("d", k)] = es.enter_context(nc.semaphore("s_d%d" % k))
        self.cnt = {e: 0 for e in ("pe", "act", "dve", "pool")}
        self.dtot = [0] * n_dma_sems
        self.rr = 0
        self.waited = {e: {} for e in self.ENG}
        self.ops = {e: [] for e in self.ENG}
        self.nops = 0
        self.dry = False

    def _deps(self, eng, reads, writes):
        deps = {}

        def add(d):
            if d is None:
                return
            sid, val = d
            if sid == eng and (eng == "pe" or not SAME_ENGINE_SYNC):
                return
            if deps.get(sid, 0) < val:
                deps[sid] = val

        for t in reads:
            add(t.w)
        for t in writes:
            add(t.w)
            for sid, val in t.r.items():
                add((sid, val))
        out = []
        wd = self.waited[eng]
        for sid, val in deps.items():
            if wd.get(sid, 0) < val:
                wd[sid] = val
                out.append((sid, val))
        return out

    def _mark(self, me, reads, writes):
        for t in reads:
            if t.r.get(me[0], 0) < me[1]:
                t.r[me[0]] = me[1]
        for t in writes:
            t.w = me
            t.r = {}

    def op(self, eng, fn, reads=(), writes=()):
        if self.dry:
            return
        waits = self._deps(eng, reads, writes)
        self.cnt[eng] += 1
        me = (eng, self.cnt[eng])
        self._mark(me, reads, writes)
        self.ops[eng].append((waits, fn, eng, 1))
        self.nops += 1

    def dma(self, fn, reads=(), writes=()):
        if self.dry:
            return
        k = self.rr
        self.rr = (self.rr + 1) % self.ndma
        sid = ("d", k)
        waits = self._deps("sp", reads, writes)
        wd = self.waited["sp"]
        if wd.get(sid, 0) < self.dtot[k]:
            wd[sid] = self.dtot[k]
            waits.append((sid, self.dtot[k]))
        self.dtot[k] += 16
        me = (sid, self.dtot[k])
        self._mark(me, reads, writes)
        self.ops["sp"].append((waits, fn, sid, 16))
        self.nops += 1

    def barrier(self):
        for e in self.ENG:
            waits = []
            wd = self.waited[e]
            for c in ("pe", "act", "dve", "pool"):
                if c != e and wd.get(c, 0) < self.cnt[c]:
                    wd[c] = self.cnt[c]
                    waits.append((c, self.cnt[c]))
            for k in range(self.ndma):
                sid = ("d", k)
                if wd.get(sid, 0) < self.dtot[k]:
                    wd[sid] = self.dtot[k]
                    waits.append((sid, self.dtot[k]))
            if waits:
                self.ops[e].append((waits, None, None, 0))

    def emit(self):
        nc = self.nc
        ops = self.ops
        self.ops = {e: [] for e in self.ENG}
        sems = self.sems

        def run(engh, lst):
            for waits, fn, sid, inc in lst:
                for s, v in waits:
                    engh.wait_ge(sems[s], v)
                if fn is not None:
                    fn(engh).then_inc(sems[sid], inc)

        with nc.Block() as block:
            @block.tensor
            def _(e):
                run(e, ops["pe"])

            @block.scalar
            def _(e):
                run(e, ops["act"])

            @block.vector
            def _(e):
                run(e, ops["dve"])

            @block.gpsimd
            def _(e):
                run(e, ops["pool"])

            @block.sync
            def _(e):
                run(e, ops["sp"])


class Ctx:
    pass


def bc(ap, shape):
    return ap.to_broadcast(list(shape))


class TL:
    def __init__(self, t, name):
        self.t = t
        self.k = Tok(name)

    def __getitem__(self, key):
        return self.t[key]


C1B = 5192
C1A = 3088


class Builder:
    def __init__(self, S, debug=False):
        self.Sq = S
        self.P = S // 128
        self.NO = self.P // 2
        self.debug = debug
        self.nc = bass.Bass("TRN2", target_bir_lowering=False)

    def dram_in(self, name, shape, dt=F32):
        return self.nc.dram_tensor(name, list(shape), dt, kind="ExternalInput").ap()

    def dram_scr(self, name, shape, dt):
        kind = "ExternalOutput" if self.debug else "Internal"
        return self.nc.dram_tensor(name, list(shape), dt, kind=kind).ap()

    def sb(self, es, name, shape, dt):
        return TL(es.enter_context(self.nc.sbuf_tensor("t_" + name, list(shape), dt)), name)

    def load(self, dst, dst_ap, src_ap):
        self.S.dma(lambda e: e.dma_start(out=dst_ap, in_=src_ap), writes=[dst.k])

    def rsqrt_small(self, v, n):
        S = self.S
        S.op("act", lambda e: e.activation(out=v[:, 0:n], in_=v[:, 0:n], func=AF.Ln), reads=[v.k], writes=[v.k])
        S.op("act", lambda e: e.activation(out=v[:, 0:n], in_=v[:, 0:n], func=AF.Exp, scale=-0.5), reads=[v.k], writes=[v.k])

    def load_weight_bf16(self, es, wb, w_ap, ncols, stg):
        S = self.S
        nk = w_ap.shape[0] // 128
        CH = stg[0].t.shape[1]
        i = 0
        for k in range(nk):
            for c0 in range(0, ncols, CH):
                cw = min(CH, ncols - c0)
                st = stg[i % len(stg)]
                S.dma(lambda e, st=st, k=k, c0=c0, cw=cw: e.dma_start(out=st[:, 0:cw], in_=w_ap[k * 128:(k + 1) * 128, c0:c0 + cw]), writes=[st.k])
                eng = ("pool", "dve", "act")[i % 3] if False else "pool"
                S.op(eng, lambda e, st=st, k=k, c0=c0, cw=cw: e.tensor_copy(out=wb[:, k, c0:c0 + cw], in_=st[:, 0:cw]), reads=[st.k], writes=[wb.k])
                i += 1

    def rms_hT(self, xb, ss, xnb, hT, gcol, psb):
        S = self.S
        junk = self.junk
        idb = self.identb
        S.op("act", lambda e: e.activation(out=junk[:, 0:1024], in_=xb[:], func=AF.Square, accum_out=ss[:, 0:1]), reads=[xb.k], writes=[junk.k, ss.k])
        S.op("dve", lambda e: e.tensor_scalar(out=ss[:, 0:1], in0=ss[:, 0:1], scalar1=1.0 / 1024, scalar2=EPS, op0=ALU.mult, op1=ALU.add), reads=[ss.k], writes=[ss.k])
        self.rsqrt_small(ss, 1)
        S.op("dve", lambda e: e.tensor_scalar(out=xnb[:], in0=xb[:], scalar1=ss[:, 0:1], scalar2=None, op0=ALU.mult), reads=[ss.k, xb.k], writes=[xnb.k])

        def tr(e):
            for kc in range(8):
                i = e.transpose(out=psb[:, kc * 128:(kc + 1) * 128], in_=xnb[:, kc * 128:(kc + 1) * 128], identity=idb[:])
            return i
        S.op("pe", tr, reads=[xnb.k, idb.k], writes=[psb.k])
        S.op("dve", lambda e: e.tensor_tensor(out=hT[:], in0=psb[:].rearrange("p (k t) -> p k t", k=8), in1=bc(gcol[:].unsqueeze(2), [128, 8, 128]), op=ALU.mult), reads=[psb.k, gcol.k], writes=[hT.k])

    def proj(self, ps, ncol, hT, wb, c0, pcol=0, start=True):
        def f(e):
            for kc in range(8):
                i = e.matmul(ps[:, pcol:pcol + ncol], lhsT=hT[:, kc, :], rhs=wb[:, kc, c0:c0 + ncol], start=(kc == 0), stop=(kc == 7))
            return i
        self.S.op("pe", f, reads=[hT.k, wb.k], writes=[ps.k])

    def setup(self, es):
        nc = self.nc
        S_, P, NO = self.Sq, self.P, self.NO
        self.S = Sched(nc, es)
        d = self.dram_in
        self.i_x = d("xseq", [S_, D])
        self.i_w1a = d("w1a", [D, C1A])
        self.i_w1b = d("w1b", [D, C1B])
        self.i_wout = d("wout", [D, D])
        self.i_wup = d("wup", [D, 4 * D])
        self.i_wdn = d("wdn", [4 * D, D])
        self.i_cosA = d("cosA", [S_, 16]); self.i_sinA = d("sinA", [S_, 16])
        self.i_cosI = d("cosI", [S_, 8]); self.i_sinI = d("sinI", [S_, 8])
        self.i_blk0 = d("blk0", [128, 128])
        self.i_identf = d("identf", [128, 128]); self.i_identb = d("identb", [128, 128], BF16)
        self.i_uinc = d("uinc", [128, 128]); self.i_causal = d("causal", [128, 128])
        self.i_mskU = d("mskU", [128, 128], BF16); self.i_mskL = d("mskL", [128, 128], BF16)
        self.i_gmix = d("gmix", [128, 8]); self.i_gmlp = d("gmlp", [128, 8])
        self.i_gq = d("gq", [128, 128]); self.i_gk = d("gk", [128, 128]); self.i_ggdn = d("ggdn", [128, 128])
        self.i_alog = d("alog", [128, 8]); self.i_dtb = d("dtb", [128, 8])
        self.i_convw = d("convw", [128, 24 * 4])
        self.o_out = nc.dram_tensor("out", [NO * 128, D], F32, kind="ExternalOutput").ap()
        s = self.dram_scr
        self.s_akT = s("s_akT", [128, 2 * S_], BF16)
        self.s_v = s("s_v", [S_, 258], BF16)
        self.s_ikT = s("s_ikT", [128, S_], F32)
        self.s_aqT = s("s_aqT", [NO, 128, 1024], BF16)
        self.s_iqT = s("s_iqT", [NO, 128, 512], F32)
        self.s_iw = s("s_iw", [NO, 128, 8], F32)
        self.s_sgb = s("s_sgb", [NO, 128, 1024], BF16)
        self.s_mg = s("s_mg", [NO, 128, 1024], BF16)
        self.s_zg = s("s_zg", [NO, 128, 1024], BF16)
        self.s_mrg = s("s_mrg", [NO, 128, 1024], BF16)
        self.k_akT = Tok("akT_d"); self.k_v = Tok("v_d"); self.k_ikT = Tok("ikT_d")
        self.k_own = [{n: Tok(n + str(j)) for n in ("aqT", "iqT", "iw", "sgb", "mg", "x1", "zg")} for j in range(NO)]
        self.ps = [TL(es.enter_context(nc.psum_tensor("ps%d" % i, [128, 512], F32)), "ps%d" % i) for i in range(7)]
        self.psb = TL(es.enter_context(nc.psum_tensor("psb", [128, 1024], BF16)), "psb")
        sb = lambda n, sh, dt: self.sb(es, n, sh, dt)
        self.identf = sb("identf", [128, 128], F32); self.identb = sb("identb", [128, 128], BF16)
        self.junk = sb("junk", [128, 1024], F32)
        for t, src in ((self.identf, self.i_identf), (self.identb, self.i_identb)):
            self.load(t, t[:], src)

    def norm_rope(self, H, src_ap, src_tok, gain, extra, out, ocol, cs, sn, tmp):
        S = self.S
        tf, sq, ssh, ra, rb = tmp
        n = H * 128
        S.op("act", lambda e: e.activation(out=tf[:, 0:n], in_=src_ap, func=AF.Copy), reads=[src_tok], writes=[tf.k])
        S.op("dve", lambda e: e.tensor_tensor(out=sq[:, 0:n], in0=tf[:, 0:n], in1=tf[:, 0:n], op=ALU.mult), reads=[tf.k], writes=[sq.k])
        S.op("dve", lambda e: e.tensor_reduce(out=ssh[:, 0:H], in_=sq[:, 0:n].rearrange("p (h d) -> p h d", h=H), axis=AX.X, op=ALU.add), reads=[sq.k], writes=[ssh.k])
        S.op("dve", lambda e: e.tensor_scalar(out=ssh[:, 0:H], in0=ssh[:, 0:H], scalar1=1.0 / 128, scalar2=EPS, op0=ALU.mult, op1=ALU.add), reads=[ssh.k], writes=[ssh.k])
        self.rsqrt_small(ssh, H)
        if extra != 1.0:
            S.op("dve", lambda e: e.tensor_scalar(out=ssh[:, 0:H], in0=ssh[:, 0:H], scalar1=extra, scalar2=None, op0=ALU.mult), reads=[ssh.k], writes=[ssh.k])
        tf3 = tf[:, 0:n].rearrange("p (h d) -> p h d", h=H)
        S.op("dve", lambda e: e.tensor_tensor(out=tf3, in0=tf3, in1=bc(ssh[:, 0:H].unsqueeze(2), [128, H, 128]), op=ALU.mult), reads=[tf.k, ssh.k], writes=[tf.k])
        S.op("pool", lambda e: e.tensor_tensor(out=tf3, in0=tf3, in1=bc(gain[:].unsqueeze(1), [128, H, 128]), op=ALU.mult), reads=[tf.k, gain.k], writes=[tf.k])
        o3 = out[:, ocol:ocol + n].rearrange("p (h d) -> p h d", h=H)
        S.op("act", lambda e: e.activation(out=out[:, ocol:ocol + n], in_=tf[:, 0:n], func=AF.Copy), reads=[tf.k], writes=[out.k])
        self.rope(tf3, tf.k, o3, out.k, H, 16, cs, sn, ra, rb)

    def rope(self, t3, ttok, o3, otok, H, hf, cs, sn, ra, rb):
        S = self.S
        x1 = t3[:, :, 0:hf]; x2 = t3[:, :, hf:2 * hf]
        c = bc(cs[:].unsqueeze(1), [128, H, hf]); s_ = bc(sn[:].unsqueeze(1), [128, H, hf])
        a3 = ra[:, 0:H * hf].rearrange("p (h d) -> p h d", h=H)
        b3 = rb[:, 0:H * hf].rearrange("p (h d) -> p h d", h=H)
        S.op("pool", lambda e: e.tensor_tensor(out=a3, in0=x1, in1=c, op=ALU.mult), reads=[ttok, cs.k], writes=[ra.k])
        S.op("pool", lambda e: e.tensor_tensor(out=b3, in0=x2, in1=s_, op=ALU.mult), reads=[ttok, sn.k], writes=[rb.k])
        S.op("dve", lambda e: e.tensor_tensor(out=o3[:, :, 0:hf], in0=a3, in1=b3, op=ALU.subtract), reads=[ra.k, rb.k], writes=[otok])
        S.op("pool", lambda e: e.tensor_tensor(out=a3, in0=x2, in1=c, op=ALU.mult), reads=[ttok, cs.k], writes=[ra.k])
        S.op("pool", lambda e: e.tensor_tensor(out=b3, in0=x1, in1=s_, op=ALU.mult), reads=[ttok, sn.k], writes=[rb.k])
        S.op("dve", lambda e: e.tensor_tensor(out=o3[:, :, hf:2 * hf], in0=a3, in1=b3, op=ALU.add), reads=[ra.k, rb.k], writes=[otok])

    def phase1b(self):
        S = self.S
        P = self.P
        ps, psb = self.ps, self.psb
        with ExitStack() as es:
            sb = lambda n, sh, dt: self.sb(es, n, sh, dt)
            wb = sb("w1b", [128, 8, C1B], BF16)
            stg = [sb("stg%d" % i, [128, 1024], F32) for i in range(2)]
            gmix = sb("gmix", [128, 8], F32); gq = sb("gq", [128, 128], F32); gk = sb("gk", [128, 128], F32)
            self.load(gmix, gmix[:], self.i_gmix); self.load(gq, gq[:], self.i_gq); self.load(gk, gk[:], self.i_gk)
            self.load_weight_bf16(es, wb, self.i_w1b, C1B, stg)
            xb = [sb("xb%d" % i, [128, 1024], F32) for i in range(2)]
            ss = sb("ss", [128, 1], F32); xnb = sb("xnb", [128, 1024], BF16); hT = sb("hT", [128, 8, 128], BF16)
            cA = sb("cA", [128, 16], F32); sA = sb("sA", [128, 16], F32); cI = sb("cI", [128, 8], F32); sI = sb("sI", [128, 8], F32)
            tmp = (sb("tf", [128, 512], F32), sb("sq", [128, 512], F32), sb("ssh", [128, 8], F32), sb("ra", [128, 64], F32), sb("rb", [128, 64], F32))
            ra, rb = tmp[3], tmp[4]
            akb = sb("akb", [128, 256], BF16); akT = sb("akTb", [128, 256], BF16)
            vaug = sb("vaug", [128, 258], BF16)
            ikf = sb("ikf", [128, 72], F32); ik2 = sb("ik2", [128, 128], F32); ikT = sb("ikTb", [128, 128], F32)
            iw = sb("iwb", [128, 8], F32)
            aqb = sb("aqb", [128, 1024], BF16); aqT = sb("aqTb", [128, 1024], BF16)
            iqf = sb("iqf", [128, 512], F32); iqo = sb("iqo", [128, 512], F32); iqT = sb("iqTb", [128, 512], F32)
            sgb = sb("sgbb", [128, 1024], BF16)
            zs = sb("zs", [128, 1024], F32); sga = sb("sga", [128, 1024], F32); zg = sb("zgb", [128, 1024], BF16)
            S.op("pool", lambda e: e.memset(vaug[:], 1.0), writes=[vaug.k])
            for p in range(P):
                own = (p % 2 == 1)
                j = p // 2
                x = xb[p % 2]
                r0 = p * 128
                self.load(x, x[:], self.i_x[r0:r0 + 128, :])
                self.load(cA, cA[:], self.i_cosA[r0:r0 + 128, :]); self.load(sA, sA[:], self.i_sinA[r0:r0 + 128, :])
                self.load(cI, cI[:], self.i_cosI[r0:r0 + 128, :]); self.load(sI, sI[:], self.i_sinI[r0:r0 + 128, :])
                self.rms_hT(x, ss, xnb, hT, gmix, psb)
                self.proj(ps[0], 512, hT, wb, 0)
                self.norm_rope(2, ps[0][:, 0:256], ps[0].k, gk, 1.0, akb, 0, cA, sA, tmp)
                S.op("act", lambda e: e.activation(out=vaug[:].rearrange("p (c d) -> p c d", c=2)[:, :, 0:128], in_=ps[0][:, 256:512].rearrange("p (c d) -> p c d", c=2), func=AF.Copy), reads=[ps[0].k], writes=[vaug.k])
                S.dma(lambda e, r0=r0: e.dma_start(out=self.s_v[r0:r0 + 128, :], in_=vaug[:]), reads=[vaug.k], writes=[self.k_v])

                def trk(e):
                    for c in range(2):
                        i = e.transpose(out=psb[:, c * 128:(c + 1) * 128], in_=akb[:, c * 128:(c + 1) * 128], identity=self.identb[:])
                    return i
                S.op("pe", trk, reads=[akb.k, self.identb.k], writes=[psb.k])
                S.op("act", lambda e: e.activation(out=akT[:], in_=psb[:, 0:256], func=AF.Copy), reads=[psb.k], writes=[akT.k])
                S.dma(lambda e, r0=r0: e.dma_start(out=self.s_akT.rearrange("h (c s) -> h c s", c=2)[:, :, r0:r0 + 128], in_=akT[:].rearrange("h (c s) -> h c s", c=2)), reads=[akT.k], writes=[self.k_akT])
                self.proj(ps[1], 72, hT, wb, 512)
                S.op("act", lambda e: e.activation(out=ikf[:], in_=ps[1][:, 0:72], func=AF.Copy), reads=[ps[1].k], writes=[ikf.k])
                S.op("dve", lambda e: e.tensor_copy(out=ik2[:, 0:64], in_=ikf[:, 0:64]), reads=[ikf.k], writes=[ik2.k])
                self.rope(ikf[:, 0:64].rearrange("p (h d) -> p h d", h=1), ikf.k, ik2[:, 0:64].rearrange("p (h d) -> p h d", h=1), ik2.k, 1, 8, cI, sI, ra, rb)
                S.op("dve", lambda e: e.tensor_copy(out=ik2[:, 64:128], in_=ik2[:, 0:64]), reads=[ik2.k], writes=[ik2.k])
                S.op("pe", lambda e: e.transpose(out=ps[2][:, 0:128], in_=ik2[:], identity=self.identf[:]), reads=[ik2.k, self.identf.k], writes=[ps[2].k])
                S.op("act", lambda e: e.activation(out=ikT[:], in_=ps[2][:, 0:128], func=AF.Copy), reads=[ps[2].k], writes=[ikT.k])
                S.dma(lambda e, r0=r0: e.dma_start(out=self.s_ikT[:, r0:r0 + 128], in_=ikT[:]), reads=[ikT.k], writes=[self.k_ikT])
                if not own:
                    continue
                ko = self.k_own[j]
                S.op("dve", lambda e: e.tensor_scalar(out=iw[:], in0=ikf[:, 64:72], scalar1=float(512.0 ** -0.5), scalar2=None, op0=ALU.mult), reads=[ikf.k], writes=[iw.k])
                S.dma(lambda e, j=j: e.dma_start(out=self.s_iw[j], in_=iw[:]), reads=[iw.k], writes=[ko["iw"]])
                for half in range(2):
                    pq = ps[3 + half]
                    self.proj(pq, 512, hT, wb, 584 + half * 512)
                    self.norm_rope(4, pq[:, 0:512], pq.k, gq, float(128.0 ** -0.5), aqb, half * 512, cA, sA, tmp)

                def trq(e):
                    for h in range(8):
                        i = e.transpose(out=psb[:, h * 128:(h + 1) * 128], in_=aqb[:, h * 128:(h + 1) * 128], identity=self.identb[:])
                    return i
                S.op("pe", trq, reads=[aqb.k, self.identb.k], writes=[psb.k])
                S.op("act", lambda e: e.activation(out=aqT[:], in_=psb[:], func=AF.Copy), reads=[psb.k], writes=[aqT.k])
                S.dma(lambda e, j=j: e.dma_start(out=self.s_aqT[j], in_=aqT[:]), reads=[aqT.k], writes=[ko["aqT"]])
                self.proj(ps[5], 512, hT, wb, 1608)
                S.op("act", lambda e: e.activation(out=iqf[:], in_=ps[5][:], func=AF.Copy), reads=[ps[5].k], writes=[iqf.k])
                S.op("dve", lambda e: e.tensor_copy(out=iqo[:], in_=iqf[:]), reads=[iqf.k], writes=[iqo.k])
                self.rope(iqf[:].rearrange("p (h d) -> p h d", h=8), iqf.k, iqo[:].rearrange("p (h d) -> p h d", h=8), iqo.k, 8, 8, cI, sI, ra, rb)

                def tri(e):
                    for g in range(4):
                        i = e.transpose(out=ps[6][:, g * 128:(g + 1) * 128], in_=iqo[:, g * 128:(g + 1) * 128], identity=self.identf[:])
                    return i
                S.op("pe", tri, reads=[iqo.k, self.identf.k], writes=[ps[6].k])
                S.op("act", lambda e: e.activation(out=iqT[:], in_=ps[6][:], func=AF.Copy), reads=[ps[6].k], writes=[iqT.k])
                S.dma(lambda e, j=j: e.dma_start(out=self.s_iqT[j], in_=iqT[:]), reads=[iqT.k], writes=[ko["iqT"]])
                for half in range(2):
                    pq = ps[half]
                    self.proj(pq, 512, hT, wb, 2120 + half * 512)
                    S.op("act", lambda e, pq=pq, half=half: e.activation(out=sgb[:, half * 512:(half + 1) * 512], in_=pq[:], func=AF.Sigmoid), reads=[pq.k], writes=[sgb.k])
                S.dma(lambda e, j=j: e.dma_start(out=self.s_sgb[j], in_=sgb[:]), reads=[sgb.k], writes=[ko["sgb"]])
                for half in range(2):
                    pq = ps[2 + half]; pg = ps[4 + half]
                    self.proj(pq, 512, hT, wb, 3144 + half * 512)
                    self.proj(pg, 512, hT, wb, 4168 + half * 512)
                    S.op("act", lambda e, pq=pq, half=half: e.activation(out=zs[:, half * 512:(half + 1) * 512], in_=pq[:], func=AF.Silu), reads=[pq.k], writes=[zs.k])
                    S.op("act", lambda e, pg=pg, half=half: e.activation(out=sga[:, half * 512:(half + 1) * 512], in_=pg[:], func=AF.Sigmoid), reads=[pg.k], writes=[sga.k])
                S.op("pool", lambda e: e.tensor_tensor(out=zg[:], in0=zs[:], in1=sga[:], op=ALU.mult), reads=[zs.k, sga.k], writes=[zg.k])
                S.dma(lambda e, j=j: e.dma_start(out=self.s_zg[j], in_=zg[:]), reads=[zg.k], writes=[ko["zg"]])
            S.barrier()
            S.emit()

    def phase2(self, NBIS=24):
        S = self.S
        P, NO, S_ = self.P, self.NO, self.Sq
        ps, psb = self.ps, self.psb
        KSEL = float(min(256, S_ // 4)) - 0.5
        with ExitStack() as es:
            sb = lambda n, sh, dt: self.sb(es, n, sh, dt)
            akT = sb("akT", [128, 2, S_], BF16)
            vall = sb("vall", [128, P, 258], BF16)
            H2 = S_ // 2
            assert H2 % 512 == 0
            ikT = sb("ikT", [128, H2], F32)
            scores = [sb("score%d" % i, [128, S_], F32) for i in range(2)]
            msk = sb("msk", [128, S_], BF16)
            jk = sb("jk", [128, S_], mybir.dt.uint8)
            gq = sb("gq2", [128, 128], F32); gk = sb("gk2", [128, 128], F32)
            negM = sb("negM", [128, 2], F32)
            causal = sb("causal", [128, 128], F32); blk0 = sb("blk0", [128, 128], F32)
            self.load(gq, gq[:], self.i_gq); self.load(gk, gk[:], self.i_gk)
            self.load(causal, causal[:], self.i_causal); self.load(blk0, blk0[:], self.i_blk0)
            S.dma(lambda e: e.dma_start(out=akT[:], in_=self.s_akT.rearrange("h (c s) -> h c s", c=2)), reads=[self.k_akT], writes=[akT.k])
            S.dma(lambda e: e.dma_start(out=vall[:], in_=self.s_v.rearrange("(p t) n -> t p n", t=128)), reads=[self.k_v], writes=[vall.k])
            S.dma(lambda e: e.dma_start(out=ikT[0:64, :], in_=self.s_ikT[0:64, 0:H2]), reads=[self.k_ikT], writes=[ikT.k])
            S.dma(lambda e: e.dma_start(out=ikT[64:128, :], in_=self.s_ikT[64:128, H2:S_]), reads=[self.k_ikT], writes=[ikT.k])
            S.op("dve", lambda e: e.tensor_reduce(out=negM[:, 0:1], in_=gq[:], axis=AX.X, op=ALU.max, apply_absolute_value=True), reads=[gq.k], writes=[negM.k])
            S.op("dve", lambda e: e.tensor_reduce(out=negM[:, 1:2], in_=gk[:], axis=AX.X, op=ALU.max, apply_absolute_value=True), reads=[gk.k], writes=[negM.k])
            S.op("dve", lambda e: e.scalar_tensor_tensor(out=negM[:, 0:1], in0=negM[:, 0:1], scalar=float(-(128.0 ** 0.5)), in1=negM[:, 1:2], op0=ALU.mult, op1=ALU.mult), reads=[negM.k], writes=[negM.k])
            aqT = sb("aqT", [128, 1024], BF16); iqT = sb("iqT", [128, 512], F32); iw = sb("iw", [128, 8], F32)
            iqT2 = sb("iqT2", [128, 512], F32)
            sgb = sb("sgb", [128, 1024], BF16); mg = sb("mg", [128, 1024], BF16)
            rl = [sb("rl%d" % i, [128, 512], F32) for i in range(2)]
            pT = [sb("pT%d" % i, [128, 512], BF16) for i in range(2)]
            mT = [sb("mT%d" % i, [128, 1024], BF16) for i in range(2)]
            sts = [sb("bst%d" % i, [128, 8], F32) for i in range(3)]
            rec = sb("rec", [128, 8], F32)
            oatt = self.junk
            mrg = sb("mrg", [128, 1024], BF16)
            accb = [ps[4], ps[5], ps[6]]
            hb = [(0, 0), (0, 1), (0, 2), (1, 0), (1, 1), (1, 2), (2, 0), (2, 1)]

            sctok = [[Tok("sc%d_%d" % (i, g)) for g in range(S_ // 512)] for i in range(2)]
            IDXC = (lambda ap: ap.bitcast(mybir.dt.float32r)) if IDX_F32R else (lambda ap: ap)

            def gI(j):
                p = 2 * j + 1
                nkeys = (p + 1) * 128
                ko = self.k_own[j]
                score = scores[j % 2]; st = sts[j % 3]
                S.dma(lambda e: e.dma_start(out=iqT[:], in_=self.s_iqT[j]), reads=[ko["iqT"]], writes=[iqT.k])
                S.dma(lambda e: e.dma_start(out=iqT2[0:64, :], in_=self.s_iqT[j][64:128, :]), reads=[ko["iqT"]], writes=[iqT2.k])
                S.dma(lambda e: e.dma_start(out=iqT2[64:128, :], in_=self.s_iqT[j][0:64, :]), reads=[ko["iqT"]], writes=[iqT2.k])
                S.dma(lambda e: e.dma_start(out=iw[:], in_=self.s_iw[j]), reads=[ko["iw"]], writes=[iw.k])
                it = 0
                nkg = (nkeys + 511) // 512
                for h in range(8):
                    for kg in range(nkg):
                        k0 = kg * 512
                        w = min(512, nkeys - k0)
                        sk = score.k
                        pb = ps[it % 2]; r = rl[it % 2]; it += 1
                        b0 = 64 if k0 >= H2 else 0
                        kk0 = k0 - (H2 if k0 >= H2 else 0)
                        qsrc = iqT if (h % 2) * 64 == b0 else iqT2
                        S.op("pe", lambda e, pb=pb, h=h, b0=b0, kk0=kk0, w=w, qsrc=qsrc: e.matmul(pb[:, 0:w], lhsT=IDXC(qsrc[b0:b0 + 64, (h // 2) * 128:(h // 2 + 1) * 128]), rhs=IDXC(ikT[b0:b0 + 64, kk0:kk0 + w]), start=True, stop=True), reads=[iqT.k, iqT2.k, ikT.k], writes=[pb.k])
                        S.op("act", lambda e, pb=pb, r=r, w=w: e.activation(out=r[:, 0:w], in_=pb[:, 0:w], func=AF.Relu), reads=[pb.k], writes=[r.k])
                        if h == 0:
                            S.op("dve", lambda e, r=r, k0=k0, w=w: e.tensor_scalar(out=score[:, k0:k0 + w], in0=r[:, 0:w], scalar1=iw[:, 0:1], scalar2=None, op0=ALU.mult), reads=[r.k, iw.k], writes=[sk])
                        else:
                            S.op("dve", lambda e, r=r, k0=k0, w=w, h=h: e.scalar_tensor_tensor(out=score[:, k0:k0 + w], in0=r[:, 0:w], scalar=iw[:, h:h + 1], in1=score[:, k0:k0 + w], op0=ALU.mult, op1=ALU.add), reads=[r.k, iw.k, sk], writes=[sk])
                        yield
                allk = sctok[j % 2][0:nkg]
                S.op("dve", lambda e: e.tensor_copy(out=st[:, 7:8], in_=st[:, 7:8]), reads=allk + [st.k], writes=[score.k, st.k])
                S.op("dve", lambda e: e.tensor_reduce(out=st[:, 5:6], in_=score[:, 0:nkeys], axis=AX.X, op=ALU.max), reads=[score.k], writes=[st.k])
                S.op("dve", lambda e: e.tensor_reduce(out=st[:, 6:7], in_=score[:, 0:nkeys], axis=AX.X, op=ALU.min), reads=[score.k], writes=[st.k])
                S.op("dve", lambda e: e.tensor_scalar(out=st[:, 0:1], in0=st[:, 6:7], scalar1=-1.0, scalar2=None, op0=ALU.add), reads=[st.k], writes=[st.k])
                S.op("dve", lambda e: e.tensor_tensor(out=st[:, 1:2], in0=st[:, 5:6], in1=st[:, 0:1], op=ALU.subtract), reads=[st.k], writes=[st.k])
                S.op("dve", lambda e: e.tensor_tensor(out=score[:, nkeys - 128:nkeys], in0=score[:, nkeys - 128:nkeys], in1=causal[:], op=ALU.add), reads=[score.k, causal.k], writes=[score.k])
                S.op("dve", lambda e: e.tensor_tensor(out=score[:, 0:128], in0=score[:, 0:128], in1=blk0[:], op=ALU.add), reads=[score.k, blk0.k], writes=[score.k])
                yield

            def gB(j):
                nkeys = (2 * j + 2) * 128
                score = scores[j % 2]; st = sts[j % 3]
                for it_ in range(1, NBIS + 1):
                    f = float(2.0 ** -it_)
                    S.op("dve", lambda e, f=f: e.scalar_tensor_tensor(out=st[:, 2:3], in0=st[:, 1:2], scalar=f, in1=st[:, 0:1], op0=ALU.mult, op1=ALU.add), reads=[st.k], writes=[st.k])
                    S.op("dve", lambda e: e.tensor_scalar(out=jk[:, 0:nkeys], in0=score[:, 0:nkeys], scalar1=st[:, 2:3], scalar2=0.0, op0=ALU.is_gt, op1=ALU.add, accum_out=st[:, 3:4]), reads=[score.k, st.k], writes=[jk.k, st.k])
                    S.op("dve", lambda e, f=f: e.tensor_scalar(out=st[:, 4:5], in0=st[:, 3:4], scalar1=KSEL, scalar2=f, op0=ALU.is_gt, op1=ALU.mult), reads=[st.k], writes=[st.k])
                    S.op("dve", lambda e: e.scalar_tensor_tensor(out=st[:, 0:1], in0=st[:, 4:5], scalar=st[:, 1:2], in1=st[:, 0:1], op0=ALU.mult, op1=ALU.add), reads=[st.k], writes=[st.k])
                    yield

            def gA(j):
                p = 2 * j + 1
                nk = p + 1
                nkeys = nk * 128
                ko = self.k_own[j]
                score = scores[j % 2]; st = sts[j % 3]
                S.op("dve", lambda e: e.tensor_scalar(out=msk[:, 0:nkeys], in0=score[:, 0:nkeys], scalar1=st[:, 0:1], scalar2=None, op0=ALU.is_gt), reads=[score.k, st.k], writes=[msk.k])
                S.dma(lambda e: e.dma_start(out=aqT[:], in_=self.s_aqT[j]), reads=[ko["aqT"]], writes=[aqT.k])
                S.dma(lambda e: e.dma_start(out=sgb[:], in_=self.s_sgb[j]), reads=[ko["sgb"]], writes=[sgb.k])
                S.dma(lambda e: e.dma_start(out=mg[:], in_=self.s_mg[j]), reads=[ko["mg"]], writes=[mg.k])
                yield
                units = [(kb, c) for kb in range(nk) for c in range(2)]

                def emit_st(u):
                    kb, c = units[u]
                    pss = ps[2 + u % 2]
                    S.op("pe", lambda e: e.matmul(pss[:], lhsT=akT[:, c, kb * 128:(kb + 1) * 128], rhs=aqT[:, c * 512:(c + 1) * 512], start=True, stop=True), reads=[akT.k, aqT.k], writes=[pss.k])

                def emit_mask(kb):
                    m = mT[(kb // 8) % 2]
                    nb = min(8, nk - kb)

                    def trm(e):
                        for i_ in range(nb):
                            ins = e.transpose(out=psb[:, i_ * 128:(i_ + 1) * 128], in_=msk[:, (kb + i_) * 128:(kb + i_ + 1) * 128], identity=self.identb[:])
                        return ins
                    S.op("pe", trm, reads=[msk.k, self.identb.k], writes=[psb.k])
                    S.op("act", lambda e: e.activation(out=m[:, 0:nb * 128], in_=psb[:, 0:nb * 128], func=AF.Copy), reads=[psb.k], writes=[m.k])

                emit_mask(0)
                emit_st(0)
                yield
                for u, (kb, c) in enumerate(units):
                    pss = ps[2 + u % 2]; pt = pT[u % 2]
                    m = mT[(kb // 8) % 2]
                    if u + 1 < len(units):
                        if units[u + 1][1] == 0 and units[u + 1][0] % 8 == 0:
                            emit_mask(units[u + 1][0])
                        emit_st(u + 1)
                    S.op("act", lambda e, pss=pss, pt=pt: e.activation(out=pt[:], in_=pss[:], func=AF.Exp, bias=negM[:, 0:1], scale=1.0), reads=[pss.k, negM.k], writes=[pt.k])
                    mi = (kb % 8) * 128
                    S.op("pool", lambda e, pt=pt, m=m, mi=mi: e.tensor_tensor(out=pt[:].rearrange("p (h q) -> p h q", h=4), in0=pt[:].rearrange("p (h q) -> p h q", h=4), in1=bc(m[:, mi:mi + 128].unsqueeze(1), [128, 4, 128]), op=ALU.mult), reads=[pt.k, m.k], writes=[pt.k])
                    yield

                    def pv(e, pt=pt, c=c, kb=kb):
                        for hh in range(4):
                            h = c * 4 + hh
                            bk, sl = hb[h]
                            ins = e.matmul(accb[bk][:, sl * 129:(sl + 1) * 129], lhsT=pt[:, hh * 128:(hh + 1) * 128], rhs=vall[:, kb, c * 129:(c + 1) * 129], start=(kb == 0 and sl == 0), stop=(kb == nk - 1), skip_group_check=True)
                        return ins
                    S.op("pe", pv, reads=[pt.k, vall.k], writes=[accb[hb[c * 4][0]].k, accb[hb[c * 4 + 3][0]].k])
                for bk, (h0, n) in enumerate(((0, 3), (3, 3), (6, 2))):
                    a3 = accb[bk][:, 0:n * 129].rearrange("p (h d) -> p h d", h=n)
                    S.op("dve", lambda e, a3=a3, h0=h0, n=n: e.reciprocal(out=rec[:, h0:h0 + n], in_=a3[:, :, 128]), reads=[accb[bk].k], writes=[rec.k])
                    S.op("dve", lambda e, a3=a3, h0=h0, n=n: e.tensor_tensor(out=oatt[:, h0 * 128:(h0 + n) * 128].rearrange("p (h d) -> p h d", h=n), in0=a3[:, :, 0:128], in1=bc(rec[:, h0:h0 + n].unsqueeze(2), [128, n, 128]), op=ALU.mult), reads=[accb[bk].k, rec.k], writes=[oatt.k])
                S.op("pool", lambda e: e.tensor_tensor(out=oatt[:], in0=oatt[:], in1=sgb[:], op=ALU.mult), reads=[oatt.k, sgb.k], writes=[oatt.k])
                S.op("pool", lambda e: e.tensor_tensor(out=mrg[:], in0=oatt[:], in1=mg[:], op=ALU.add), reads=[oatt.k, mg.k], writes=[mrg.k])
                S.dma(lambda e: e.dma_start(out=self.s_mrg[j], in_=mrg[:]), reads=[mrg.k], writes=[ko["x1"]])
                yield

            def count(g):
                n = 0
                for _ in g:
                    n += 1
                return n

            for t in range(NO + 2):
                gens = []
                for mk, jj in ((gA, t - 2), (gB, t - 1), (gI, t)):
                    if 0 <= jj < NO:
                        gens.append(mk(jj))
                sizes = []
                for mk, jj in ((gA, t - 2), (gB, t - 1), (gI, t)):
                    if 0 <= jj < NO:
                        nk = 2 * jj + 2
                        if mk is gA:
                            sizes.append(2 + 2 * nk)
                        elif mk is gB:
                            sizes.append(NBIS)
                        else:
                            sizes.append(((nk * 128 + 511) // 512) * 8 + 1)
                nmax = max(sizes)
                done = [0] * len(gens)
                if 0 <= t - 2 < NO:
                    next(gens[0], None)
                    done[0] = 1
                for k in range(nmax):
                    for gi, g in enumerate(gens):
                        tgt = ((k + 1) * sizes[gi]) // nmax
                        while done[gi] < tgt:
                            next(g, None)
                            done[gi] += 1
                for g in gens:
                    for _ in g:
                        pass
            S.barrier()
            S.emit()

    def phase3(self, G=2):
        S = self.S
        NO = self.NO
        ps, psb = self.ps, self.psb
        G = min(G, NO)
        with ExitStack() as es:
            sb = lambda n, sh, dt: self.sb(es, n, sh, dt)
            wup = sb("wupb", [128, 8, 4096], BF16); wdn = sb("wdnb", [128, 32, 1024], BF16)
            woutb = sb("woutb", [128, 8, 1024], BF16)
            stg = [sb("stg3_%d" % i, [128, 512], F32) for i in range(2)]
            gmlp = sb("gmlp", [128, 8], F32)
            self.load(gmlp, gmlp[:], self.i_gmlp)
            self.load_weight_bf16(es, woutb, self.i_wout, 1024, stg)
            self.load_weight_bf16(es, wup, self.i_wup, 4096, stg)
            self.load_weight_bf16(es, wdn, self.i_wdn, 1024, stg)
            x1 = sb("x1g", [128, G, 1024], F32)
            xb = sb("xb3", [128, 1024], F32); mrg = sb("mrg3", [128, 1024], BF16); mT2 = sb("mT2", [128, 8, 128], BF16)
            ss = sb("ss3", [128, 1], F32); xnb = sb("xnb3", [128, 1024], BF16)
            h2T = sb("h2T", [128, 8, G * 128], BF16)
            hTb = sb("hTb3", [128, 8, 128], BF16)
            hid = sb("hidT", [128, 32, G * 128], BF16)
            rl = [sb("rl3_%d" % i, [128, G * 128], F32) for i in range(2)]
            ot = [sb("ot%d" % i, [128, 1024], F32) for i in range(2)]
            for g0 in range(0, NO, G):
                for b in range(G):
                    j = g0 + b
                    p = 2 * j + 1
                    S.dma(lambda e, j=j: e.dma_start(out=mrg[:], in_=self.s_mrg[j]), reads=[self.k_own[j]["x1"]], writes=[mrg.k])
                    S.dma(lambda e, p=p: e.dma_start(out=xb[:], in_=self.i_x[p * 128:(p + 1) * 128, :]), writes=[xb.k])

                    def trg(e):
                        for kc in range(8):
                            ins = e.transpose(out=psb[:, kc * 128:(kc + 1) * 128], in_=mrg[:, kc * 128:(kc + 1) * 128], identity=self.identb[:])
                        return ins
                    S.op("pe", trg, reads=[mrg.k, self.identb.k], writes=[psb.k])
                    S.op("act", lambda e: e.activation(out=mT2[:].rearrange("p k t -> p (k t)"), in_=psb[:], func=AF.Copy), reads=[psb.k], writes=[mT2.k])
                    for half in range(2):
                        pq = ps[4 + half]
                        self.proj(pq, 512, mT2, woutb, half * 512)
                        S.op("dve", lambda e, pq=pq, half=half, b=b: e.tensor_tensor(out=x1[:, b, half * 512:(half + 1) * 512], in0=xb[:, half * 512:(half + 1) * 512], in1=pq[:], op=ALU.add), reads=[xb.k, pq.k], writes=[x1.k])
                    xbv = TL(x1.t[:, b, :], "x")
                    xbv.k = x1.k
                    self.rms_hT(xbv, ss, xnb, hTb, gmlp, psb)
                    S.op("pool", lambda e, b=b: e.tensor_copy(out=h2T[:, :, b * 128:(b + 1) * 128], in_=hTb[:]), reads=[hTb.k], writes=[h2T.k])
                for f in range(32):
                    pq = ps[f % 2]; r = rl[f % 2]

                    def up(e, pq=pq, f=f):
                        for kc in range(8):
                            ins = e.matmul(pq[:, 0:G * 128], lhsT=wup[:, kc, f * 128:(f + 1) * 128], rhs=h2T[:, kc, :], start=(kc == 0), stop=(kc == 7))
                        return ins
                    S.op("pe", up, reads=[wup.k, h2T.k], writes=[pq.k])
                    S.op("act", lambda e, pq=pq, r=r: e.activation(out=r[:], in_=pq[:, 0:G * 128], func=AF.Relu), reads=[pq.k], writes=[r.k])
                    S.op("dve" if f % 2 else "pool", lambda e, r=r, f=f: e.tensor_tensor(out=hid[:, f, :], in0=r[:], in1=r[:], op=ALU.mult), reads=[r.k], writes=[hid.k])
                for b in range(G):
                    j = g0 + b
                    o = ot[b % 2]
                    for half in range(2):
                        pq = ps[2 + half]

                        def dn(e, pq=pq, b=b, half=half):
                            for f in range(32):
                                ins = e.matmul(pq[:], lhsT=hid[:, f, b * 128:(b + 1) * 128], rhs=wdn[:, f, half * 512:(half + 1) * 512], start=(f == 0), stop=(f == 31))
                            return ins
                        S.op("pe", dn, reads=[hid.k, wdn.k], writes=[pq.k])
                        S.op("dve", lambda e, pq=pq, o=o, b=b, half=half: e.tensor_tensor(out=o[:, half * 512:(half + 1) * 512], in0=x1[:, b, half * 512:(half + 1) * 512], in1=pq[:], op=ALU.add), reads=[x1.k, pq.k], writes=[o.k])
                    S.dma(lambda e, j=j, o=o: e.dma_start(out=self.o_out[j * 128:(j + 1) * 128, :], in_=o[:]), reads=[o.k], writes=[])
            S.barrier()
            S.emit()

    def phase1a(self):
        S = self.S
        P = self.P
        ps, psb = self.ps, self.psb
        with ExitStack() as es:
            sb = lambda n, sh, dt: self.sb(es, n, sh, dt)
            wb = sb("w1a", [128, 8, C1A], BF16)
            stg = [sb("stga%d" % i, [128, 1024], F32) for i in range(2)]
            gmix = sb("gmixa", [128, 8], F32); ggdn = sb("ggdn", [128, 128], F32)
            aexp = sb("aexp", [128, 8], F32); dtb = sb("dtb", [128, 8], F32)
            convw = sb("convw", [128, 24, 4], F32)
            uinc = sb("uinc", [128, 128], F32); onesf = sb("onesf", [128, 128], F32); onesb = sb("onesb", [128, 128], BF16)
            mskL4 = sb("mskL4", [128, 4, 128], BF16); mskU4 = sb("mskU4", [128, 4, 128], BF16)
            self.load(gmix, gmix[:], self.i_gmix); self.load(ggdn, ggdn[:], self.i_ggdn)
            self.load(aexp, aexp[:], self.i_alog); self.load(dtb, dtb[:], self.i_dtb)
            self.load(convw, convw[:].rearrange("p t j -> p (t j)"), self.i_convw)
            self.load(uinc, uinc[:], self.i_uinc)
            for q4 in range(4):
                self.load(mskL4, mskL4[:, q4, :], self.i_mskL); self.load(mskU4, mskU4[:, q4, :], self.i_mskU)
            S.op("pool", lambda e: e.memset(onesf[:], 1.0), writes=[onesf.k])
            S.op("pool", lambda e: e.memset(onesb[:], 1.0), writes=[onesb.k])
            S.op("act", lambda e: e.activation(out=aexp[:], in_=aexp[:], func=AF.Exp), reads=[aexp.k], writes=[aexp.k])
            self.load_weight_bf16(es, wb, self.i_w1a, C1A, stg)
            xb = sb("xba", [128, 1024], F32); ss = sb("ssa", [128, 1], F32); xnb = sb("xnba", [128, 1024], BF16); hT = sb("hTa", [128, 8, 128], BF16)
            U = sb("U", [128, 24, 131], F32)
            Yg = sb("Yg", [128, 8, 128], F32); Ct = sb("Ct", [128, 8, 128], F32)
            sq = sb("sqa", [128, 1024], BF16); rn = sb("rn", [128, 1024], F32)
            qT = sb("qT", [128, 8, 128], BF16); kT2 = [sb("kT%d" % i, [128, 8, 128], BF16) for i in range(2)]; vTb = sb("vTb", [128, 8, 128], BF16)
            ktok = sb("ktok", [128, 8, 128], F32)
            kbg2 = [sb("kbg%d" % i, [128, 8, 128], BF16) for i in range(2)]; kdec2 = [sb("kdec%d" % i, [128, 8, 128], BF16) for i in range(2)]; vb2 = [sb("vb%d" % i, [128, 8, 128], BF16) for i in range(2)]
            smm = [sb("sm_%d" % i, [128, 96], F32) for i in range(2)]
            sm2 = sb("sm2", [128, 16], F32); sm3 = sb("sm3", [128, 8], F32)
            Rall = sb("Rall", [128, 8, 128], F32)
            Dls2 = [sb("Dls%d" % i, [128, 8, 128], F32) for i in range(2)]; DT = sb("DTm", [128, 8, 128], F32); egr = sb("egr", [128, 8, 128], F32)
            qdT2 = [sb("qdT%d" % i, [128, 8, 128], BF16) for i in range(2)]; QKT2 = [sb("QKT%d" % i, [128, 8, 128], BF16) for i in range(2)]
            Am = [sb("Am%d" % i, [128, 8, 128], F32) for i in range(2)]
            Bm = [sb("Bm%d" % i, [128, 8, 128], F32) for i in range(2)]
            Pm = sb("Pm", [128, 8, 128], F32); TTb = sb("TTb", [128, 8, 128], BF16)
            u = sb("u", [128, 8, 128], F32); wT = sb("wT", [128, 8, 128], BF16); vn = sb("vn", [128, 8, 128], BF16)
            S32 = sb("S32", [128, 8, 128], F32); Stmp = sb("Stmp", [128, 8, 128], F32); Sbf = sb("Sbf", [128, 8, 128], BF16)
            o = sb("o", [128, 8, 128], F32); osq = sb("osq", [128, 8, 128], F32)
            zg2 = [sb("zga%d" % i, [128, 1024], BF16) for i in range(2)]; mgt = sb("mgt", [128, 1024], BF16)
            S.op("pool", lambda e: e.memset(U[:], 0.0), writes=[U.k])
            S.op("pool", lambda e: e.memset(S32[:], 0.0), writes=[S32.k])
            S.op("pool", lambda e: e.memset(Sbf[:], 0.0), writes=[Sbf.k])
            f4 = lambda t, g: t[:, g * 4:(g + 1) * 4, :].rearrange("p h d -> p (h d)")

            def batch_mm(banks, fn_h):
                for g in range(2):
                    def f(e, g=g):
                        for hh in range(4):
                            h = g * 4 + hh
                            ins = fn_h(e, banks[g][:, hh * 128:(hh + 1) * 128], h)
                        return ins
                    yield g, f

            def gF(p):
                own = (p % 2 == 1)
                j = p // 2
                r0 = p * 128
                DB = p % 2
                kT = kT2[DB]; kbg = kbg2[DB]; kdec = kdec2[DB]; vb = vb2[DB]; sm = smm[DB]; Dls = Dls2[DB]
                qdT = qdT2[DB]; QKT = QKT2[DB]; zg = zg2[DB]
                self.load(xb, xb[:], self.i_x[r0:r0 + 128, :])
                if own:
                    S.dma(lambda e, j=j: e.dma_start(out=zg[:], in_=self.s_zg[j]), reads=[self.k_own[j]["zg"]], writes=[zg.k])
                self.rms_hT(xb, ss, xnb, hT, gmix, psb)
                yield
                for grp in range(6):
                    pq = ps[4 + grp % 2]

                    def fm(e, pq=pq, grp=grp):
                        for t4 in range(4):
                            ct = grp * 4 + t4
                            for kc in range(8):
                                ins = e.matmul(pq[:, t4 * 128:(t4 + 1) * 128], lhsT=wb[:, kc, ct * 128:(ct + 1) * 128], rhs=hT[:, kc, :], start=(kc == 0), stop=(kc == 7), skip_group_check=True)
                        return ins
                    S.op("pe", fm, reads=[wb.k, hT.k], writes=[pq.k])
                    yield
                    S.op("act", lambda e, pq=pq, grp=grp: e.activation(out=U[:, grp * 4:(grp + 1) * 4, 3:131], in_=pq[:].rearrange("p (t n) -> p t n", t=4), func=AF.Copy), reads=[pq.k], writes=[U.k])
                    yield
                self.proj(ps[6], 16, hT, wb, 3072)
                yield
                S.op("act", lambda e: e.activation(out=sm[:, 0:16], in_=ps[6][:, 0:16], func=AF.Copy), reads=[ps[6].k], writes=[sm.k])
                yield
                S.op("act", lambda e: e.activation(out=sm[:, 16:24], in_=sm[:, 8:16], func=AF.Sigmoid), reads=[sm.k], writes=[sm.k])
                yield
                S.op("dve", lambda e: e.tensor_tensor(out=sm[:, 24:32], in0=sm[:, 0:8], in1=dtb[:], op=ALU.add), reads=[sm.k, dtb.k], writes=[sm.k])
                yield
                S.op("act", lambda e: e.activation(out=sm[:, 32:40], in_=sm[:, 24:32], func=AF.Abs), reads=[sm.k], writes=[sm.k])
                yield
                S.op("act", lambda e: e.activation(out=sm[:, 32:40], in_=sm[:, 32:40], func=AF.Exp, scale=-1.0), reads=[sm.k], writes=[sm.k])
                yield
                S.op("dve", lambda e: e.tensor_scalar(out=sm[:, 32:40], in0=sm[:, 32:40], scalar1=1.0, scalar2=None, op0=ALU.add), reads=[sm.k], writes=[sm.k])
                yield
                S.op("act", lambda e: e.activation(out=sm[:, 32:40], in_=sm[:, 32:40], func=AF.Ln), reads=[sm.k], writes=[sm.k])
                yield
                S.op("dve", lambda e: e.scalar_tensor_tensor(out=sm[:, 40:48], in0=sm[:, 24:32], scalar=0.0, in1=sm[:, 32:40], op0=ALU.max, op1=ALU.add), reads=[sm.k], writes=[sm.k])
                yield
                S.op("dve", lambda e: e.scalar_tensor_tensor(out=sm[:, 48:56], in0=sm[:, 40:48], scalar=-1.0, in1=aexp[:], op0=ALU.mult, op1=ALU.mult), reads=[sm.k, aexp.k], writes=[sm.k])
                yield

                def cs(e):
                    e.matmul(ps[6][:, 16:24], lhsT=uinc[:], rhs=sm[:, 48:56], start=True, stop=True, skip_group_check=True)
                    return e.matmul(ps[6][:, 24:32], lhsT=onesf[:], rhs=sm[:, 48:56], start=True, stop=True, skip_group_check=True)
                S.op("pe", cs, reads=[uinc.k, onesf.k, sm.k], writes=[ps[6].k])
                yield
                S.op("act", lambda e: e.activation(out=sm[:, 56:72], in_=ps[6][:, 16:32], func=AF.Copy), reads=[ps[6].k], writes=[sm.k])
                yield
                S.op("act", lambda e: e.activation(out=sm[:, 72:80], in_=sm[:, 56:64], func=AF.Exp), reads=[sm.k], writes=[sm.k])
                yield
                S.op("dve", lambda e: e.tensor_tensor(out=sm[:, 80:88], in0=sm[:, 64:72], in1=sm[:, 56:64], op=ALU.subtract), reads=[sm.k], writes=[sm.k])
                yield
                S.op("act", lambda e: e.activation(out=sm[:, 80:96], in_=sm[:, 80:96] if False else sm[:, 80:88], func=AF.Exp) if False else e.activation(out=sm[:, 80:88], in_=sm[:, 80:88], func=AF.Exp), reads=[sm.k], writes=[sm.k])
                yield
                S.op("act", lambda e: e.activation(out=sm[:, 88:96], in_=sm[:, 64:72], func=AF.Exp), reads=[sm.k], writes=[sm.k])
                yield
                S.op("dve", lambda e: e.tensor_tensor(out=sm2[:, 0:8], in0=sm[:, 16:24], in1=sm[:, 72:80], op=ALU.mult), reads=[sm.k], writes=[sm2.k])
                yield
                for grp, dst in ((0, qT), (1, kT), (2, vTb)):
                    if grp == 0 and not own:
                        continue
                    Ug = lambda jj, grp=grp: U[:, grp * 8:(grp + 1) * 8, jj:jj + 128]
                    cw = lambda jj, grp=grp: bc(convw[:, grp * 8:(grp + 1) * 8, jj:jj + 1], [128, 8, 128])
                    S.op("dve", lambda e, Ug=Ug, cw=cw: e.tensor_tensor(out=Yg[:], in0=Ug(0), in1=cw(0), op=ALU.mult), reads=[U.k, convw.k], writes=[Yg.k])
                    yield
                    for jj in range(1, 4):
                        S.op("pool", lambda e, Ug=Ug, cw=cw, jj=jj: e.tensor_tensor(out=Ct[:], in0=Ug(jj), in1=cw(jj), op=ALU.mult), reads=[U.k, convw.k], writes=[Ct.k])
                        S.op("dve", lambda e: e.tensor_tensor(out=Yg[:], in0=Yg[:], in1=Ct[:], op=ALU.add), reads=[Yg.k, Ct.k], writes=[Yg.k])
                    Yf = Yg[:].rearrange("p h d -> p (h d)")
                    if grp == 2:
                        S.op("act", lambda e, Yf=Yf: e.activation(out=vTb[:].rearrange("p h d -> p (h d)"), in_=Yf, func=AF.Silu), reads=[Yg.k], writes=[vTb.k])
                        continue
                    S.op("act", lambda e, Yf=Yf: e.activation(out=Yf, in_=Yf, func=AF.Silu), reads=[Yg.k], writes=[Yg.k])
                    yield
                    S.op("pool", lambda e, Yf=Yf: e.tensor_tensor(out=sq[:], in0=Yf, in1=Yf, op=ALU.mult), reads=[Yg.k], writes=[sq.k])
                    yield
                    for g in range(2):
                        pq = ps[4 + g]
                        S.op("pe", lambda e, pq=pq, g=g: e.matmul(pq[:], lhsT=onesb[:], rhs=sq[:, g * 512:(g + 1) * 512], start=True, stop=True), reads=[onesb.k, sq.k], writes=[pq.k])
                        S.op("dve", lambda e, pq=pq, g=g: e.tensor_scalar(out=rn[:, g * 512:(g + 1) * 512], in0=pq[:], scalar1=EPS, scalar2=None, op0=ALU.add), reads=[pq.k], writes=[rn.k])
                    S.op("act", lambda e: e.activation(out=rn[:], in_=rn[:], func=AF.Ln), reads=[rn.k], writes=[rn.k])
                    yield
                    S.op("act", lambda e: e.activation(out=rn[:], in_=rn[:], func=AF.Exp, scale=-0.5), reads=[rn.k], writes=[rn.k])
                    yield
                    sc = float(128.0 ** -0.5) if grp == 0 else 1.0
                    S.op("dve", lambda e, Yf=Yf, dst=dst, sc=sc: e.scalar_tensor_tensor(out=dst[:].rearrange("p h d -> p (h d)"), in0=Yf, scalar=sc, in1=rn[:], op0=ALU.mult, op1=ALU.mult), reads=[Yg.k, rn.k], writes=[dst.k])
                    yield
                S.op("pool", lambda e: e.tensor_copy(out=U[:, :, 0:3], in_=U[:, :, 128:131]), reads=[U.k], writes=[U.k])
                yield
                def trk(e):
                    for h in range(8):
                        ins = e.transpose(out=psb[:, h * 128:(h + 1) * 128], in_=kT[:, h, :], identity=self.identb[:])
                    return ins
                S.op("pe", trk, reads=[kT.k, self.identb.k], writes=[psb.k])
                yield
                S.op("act", lambda e: e.activation(out=ktok[:].rearrange("p h d -> p (h d)"), in_=psb[:], func=AF.Copy), reads=[psb.k], writes=[ktok.k])
                yield
                S.op("dve", lambda e: e.tensor_tensor(out=kbg[:], in0=ktok[:], in1=bc(sm2[:, 0:8].unsqueeze(2), [128, 8, 128]), op=ALU.mult), reads=[ktok.k, sm2.k], writes=[kbg.k])
                yield
                S.op("pool", lambda e: e.tensor_tensor(out=kdec[:], in0=ktok[:], in1=bc(sm[:, 80:88].unsqueeze(2), [128, 8, 128]), op=ALU.mult), reads=[ktok.k, sm.k], writes=[kdec.k])
                yield

                def trv(e):
                    for h in range(8):
                        ins = e.transpose(out=psb[:, h * 128:(h + 1) * 128], in_=vTb[:, h, :], identity=self.identb[:])
                    return ins
                S.op("pe", trv, reads=[vTb.k, self.identb.k], writes=[psb.k])
                yield
                S.op("dve", lambda e: e.tensor_tensor(out=vb[:], in0=psb[:].rearrange("p (h d) -> p h d", h=8), in1=bc(sm[:, 16:24].unsqueeze(2), [128, 8, 128]), op=ALU.mult), reads=[psb.k, sm.k], writes=[vb.k])
                yield
                S.op("dve", lambda e: e.tensor_tensor(out=Rall[:], in0=bc(uinc[:].unsqueeze(1), [128, 8, 128]), in1=bc(sm[:, 48:56].unsqueeze(2), [128, 8, 128]), op=ALU.mult), reads=[uinc.k, sm.k], writes=[Rall.k])
                yield
                for g in range(2):
                    pq = ps[4 + g]

                    def grl(e, pq=pq, g=g):
                        e.matmul(pq[:], lhsT=onesf[:], rhs=f4(Rall, g), start=True, stop=False)
                        return e.matmul(pq[:], lhsT=self.identb[:], rhs=mskL4[:].rearrange("p h d -> p (h d)"), start=False, stop=True)
                    S.op("pe", grl, reads=[onesf.k, Rall.k, self.identb.k, mskL4.k], writes=[pq.k])
                    yield
                    for hh in range(4):
                        h = g * 4 + hh
                        S.op("act", lambda e, pq=pq, h=h, hh=hh: e.activation(out=Dls[:, h, :], in_=pq[:, hh * 128:(hh + 1) * 128], func=AF.Exp, scale=-1.0, bias=sm[:, 56 + h:57 + h]), reads=[pq.k, sm.k], writes=[Dls.k])
                if own:
                    for g in range(2):
                        pq = ps[4 + g]
                        S.op("pe", lambda e, pq=pq, g=g: e.matmul(pq[:], lhsT=onesf[:], rhs=f4(Rall, g), start=True, stop=False), reads=[onesf.k, Rall.k], writes=[pq.k])
                        S.op("act", lambda e, pq=pq, g=g: e.activation(out=f4(egr, g), in_=pq[:], func=AF.Exp), reads=[pq.k], writes=[egr.k])
                        S.op("pe", lambda e, pq=pq: e.matmul(pq[:], lhsT=self.identb[:], rhs=mskU4[:].rearrange("p h d -> p (h d)"), start=False, stop=True), reads=[self.identb.k, mskU4.k], writes=[pq.k])
                        S.op("pool", lambda e, g=g: e.tensor_scalar(out=sm2[:, 8:16], in0=sm[:, 56:64], scalar1=-1.0, scalar2=None, op0=ALU.mult), reads=[sm.k], writes=[sm2.k])
                        for hh in range(4):
                            h = g * 4 + hh
                            S.op("act", lambda e, pq=pq, h=h, hh=hh: e.activation(out=DT[:, h, :], in_=pq[:, hh * 128:(hh + 1) * 128], func=AF.Exp, scale=1.0, bias=sm2[:, 8 + h:9 + h]), reads=[pq.k, sm2.k], writes=[DT.k])
                    S.op("pool", lambda e: e.tensor_tensor(out=qdT[:], in0=qT[:], in1=egr[:], op=ALU.mult), reads=[qT.k, egr.k], writes=[qdT.k])
                    yield
                    for g, f in batch_mm((ps[4], ps[5]), lambda e, out, h: e.matmul(out, lhsT=kT[:, h, :], rhs=qT[:, h, :], start=True, stop=True, skip_group_check=True)):
                        S.op("pe", f, reads=[kT.k, qT.k], writes=[ps[4 + g].k])
                        S.op("dve", lambda e, g=g: e.tensor_tensor(out=f4(QKT, g), in0=ps[4 + g][:], in1=f4(DT, g), op=ALU.mult), reads=[ps[4 + g].k, DT.k], writes=[QKT.k])

                yield

            def gNR(p):
                own = (p % 2 == 1)
                j = p // 2
                r0 = p * 128
                DB = p % 2
                kT = kT2[DB]; kbg = kbg2[DB]; kdec = kdec2[DB]; vb = vb2[DB]; sm = smm[DB]; Dls = Dls2[DB]
                qdT = qdT2[DB]; QKT = QKT2[DB]; zg = zg2[DB]
                A0, B0 = Am[0], Bm[0]
                for g, f in batch_mm((ps[0], ps[1]), lambda e, out, h: e.matmul(out, lhsT=kT[:, h, :], rhs=kT[:, h, :], start=True, stop=True, skip_group_check=True)):
                    S.op("pe", f, reads=[kT.k], writes=[ps[g].k])
                    yield
                    for hh in range(4):
                        h = g * 4 + hh
                        S.op("dve", lambda e, g=g, h=h, hh=hh: e.scalar_tensor_tensor(out=A0[:, h, :], in0=ps[g][:, hh * 128:(hh + 1) * 128], scalar=sm[:, 16 + h:17 + h], in1=Dls[:, h, :], op0=ALU.mult, op1=ALU.mult), reads=[ps[g].k, sm.k, Dls.k], writes=[A0.k])
                for g, f in batch_mm((ps[2], ps[3]), lambda e, out, h: e.transpose(out=out, in_=A0[:, h, :], identity=self.identf[:])):
                    S.op("pe", f, reads=[A0.k, self.identf.k], writes=[ps[2 + g].k])
                    yield
                    S.op("act", lambda e, g=g: e.activation(out=f4(B0, g), in_=ps[2 + g][:], func=AF.Copy), reads=[ps[2 + g].k], writes=[B0.k])
                    yield
                S.op("pool", lambda e: e.tensor_tensor(out=Pm[:], in0=bc(self.identf[:].unsqueeze(1), [128, 8, 128]), in1=B0[:], op=ALU.subtract), reads=[self.identf.k, B0.k], writes=[Pm.k])
                yield
                cur = 0
                for lv in range(6):
                    Ac, Bc = Am[cur], Bm[cur]
                    An, Bn_ = Am[1 - cur], Bm[1 - cur]
                    last = (lv == 5)
                    for g, f in batch_mm((ps[0], ps[1]), lambda e, out, h, Ac=Ac, Bc=Bc: e.matmul(out, lhsT=Bc[:, h, :], rhs=Ac[:, h, :], start=True, stop=True, skip_group_check=True)):
                        S.op("pe", f, reads=[Ac.k, Bc.k], writes=[ps[g].k])
                        S.op("act", lambda e, g=g, An=An: e.activation(out=f4(An, g), in_=ps[g][:], func=AF.Copy), reads=[ps[g].k], writes=[An.k])
                    if not last:
                        for g, f in batch_mm((ps[2], ps[3]), lambda e, out, h, Ac=Ac, Bc=Bc: e.matmul(out, lhsT=Ac[:, h, :], rhs=Bc[:, h, :], start=True, stop=True, skip_group_check=True)):
                            S.op("pe", f, reads=[Ac.k, Bc.k], writes=[ps[2 + g].k])
                            S.op("dve", lambda e, g=g, Bn_=Bn_: e.tensor_copy(out=f4(Bn_, g), in_=ps[2 + g][:]), reads=[ps[2 + g].k], writes=[Bn_.k])
                    for g, f in batch_mm((ps[0], ps[1]), lambda e, out, h, An=An: e.matmul(out, lhsT=An[:, h, :], rhs=Pm[:, h, :], start=True, stop=True, skip_group_check=True)):
                        S.op("pe", f, reads=[An.k, Pm.k], writes=[ps[g].k])
                        S.op("dve", lambda e, g=g: e.tensor_tensor(out=f4(Pm, g), in0=f4(Pm, g), in1=ps[g][:], op=ALU.add), reads=[Pm.k, ps[g].k], writes=[Pm.k])
                    cur = 1 - cur
                S.op("pool", lambda e: e.tensor_copy(out=TTb[:], in_=Pm[:]), reads=[Pm.k], writes=[TTb.k])
                yield
                for g, f in batch_mm((ps[0], ps[1]), lambda e, out, h: e.matmul(out, lhsT=TTb[:, h, :], rhs=vb[:, h, :], start=True, stop=True, skip_group_check=True)):
                    S.op("pe", f, reads=[TTb.k, vb.k], writes=[ps[g].k])
                    yield
                    S.op("act", lambda e, g=g: e.activation(out=f4(u, g), in_=ps[g][:], func=AF.Copy), reads=[ps[g].k], writes=[u.k])
                    yield
                for g, f in batch_mm((ps[2], ps[3]), lambda e, out, h: e.matmul(out, lhsT=kbg[:, h, :], rhs=TTb[:, h, :], start=True, stop=True, skip_group_check=True)):
                    S.op("pe", f, reads=[TTb.k, kbg.k], writes=[ps[2 + g].k])
                    yield
                    S.op("act", lambda e, g=g: e.activation(out=f4(wT, g), in_=ps[2 + g][:], func=AF.Copy), reads=[ps[2 + g].k], writes=[wT.k])
                    yield
                for g, f in batch_mm((ps[0], ps[1]), lambda e, out, h: e.matmul(out, lhsT=wT[:, h, :], rhs=Sbf[:, h, :], start=True, stop=True, skip_group_check=True)):
                    S.op("pe", f, reads=[wT.k, Sbf.k], writes=[ps[g].k])
                    yield
                    S.op("dve", lambda e, g=g: e.tensor_tensor(out=f4(vn, g), in0=f4(u, g), in1=ps[g][:], op=ALU.subtract), reads=[u.k, ps[g].k], writes=[vn.k])
                    yield
                if own:
                    for g in range(2):
                        def fo(e, g=g):
                            for hh in range(4):
                                h = g * 4 + hh
                                e.matmul(ps[g][:, hh * 128:(hh + 1) * 128], lhsT=qdT[:, h, :], rhs=Sbf[:, h, :], start=True, stop=False, skip_group_check=True)
                                ins = e.matmul(ps[g][:, hh * 128:(hh + 1) * 128], lhsT=QKT[:, h, :], rhs=vn[:, h, :], start=False, stop=True, skip_group_check=True)
                            return ins
                        S.op("pe", fo, reads=[qdT.k, Sbf.k, QKT.k, vn.k], writes=[ps[g].k])
                        S.op("act", lambda e, g=g: e.activation(out=f4(o, g), in_=ps[g][:], func=AF.Copy), reads=[ps[g].k], writes=[o.k])
                S.op("pool", lambda e: e.tensor_tensor(out=Stmp[:], in0=S32[:], in1=bc(sm[:, 88:96].unsqueeze(2), [128, 8, 128]), op=ALU.mult), reads=[S32.k, sm.k], writes=[Stmp.k])
                yield
                for g, f in batch_mm((ps[2], ps[3]), lambda e, out, h: e.matmul(out, lhsT=kdec[:, h, :], rhs=vn[:, h, :], start=True, stop=True, skip_group_check=True)):
                    S.op("pe", f, reads=[kdec.k, vn.k], writes=[ps[2 + g].k])
                    yield
                    S.op("dve", lambda e, g=g: e.tensor_tensor(out=f4(S32, g), in0=f4(Stmp, g), in1=ps[2 + g][:], op=ALU.add), reads=[Stmp.k, ps[2 + g].k], writes=[S32.k])
                    yield
                S.op("act", lambda e: e.activation(out=Sbf[:].rearrange("p h d -> p (h d)"), in_=S32[:].rearrange("p h d -> p (h d)"), func=AF.Copy), reads=[S32.k], writes=[Sbf.k])
                yield
                if own:
                    S.op("pool", lambda e: e.tensor_tensor(out=osq[:], in0=o[:], in1=o[:], op=ALU.mult), reads=[o.k], writes=[osq.k])
                    yield
                    S.op("dve", lambda e: e.tensor_reduce(out=sm3[:, 0:8], in_=osq[:], axis=AX.X, op=ALU.add), reads=[osq.k], writes=[sm2.k])
                    yield
                    S.op("dve", lambda e: e.tensor_scalar(out=sm3[:, 0:8], in0=sm3[:, 0:8], scalar1=1.0 / 128, scalar2=EPS, op0=ALU.mult, op1=ALU.add), reads=[sm2.k], writes=[sm2.k])
                    yield
                    S.op("act", lambda e: e.activation(out=sm3[:, 0:8], in_=sm3[:, 0:8], func=AF.Ln), reads=[sm2.k], writes=[sm2.k])
                    yield
                    S.op("act", lambda e: e.activation(out=sm3[:, 0:8], in_=sm3[:, 0:8], func=AF.Exp, scale=-0.5), reads=[sm2.k], writes=[sm2.k])
                    yield
                    S.op("dve", lambda e: e.tensor_tensor(out=o[:], in0=o[:], in1=bc(sm3[:, 0:8].unsqueeze(2), [128, 8, 128]), op=ALU.mult), reads=[o.k, sm2.k], writes=[o.k])
                    yield
                    S.op("pool", lambda e: e.tensor_tensor(out=o[:], in0=o[:], in1=bc(ggdn[:].unsqueeze(1), [128, 8, 128]), op=ALU.mult), reads=[o.k, ggdn.k], writes=[o.k])
                    yield
                    S.op("dve", lambda e: e.tensor_tensor(out=mgt[:], in0=o[:].rearrange("p h d -> p (h d)"), in1=zg[:], op=ALU.mult), reads=[o.k, zg.k], writes=[mgt.k])
                    yield
                    S.dma(lambda e, j=j: e.dma_start(out=self.s_mg[j], in_=mgt[:]), reads=[mgt.k], writes=[self.k_own[j]["mg"]])

                yield

            def run_pair(ga, gb, na, nb):
                nmax = max(na, nb, 1)
                da = db = 0
                for k in range(nmax):
                    ta = ((k + 1) * na) // nmax
                    tb = ((k + 1) * nb) // nmax
                    while da < ta:
                        next(ga, None); da += 1
                    while db < tb:
                        next(gb, None); db += 1
                for _ in ga:
                    pass
                for _ in gb:
                    pass

            def count(mk, p):
                S.dry = True
                n = sum(1 for _ in mk(p))
                S.dry = False
                return n
            nF = [count(gF, 0), count(gF, 1)]
            nN = [count(gNR, 0), count(gNR, 1)]
            for _ in gF(0):
                pass
            for p in range(P):
                if p + 1 < P:
                    run_pair(gNR(p), gF(p + 1), nN[p % 2], nF[(p + 1) % 2])
                else:
                    for _ in gNR(p):
                        pass
            S.barrier()
            S.emit()

    def phase1a_stub(self):
        S = self.S
        with ExitStack() as es:
            z = self.sb(es, "zmg", [128, 1024], BF16)
            S.op("pool", lambda e: e.memset(z[:], 0.0), writes=[z.k])
            for j in range(self.NO):
                S.dma(lambda e, j=j: e.dma_start(out=self.s_mg[j], in_=z[:]), reads=[z.k], writes=[self.k_own[j]["mg"]])
            S.barrier()
            S.emit()


def build(S, debug=False, gdn=True):
    B = Builder(S, debug)
    with ExitStack() as es:
        B.setup(es)
        B.phase1b()
        if gdn:
            B.phase1a()
        else:
            B.phase1a_stub()
        B.phase2()
        B.phase3()
    return B.nc


IN_SPLITS = (3072, 1024, 8, 8, 1024, 256, 256, 512, 64, 8, 1024, 1024)


def host_consts():
    bf = ml_dtypes.bfloat16
    i = np.arange(128)
    c = {}
    c["identf"] = np.eye(128, dtype=np.float32)
    c["identb"] = np.eye(128).astype(bf)
    c["uinc"] = (i[:, None] <= i[None, :]).astype(np.float32)
    c["causal"] = np.where(i[None, :] <= i[:, None], 0.0, NEG).astype(np.float32)
    c["mskU"] = np.where(i[None, :] >= i[:, None], 0.0, -30000.0).astype(bf)
    c["mskL"] = np.where(i[None, :] < i[:, None], 0.0, 30000.0).astype(bf)
    return c


def rope_tab(pos, rot):
    inv = (np.float32(500000.0) ** (-np.arange(0, rot, 2, dtype=np.float32) / np.float32(rot))).astype(np.float32)
    ang = pos.astype(np.float32)[:, None] * inv[None, :]
    return np.cos(ang).astype(np.float32), np.sin(ang).astype(np.float32)


def make_in_maps(S, x, norm_mix_g, w_in, conv_w, a_log, dt_bias, gdn_norm_g, q_norm_g, k_norm_g,
                 w_out, norm_mlp_g, w_mlp_up, w_mlp_down):
    Bn = x.shape[0]
    f = np.float32
    w_in = np.asarray(w_in[0], f)
    pts = np.cumsum((0,) + IN_SPLITS)
    seg = {n: w_in[:, pts[i]:pts[i + 1]] for i, n in enumerate(
        ("qkv", "z", "ga", "gb", "aq", "ak", "av", "iq", "ik", "iw", "gatea", "gateb"))}
    w1a = np.ascontiguousarray(np.concatenate([seg["qkv"], seg["ga"], seg["gb"]], 1))
    w1b = np.ascontiguousarray(np.concatenate([seg["ak"], seg["av"], seg["ik"], seg["iw"], seg["aq"], seg["iq"], seg["gateb"], seg["z"], seg["gatea"]], 1))
    col = lambda v: np.ascontiguousarray(np.asarray(v, f).reshape(8, 128).T)
    rep = lambda v: np.ascontiguousarray(np.broadcast_to(np.asarray(v, f)[None, :], (128, len(v))))
    common = host_consts()
    common.update(dict(
        w1a=w1a, w1b=w1b, wout=np.ascontiguousarray(w_out[0], f), wup=np.ascontiguousarray(w_mlp_up[0], f),
        wdn=np.ascontiguousarray(w_mlp_down[0], f),
        gmix=col(norm_mix_g[0]), gmlp=col(norm_mlp_g[0]), gq=rep(q_norm_g[0]), gk=rep(k_norm_g[0]),
        ggdn=rep(gdn_norm_g[0]), alog=rep(a_log[0]), dtb=rep(dt_bias[0]),
        convw=np.ascontiguousarray(np.asarray(conv_w[0], f).reshape(4, 24, 128).transpose(2, 1, 0).reshape(128, 96)),
    ))
    maps = []
    for b in range(Bn):
        for c in range(2):
            m = dict(common)
            xb = np.asarray(x[b], f)
            if c == 0:
                xs = np.concatenate([np.zeros((128, D), f), xb[:S - 128]], 0)
            else:
                xs = xb
            pos = np.maximum(np.arange(S) + (c - 1) * 128, 0)
            m["xseq"] = np.ascontiguousarray(xs)
            m["cosA"], m["sinA"] = rope_tab(pos, 32)
            m["cosI"], m["sinI"] = rope_tab(pos, 16)
            m["blk0"] = np.full((128, 128), NEG if c == 0 else 0.0, f)
            maps.append(m)
    return maps


def assemble(S, results, Bn):
    out = np.zeros((Bn, S, D), np.float32)
    ov = out.reshape(Bn, S // 256, 2, 128, D)
    for b in range(Bn):
        for c in range(2):
            r = results[b * 2 + c]["out"].reshape(S // 256, 128, D)
            ov[b, :, c] = r
    return out


_NC_CACHE = {}


def kernel(**inputs):
    x = np.asarray(inputs["x"])
    Bn, S, _ = x.shape
    if S not in _NC_CACHE:
        _NC_CACHE[S] = build(S)
    nc = _NC_CACHE[S]
    maps = make_in_maps(S, **{k: np.asarray(v) for k, v in inputs.items()})
    res = run_bass_kernel_spmd(nc, maps, core_ids=list(range(2 * Bn)))
    return assemble(S, res.results, Bn)
```

```python
from contextlib import ExitStack
import numpy as np
import ml_dtypes
import concourse.bass as bass
import concourse.mybir as mybir
from concourse.bass_utils import run_bass_kernel_spmd

F32 = mybir.dt.float32
BF16 = mybir.dt.bfloat16
AF = mybir.ActivationFunctionType
ALU = mybir.AluOpType
AX = mybir.AxisListType

D = 1024
EPS = 1e-6
NEG = -1.0e30
SAME_ENGINE_SYNC = True
IDX_F32R = False
NEU_F32R = True
NC_ = lambda ap: ap
NDT = mybir.dt.float32r if NEU_F32R else F32
RF = (lambda ap: ap.bitcast(F32)) if NEU_F32R else (lambda ap: ap)


class Tok:
    __slots__ = ("name", "w", "r")

    def __init__(self, name):
        self.name = name
        self.w = None
        self.r = {}


class Sched:
    ENG = ("pe", "act", "dve", "pool", "sp")

    def __init__(self, nc, es, n_dma_sems=24):
        self.nc = nc
        self.sems = {}
        for e in ("pe", "act", "dve", "pool"):
            self.sems[e] = es.enter_context(nc.semaphore("s_" + e))
        self.ndma = n_dma_sems
        for k in range(n_dma_sems):
            self.sems[("d", k)] = es.enter_context(nc.semaphore("s_d%d" % k))
        self.cnt = {e: 0 for e in ("pe", "act", "dve", "pool")}
        self.dtot = [0] * n_dma_sems
        self.rr = 0
        self.waited = {e: {} for e in self.ENG}
        self.ops = {e: [] for e in self.ENG}
        self.nops = 0
        self.dry = False

    def _deps(self, eng, reads, writes):
        deps = {}

        def add(d):
            if d is None:
                return
            sid, val = d
            if sid == eng and (eng == "pe" or not SAME_ENGINE_SYNC):
                return
            if deps.get(sid, 0) < val:
                deps[sid] = val

        for t in reads:
            add(t.w)
        for t in writes:
            add(t.w)
            for sid, val in t.r.items():
                add((sid, val))
        out = []
        wd = self.waited[eng]
        for sid, val in deps.items():
            if wd.get(sid, 0) < val:
                wd[sid] = val
                out.append((sid, val))
        return out

    def _mark(self, me, reads, writes):
        for t in reads:
            if t.r.get(me[0], 0) < me[1]:
                t.r[me[0]] = me[1]
        for t in writes:
            t.w = me
            t.r = {}

    def op(self, eng, fn, reads=(), writes=()):
        if self.dry:
            return
        waits = self._deps(eng, reads, writes)
        self.cnt[eng] += 1
        me = (eng, self.cnt[eng])
        self._mark(me, reads, writes)
        self.ops[eng].append((waits, fn, eng, 1))
        self.nops += 1

    def dma(self, fn, reads=(), writes=()):
        if self.dry:
            return
        k = self.rr
        self.rr = (self.rr + 1) % self.ndma
        sid = ("d", k)
        waits = self._deps("sp", reads, writes)
        wd = self.waited["sp"]
        if wd.get(sid, 0) < self.dtot[k]:
            wd[sid] = self.dtot[k]
            waits.append((sid, self.dtot[k]))
        self.dtot[k] += 16
        me = (sid, self.dtot[k])
        self._mark(me, reads, writes)
        self.ops["sp"].append((waits, fn, sid, 16))
        self.nops += 1

    def barrier(self):
        for e in self.ENG:
            waits = []
            wd = self.waited[e]
            for c in ("pe", "act", "dve", "pool"):
                if c != e and wd.get(c, 0) < self.cnt[c]:
                    wd[c] = self.cnt[c]
                    waits.append((c, self.cnt[c]))
            for k in range(self.ndma):
                sid = ("d", k)
                if wd.get(sid, 0) < self.dtot[k]:
                    wd[sid] = self.dtot[k]
                    waits.append((sid, self.dtot[k]))
            if waits:
                self.ops[e].append((waits, None, None, 0))

    def emit(self):
        nc = self.nc
        ops = self.ops
        self.ops = {e: [] for e in self.ENG}
        sems = self.sems

        def run(engh, lst):
            for waits, fn, sid, inc in lst:
                for s, v in waits:
                    engh.wait_ge(sems[s], v)
                if fn is not None:
                    fn(engh).then_inc(sems[sid], inc)

        with nc.Block() as block:
            @block.tensor
            def _(e):
                run(e, ops["pe"])

            @block.scalar
            def _(e):
                run(e, ops["act"])

            @block.vector
            def _(e):
                run(e, ops["dve"])

            @block.gpsimd
            def _(e):
                run(e, ops["pool"])

            @block.sync
            def _(e):
                run(e, ops["sp"])


class Ctx:
    pass


def bc(ap, shape):
    return ap.to_broadcast(list(shape))


class TL:
    def __init__(self, t, name):
        self.t = t
        self.k = Tok(name)

    def __getitem__(self, key):
        return self.t[key]


C1B = 5192
C1A = 3088


class Builder:
    def __init__(self, S, debug=False):
        self.Sq = S
        self.P = S // 128
        self.NO = self.P // 2
        self.debug = debug
        self.nc = bass.Bass("TRN2", target_bir_lowering=False)

    def dram_in(self, name, shape, dt=F32):
        return self.nc.dram_tensor(name, list(shape), dt, kind="ExternalInput").ap()

    def dram_scr(self, name, shape, dt):
        kind = "ExternalOutput" if self.debug else "Internal"
        return self.nc.dram_tensor(name, list(shape), dt, kind=kind).ap()

    def sb(self, es, name, shape, dt):
        return TL(es.enter_context(self.nc.sbuf_tensor("t_" + name, list(shape), dt)), name)

    def load(self, dst, dst_ap, src_ap):
        self.S.dma(lambda e: e.dma_start(out=dst_ap, in_=src_ap), writes=[dst.k])

    def rsqrt_small(self, v, n):
        S = self.S
        S.op("act", lambda e: e.activation(out=v[:, 0:n], in_=v[:, 0:n], func=AF.Ln), reads=[v.k], writes=[v.k])
        S.op("act", lambda e: e.activation(out=v[:, 0:n], in_=v[:, 0:n], func=AF.Exp, scale=-0.5), reads=[v.k], writes=[v.k])

    def load_weight_bf16(self, es, wb, w_ap, ncols, stg):
        S = self.S
        nk = w_ap.shape[0] // 128
        CH = stg[0].t.shape[1]
        i = 0
        for k in range(nk):
            for c0 in range(0, ncols, CH):
                cw = min(CH, ncols - c0)
                st = stg[i % len(stg)]
                S.dma(lambda e, st=st, k=k, c0=c0, cw=cw: e.dma_start(out=st[:, 0:cw], in_=w_ap[k * 128:(k + 1) * 128, c0:c0 + cw]), writes=[st.k])
                eng = ("pool", "dve", "act")[i % 3] if False else "pool"
                S.op(eng, lambda e, st=st, k=k, c0=c0, cw=cw: e.tensor_copy(out=wb[:, k, c0:c0 + cw], in_=st[:, 0:cw]), reads=[st.k], writes=[wb.k])
                i += 1

    def rms_hT(self, xb, ss, xnb, hT, gcol, psb):
        S = self.S
        junk = self.junk
        idb = self.identb
        S.op("act", lambda e: e.activation(out=junk[:, 0:1024], in_=xb[:], func=AF.Square, accum_out=ss[:, 0:1]), reads=[xb.k], writes=[junk.k, ss.k])
        S.op("dve", lambda e: e.tensor_scalar(out=ss[:, 0:1], in0=ss[:, 0:1], scalar1=1.0 / 1024, scalar2=EPS, op0=ALU.mult, op1=ALU.add), reads=[ss.k], writes=[ss.k])
        self.rsqrt_small(ss, 1)
        S.op("dve", lambda e: e.tensor_scalar(out=xnb[:], in0=xb[:], scalar1=ss[:, 0:1], scalar2=None, op0=ALU.mult), reads=[ss.k, xb.k], writes=[xnb.k])

        def tr(e):
            for kc in range(8):
                i = e.transpose(out=psb[:, kc * 128:(kc + 1) * 128], in_=xnb[:, kc * 128:(kc + 1) * 128], identity=idb[:])
            return i
        S.op("pe", tr, reads=[xnb.k, idb.k], writes=[psb.k])
        S.op("dve", lambda e: e.tensor_tensor(out=hT[:], in0=psb[:].rearrange("p (k t) -> p k t", k=8), in1=bc(gcol[:].unsqueeze(2), [128, 8, 128]), op=ALU.mult), reads=[psb.k, gcol.k], writes=[hT.k])

    def proj(self, ps, ncol, hT, wb, c0, pcol=0, start=True):
        def f(e):
            for kc in range(8):
                i = e.matmul(ps[:, pcol:pcol + ncol], lhsT=hT[:, kc, :], rhs=wb[:, kc, c0:c0 + ncol], start=(kc == 0), stop=(kc == 7))
            return i
        self.S.op("pe", f, reads=[hT.k, wb.k], writes=[ps.k])

    def setup(self, es):
        nc = self.nc
        S_, P, NO = self.Sq, self.P, self.NO
        self.S = Sched(nc, es)
        d = self.dram_in
        self.i_x = d("xseq", [S_, D])
        self.i_w1a = d("w1a", [D, C1A])
        self.i_w1b = d("w1b", [D, C1B])
        self.i_wout = d("wout", [D, D])
        self.i_wup = d("wup", [D, 4 * D])
        self.i_wdn = d("wdn", [4 * D, D])
        self.i_cosA = d("cosA", [S_, 16]); self.i_sinA = d("sinA", [S_, 16])
        self.i_cosI = d("cosI", [S_, 8]); self.i_sinI = d("sinI", [S_, 8])
        self.i_blk0 = d("blk0", [128, 128])
        self.i_identf = d("identf", [128, 128]); self.i_identb = d("identb", [128, 128], BF16)
        self.i_uinc = d("uinc", [128, 128]); self.i_causal = d("causal", [128, 128])
        self.i_mskU = d("mskU", [128, 128], BF16); self.i_mskL = d("mskL", [128, 128], BF16)
        self.i_gmix = d("gmix", [128, 8]); self.i_gmlp = d("gmlp", [128, 8])
        self.i_gq = d("gq", [128, 128]); self.i_gk = d("gk", [128, 128]); self.i_ggdn = d("ggdn", [128, 128])
        self.i_alog = d("alog", [128, 8]); self.i_dtb = d("dtb", [128, 8])
        self.i_convw = d("convw", [128, 24 * 4])
        self.o_out = nc.dram_tensor("out", [NO * 128, D], F32, kind="ExternalOutput").ap()
        s = self.dram_scr
        self.s_akT = s("s_akT", [128, 2 * S_], BF16)
        self.s_v = s("s_v", [S_, 258], BF16)
        self.s_ikT = s("s_ikT", [128, S_], F32)
        self.s_aqT = s("s_aqT", [NO, 128, 1024], BF16)
        self.s_iqT = s("s_iqT", [NO, 128, 512], F32)
        self.s_iw = s("s_iw", [NO, 128, 8], F32)
        self.s_sgb = s("s_sgb", [NO, 128, 1024], BF16)
        self.s_mg = s("s_mg", [NO, 128, 1024], BF16)
        self.s_zg = s("s_zg", [NO, 128, 1024], BF16)
        self.s_mrg = s("s_mrg", [NO, 128, 1024], BF16)
        self.k_akT = Tok("akT_d"); self.k_v = Tok("v_d"); self.k_ikT = Tok("ikT_d")
        self.k_own = [{n: Tok(n + str(j)) for n in ("aqT", "iqT", "iw", "sgb", "mg", "x1", "zg")} for j in range(NO)]
        self.ps = [TL(es.enter_context(nc.psum_tensor("ps%d" % i, [128, 512], F32)), "ps%d" % i) for i in range(7)]
        self.psb = TL(es.enter_context(nc.psum_tensor("psb", [128, 1024], BF16)), "psb")
        sb = lambda n, sh, dt: self.sb(es, n, sh, dt)
        self.identf = sb("identf", [128, 128], F32); self.identb = sb("identb", [128, 128], BF16)
        self.junk = sb("junk", [128, 1024], F32)
        for t, src in ((self.identf, self.i_identf), (self.identb, self.i_identb)):
            self.load(t, t[:], src)

    def norm_rope(self, H, src_ap, src_tok, gain, extra, out, ocol, cs, sn, tmp):
        S = self.S
        tf, sq, ssh, ra, rb = tmp
        n = H * 128
        S.op("act", lambda e: e.activation(out=tf[:, 0:n], in_=src_ap, func=AF.Copy), reads=[src_tok], writes=[tf.k])
        S.op("dve", lambda e: e.tensor_tensor(out=sq[:, 0:n], in0=tf[:, 0:n], in1=tf[:, 0:n], op=ALU.mult), reads=[tf.k], writes=[sq.k])
        S.op("dve", lambda e: e.tensor_reduce(out=ssh[:, 0:H], in_=sq[:, 0:n].rearrange("p (h d) -> p h d", h=H), axis=AX.X, op=ALU.add), reads=[sq.k], writes=[ssh.k])
        S.op("dve", lambda e: e.tensor_scalar(out=ssh[:, 0:H], in0=ssh[:, 0:H], scalar1=1.0 / 128, scalar2=EPS, op0=ALU.mult, op1=ALU.add), reads=[ssh.k], writes=[ssh.k])
        self.rsqrt_small(ssh, H)
        if extra != 1.0:
            S.op("dve", lambda e: e.tensor_scalar(out=ssh[:, 0:H], in0=ssh[:, 0:H], scalar1=extra, scalar2=None, op0=ALU.mult), reads=[ssh.k], writes=[ssh.k])
        tf3 = tf[:, 0:n].rearrange("p (h d) -> p h d", h=H)
        S.op("dve", lambda e: e.tensor_tensor(out=tf3, in0=tf3, in1=bc(ssh[:, 0:H].unsqueeze(2), [128, H, 128]), op=ALU.mult), reads=[tf.k, ssh.k], writes=[tf.k])
        S.op("pool", lambda e: e.tensor_tensor(out=tf3, in0=tf3, in1=bc(gain[:].unsqueeze(1), [128, H, 128]), op=ALU.mult), reads=[tf.k, gain.k], writes=[tf.k])
        o3 = out[:, ocol:ocol + n].rearrange("p (h d) -> p h d", h=H)
        S.op("act", lambda e: e.activation(out=out[:, ocol:ocol + n], in_=tf[:, 0:n], func=AF.Copy), reads=[tf.k], writes=[out.k])
        self.rope(tf3, tf.k, o3, out.k, H, 16, cs, sn, ra, rb)

    def rope(self, t3, ttok, o3, otok, H, hf, cs, sn, ra, rb):
        S = self.S
        x1 = t3[:, :, 0:hf]; x2 = t3[:, :, hf:2 * hf]
        c = bc(cs[:].unsqueeze(1), [128, H, hf]); s_ = bc(sn[:].unsqueeze(1), [128, H, hf])
        a3 = ra[:, 0:H * hf].rearrange("p (h d) -> p h d", h=H)
        b3 = rb[:, 0:H * hf].rearrange("p (h d) -> p h d", h=H)
        S.op("pool", lambda e: e.tensor_tensor(out=a3, in0=x1, in1=c, op=ALU.mult), reads=[ttok, cs.k], writes=[ra.k])
        S.op("pool", lambda e: e.tensor_tensor(out=b3, in0=x2, in1=s_, op=ALU.mult), reads=[ttok, sn.k], writes=[rb.k])
        S.op("dve", lambda e: e.tensor_tensor(out=o3[:, :, 0:hf], in0=a3, in1=b3, op=ALU.subtract), reads=[ra.k, rb.k], writes=[otok])
        S.op("pool", lambda e: e.tensor_tensor(out=a3, in0=x2, in1=c, op=ALU.mult), reads=[ttok, cs.k], writes=[ra.k])
        S.op("pool", lambda e: e.tensor_tensor(out=b3, in0=x1, in1=s_, op=ALU.mult), reads=[ttok, sn.k], writes=[rb.k])
        S.op("dve", lambda e: e.tensor_tensor(out=o3[:, :, hf:2 * hf], in0=a3, in1=b3, op=ALU.add), reads=[ra.k, rb.k], writes=[otok])

    def phase1b(self):
        S = self.S
        P = self.P
        ps, psb = self.ps, self.psb
        with ExitStack() as es:
            sb = lambda n, sh, dt: self.sb(es, n, sh, dt)
            wb = sb("w1b", [128, 8, C1B], BF16)
            stg = [sb("stg%d" % i, [128, 1024], F32) for i in range(2)]
            gmix = sb("gmix", [128, 8], F32); gq = sb("gq", [128, 128], F32); gk = sb("gk", [128, 128], F32)
            self.load(gmix, gmix[:], self.i_gmix); self.load(gq, gq[:], self.i_gq); self.load(gk, gk[:], self.i_gk)
            self.load_weight_bf16(es, wb, self.i_w1b, C1B, stg)
            xb = [sb("xb%d" % i, [128, 1024], F32) for i in range(2)]
            ss = sb("ss", [128, 1], F32); xnb = sb("xnb", [128, 1024], BF16); hT = sb("hT", [128, 8, 128], BF16)
            cA = sb("cA", [128, 16], F32); sA = sb("sA", [128, 16], F32); cI = sb("cI", [128, 8], F32); sI = sb("sI", [128, 8], F32)
            tmp = (sb("tf", [128, 512], F32), sb("sq", [128, 512], F32), sb("ssh", [128, 8], F32), sb("ra", [128, 64], F32), sb("rb", [128, 64], F32))
            ra, rb = tmp[3], tmp[4]
            akb = sb("akb", [128, 256], BF16); akT = sb("akTb", [128, 256], BF16)
            vaug = sb("vaug", [128, 258], BF16)
            ikf = sb("ikf", [128, 72], F32); ik2 = sb("ik2", [128, 128], F32); ikT = sb("ikTb", [128, 128], F32)
            iw = sb("iwb", [128, 8], F32)
            aqb = sb("aqb", [128, 1024], BF16); aqT = sb("aqTb", [128, 1024], BF16)
            iqf = sb("iqf", [128, 512], F32); iqo = sb("iqo", [128, 512], F32); iqT = sb("iqTb", [128, 512], F32)
            sgb = sb("sgbb", [128, 1024], BF16)
            zs = sb("zs", [128, 1024], F32); sga = sb("sga", [128, 1024], F32); zg = sb("zgb", [128, 1024], BF16)
            S.op("pool", lambda e: e.memset(vaug[:], 1.0), writes=[vaug.k])
            for p in range(P):
                own = (p % 2 == 1)
                j = p // 2
                x = xb[p % 2]
                r0 = p * 128
                self.load(x, x[:], self.i_x[r0:r0 + 128, :])
                self.load(cA, cA[:], self.i_cosA[r0:r0 + 128, :]); self.load(sA, sA[:], self.i_sinA[r0:r0 + 128, :])
                self.load(cI, cI[:], self.i_cosI[r0:r0 + 128, :]); self.load(sI, sI[:], self.i_sinI[r0:r0 + 128, :])
                self.rms_hT(x, ss, xnb, hT, gmix, psb)
                self.proj(ps[0], 512, hT, wb, 0)
                self.norm_rope(2, ps[0][:, 0:256], ps[0].k, gk, 1.0, akb, 0, cA, sA, tmp)
                S.op("act", lambda e: e.activation(out=vaug[:].rearrange("p (c d) -> p c d", c=2)[:, :, 0:128], in_=ps[0][:, 256:512].rearrange("p (c d) -> p c d", c=2), func=AF.Copy), reads=[ps[0].k], writes=[vaug.k])
                S.dma(lambda e, r0=r0: e.dma_start(out=self.s_v[r0:r0 + 128, :], in_=vaug[:]), reads=[vaug.k], writes=[self.k_v])

                def trk(e):
                    for c in range(2):
                        i = e.transpose(out=psb[:, c * 128:(c + 1) * 128], in_=akb[:, c * 128:(c + 1) * 128], identity=self.identb[:])
                    return i
                S.op("pe", trk, reads=[akb.k, self.identb.k], writes=[psb.k])
                S.op("act", lambda e: e.activation(out=akT[:], in_=psb[:, 0:256], func=AF.Copy), reads=[psb.k], writes=[akT.k])
                S.dma(lambda e, r0=r0: e.dma_start(out=self.s_akT.rearrange("h (c s) -> h c s", c=2)[:, :, r0:r0 + 128], in_=akT[:].rearrange("h (c s) -> h c s", c=2)), reads=[akT.k], writes=[self.k_akT])
                self.proj(ps[1], 72, hT, wb, 512)
                S.op("act", lambda e: e.activation(out=ikf[:], in_=ps[1][:, 0:72], func=AF.Copy), reads=[ps[1].k], writes=[ikf.k])
                S.op("dve", lambda e: e.tensor_copy(out=ik2[:, 0:64], in_=ikf[:, 0:64]), reads=[ikf.k], writes=[ik2.k])
                self.rope(ikf[:, 0:64].rearrange("p (h d) -> p h d", h=1), ikf.k, ik2[:, 0:64].rearrange("p (h d) -> p h d", h=1), ik2.k, 1, 8, cI, sI, ra, rb)
                S.op("dve", lambda e: e.tensor_copy(out=ik2[:, 64:128], in_=ik2[:, 0:64]), reads=[ik2.k], writes=[ik2.k])
                S.op("pe", lambda e: e.transpose(out=ps[2][:, 0:128], in_=ik2[:], identity=self.identf[:]), reads=[ik2.k, self.identf.k], writes=[ps[2].k])
                S.op("act", lambda e: e.activation(out=ikT[:], in_=ps[2][:, 0:128], func=AF.Copy), reads=[ps[2].k], writes=[ikT.k])
                S.dma(lambda e, r0=r0: e.dma_start(out=self.s_ikT[:, r0:r0 + 128], in_=ikT[:]), reads=[ikT.k], writes=[self.k_ikT])
                if not own:
                    continue
                ko = self.k_own[j]
                S.op("dve", lambda e: e.tensor_scalar(out=iw[:], in0=ikf[:, 64:72], scalar1=float(512.0 ** -0.5), scalar2=None, op0=ALU.mult), reads=[ikf.k], writes=[iw.k])
                S.dma(lambda e, j=j: e.dma_start(out=self.s_iw[j], in_=iw[:]), reads=[iw.k], writes=[ko["iw"]])
                for half in range(2):
                    pq = ps[3 + half]
                    self.proj(pq, 512, hT, wb, 584 + half * 512)
                    self.norm_rope(4, pq[:, 0:512], pq.k, gq, float(128.0 ** -0.5), aqb, half * 512, cA, sA, tmp)

                def trq(e):
                    for h in range(8):
                        i = e.transpose(out=psb[:, h * 128:(h + 1) * 128], in_=aqb[:, h * 128:(h + 1) * 128], identity=self.identb[:])
                    return i
                S.op("pe", trq, reads=[aqb.k, self.identb.k], writes=[psb.k])
                S.op("act", lambda e: e.activation(out=aqT[:], in_=psb[:], func=AF.Copy), reads=[psb.k], writes=[aqT.k])
                S.dma(lambda e, j=j: e.dma_start(out=self.s_aqT[j], in_=aqT[:]), reads=[aqT.k], writes=[ko["aqT"]])
                self.proj(ps[5], 512, hT, wb, 1608)
                S.op("act", lambda e: e.activation(out=iqf[:], in_=ps[5][:], func=AF.Copy), reads=[ps[5].k], writes=[iqf.k])
                S.op("dve", lambda e: e.tensor_copy(out=iqo[:], in_=iqf[:]), reads=[iqf.k], writes=[iqo.k])
                self.rope(iqf[:].rearrange("p (h d) -> p h d", h=8), iqf.k, iqo[:].rearrange("p (h d) -> p h d", h=8), iqo.k, 8, 8, cI, sI, ra, rb)

                def tri(e):
                    for g in range(4):
                        i = e.transpose(out=ps[6][:, g * 128:(g + 1) * 128], in_=iqo[:, g * 128:(g + 1) * 128], identity=self.identf[:])
                    return i
                S.op("pe", tri, reads=[iqo.k, self.identf.k], writes=[ps[6].k])
                S.op("act", lambda e: e.activation(out=iqT[:], in_=ps[6][:], func=AF.Copy), reads=[ps[6].k], writes=[iqT.k])
                S.dma(lambda e, j=j: e.dma_start(out=self.s_iqT[j], in_=iqT[:]), reads=[iqT.k], writes=[ko["iqT"]])
                for half in range(2):
                    pq = ps[half]
                    self.proj(pq, 512, hT, wb, 2120 + half * 512)
                    S.op("act", lambda e, pq=pq, half=half: e.activation(out=sgb[:, half * 512:(half + 1) * 512], in_=pq[:], func=AF.Sigmoid), reads=[pq.k], writes=[sgb.k])
                S.dma(lambda e, j=j: e.dma_start(out=self.s_sgb[j], in_=sgb[:]), reads=[sgb.k], writes=[ko["sgb"]])
                for half in range(2):
                    pq = ps[2 + half]; pg = ps[4 + half]
                    self.proj(pq, 512, hT, wb, 3144 + half * 512)
                    self.proj(pg, 512, hT, wb, 4168 + half * 512)
                    S.op("act", lambda e, pq=pq, half=half: e.activation(out=zs[:, half * 512:(half + 1) * 512], in_=pq[:], func=AF.Silu), reads=[pq.k], writes=[zs.k])
                    S.op("act", lambda e, pg=pg, half=half: e.activation(out=sga[:, half * 512:(half + 1) * 512], in_=pg[:], func=AF.Sigmoid), reads=[pg.k], writes=[sga.k])
                S.op("pool", lambda e: e.tensor_tensor(out=zg[:], in0=zs[:], in1=sga[:], op=ALU.mult), reads=[zs.k, sga.k], writes=[zg.k])
                S.dma(lambda e, j=j: e.dma_start(out=self.s_zg[j], in_=zg[:]), reads=[zg.k], writes=[ko["zg"]])
            S.barrier()
            S.emit()

    def phase2(self, NBIS=24):
        S = self.S
        P, NO, S_ = self.P, self.NO, self.Sq
        ps, psb = self.ps, self.psb
        KSEL = float(min(256, S_ // 4)) - 0.5
        with ExitStack() as es:
            sb = lambda n, sh, dt: self.sb(es, n, sh, dt)
            akT = sb("akT", [128, 2, S_], BF16)
            vall = sb("vall", [128, P, 258], BF16)
            H2 = S_ // 2
            assert H2 % 512 == 0
            ikT = sb("ikT", [128, H2], F32)
            scores = [sb("score%d" % i, [128, S_], F32) for i in range(2)]
            msk = sb("msk", [128, S_], BF16)
            jk = sb("jk", [128, S_], mybir.dt.uint8)
            gq = sb("gq2", [128, 128], F32); gk = sb("gk2", [128, 128], F32)
            negM = sb("negM", [128, 2], F32)
            causal = sb("causal", [128, 128], F32); blk0 = sb("blk0", [128, 128], F32)
            self.load(gq, gq[:], self.i_gq); self.load(gk, gk[:], self.i_gk)
            self.load(causal, causal[:], self.i_causal); self.load(blk0, blk0[:], self.i_blk0)
            S.dma(lambda e: e.dma_start(out=akT[:], in_=self.s_akT.rearrange("h (c s) -> h c s", c=2)), reads=[self.k_akT], writes=[akT.k])
            S.dma(lambda e: e.dma_start(out=vall[:], in_=self.s_v.rearrange("(p t) n -> t p n", t=128)), reads=[self.k_v], writes=[vall.k])
            S.dma(lambda e: e.dma_start(out=ikT[0:64, :], in_=self.s_ikT[0:64, 0:H2]), reads=[self.k_ikT], writes=[ikT.k])
            S.dma(lambda e: e.dma_start(out=ikT[64:128, :], in_=self.s_ikT[64:128, H2:S_]), reads=[self.k_ikT], writes=[ikT.k])
            S.op("dve", lambda e: e.tensor_reduce(out=negM[:, 0:1], in_=gq[:], axis=AX.X, op=ALU.max, apply_absolute_value=True), reads=[gq.k], writes=[negM.k])
            S.op("dve", lambda e: e.tensor_reduce(out=negM[:, 1:2], in_=gk[:], axis=AX.X, op=ALU.max, apply_absolute_value=True), reads=[gk.k], writes=[negM.k])
            S.op("dve", lambda e: e.scalar_tensor_tensor(out=negM[:, 0:1], in0=negM[:, 0:1], scalar=float(-(128.0 ** 0.5)), in1=negM[:, 1:2], op0=ALU.mult, op1=ALU.mult), reads=[negM.k], writes=[negM.k])
            aqT = sb("aqT", [128, 1024], BF16); iqT = sb("iqT", [128, 512], F32); iw = sb("iw", [128, 8], F32)
            iqT2 = sb("iqT2", [128, 512], F32)
            sgb = sb("sgb", [128, 1024], BF16); mg = sb("mg", [128, 1024], BF16)
            rl = [sb("rl%d" % i, [128, 512], F32) for i in range(2)]
            pT = [sb("pT%d" % i, [128, 512], BF16) for i in range(2)]
            mT = [sb("mT%d" % i, [128, 1024], BF16) for i in range(2)]
            sts = [sb("bst%d" % i, [128, 8], F32) for i in range(3)]
            rec = sb("rec", [128, 8], F32)
            oatt = self.junk
            mrg = sb("mrg", [128, 1024], BF16)
            accb = [ps[4], ps[5], ps[6]]
            hb = [(0, 0), (0, 1), (0, 2), (1, 0), (1, 1), (1, 2), (2, 0), (2, 1)]

            sctok = [[Tok("sc%d_%d" % (i, g)) for g in range(S_ // 512)] for i in range(2)]
            IDXC = (lambda ap: ap.bitcast(mybir.dt.float32r)) if IDX_F32R else (lambda ap: ap)

            def gI(j):
                p = 2 * j + 1
                nkeys = (p + 1) * 128
                ko = self.k_own[j]
                score = scores[j % 2]; st = sts[j % 3]
                S.dma(lambda e: e.dma_start(out=iqT[:], in_=self.s_iqT[j]), reads=[ko["iqT"]], writes=[iqT.k])
                S.dma(lambda e: e.dma_start(out=iqT2[0:64, :], in_=self.s_iqT[j][64:128, :]), reads=[ko["iqT"]], writes=[iqT2.k])
                S.dma(lambda e: e.dma_start(out=iqT2[64:128, :], in_=self.s_iqT[j][0:64, :]), reads=[ko["iqT"]], writes=[iqT2.k])
                S.dma(lambda e: e.dma_start(out=iw[:], in_=self.s_iw[j]), reads=[ko["iw"]], writes=[iw.k])
                it = 0
                nkg = (nkeys + 511) // 512
                for h in range(8):
                    for kg in range(nkg):
                        k0 = kg * 512
                        w = min(512, nkeys - k0)
                        sk = score.k
                        pb = ps[it % 2]; r = rl[it % 2]; it += 1
                        b0 = 64 if k0 >= H2 else 0
                        kk0 = k0 - (H2 if k0 >= H2 else 0)
                        qsrc = iqT if (h % 2) * 64 == b0 else iqT2
                        S.op("pe", lambda e, pb=pb, h=h, b0=b0, kk0=kk0, w=w, qsrc=qsrc: e.matmul(pb[:, 0:w], lhsT=IDXC(qsrc[b0:b0 + 64, (h // 2) * 128:(h // 2 + 1) * 128]), rhs=IDXC(ikT[b0:b0 + 64, kk0:kk0 + w]), start=True, stop=True), reads=[iqT.k, iqT2.k, ikT.k], writes=[pb.k])
                        S.op("act", lambda e, pb=pb, r=r, w=w: e.activation(out=r[:, 0:w], in_=pb[:, 0:w], func=AF.Relu), reads=[pb.k], writes=[r.k])
                        if h == 0:
                            S.op("dve", lambda e, r=r, k0=k0, w=w: e.tensor_scalar(out=score[:, k0:k0 + w], in0=r[:, 0:w], scalar1=iw[:, 0:1], scalar2=None, op0=ALU.mult), reads=[r.k, iw.k], writes=[sk])
                        else:
                            S.op("dve", lambda e, r=r, k0=k0, w=w, h=h: e.scalar_tensor_tensor(out=score[:, k0:k0 + w], in0=r[:, 0:w], scalar=iw[:, h:h + 1], in1=score[:, k0:k0 + w], op0=ALU.mult, op1=ALU.add), reads=[r.k, iw.k, sk], writes=[sk])
                        yield
                allk = sctok[j % 2][0:nkg]
                S.op("dve", lambda e: e.tensor_copy(out=st[:, 7:8], in_=st[:, 7:8]), reads=allk + [st.k], writes=[score.k, st.k])
                S.op("dve", lambda e: e.tensor_reduce(out=st[:, 5:6], in_=score[:, 0:nkeys], axis=AX.X, op=ALU.max), reads=[score.k], writes=[st.k])
                S.op("dve", lambda e: e.tensor_reduce(out=st[:, 6:7], in_=score[:, 0:nkeys], axis=AX.X, op=ALU.min), reads=[score.k], writes=[st.k])
                S.op("dve", lambda e: e.tensor_scalar(out=st[:, 0:1], in0=st[:, 6:7], scalar1=-1.0, scalar2=None, op0=ALU.add), reads=[st.k], writes=[st.k])
                S.op("dve", lambda e: e.tensor_tensor(out=st[:, 1:2], in0=st[:, 5:6], in1=st[:, 0:1], op=ALU.subtract), reads=[st.k], writes=[st.k])
                S.op("dve", lambda e: e.tensor_tensor(out=score[:, nkeys - 128:nkeys], in0=score[:, nkeys - 128:nkeys], in1=causal[:], op=ALU.add), reads=[score.k, causal.k], writes=[score.k])
                S.op("dve", lambda e: e.tensor_tensor(out=score[:, 0:128], in0=score[:, 0:128], in1=blk0[:], op=ALU.add), reads=[score.k, blk0.k], writes=[score.k])
                yield

            def gB(j):
                nkeys = (2 * j + 2) * 128
                score = scores[j % 2]; st = sts[j % 3]
                for it_ in range(1, NBIS + 1):
                    f = float(2.0 ** -it_)
                    S.op("dve", lambda e, f=f: e.scalar_tensor_tensor(out=st[:, 2:3], in0=st[:, 1:2], scalar=f, in1=st[:, 0:1], op0=ALU.mult, op1=ALU.add), reads=[st.k], writes=[st.k])
                    S.op("dve", lambda e: e.tensor_scalar(out=jk[:, 0:nkeys], in0=score[:, 0:nkeys], scalar1=st[:, 2:3], scalar2=0.0, op0=ALU.is_gt, op1=ALU.add, accum_out=st[:, 3:4]), reads=[score.k, st.k], writes=[jk.k, st.k])
                    S.op("dve", lambda e, f=f: e.tensor_scalar(out=st[:, 4:5], in0=st[:, 3:4], scalar1=KSEL, scalar2=f, op0=ALU.is_gt, op1=ALU.mult), reads=[st.k], writes=[st.k])
                    S.op("dve", lambda e: e.scalar_tensor_tensor(out=st[:, 0:1], in0=st[:, 4:5], scalar=st[:, 1:2], in1=st[:, 0:1], op0=ALU.mult, op1=ALU.add), reads=[st.k], writes=[st.k])
                    yield

            def gA(j):
                p = 2 * j + 1
                nk = p + 1
                nkeys = nk * 128
                ko = self.k_own[j]
                score = scores[j % 2]; st = sts[j % 3]
                S.op("dve", lambda e: e.tensor_scalar(out=msk[:, 0:nkeys], in0=score[:, 0:nkeys], scalar1=st[:, 0:1], scalar2=None, op0=ALU.is_gt), reads=[score.k, st.k], writes=[msk.k])
                S.dma(lambda e: e.dma_start(out=aqT[:], in_=self.s_aqT[j]), reads=[ko["aqT"]], writes=[aqT.k])
                S.dma(lambda e: e.dma_start(out=sgb[:], in_=self.s_sgb[j]), reads=[ko["sgb"]], writes=[sgb.k])
                S.dma(lambda e: e.dma_start(out=mg[:], in_=self.s_mg[j]), reads=[ko["mg"]], writes=[mg.k])
                yield
                units = [(kb, c) for kb in range(nk) for c in range(2)]

                def emit_st(u):
                    kb, c = units[u]
                    pss = ps[2 + u % 2]
                    S.op("pe", lambda e: e.matmul(pss[:], lhsT=akT[:, c, kb * 128:(kb + 1) * 128], rhs=aqT[:, c * 512:(c + 1) * 512], start=True, stop=True), reads=[akT.k, aqT.k], writes=[pss.k])

                def emit_mask(kb):
                    m = mT[(kb // 8) % 2]
                    nb = min(8, nk - kb)

                    def trm(e):
                        for i_ in range(nb):
                            ins = e.transpose(out=psb[:, i_ * 128:(i_ + 1) * 128], in_=msk[:, (kb + i_) * 128:(kb + i_ + 1) * 128], identity=self.identb[:])
                        return ins
                    S.op("pe", trm, reads=[msk.k, self.identb.k], writes=[psb.k])
                    S.op("act", lambda e: e.activation(out=m[:, 0:nb * 128], in_=psb[:, 0:nb * 128], func=AF.Copy), reads=[psb.k], writes=[m.k])

                emit_mask(0)
                emit_st(0)
                yield
                for u, (kb, c) in enumerate(units):
                    pss = ps[2 + u % 2]; pt = pT[u % 2]
                    m = mT[(kb // 8) % 2]
                    if u + 1 < len(units):
                        if units[u + 1][1] == 0 and units[u + 1][0] % 8 == 0:
                            emit_mask(units[u + 1][0])
                        emit_st(u + 1)
                    S.op("act", lambda e, pss=pss, pt=pt: e.activation(out=pt[:], in_=pss[:], func=AF.Exp, bias=negM[:, 0:1], scale=1.0), reads=[pss.k, negM.k], writes=[pt.k])
                    mi = (kb % 8) * 128
                    S.op("pool", lambda e, pt=pt, m=m, mi=mi: e.tensor_tensor(out=pt[:].rearrange("p (h q) -> p h q", h=4), in0=pt[:].rearrange("p (h q) -> p h q", h=4), in1=bc(m[:, mi:mi + 128].unsqueeze(1), [128, 4, 128]), op=ALU.mult), reads=[pt.k, m.k], writes=[pt.k])
                    yield

                    def pv(e, pt=pt, c=c, kb=kb):
                        for hh in range(4):
                            h = c * 4 + hh
                            bk, sl = hb[h]
                            ins = e.matmul(accb[bk][:, sl * 129:(sl + 1) * 129], lhsT=pt[:, hh * 128:(hh + 1) * 128], rhs=vall[:, kb, c * 129:(c + 1) * 129], start=(kb == 0 and sl == 0), stop=(kb == nk - 1), skip_group_check=True)
                        return ins
                    S.op("pe", pv, reads=[pt.k, vall.k], writes=[accb[hb[c * 4][0]].k, accb[hb[c * 4 + 3][0]].k])
                for bk, (h0, n) in enumerate(((0, 3), (3, 3), (6, 2))):
                    a3 = accb[bk][:, 0:n * 129].rearrange("p (h d) -> p h d", h=n)
                    S.op("dve", lambda e, a3=a3, h0=h0, n=n: e.reciprocal(out=rec[:, h0:h0 + n], in_=a3[:, :, 128]), reads=[accb[bk].k], writes=[rec.k])
                    S.op("dve", lambda e, a3=a3, h0=h0, n=n: e.tensor_tensor(out=oatt[:, h0 * 128:(h0 + n) * 128].rearrange("p (h d) -> p h d", h=n), in0=a3[:, :, 0:128], in1=bc(rec[:, h0:h0 + n].unsqueeze(2), [128, n, 128]), op=ALU.mult), reads=[accb[bk].k, rec.k], writes=[oatt.k])
                S.op("pool", lambda e: e.tensor_tensor(out=oatt[:], in0=oatt[:], in1=sgb[:], op=ALU.mult), reads=[oatt.k, sgb.k], writes=[oatt.k])
                S.op("pool", lambda e: e.tensor_tensor(out=mrg[:], in0=oatt[:], in1=mg[:], op=ALU.add), reads=[oatt.k, mg.k], writes=[mrg.k])
                S.dma(lambda e: e.dma_start(out=self.s_mrg[j], in_=mrg[:]), reads=[mrg.k], writes=[ko["x1"]])
                yield

            def count(g):
                n = 0
                for _ in g:
                    n += 1
                return n

            for t in range(NO + 2):
                gens = []
                for mk, jj in ((gA, t - 2), (gB, t - 1), (gI, t)):
                    if 0 <= jj < NO:
                        gens.append(mk(jj))
                sizes = []
                for mk, jj in ((gA, t - 2), (gB, t - 1), (gI, t)):
                    if 0 <= jj < NO:
                        nk = 2 * jj + 2
                        if mk is gA:
                            sizes.append(2 + 2 * nk)
                        elif mk is gB:
                            sizes.append(NBIS)
                        else:
                            sizes.append(((nk * 128 + 511) // 512) * 8 + 1)
                nmax = max(sizes)
                done = [0] * len(gens)
                if 0 <= t - 2 < NO:
                    next(gens[0], None)
                    done[0] = 1
                for k in range(nmax):
                    for gi, g in enumerate(gens):
                        tgt = ((k + 1) * sizes[gi]) // nmax
                        while done[gi] < tgt:
                            next(g, None)
                            done[gi] += 1
                for g in gens:
                    for _ in g:
                        pass
            S.barrier()
            S.emit()

    def phase3(self, G=2):
        S = self.S
        NO = self.NO
        ps, psb = self.ps, self.psb
        G = min(G, NO)
        with ExitStack() as es:
            sb = lambda n, sh, dt: self.sb(es, n, sh, dt)
            wup = sb("wupb", [128, 8, 4096], BF16); wdn = sb("wdnb", [128, 32, 1024], BF16)
            woutb = sb("woutb", [128, 8, 1024], BF16)
            stg = [sb("stg3_%d" % i, [128, 512], F32) for i in range(2)]
            gmlp = sb("gmlp", [128, 8], F32)
            self.load(gmlp, gmlp[:], self.i_gmlp)
            self.load_weight_bf16(es, woutb, self.i_wout, 1024, stg)
            self.load_weight_bf16(es, wup, self.i_wup, 4096, stg)
            self.load_weight_bf16(es, wdn, self.i_wdn, 1024, stg)
            x1 = sb("x1g", [128, G, 1024], F32)
            xb = sb("xb3", [128, 1024], F32); mrg = sb("mrg3", [128, 1024], BF16); mT2 = sb("mT2", [128, 8, 128], BF16)
            ss = sb("ss3", [128, 1], F32); xnb = sb("xnb3", [128, 1024], BF16)
            h2T = sb("h2T", [128, 8, G * 128], BF16)
            hTb = sb("hTb3", [128, 8, 128], BF16)
            hid = sb("hidT", [128, 32, G * 128], BF16)
            rl = [sb("rl3_%d" % i, [128, G * 128], F32) for i in range(2)]
            ot = [sb("ot%d" % i, [128, 1024], F32) for i in range(2)]
            for g0 in range(0, NO, G):
                for b in range(G):
                    j = g0 + b
                    p = 2 * j + 1
                    S.dma(lambda e, j=j: e.dma_start(out=mrg[:], in_=self.s_mrg[j]), reads=[self.k_own[j]["x1"]], writes=[mrg.k])
                    S.dma(lambda e, p=p: e.dma_start(out=xb[:], in_=self.i_x[p * 128:(p + 1) * 128, :]), writes=[xb.k])

                    def trg(e):
                        for kc in range(8):
                            ins = e.transpose(out=psb[:, kc * 128:(kc + 1) * 128], in_=mrg[:, kc * 128:(kc + 1) * 128], identity=self.identb[:])
                        return ins
                    S.op("pe", trg, reads=[mrg.k, self.identb.k], writes=[psb.k])
                    S.op("act", lambda e: e.activation(out=mT2[:].rearrange("p k t -> p (k t)"), in_=psb[:], func=AF.Copy), reads=[psb.k], writes=[mT2.k])
                    for half in range(2):
                        pq = ps[4 + half]
                        self.proj(pq, 512, mT2, woutb, half * 512)
                        S.op("dve", lambda e, pq=pq, half=half, b=b: e.tensor_tensor(out=x1[:, b, half * 512:(half + 1) * 512], in0=xb[:, half * 512:(half + 1) * 512], in1=pq[:], op=ALU.add), reads=[xb.k, pq.k], writes=[x1.k])
                    xbv = TL(x1.t[:, b, :], "x")
                    xbv.k = x1.k
                    self.rms_hT(xbv, ss, xnb, hTb, gmlp, psb)
                    S.op("pool", lambda e, b=b: e.tensor_copy(out=h2T[:, :, b * 128:(b + 1) * 128], in_=hTb[:]), reads=[hTb.k], writes=[h2T.k])
                for f in range(32):
                    pq = ps[f % 2]; r = rl[f % 2]

                    def up(e, pq=pq, f=f):
                        for kc in range(8):
                            ins = e.matmul(pq[:, 0:G * 128], lhsT=wup[:, kc, f * 128:(f + 1) * 128], rhs=h2T[:, kc, :], start=(kc == 0), stop=(kc == 7))
                        return ins
                    S.op("pe", up, reads=[wup.k, h2T.k], writes=[pq.k])
                    S.op("act", lambda e, pq=pq, r=r: e.activation(out=r[:], in_=pq[:, 0:G * 128], func=AF.Relu), reads=[pq.k], writes=[r.k])
                    S.op("dve" if f % 2 else "pool", lambda e, r=r, f=f: e.tensor_tensor(out=hid[:, f, :], in0=r[:], in1=r[:], op=ALU.mult), reads=[r.k], writes=[hid.k])
                for b in range(G):
                    j = g0 + b
                    o = ot[b % 2]
                    for half in range(2):
                        pq = ps[2 + half]

                        def dn(e, pq=pq, b=b, half=half):
                            for f in range(32):
                                ins = e.matmul(pq[:], lhsT=hid[:, f, b * 128:(b + 1) * 128], rhs=wdn[:, f, half * 512:(half + 1) * 512], start=(f == 0), stop=(f == 31))
                            return ins
                        S.op("pe", dn, reads=[hid.k, wdn.k], writes=[pq.k])
                        S.op("dve", lambda e, pq=pq, o=o, b=b, half=half: e.tensor_tensor(out=o[:, half * 512:(half + 1) * 512], in0=x1[:, b, half * 512:(half + 1) * 512], in1=pq[:], op=ALU.add), reads=[x1.k, pq.k], writes=[o.k])
                    S.dma(lambda e, j=j, o=o: e.dma_start(out=self.o_out[j * 128:(j + 1) * 128, :], in_=o[:]), reads=[o.k], writes=[])
            S.barrier()
            S.emit()

    def phase1a(self):
        S = self.S
        P = self.P
        ps, psb = self.ps, self.psb
        with ExitStack() as es:
            sb = lambda n, sh, dt: self.sb(es, n, sh, dt)
            wb = sb("w1a", [128, 8, C1A], BF16)
            stg = [sb("stga%d" % i, [128, 1024], F32) for i in range(2)]
            gmix = sb("gmixa", [128, 8], F32); ggdn = sb("ggdn", [128, 128], F32)
            aexp = sb("aexp", [128, 8], F32); dtb = sb("dtb", [128, 8], F32)
            convw = sb("convw", [128, 24, 4], F32)
            uinc = sb("uinc", [128, 128], F32); onesf = sb("onesf", [128, 128], F32); onesb = sb("onesb", [128, 128], BF16)
            mskL4 = sb("mskL4", [128, 4, 128], BF16); mskU4 = sb("mskU4", [128, 4, 128], BF16)
            self.load(gmix, gmix[:], self.i_gmix); self.load(ggdn, ggdn[:], self.i_ggdn)
            self.load(aexp, aexp[:], self.i_alog); self.load(dtb, dtb[:], self.i_dtb)
            self.load(convw, convw[:].rearrange("p t j -> p (t j)"), self.i_convw)
            self.load(uinc, uinc[:], self.i_uinc)
            for q4 in range(4):
                self.load(mskL4, mskL4[:, q4, :], self.i_mskL); self.load(mskU4, mskU4[:, q4, :], self.i_mskU)
            S.op("pool", lambda e: e.memset(onesf[:], 1.0), writes=[onesf.k])
            S.op("pool", lambda e: e.memset(onesb[:], 1.0), writes=[onesb.k])
            S.op("act", lambda e: e.activation(out=aexp[:], in_=aexp[:], func=AF.Exp), reads=[aexp.k], writes=[aexp.k])
            self.load_weight_bf16(es, wb, self.i_w1a, C1A, stg)
            xb = sb("xba", [128, 1024], F32); ss = sb("ssa", [128, 1], F32); xnb = sb("xnba", [128, 1024], BF16); hT = sb("hTa", [128, 8, 128], BF16)
            U = sb("U", [128, 24, 131], F32)
            Yg = sb("Yg", [128, 8, 128], F32); Ct = sb("Ct", [128, 8, 128], F32)
            sq = sb("sqa", [128, 1024], BF16); rn = sb("rn", [128, 1024], F32)
            qT = sb("qT", [128, 8, 128], BF16); kT2 = [sb("kT%d" % i, [128, 8, 128], BF16) for i in range(2)]; vTb = sb("vTb", [128, 8, 128], BF16)
            ktok = sb("ktok", [128, 8, 128], F32)
            kbg2 = [sb("kbg%d" % i, [128, 8, 128], BF16) for i in range(2)]; kdec2 = [sb("kdec%d" % i, [128, 8, 128], BF16) for i in range(2)]; vb2 = [sb("vb%d" % i, [128, 8, 128], BF16) for i in range(2)]
            smm = [sb("sm_%d" % i, [128, 96], F32) for i in range(2)]
            sm2 = sb("sm2", [128, 16], F32); sm3 = sb("sm3", [128, 8], F32)
            Rall = sb("Rall", [128, 8, 128], F32)
            Dls2 = [sb("Dls%d" % i, [128, 8, 128], F32) for i in range(2)]; DT = sb("DTm", [128, 8, 128], F32); egr = sb("egr", [128, 8, 128], F32)
            qdT2 = [sb("qdT%d" % i, [128, 8, 128], BF16) for i in range(2)]; QKT2 = [sb("QKT%d" % i, [128, 8, 128], BF16) for i in range(2)]
            Am = [sb("Am%d" % i, [128, 8, 128], NDT) for i in range(2)]
            Bm = [sb("Bm%d" % i, [128, 8, 128], NDT) for i in range(2)]
            Pm = sb("Pm", [128, 8, 128], NDT); TTb = sb("TTb", [128, 8, 128], BF16)
            u = sb("u", [128, 8, 128], F32); wT = sb("wT", [128, 8, 128], BF16); vn = sb("vn", [128, 8, 128], BF16)
            S32 = sb("S32", [128, 8, 128], F32); Stmp = sb("Stmp", [128, 8, 128], F32); Sbf = sb("Sbf", [128, 8, 128], BF16)
            o = sb("o", [128, 8, 128], F32); osq = sb("osq", [128, 8, 128], F32)
            zg2 = [sb("zga%d" % i, [128, 1024], BF16) for i in range(2)]; mgt = sb("mgt", [128, 1024], BF16)
            S.op("pool", lambda e: e.memset(U[:], 0.0), writes=[U.k])
            S.op("pool", lambda e: e.memset(S32[:], 0.0), writes=[S32.k])
            S.op("pool", lambda e: e.memset(Sbf[:], 0.0), writes=[Sbf.k])
            f4 = lambda t, g: t[:, g * 4:(g + 1) * 4, :].rearrange("p h d -> p (h d)")

            def batch_mm(banks, fn_h):
                for g in range(2):
                    def f(e, g=g):
                        for hh in range(4):
                            h = g * 4 + hh
                            ins = fn_h(e, banks[g][:, hh * 128:(hh + 1) * 128], h)
                        return ins
                    yield g, f

            def gF(p):
                own = (p % 2 == 1)
                j = p // 2
                r0 = p * 128
                DB = p % 2
                kT = kT2[DB]; kbg = kbg2[DB]; kdec = kdec2[DB]; vb = vb2[DB]; sm = smm[DB]; Dls = Dls2[DB]
                qdT = qdT2[DB]; QKT = QKT2[DB]; zg = zg2[DB]
                self.load(xb, xb[:], self.i_x[r0:r0 + 128, :])
                if own:
                    S.dma(lambda e, j=j: e.dma_start(out=zg[:], in_=self.s_zg[j]), reads=[self.k_own[j]["zg"]], writes=[zg.k])
                self.rms_hT(xb, ss, xnb, hT, gmix, psb)
                yield
                for grp in range(6):
                    pq = ps[4 + grp % 2]

                    def fm(e, pq=pq, grp=grp):
                        for t4 in range(4):
                            ct = grp * 4 + t4
                            for kc in range(8):
                                ins = e.matmul(pq[:, t4 * 128:(t4 + 1) * 128], lhsT=wb[:, kc, ct * 128:(ct + 1) * 128], rhs=hT[:, kc, :], start=(kc == 0), stop=(kc == 7), skip_group_check=True)
                        return ins
                    S.op("pe", fm, reads=[wb.k, hT.k], writes=[pq.k])
                    yield
                    S.op("act", lambda e, pq=pq, grp=grp: e.activation(out=U[:, grp * 4:(grp + 1) * 4, 3:131], in_=pq[:].rearrange("p (t n) -> p t n", t=4), func=AF.Copy), reads=[pq.k], writes=[U.k])
                    yield
                self.proj(ps[6], 16, hT, wb, 3072)
                yield
                S.op("act", lambda e: e.activation(out=sm[:, 0:16], in_=ps[6][:, 0:16], func=AF.Copy), reads=[ps[6].k], writes=[sm.k])
                yield
                S.op("act", lambda e: e.activation(out=sm[:, 16:24], in_=sm[:, 8:16], func=AF.Sigmoid), reads=[sm.k], writes=[sm.k])
                yield
                S.op("dve", lambda e: e.tensor_tensor(out=sm[:, 24:32], in0=sm[:, 0:8], in1=dtb[:], op=ALU.add), reads=[sm.k, dtb.k], writes=[sm.k])
                yield
                S.op("act", lambda e: e.activation(out=sm[:, 32:40], in_=sm[:, 24:32], func=AF.Abs), reads=[sm.k], writes=[sm.k])
                yield
                S.op("act", lambda e: e.activation(out=sm[:, 32:40], in_=sm[:, 32:40], func=AF.Exp, scale=-1.0), reads=[sm.k], writes=[sm.k])
                yield
                S.op("dve", lambda e: e.tensor_scalar(out=sm[:, 32:40], in0=sm[:, 32:40], scalar1=1.0, scalar2=None, op0=ALU.add), reads=[sm.k], writes=[sm.k])
                yield
                S.op("act", lambda e: e.activation(out=sm[:, 32:40], in_=sm[:, 32:40], func=AF.Ln), reads=[sm.k], writes=[sm.k])
                yield
                S.op("dve", lambda e: e.scalar_tensor_tensor(out=sm[:, 40:48], in0=sm[:, 24:32], scalar=0.0, in1=sm[:, 32:40], op0=ALU.max, op1=ALU.add), reads=[sm.k], writes=[sm.k])
                yield
                S.op("dve", lambda e: e.scalar_tensor_tensor(out=sm[:, 48:56], in0=sm[:, 40:48], scalar=-1.0, in1=aexp[:], op0=ALU.mult, op1=ALU.mult), reads=[sm.k, aexp.k], writes=[sm.k])
                yield

                def cs(e):
                    e.matmul(ps[6][:, 16:24], lhsT=uinc[:], rhs=sm[:, 48:56], start=True, stop=True, skip_group_check=True)
                    return e.matmul(ps[6][:, 24:32], lhsT=onesf[:], rhs=sm[:, 48:56], start=True, stop=True, skip_group_check=True)
                S.op("pe", cs, reads=[uinc.k, onesf.k, sm.k], writes=[ps[6].k])
                yield
                S.op("act", lambda e: e.activation(out=sm[:, 56:72], in_=ps[6][:, 16:32], func=AF.Copy), reads=[ps[6].k], writes=[sm.k])
                yield
                S.op("act", lambda e: e.activation(out=sm[:, 72:80], in_=sm[:, 56:64], func=AF.Exp), reads=[sm.k], writes=[sm.k])
                yield
                S.op("dve", lambda e: e.tensor_tensor(out=sm[:, 80:88], in0=sm[:, 64:72], in1=sm[:, 56:64], op=ALU.subtract), reads=[sm.k], writes=[sm.k])
                yield
                S.op("act", lambda e: e.activation(out=sm[:, 80:96], in_=sm[:, 80:96] if False else sm[:, 80:88], func=AF.Exp) if False else e.activation(out=sm[:, 80:88], in_=sm[:, 80:88], func=AF.Exp), reads=[sm.k], writes=[sm.k])
                yield
                S.op("act", lambda e: e.activation(out=sm[:, 88:96], in_=sm[:, 64:72], func=AF.Exp), reads=[sm.k], writes=[sm.k])
                yield
                S.op("dve", lambda e: e.tensor_tensor(out=sm2[:, 0:8], in0=sm[:, 16:24], in1=sm[:, 72:80], op=ALU.mult), reads=[sm.k], writes=[sm2.k])
                yield
                for grp, dst in ((0, qT), (1, kT), (2, vTb)):
                    if grp == 0 and not own:
                        continue
                    Ug = lambda jj, grp=grp: U[:, grp * 8:(grp + 1) * 8, jj:jj + 128]
                    cw = lambda jj, grp=grp: bc(convw[:, grp * 8:(grp + 1) * 8, jj:jj + 1], [128, 8, 128])
                    S.op("dve", lambda e, Ug=Ug, cw=cw: e.tensor_tensor(out=Yg[:], in0=Ug(0), in1=cw(0), op=ALU.mult), reads=[U.k, convw.k], writes=[Yg.k])
                    yield
                    for jj in range(1, 4):
                        S.op("pool", lambda e, Ug=Ug, cw=cw, jj=jj: e.tensor_tensor(out=Ct[:], in0=Ug(jj), in1=cw(jj), op=ALU.mult), reads=[U.k, convw.k], writes=[Ct.k])
                        S.op("dve", lambda e: e.tensor_tensor(out=Yg[:], in0=Yg[:], in1=Ct[:], op=ALU.add), reads=[Yg.k, Ct.k], writes=[Yg.k])
                    Yf = Yg[:].rearrange("p h d -> p (h d)")
                    if grp == 2:
                        S.op("act", lambda e, Yf=Yf: e.activation(out=vTb[:].rearrange("p h d -> p (h d)"), in_=Yf, func=AF.Silu), reads=[Yg.k], writes=[vTb.k])
                        continue
                    S.op("act", lambda e, Yf=Yf: e.activation(out=Yf, in_=Yf, func=AF.Silu), reads=[Yg.k], writes=[Yg.k])
                    yield
                    S.op("pool", lambda e, Yf=Yf: e.tensor_tensor(out=sq[:], in0=Yf, in1=Yf, op=ALU.mult), reads=[Yg.k], writes=[sq.k])
                    yield
                    for g in range(2):
                        pq = ps[4 + g]
                        S.op("pe", lambda e, pq=pq, g=g: e.matmul(pq[:], lhsT=onesb[:], rhs=sq[:, g * 512:(g + 1) * 512], start=True, stop=True), reads=[onesb.k, sq.k], writes=[pq.k])
                        S.op("dve", lambda e, pq=pq, g=g: e.tensor_scalar(out=rn[:, g * 512:(g + 1) * 512], in0=pq[:], scalar1=EPS, scalar2=None, op0=ALU.add), reads=[pq.k], writes=[rn.k])
                    S.op("act", lambda e: e.activation(out=rn[:], in_=rn[:], func=AF.Ln), reads=[rn.k], writes=[rn.k])
                    yield
                    S.op("act", lambda e: e.activation(out=rn[:], in_=rn[:], func=AF.Exp, scale=-0.5), reads=[rn.k], writes=[rn.k])
                    yield
                    sc = float(128.0 ** -0.5) if grp == 0 else 1.0
                    S.op("dve", lambda e, Yf=Yf, dst=dst, sc=sc: e.scalar_tensor_tensor(out=dst[:].rearrange("p h d -> p (h d)"), in0=Yf, scalar=sc, in1=rn[:], op0=ALU.mult, op1=ALU.mult), reads=[Yg.k, rn.k], writes=[dst.k])
                    yield
                S.op("pool", lambda e: e.tensor_copy(out=U[:, :, 0:3], in_=U[:, :, 128:131]), reads=[U.k], writes=[U.k])
                yield
                def trk(e):
                    for h in range(8):
                        ins = e.transpose(out=psb[:, h * 128:(h + 1) * 128], in_=kT[:, h, :], identity=self.identb[:])
                    return ins
                S.op("pe", trk, reads=[kT.k, self.identb.k], writes=[psb.k])
                yield
                S.op("act", lambda e: e.activation(out=ktok[:].rearrange("p h d -> p (h d)"), in_=psb[:], func=AF.Copy), reads=[psb.k], writes=[ktok.k])
                yield
                S.op("dve", lambda e: e.tensor_tensor(out=kbg[:], in0=ktok[:], in1=bc(sm2[:, 0:8].unsqueeze(2), [128, 8, 128]), op=ALU.mult), reads=[ktok.k, sm2.k], writes=[kbg.k])
                yield
                S.op("pool", lambda e: e.tensor_tensor(out=kdec[:], in0=ktok[:], in1=bc(sm[:, 80:88].unsqueeze(2), [128, 8, 128]), op=ALU.mult), reads=[ktok.k, sm.k], writes=[kdec.k])
                yield

                def trv(e):
                    for h in range(8):
                        ins = e.transpose(out=psb[:, h * 128:(h + 1) * 128], in_=vTb[:, h, :], identity=self.identb[:])
                    return ins
                S.op("pe", trv, reads=[vTb.k, self.identb.k], writes=[psb.k])
                yield
                S.op("dve", lambda e: e.tensor_tensor(out=vb[:], in0=psb[:].rearrange("p (h d) -> p h d", h=8), in1=bc(sm[:, 16:24].unsqueeze(2), [128, 8, 128]), op=ALU.mult), reads=[psb.k, sm.k], writes=[vb.k])
                yield
                S.op("dve", lambda e: e.tensor_tensor(out=Rall[:], in0=bc(uinc[:].unsqueeze(1), [128, 8, 128]), in1=bc(sm[:, 48:56].unsqueeze(2), [128, 8, 128]), op=ALU.mult), reads=[uinc.k, sm.k], writes=[Rall.k])
                yield
                for g in range(2):
                    pq = ps[4 + g]

                    def grl(e, pq=pq, g=g):
                        e.matmul(pq[:], lhsT=onesf[:], rhs=f4(Rall, g), start=True, stop=False)
                        return e.matmul(pq[:], lhsT=self.identb[:], rhs=mskL4[:].rearrange("p h d -> p (h d)"), start=False, stop=True)
                    S.op("pe", grl, reads=[onesf.k, Rall.k, self.identb.k, mskL4.k], writes=[pq.k])
                    yield
                    for hh in range(4):
                        h = g * 4 + hh
                        S.op("act", lambda e, pq=pq, h=h, hh=hh: e.activation(out=Dls[:, h, :], in_=pq[:, hh * 128:(hh + 1) * 128], func=AF.Exp, scale=-1.0, bias=sm[:, 56 + h:57 + h]), reads=[pq.k, sm.k], writes=[Dls.k])
                if own:
                    for g in range(2):
                        pq = ps[4 + g]
                        S.op("pe", lambda e, pq=pq, g=g: e.matmul(pq[:], lhsT=onesf[:], rhs=f4(Rall, g), start=True, stop=False), reads=[onesf.k, Rall.k], writes=[pq.k])
                        S.op("act", lambda e, pq=pq, g=g: e.activation(out=f4(egr, g), in_=pq[:], func=AF.Exp), reads=[pq.k], writes=[egr.k])
                        S.op("pe", lambda e, pq=pq: e.matmul(pq[:], lhsT=self.identb[:], rhs=mskU4[:].rearrange("p h d -> p (h d)"), start=False, stop=True), reads=[self.identb.k, mskU4.k], writes=[pq.k])
                        S.op("pool", lambda e, g=g: e.tensor_scalar(out=sm2[:, 8:16], in0=sm[:, 56:64], scalar1=-1.0, scalar2=None, op0=ALU.mult), reads=[sm.k], writes=[sm2.k])
                        for hh in range(4):
                            h = g * 4 + hh
                            S.op("act", lambda e, pq=pq, h=h, hh=hh: e.activation(out=DT[:, h, :], in_=pq[:, hh * 128:(hh + 1) * 128], func=AF.Exp, scale=1.0, bias=sm2[:, 8 + h:9 + h]), reads=[pq.k, sm2.k], writes=[DT.k])
                    S.op("pool", lambda e: e.tensor_tensor(out=qdT[:], in0=qT[:], in1=egr[:], op=ALU.mult), reads=[qT.k, egr.k], writes=[qdT.k])
                    yield
                    for g, f in batch_mm((ps[4], ps[5]), lambda e, out, h: e.matmul(out, lhsT=kT[:, h, :], rhs=qT[:, h, :], start=True, stop=True, skip_group_check=True)):
                        S.op("pe", f, reads=[kT.k, qT.k], writes=[ps[4 + g].k])
                        S.op("dve", lambda e, g=g: e.tensor_tensor(out=f4(QKT, g), in0=ps[4 + g][:], in1=f4(DT, g), op=ALU.mult), reads=[ps[4 + g].k, DT.k], writes=[QKT.k])

                yield

            def gNR(p):
                own = (p % 2 == 1)
                j = p // 2
                r0 = p * 128
                DB = p % 2
                kT = kT2[DB]; kbg = kbg2[DB]; kdec = kdec2[DB]; vb = vb2[DB]; sm = smm[DB]; Dls = Dls2[DB]
                qdT = qdT2[DB]; QKT = QKT2[DB]; zg = zg2[DB]
                A0, B0 = Am[0], Bm[0]
                for g, f in batch_mm((ps[0], ps[1]), lambda e, out, h: e.matmul(out, lhsT=kT[:, h, :], rhs=kT[:, h, :], start=True, stop=True, skip_group_check=True)):
                    S.op("pe", f, reads=[kT.k], writes=[ps[g].k])
                    yield
                    for hh in range(4):
                        h = g * 4 + hh
                        S.op("dve", lambda e, g=g, h=h, hh=hh: e.scalar_tensor_tensor(out=A0[:, h, :], in0=ps[g][:, hh * 128:(hh + 1) * 128], scalar=sm[:, 16 + h:17 + h], in1=Dls[:, h, :], op0=ALU.mult, op1=ALU.mult), reads=[ps[g].k, sm.k, Dls.k], writes=[A0.k])
                for g, f in batch_mm((ps[2], ps[3]), lambda e, out, h: e.transpose(out=out, in_=RF(A0[:, h, :]), identity=self.identf[:])):
                    S.op("pe", f, reads=[A0.k, self.identf.k], writes=[ps[2 + g].k])
                    yield
                    S.op("act", lambda e, g=g: e.activation(out=f4(B0, g), in_=ps[2 + g][:], func=AF.Copy), reads=[ps[2 + g].k], writes=[B0.k])
                    yield
                S.op("pool", lambda e: e.tensor_tensor(out=Pm[:], in0=bc(self.identf[:].unsqueeze(1), [128, 8, 128]), in1=RF(B0[:]), op=ALU.subtract), reads=[self.identf.k, B0.k], writes=[Pm.k])
                yield
                cur = 0
                for lv in range(6):
                    Ac, Bc = Am[cur], Bm[cur]
                    An, Bn_ = Am[1 - cur], Bm[1 - cur]
                    last = (lv == 5)
                    for g, f in batch_mm((ps[0], ps[1]), lambda e, out, h, Ac=Ac, Bc=Bc: e.matmul(out, lhsT=NC_(Bc[:, h, :]), rhs=NC_(Ac[:, h, :]), start=True, stop=True, skip_group_check=True)):
                        S.op("pe", f, reads=[Ac.k, Bc.k], writes=[ps[g].k])
                        S.op("act", lambda e, g=g, An=An: e.activation(out=f4(An, g), in_=ps[g][:], func=AF.Copy), reads=[ps[g].k], writes=[An.k])
                    if not last:
                        for g, f in batch_mm((ps[2], ps[3]), lambda e, out, h, Ac=Ac, Bc=Bc: e.matmul(out, lhsT=NC_(Ac[:, h, :]), rhs=NC_(Bc[:, h, :]), start=True, stop=True, skip_group_check=True)):
                            S.op("pe", f, reads=[Ac.k, Bc.k], writes=[ps[2 + g].k])
                            S.op("dve", lambda e, g=g, Bn_=Bn_: e.tensor_copy(out=f4(Bn_, g), in_=ps[2 + g][:]), reads=[ps[2 + g].k], writes=[Bn_.k])
                    for g, f in batch_mm((ps[0], ps[1]), lambda e, out, h, An=An: e.matmul(out, lhsT=NC_(An[:, h, :]), rhs=NC_(Pm[:, h, :]), start=True, stop=True, skip_group_check=True)):
                        S.op("pe", f, reads=[An.k, Pm.k], writes=[ps[g].k])
                        S.op("dve", lambda e, g=g: e.tensor_tensor(out=f4(Pm, g), in0=RF(f4(Pm, g)), in1=ps[g][:], op=ALU.add), reads=[Pm.k, ps[g].k], writes=[Pm.k])
                    cur = 1 - cur
                S.op("pool", lambda e: e.tensor_copy(out=TTb[:], in_=RF(Pm[:])), reads=[Pm.k], writes=[TTb.k])
                yield
                for g, f in batch_mm((ps[0], ps[1]), lambda e, out, h: e.matmul(out, lhsT=TTb[:, h, :], rhs=vb[:, h, :], start=True, stop=True, skip_group_check=True)):
                    S.op("pe", f, reads=[TTb.k, vb.k], writes=[ps[g].k])
                    yield
                    S.op("act", lambda e, g=g: e.activation(out=f4(u, g), in_=ps[g][:], func=AF.Copy), reads=[ps[g].k], writes=[u.k])
                    yield
                for g, f in batch_mm((ps[2], ps[3]), lambda e, out, h: e.matmul(out, lhsT=kbg[:, h, :], rhs=TTb[:, h, :], start=True, stop=True, skip_group_check=True)):
                    S.op("pe", f, reads=[TTb.k, kbg.k], writes=[ps[2 + g].k])
                    yield
                    S.op("act", lambda e, g=g: e.activation(out=f4(wT, g), in_=ps[2 + g][:], func=AF.Copy), reads=[ps[2 + g].k], writes=[wT.k])
                    yield
                for g, f in batch_mm((ps[0], ps[1]), lambda e, out, h: e.matmul(out, lhsT=wT[:, h, :], rhs=Sbf[:, h, :], start=True, stop=True, skip_group_check=True)):
                    S.op("pe", f, reads=[wT.k, Sbf.k], writes=[ps[g].k])
                    yield
                    S.op("dve", lambda e, g=g: e.tensor_tensor(out=f4(vn, g), in0=f4(u, g), in1=ps[g][:], op=ALU.subtract), reads=[u.k, ps[g].k], writes=[vn.k])
                    yield
                if own:
                    for g in range(2):
                        def fo(e, g=g):
                            for hh in range(4):
                                h = g * 4 + hh
                                e.matmul(ps[g][:, hh * 128:(hh + 1) * 128], lhsT=qdT[:, h, :], rhs=Sbf[:, h, :], start=True, stop=False, skip_group_check=True)
                                ins = e.matmul(ps[g][:, hh * 128:(hh + 1) * 128], lhsT=QKT[:, h, :], rhs=vn[:, h, :], start=False, stop=True, skip_group_check=True)
                            return ins
                        S.op("pe", fo, reads=[qdT.k, Sbf.k, QKT.k, vn.k], writes=[ps[g].k])
                        S.op("act", lambda e, g=g: e.activation(out=f4(o, g), in_=ps[g][:], func=AF.Copy), reads=[ps[g].k], writes=[o.k])
                S.op("pool", lambda e: e.tensor_tensor(out=Stmp[:], in0=S32[:], in1=bc(sm[:, 88:96].unsqueeze(2), [128, 8, 128]), op=ALU.mult), reads=[S32.k, sm.k], writes=[Stmp.k])
                yield
                for g, f in batch_mm((ps[2], ps[3]), lambda e, out, h: e.matmul(out, lhsT=kdec[:, h, :], rhs=vn[:, h, :], start=True, stop=True, skip_group_check=True)):
                    S.op("pe", f, reads=[kdec.k, vn.k], writes=[ps[2 + g].k])
                    yield
                    S.op("dve", lambda e, g=g: e.tensor_tensor(out=f4(S32, g), in0=f4(Stmp, g), in1=ps[2 + g][:], op=ALU.add), reads=[Stmp.k, ps[2 + g].k], writes=[S32.k])
                    yield
                S.op("act", lambda e: e.activation(out=Sbf[:].rearrange("p h d -> p (h d)"), in_=S32[:].rearrange("p h d -> p (h d)"), func=AF.Copy), reads=[S32.k], writes=[Sbf.k])
                yield
                if own:
                    S.op("pool", lambda e: e.tensor_tensor(out=osq[:], in0=o[:], in1=o[:], op=ALU.mult), reads=[o.k], writes=[osq.k])
                    yield
                    S.op("dve", lambda e: e.tensor_reduce(out=sm3[:, 0:8], in_=osq[:], axis=AX.X, op=ALU.add), reads=[osq.k], writes=[sm2.k])
                    yield
                    S.op("dve", lambda e: e.tensor_scalar(out=sm3[:, 0:8], in0=sm3[:, 0:8], scalar1=1.0 / 128, scalar2=EPS, op0=ALU.mult, op1=ALU.add), reads=[sm2.k], writes=[sm2.k])
                    yield
                    S.op("act", lambda e: e.activation(out=sm3[:, 0:8], in_=sm3[:, 0:8], func=AF.Ln), reads=[sm2.k], writes=[sm2.k])
                    yield
                    S.op("act", lambda e: e.activation(out=sm3[:, 0:8], in_=sm3[:, 0:8], func=AF.Exp, scale=-0.5), reads=[sm2.k], writes=[sm2.k])
                    yield
                    S.op("dve", lambda e: e.tensor_tensor(out=o[:], in0=o[:], in1=bc(sm3[:, 0:8].unsqueeze(2), [128, 8, 128]), op=ALU.mult), reads=[o.k, sm2.k], writes=[o.k])
                    yield
                    S.op("pool", lambda e: e.tensor_tensor(out=o[:], in0=o[:], in1=bc(ggdn[:].unsqueeze(1), [128, 8, 128]), op=ALU.mult), reads=[o.k, ggdn.k], writes=[o.k])
                    yield
                    S.op("dve", lambda e: e.tensor_tensor(out=mgt[:], in0=o[:].rearrange("p h d -> p (h d)"), in1=zg[:], op=ALU.mult), reads=[o.k, zg.k], writes=[mgt.k])
                    yield
                    S.dma(lambda e, j=j: e.dma_start(out=self.s_mg[j], in_=mgt[:]), reads=[mgt.k], writes=[self.k_own[j]["mg"]])

                yield

            def run_pair(ga, gb, na, nb):
                nmax = max(na, nb, 1)
                da = db = 0
                for k in range(nmax):
                    ta = ((k + 1) * na) // nmax
                    tb = ((k + 1) * nb) // nmax
                    while da < ta:
                        next(ga, None); da += 1
                    while db < tb:
                        next(gb, None); db += 1
                for _ in ga:
                    pass
                for _ in gb:
                    pass

            def count(mk, p):
                S.dry = True
                n = sum(1 for _ in mk(p))
                S.dry = False
                return n
            nF = [count(gF, 0), count(gF, 1)]
            nN = [count(gNR, 0), count(gNR, 1)]
            for _ in gF(0):
                pass
            for p in range(P):
                if p + 1 < P:
                    run_pair(gNR(p), gF(p + 1), nN[p % 2], nF[(p + 1) % 2])
                else:
                    for _ in gNR(p):
                        pass
            S.barrier()
            S.emit()

    def phase1a_stub(self):
        S = self.S
        with ExitStack() as es:
            z = self.sb(es, "zmg", [128, 1024], BF16)
            S.op("pool", lambda e: e.memset(z[:], 0.0), writes=[z.k])
            for j in range(self.NO):
                S.dma(lambda e, j=j: e.dma_start(out=self.s_mg[j], in_=z[:]), reads=[z.k], writes=[self.k_own[j]["mg"]])
            S.barrier()
            S.emit()


def build(S, debug=False, gdn=True):
    B = Builder(S, debug)
    with ExitStack() as es:
        B.setup(es)
        B.phase1b()
        if gdn:
            B.phase1a()
        else:
            B.phase1a_stub()
        B.phase2()
        B.phase3()
    return B.nc


IN_SPLITS = (3072, 1024, 8, 8, 1024, 256, 256, 512, 64, 8, 1024, 1024)


def host_consts():
    bf = ml_dtypes.bfloat16
    i = np.arange(128)
    c = {}
    c["identf"] = np.eye(128, dtype=np.float32)
    c["identb"] = np.eye(128).astype(bf)
    c["uinc"] = (i[:, None] <= i[None, :]).astype(np.float32)
    c["causal"] = np.where(i[None, :] <= i[:, None], 0.0, NEG).astype(np.float32)
    c["mskU"] = np.where(i[None, :] >= i[:, None], 0.0, -30000.0).astype(bf)
    c["mskL"] = np.where(i[None, :] < i[:, None], 0.0, 30000.0).astype(bf)
    return c


def rope_tab(pos, rot):
    inv = (np.float32(500000.0) ** (-np.arange(0, rot, 2, dtype=np.float32) / np.float32(rot))).astype(np.float32)
    ang = pos.astype(np.float32)[:, None] * inv[None, :]
    return np.cos(ang).astype(np.float32), np.sin(ang).astype(np.float32)


def make_in_maps(S, x, norm_mix_g, w_in, conv_w, a_log, dt_bias, gdn_norm_g, q_norm_g, k_norm_g,
                 w_out, norm_mlp_g, w_mlp_up, w_mlp_down):
    Bn = x.shape[0]
    f = np.float32
    w_in = np.asarray(w_in[0], f)
    pts = np.cumsum((0,) + IN_SPLITS)
    seg = {n: w_in[:, pts[i]:pts[i + 1]] for i, n in enumerate(
        ("qkv", "z", "ga", "gb", "aq", "ak", "av", "iq", "ik", "iw", "gatea", "gateb"))}
    w1a = np.ascontiguousarray(np.concatenate([seg["qkv"], seg["ga"], seg["gb"]], 1))
    w1b = np.ascontiguousarray(np.concatenate([seg["ak"], seg["av"], seg["ik"], seg["iw"], seg["aq"], seg["iq"], seg["gateb"], seg["z"], seg["gatea"]], 1))
    col = lambda v: np.ascontiguousarray(np.asarray(v, f).reshape(8, 128).T)
    rep = lambda v: np.ascontiguousarray(np.broadcast_to(np.asarray(v, f)[None, :], (128, len(v))))
    common = host_consts()
    common.update(dict(
        w1a=w1a, w1b=w1b, wout=np.ascontiguousarray(w_out[0], f), wup=np.ascontiguousarray(w_mlp_up[0], f),
        wdn=np.ascontiguousarray(w_mlp_down[0], f),
        gmix=col(norm_mix_g[0]), gmlp=col(norm_mlp_g[0]), gq=rep(q_norm_g[0]), gk=rep(k_norm_g[0]),
        ggdn=rep(gdn_norm_g[0]), alog=rep(a_log[0]), dtb=rep(dt_bias[0]),
        convw=np.ascontiguousarray(np.asarray(conv_w[0], f).reshape(4, 24, 128).transpose(2, 1, 0).reshape(128, 96)),
    ))
    maps = []
    for b in range(Bn):
        for c in range(2):
            m = dict(common)
            xb = np.asarray(x[b], f)
            if c == 0:
                xs = np.concatenate([np.zeros((128, D), f), xb[:S - 128]], 0)
            else:
                xs = xb
            pos = np.maximum(np.arange(S) + (c - 1) * 128, 0)
            m["xseq"] = np.ascontiguousarray(xs)
            m["cosA"], m["sinA"] = rope_tab(pos, 32)
            m["cosI"], m["sinI"] = rope_tab(pos, 16)
            m["blk0"] = np.full((128, 128), NEG if c == 0 else 0.0, f)
            maps.append(m)
    return maps


def assemble(S, results, Bn):
    out = np.zeros((Bn, S, D), np.float32)
    ov = out.reshape(Bn, S // 256, 2, 128, D)
    for b in range(Bn):
        for c in range(2):
            r = results[b * 2 + c]["out"].reshape(S // 256, 128, D)
            ov[b, :, c] = r
    return out


_NC_CACHE = {}


def kernel(**inputs):
    x = np.asarray(inputs["x"])
    Bn, S, _ = x.shape
    if S not in _NC_CACHE:
        _NC_CACHE[S] = build(S)
    nc = _NC_CACHE[S]
    maps = make_in_maps(S, **{k: np.asarray(v) for k, v in inputs.items()})
    res = run_bass_kernel_spmd(nc, maps, core_ids=list(range(2 * Bn)))
    return assemble(S, res.results, Bn)
```

```python
from contextlib import ExitStack
import numpy as np
import ml_dtypes
import concourse.bass as bass
import concourse.mybir as mybir
from concourse.bass_utils import run_bass_kernel_spmd

F32 = mybir.dt.float32
BF16 = mybir.dt.bfloat16
AF = mybir.ActivationFunctionType
ALU = mybir.AluOpType
AX = mybir.AxisListType

D = 1024
EPS = 1e-6
NEG = -1.0e30
SAME_ENGINE_SYNC = True
IDX_F32R = False
NEU_F32R = True
NC_ = lambda ap: ap
NDT = mybir.dt.float32r if NEU_F32R else F32
RF = (lambda ap: ap.bitcast(F32)) if NEU_F32R else (lambda ap: ap)


class Tok:
    __slots__ = ("name", "w", "r")

    def __init__(self, name):
        self.name = name
        self.w = None
        self.r = {}


class Sched:
    ENG = ("pe", "act", "dve", "pool", "sp")

    def __init__(self, nc, es, n_dma_sems=24):
        self.nc = nc
        self.sems = {}
        for e in ("pe", "act", "dve", "pool"):
            self.sems[e] = es.enter_context(nc.semaphore("s_" + e))
        self.ndma = n_dma_sems
        for k in range(n_dma_sems):
            self.sems[("d", k)] = es.enter_context(nc.semaphore("s_d%d" % k))
        self.cnt = {e: 0 for e in ("pe", "act", "dve", "pool")}
        self.dtot = [0] * n_dma_sems
        self.rr = 0
        self.waited = {e: {} for e in self.ENG}
        self.ops = {e: [] for e in self.ENG}
        self.nops = 0
        self.dry = False

    def _deps(self, eng, reads, writes):
        deps = {}

        def add(d):
            if d is None:
                return
            sid, val = d
            if sid == eng and (eng == "pe" or not SAME_ENGINE_SYNC):
                return
            if deps.get(sid, 0) < val:
                deps[sid] = val

        for t in reads:
            add(t.w)
        for t in writes:
            add(t.w)
            for sid, val in t.r.items():
                add((sid, val))
        out = []
        wd = self.waited[eng]
        for sid, val in deps.items():
            if wd.get(sid, 0) < val:
                wd[sid] = val
                out.append((sid, val))
        return out

    def _mark(self, me, reads, writes):
        for t in reads:
            if t.r.get(me[0], 0) < me[1]:
                t.r[me[0]] = me[1]
        for t in writes:
            t.w = me
            t.r = {}

    def op(self, eng, fn, reads=(), writes=()):
        if self.dry:
            return
        waits = self._deps(eng, reads, writes)
        self.cnt[eng] += 1
        me = (eng, self.cnt[eng])
        self._mark(me, reads, writes)
        self.ops[eng].append((waits, fn, eng, 1))
        self.nops += 1

    def dma(self, fn, reads=(), writes=()):
        if self.dry:
            return
        k = self.rr
        self.rr = (self.rr + 1) % self.ndma
        sid = ("d", k)
        waits = self._deps("sp", reads, writes)
        wd = self.waited["sp"]
        if wd.get(sid, 0) < self.dtot[k]:
            wd[sid] = self.dtot[k]
            waits.append((sid, self.dtot[k]))
        self.dtot[k] += 16
        me = (sid, self.dtot[k])
        self._mark(me, reads, writes)
        self.ops["sp"].append((waits, fn, sid, 16))
        self.nops += 1

    def barrier(self):
        for e in self.ENG:
            waits = []
            wd = self.waited[e]
            for c in ("pe", "act", "dve", "pool"):
                if c != e and wd.get(c, 0) < self.cnt[c]:
                    wd[c] = self.cnt[c]
                    waits.append((c, self.cnt[c]))
            for k in range(self.ndma):
                sid = ("d", k)
                if wd.get(sid, 0) < self.dtot[k]:
                    wd[sid] = self.dtot[k]
                    waits.append((sid, self.dtot[k]))
            if waits:
                self.ops[e].append((waits, None, None, 0))

    def emit(self):
        nc = self.nc
        ops = self.ops
        self.ops = {e: [] for e in self.ENG}
        sems = self.sems

        def run(engh, lst):
            for waits, fn, sid, inc in lst:
                for s, v in waits:
                    engh.wait_ge(sems[s], v)
                if fn is not None:
                    fn(engh).then_inc(sems[sid], inc)

        with nc.Block() as block:
            @block.tensor
            def _(e):
                run(e, ops["pe"])

            @block.scalar
            def _(e):
                run(e, ops["act"])

            @block.vector
            def _(e):
                run(e, ops["dve"])

            @block.gpsimd
            def _(e):
                run(e, ops["pool"])

            @block.sync
            def _(e):
                run(e, ops["sp"])


class Ctx:
    pass


def bc(ap, shape):
    return ap.to_broadcast(list(shape))


class TL:
    def __init__(self, t, name):
        self.t = t
        self.k = Tok(name)

    def __getitem__(self, key):
        return self.t[key]


C1B = 5192
C1A = 3088


class Builder:
    def __init__(self, S, debug=False):
        self.Sq = S
        self.P = S // 128
        self.NO = self.P // 2
        self.debug = debug
        self.nc = bass.Bass("TRN2", target_bir_lowering=False)

    def dram_in(self, name, shape, dt=F32):
        return self.nc.dram_tensor(name, list(shape), dt, kind="ExternalInput").ap()

    def dram_scr(self, name, shape, dt):
        kind = "ExternalOutput" if self.debug else "Internal"
        return self.nc.dram_tensor(name, list(shape), dt, kind=kind).ap()

    def sb(self, es, name, shape, dt):
        return TL(es.enter_context(self.nc.sbuf_tensor("t_" + name, list(shape), dt)), name)

    def load(self, dst, dst_ap, src_ap):
        self.S.dma(lambda e: e.dma_start(out=dst_ap, in_=src_ap), writes=[dst.k])

    def rsqrt_small(self, v, n):
        S = self.S
        S.op("act", lambda e: e.activation(out=v[:, 0:n], in_=v[:, 0:n], func=AF.Ln), reads=[v.k], writes=[v.k])
        S.op("act", lambda e: e.activation(out=v[:, 0:n], in_=v[:, 0:n], func=AF.Exp, scale=-0.5), reads=[v.k], writes=[v.k])

    def load_weight_bf16(self, es, wb, w_ap, ncols, stg):
        S = self.S
        nk = w_ap.shape[0] // 128
        CH = stg[0].t.shape[1]
        i = 0
        for k in range(nk):
            for c0 in range(0, ncols, CH):
                cw = min(CH, ncols - c0)
                st = stg[i % len(stg)]
                S.dma(lambda e, st=st, k=k, c0=c0, cw=cw: e.dma_start(out=st[:, 0:cw], in_=w_ap[k * 128:(k + 1) * 128, c0:c0 + cw]), writes=[st.k])
                eng = ("pool", "dve", "act")[i % 3] if False else "pool"
                S.op(eng, lambda e, st=st, k=k, c0=c0, cw=cw: e.tensor_copy(out=wb[:, k, c0:c0 + cw], in_=st[:, 0:cw]), reads=[st.k], writes=[wb.k])
                i += 1

    def rms_hT(self, xb, ss, xnb, hT, gcol, psb):
        S = self.S
        junk = self.junk
        idb = self.identb
        S.op("act", lambda e: e.activation(out=junk[:, 0:1024], in_=xb[:], func=AF.Square, accum_out=ss[:, 0:1]), reads=[xb.k], writes=[junk.k, ss.k])
        S.op("dve", lambda e: e.tensor_scalar(out=ss[:, 0:1], in0=ss[:, 0:1], scalar1=1.0 / 1024, scalar2=EPS, op0=ALU.mult, op1=ALU.add), reads=[ss.k], writes=[ss.k])
        self.rsqrt_small(ss, 1)
        S.op("dve", lambda e: e.tensor_scalar(out=xnb[:], in0=xb[:], scalar1=ss[:, 0:1], scalar2=None, op0=ALU.mult), reads=[ss.k, xb.k], writes=[xnb.k])

        def tr(e):
            for kc in range(8):
                i = e.transpose(out=psb[:, kc * 128:(kc + 1) * 128], in_=xnb[:, kc * 128:(kc + 1) * 128], identity=idb[:])
            return i
        S.op("pe", tr, reads=[xnb.k, idb.k], writes=[psb.k])
        S.op("dve", lambda e: e.tensor_tensor(out=hT[:], in0=psb[:].rearrange("p (k t) -> p k t", k=8), in1=bc(gcol[:].unsqueeze(2), [128, 8, 128]), op=ALU.mult), reads=[psb.k, gcol.k], writes=[hT.k])

    def proj(self, ps, ncol, hT, wb, c0, pcol=0, start=True):
        def f(e):
            for kc in range(8):
                i = e.matmul(ps[:, pcol:pcol + ncol], lhsT=hT[:, kc, :], rhs=wb[:, kc, c0:c0 + ncol], start=(kc == 0), stop=(kc == 7))
            return i
        self.S.op("pe", f, reads=[hT.k, wb.k], writes=[ps.k])

    def setup(self, es):
        nc = self.nc
        S_, P, NO = self.Sq, self.P, self.NO
        self.S = Sched(nc, es)
        d = self.dram_in
        self.i_x = d("xseq", [S_, D])
        self.i_w1a = d("w1a", [D, C1A])
        self.i_w1b = d("w1b", [D, C1B])
        self.i_wout = d("wout", [D, D])
        self.i_wup = d("wup", [D, 4 * D])
        self.i_wdn = d("wdn", [4 * D, D])
        self.i_cosA = d("cosA", [S_, 16]); self.i_sinA = d("sinA", [S_, 16])
        self.i_cosI = d("cosI", [S_, 8]); self.i_sinI = d("sinI", [S_, 8])
        self.i_blk0 = d("blk0", [128, 128])
        self.i_identf = d("identf", [128, 128]); self.i_identb = d("identb", [128, 128], BF16)
        self.i_uinc = d("uinc", [128, 128]); self.i_causal = d("causal", [128, 128])
        self.i_mskU = d("mskU", [128, 128], BF16); self.i_mskL = d("mskL", [128, 128], BF16)
        self.i_gmix = d("gmix", [128, 8]); self.i_gmlp = d("gmlp", [128, 8])
        self.i_gq = d("gq", [128, 128]); self.i_gk = d("gk", [128, 128]); self.i_ggdn = d("ggdn", [128, 128])
        self.i_alog = d("alog", [128, 8]); self.i_dtb = d("dtb", [128, 8])
        self.i_convw = d("convw", [128, 24 * 4])
        self.o_out = nc.dram_tensor("out", [NO * 128, D], F32, kind="ExternalOutput").ap()
        s = self.dram_scr
        self.s_akT = s("s_akT", [128, 2 * S_], BF16)
        self.s_v = s("s_v", [S_, 258], BF16)
        self.s_ikT = s("s_ikT", [128, S_], F32)
        self.s_aqT = s("s_aqT", [NO, 128, 1024], BF16)
        self.s_iqT = s("s_iqT", [NO, 128, 512], F32)
        self.s_iw = s("s_iw", [NO, 128, 8], F32)
        self.s_sgb = s("s_sgb", [NO, 128, 1024], BF16)
        self.s_mg = s("s_mg", [NO, 128, 1024], BF16)
        self.s_zg = s("s_zg", [NO, 128, 1024], BF16)
        self.s_mrg = s("s_mrg", [NO, 128, 1024], BF16)
        self.k_akT = Tok("akT_d"); self.k_v = Tok("v_d"); self.k_ikT = Tok("ikT_d")
        self.k_own = [{n: Tok(n + str(j)) for n in ("aqT", "iqT", "iw", "sgb", "mg", "x1", "zg")} for j in range(NO)]
        self.ps = [TL(es.enter_context(nc.psum_tensor("ps%d" % i, [128, 512], F32)), "ps%d" % i) for i in range(7)]
        self.psb = TL(es.enter_context(nc.psum_tensor("psb", [128, 1024], BF16)), "psb")
        sb = lambda n, sh, dt: self.sb(es, n, sh, dt)
        self.identf = sb("identf", [128, 128], F32); self.identb = sb("identb", [128, 128], BF16)
        self.junk = sb("junk", [128, 1024], F32)
        for t, src in ((self.identf, self.i_identf), (self.identb, self.i_identb)):
            self.load(t, t[:], src)

    def norm_rope(self, H, src_ap, src_tok, gain, extra, out, ocol, cs, sn, tmp):
        S = self.S
        tf, sq, ssh, ra, rb = tmp
        n = H * 128
        S.op("act", lambda e: e.activation(out=tf[:, 0:n], in_=src_ap, func=AF.Copy), reads=[src_tok], writes=[tf.k])
        S.op("dve", lambda e: e.tensor_tensor(out=sq[:, 0:n], in0=tf[:, 0:n], in1=tf[:, 0:n], op=ALU.mult), reads=[tf.k], writes=[sq.k])
        S.op("dve", lambda e: e.tensor_reduce(out=ssh[:, 0:H], in_=sq[:, 0:n].rearrange("p (h d) -> p h d", h=H), axis=AX.X, op=ALU.add), reads=[sq.k], writes=[ssh.k])
        S.op("dve", lambda e: e.tensor_scalar(out=ssh[:, 0:H], in0=ssh[:, 0:H], scalar1=1.0 / 128, scalar2=EPS, op0=ALU.mult, op1=ALU.add), reads=[ssh.k], writes=[ssh.k])
        self.rsqrt_small(ssh, H)
        if extra != 1.0:
            S.op("dve", lambda e: e.tensor_scalar(out=ssh[:, 0:H], in0=ssh[:, 0:H], scalar1=extra, scalar2=None, op0=ALU.mult), reads=[ssh.k], writes=[ssh.k])
        tf3 = tf[:, 0:n].rearrange("p (h d) -> p h d", h=H)
        S.op("dve", lambda e: e.tensor_tensor(out=tf3, in0=tf3, in1=bc(ssh[:, 0:H].unsqueeze(2), [128, H, 128]), op=ALU.mult), reads=[tf.k, ssh.k], writes=[tf.k])
        S.op("pool", lambda e: e.tensor_tensor(out=tf3, in0=tf3, in1=bc(gain[:].unsqueeze(1), [128, H, 128]), op=ALU.mult), reads=[tf.k, gain.k], writes=[tf.k])
        o3 = out[:, ocol:ocol + n].rearrange("p (h d) -> p h d", h=H)
        S.op("act", lambda e: e.activation(out=out[:, ocol:ocol + n], in_=tf[:, 0:n], func=AF.Copy), reads=[tf.k], writes=[out.k])
        self.rope(tf3, tf.k, o3, out.k, H, 16, cs, sn, ra, rb)

    def rope(self, t3, ttok, o3, otok, H, hf, cs, sn, ra, rb):
        S = self.S
        x1 = t3[:, :, 0:hf]; x2 = t3[:, :, hf:2 * hf]
        c = bc(cs[:].unsqueeze(1), [128, H, hf]); s_ = bc(sn[:].unsqueeze(1), [128, H, hf])
        a3 = ra[:, 0:H * hf].rearrange("p (h d) -> p h d", h=H)
        b3 = rb[:, 0:H * hf].rearrange("p (h d) -> p h d", h=H)
        S.op("pool", lambda e: e.tensor_tensor(out=a3, in0=x1, in1=c, op=ALU.mult), reads=[ttok, cs.k], writes=[ra.k])
        S.op("pool", lambda e: e.tensor_tensor(out=b3, in0=x2, in1=s_, op=ALU.mult), reads=[ttok, sn.k], writes=[rb.k])
        S.op("dve", lambda e: e.tensor_tensor(out=o3[:, :, 0:hf], in0=a3, in1=b3, op=ALU.subtract), reads=[ra.k, rb.k], writes=[otok])
        S.op("pool", lambda e: e.tensor_tensor(out=a3, in0=x2, in1=c, op=ALU.mult), reads=[ttok, cs.k], writes=[ra.k])
        S.op("pool", lambda e: e.tensor_tensor(out=b3, in0=x1, in1=s_, op=ALU.mult), reads=[ttok, sn.k], writes=[rb.k])
        S.op("dve", lambda e: e.tensor_tensor(out=o3[:, :, hf:2 * hf], in0=a3, in1=b3, op=ALU.add), reads=[ra.k, rb.k], writes=[otok])

    def phase1b(self):
        S = self.S
        P = self.P
        ps, psb = self.ps, self.psb
        with ExitStack() as es:
            sb = lambda n, sh, dt: self.sb(es, n, sh, dt)
            wb = sb("w1b", [128, 8, C1B], BF16)
            stg = [sb("stg%d" % i, [128, 1024], F32) for i in range(2)]
            gmix = sb("gmix", [128, 8], F32); gq = sb("gq", [128, 128], F32); gk = sb("gk", [128, 128], F32)
            self.load(gmix, gmix[:], self.i_gmix); self.load(gq, gq[:], self.i_gq); self.load(gk, gk[:], self.i_gk)
            self.load_weight_bf16(es, wb, self.i_w1b, C1B, stg)
            xb = [sb("xb%d" % i, [128, 1024], F32) for i in range(2)]
            ss = sb("ss", [128, 1], F32); xnb = sb("xnb", [128, 1024], BF16); hT = sb("hT", [128, 8, 128], BF16)
            cA = sb("cA", [128, 16], F32); sA = sb("sA", [128, 16], F32); cI = sb("cI", [128, 8], F32); sI = sb("sI", [128, 8], F32)
            tmp = (sb("tf", [128, 512], F32), sb("sq", [128, 512], F32), sb("ssh", [128, 8], F32), sb("ra", [128, 64], F32), sb("rb", [128, 64], F32))
            ra, rb = tmp[3], tmp[4]
            akb = sb("akb", [128, 256], BF16); akT = sb("akTb", [128, 256], BF16)
            vaug = sb("vaug", [128, 258], BF16)
            ikf = sb("ikf", [128, 72], F32); ik2 = sb("ik2", [128, 128], F32); ikT = sb("ikTb", [128, 128], F32)
            iw = sb("iwb", [128, 8], F32)
            aqb = sb("aqb", [128, 1024], BF16); aqT = sb("aqTb", [128, 1024], BF16)
            iqf = sb("iqf", [128, 512], F32); iqo = sb("iqo", [128, 512], F32); iqT = sb("iqTb", [128, 512], F32)
            sgb = sb("sgbb", [128, 1024], BF16)
            zs = sb("zs", [128, 1024], F32); sga = sb("sga", [128, 1024], F32); zg = sb("zgb", [128, 1024], BF16)
            S.op("pool", lambda e: e.memset(vaug[:], 1.0), writes=[vaug.k])
            for p in range(P):
                own = (p % 2 == 1)
                j = p // 2
                x = xb[p % 2]
                r0 = p * 128
                self.load(x, x[:], self.i_x[r0:r0 + 128, :])
                self.load(cA, cA[:], self.i_cosA[r0:r0 + 128, :]); self.load(sA, sA[:], self.i_sinA[r0:r0 + 128, :])
                self.load(cI, cI[:], self.i_cosI[r0:r0 + 128, :]); self.load(sI, sI[:], self.i_sinI[r0:r0 + 128, :])
                self.rms_hT(x, ss, xnb, hT, gmix, psb)
                self.proj(ps[0], 512, hT, wb, 0)
                self.norm_rope(2, ps[0][:, 0:256], ps[0].k, gk, 1.0, akb, 0, cA, sA, tmp)
                S.op("act", lambda e: e.activation(out=vaug[:].rearrange("p (c d) -> p c d", c=2)[:, :, 0:128], in_=ps[0][:, 256:512].rearrange("p (c d) -> p c d", c=2), func=AF.Copy), reads=[ps[0].k], writes=[vaug.k])
                S.dma(lambda e, r0=r0: e.dma_start(out=self.s_v[r0:r0 + 128, :], in_=vaug[:]), reads=[vaug.k], writes=[self.k_v])

                def trk(e):
                    for c in range(2):
                        i = e.transpose(out=psb[:, c * 128:(c + 1) * 128], in_=akb[:, c * 128:(c + 1) * 128], identity=self.identb[:])
                    return i
                S.op("pe", trk, reads=[akb.k, self.identb.k], writes=[psb.k])
                S.op("act", lambda e: e.activation(out=akT[:], in_=psb[:, 0:256], func=AF.Copy), reads=[psb.k], writes=[akT.k])
                S.dma(lambda e, r0=r0: e.dma_start(out=self.s_akT.rearrange("h (c s) -> h c s", c=2)[:, :, r0:r0 + 128], in_=akT[:].rearrange("h (c s) -> h c s", c=2)), reads=[akT.k], writes=[self.k_akT])
                self.proj(ps[1], 72, hT, wb, 512)
                S.op("act", lambda e: e.activation(out=ikf[:], in_=ps[1][:, 0:72], func=AF.Copy), reads=[ps[1].k], writes=[ikf.k])
                S.op("dve", lambda e: e.tensor_copy(out=ik2[:, 0:64], in_=ikf[:, 0:64]), reads=[ikf.k], writes=[ik2.k])
                self.rope(ikf[:, 0:64].rearrange("p (h d) -> p h d", h=1), ikf.k, ik2[:, 0:64].rearrange("p (h d) -> p h d", h=1), ik2.k, 1, 8, cI, sI, ra, rb)
                S.op("dve", lambda e: e.tensor_copy(out=ik2[:, 64:128], in_=ik2[:, 0:64]), reads=[ik2.k], writes=[ik2.k])
                S.op("pe", lambda e: e.transpose(out=ps[2][:, 0:128], in_=ik2[:], identity=self.identf[:]), reads=[ik2.k, self.identf.k], writes=[ps[2].k])
                S.op("act", lambda e: e.activation(out=ikT[:], in_=ps[2][:, 0:128], func=AF.Copy), reads=[ps[2].k], writes=[ikT.k])
                S.dma(lambda e, r0=r0: e.dma_start(out=self.s_ikT[:, r0:r0 + 128], in_=ikT[:]), reads=[ikT.k], writes=[self.k_ikT])
                if not own:
                    continue
                ko = self.k_own[j]
                S.op("dve", lambda e: e.tensor_scalar(out=iw[:], in0=ikf[:, 64:72], scalar1=float(512.0 ** -0.5), scalar2=None, op0=ALU.mult), reads=[ikf.k], writes=[iw.k])
                S.dma(lambda e, j=j: e.dma_start(out=self.s_iw[j], in_=iw[:]), reads=[iw.k], writes=[ko["iw"]])
                for half in range(2):
                    pq = ps[3 + half]
                    self.proj(pq, 512, hT, wb, 584 + half * 512)
                    self.norm_rope(4, pq[:, 0:512], pq.k, gq, float(128.0 ** -0.5), aqb, half * 512, cA, sA, tmp)

                def trq(e):
                    for h in range(8):
                        i = e.transpose(out=psb[:, h * 128:(h + 1) * 128], in_=aqb[:, h * 128:(h + 1) * 128], identity=self.identb[:])
                    return i
                S.op("pe", trq, reads=[aqb.k, self.identb.k], writes=[psb.k])
                S.op("act", lambda e: e.activation(out=aqT[:], in_=psb[:], func=AF.Copy), reads=[psb.k], writes=[aqT.k])
                S.dma(lambda e, j=j: e.dma_start(out=self.s_aqT[j], in_=aqT[:]), reads=[aqT.k], writes=[ko["aqT"]])
                self.proj(ps[5], 512, hT, wb, 1608)
                S.op("act", lambda e: e.activation(out=iqf[:], in_=ps[5][:], func=AF.Copy), reads=[ps[5].k], writes=[iqf.k])
                S.op("dve", lambda e: e.tensor_copy(out=iqo[:], in_=iqf[:]), reads=[iqf.k], writes=[iqo.k])
                self.rope(iqf[:].rearrange("p (h d) -> p h d", h=8), iqf.k, iqo[:].rearrange("p (h d) -> p h d", h=8), iqo.k, 8, 8, cI, sI, ra, rb)

                def tri(e):
                    for g in range(4):
                        i = e.transpose(out=ps[6][:, g * 128:(g + 1) * 128], in_=iqo[:, g * 128:(g + 1) * 128], identity=self.identf[:])
                    return i
                S.op("pe", tri, reads=[iqo.k, self.identf.k], writes=[ps[6].k])
                S.op("act", lambda e: e.activation(out=iqT[:], in_=ps[6][:], func=AF.Copy), reads=[ps[6].k], writes=[iqT.k])
                S.dma(lambda e, j=j: e.dma_start(out=self.s_iqT[j], in_=iqT[:]), reads=[iqT.k], writes=[ko["iqT"]])
                for half in range(2):
                    pq = ps[half]
                    self.proj(pq, 512, hT, wb, 2120 + half * 512)
                    S.op("act", lambda e, pq=pq, half=half: e.activation(out=sgb[:, half * 512:(half + 1) * 512], in_=pq[:], func=AF.Sigmoid), reads=[pq.k], writes=[sgb.k])
                S.dma(lambda e, j=j: e.dma_start(out=self.s_sgb[j], in_=sgb[:]), reads=[sgb.k], writes=[ko["sgb"]])
                for half in range(2):
                    pq = ps[2 + half]; pg = ps[4 + half]
                    self.proj(pq, 512, hT, wb, 3144 + half * 512)
                    self.proj(pg, 512, hT, wb, 4168 + half * 512)
                    S.op("act", lambda e, pq=pq, half=half: e.activation(out=zs[:, half * 512:(half + 1) * 512], in_=pq[:], func=AF.Silu), reads=[pq.k], writes=[zs.k])
                    S.op("act", lambda e, pg=pg, half=half: e.activation(out=sga[:, half * 512:(half + 1) * 512], in_=pg[:], func=AF.Sigmoid), reads=[pg.k], writes=[sga.k])
                S.op("pool", lambda e: e.tensor_tensor(out=zg[:], in0=zs[:], in1=sga[:], op=ALU.mult), reads=[zs.k, sga.k], writes=[zg.k])
                S.dma(lambda e, j=j: e.dma_start(out=self.s_zg[j], in_=zg[:]), reads=[zg.k], writes=[ko["zg"]])
            S.barrier()
            S.emit()

    def phase2(self, NBIS=24):
        S = self.S
        P, NO, S_ = self.P, self.NO, self.Sq
        ps, psb = self.ps, self.psb
        KSEL = float(min(256, S_ // 4)) - 0.5
        with ExitStack() as es:
            sb = lambda n, sh, dt: self.sb(es, n, sh, dt)
            akT = sb("akT", [128, 2, S_], BF16)
            vall = sb("vall", [128, P, 258], BF16)
            H2 = S_ // 2
            assert H2 % 512 == 0
            ikT = sb("ikT", [128, H2], F32)
            scores = [sb("score%d" % i, [128, S_], F32) for i in range(2)]
            msk = sb("msk", [128, S_], BF16)
            jk = sb("jk", [128, S_], mybir.dt.uint8)
            gq = sb("gq2", [128, 128], F32); gk = sb("gk2", [128, 128], F32)
            negM = sb("negM", [128, 2], F32)
            causal = sb("causal", [128, 128], F32); blk0 = sb("blk0", [128, 128], F32)
            self.load(gq, gq[:], self.i_gq); self.load(gk, gk[:], self.i_gk)
            self.load(causal, causal[:], self.i_causal); self.load(blk0, blk0[:], self.i_blk0)
            S.dma(lambda e: e.dma_start(out=akT[:], in_=self.s_akT.rearrange("h (c s) -> h c s", c=2)), reads=[self.k_akT], writes=[akT.k])
            S.dma(lambda e: e.dma_start(out=vall[:], in_=self.s_v.rearrange("(p t) n -> t p n", t=128)), reads=[self.k_v], writes=[vall.k])
            S.dma(lambda e: e.dma_start(out=ikT[0:64, :], in_=self.s_ikT[0:64, 0:H2]), reads=[self.k_ikT], writes=[ikT.k])
            S.dma(lambda e: e.dma_start(out=ikT[64:128, :], in_=self.s_ikT[64:128, H2:S_]), reads=[self.k_ikT], writes=[ikT.k])
            S.op("dve", lambda e: e.tensor_reduce(out=negM[:, 0:1], in_=gq[:], axis=AX.X, op=ALU.max, apply_absolute_value=True), reads=[gq.k], writes=[negM.k])
            S.op("dve", lambda e: e.tensor_reduce(out=negM[:, 1:2], in_=gk[:], axis=AX.X, op=ALU.max, apply_absolute_value=True), reads=[gk.k], writes=[negM.k])
            S.op("dve", lambda e: e.scalar_tensor_tensor(out=negM[:, 0:1], in0=negM[:, 0:1], scalar=float(-(128.0 ** 0.5)), in1=negM[:, 1:2], op0=ALU.mult, op1=ALU.mult), reads=[negM.k], writes=[negM.k])
            aqT = sb("aqT", [128, 1024], BF16); iqT = sb("iqT", [128, 512], F32); iw = sb("iw", [128, 8], F32)
            iqT2 = sb("iqT2", [128, 512], F32)
            sgb = sb("sgb", [128, 1024], BF16); mg = sb("mg", [128, 1024], BF16)
            rl = [sb("rl%d" % i, [128, 512], F32) for i in range(2)]
            pT = [sb("pT%d" % i, [128, 512], BF16) for i in range(2)]
            mT = [sb("mT%d" % i, [128, 1024], BF16) for i in range(2)]
            sts = [sb("bst%d" % i, [128, 8], F32) for i in range(3)]
            rec = sb("rec", [128, 8], F32)
            oatt = self.junk
            mrg = sb("mrg", [128, 1024], BF16)
            accb = [ps[4], ps[5], ps[6]]
            hb = [(0, 0), (0, 1), (0, 2), (1, 0), (1, 1), (1, 2), (2, 0), (2, 1)]

            sctok = [[Tok("sc%d_%d" % (i, g)) for g in range(S_ // 512)] for i in range(2)]
            IDXC = (lambda ap: ap.bitcast(mybir.dt.float32r)) if IDX_F32R else (lambda ap: ap)

            def gI(j):
                p = 2 * j + 1
                nkeys = (p + 1) * 128
                ko = self.k_own[j]
                score = scores[j % 2]; st = sts[j % 3]
                S.dma(lambda e: e.dma_start(out=iqT[:], in_=self.s_iqT[j]), reads=[ko["iqT"]], writes=[iqT.k])
                S.dma(lambda e: e.dma_start(out=iqT2[0:64, :], in_=self.s_iqT[j][64:128, :]), reads=[ko["iqT"]], writes=[iqT2.k])
                S.dma(lambda e: e.dma_start(out=iqT2[64:128, :], in_=self.s_iqT[j][0:64, :]), reads=[ko["iqT"]], writes=[iqT2.k])
                S.dma(lambda e: e.dma_start(out=iw[:], in_=self.s_iw[j]), reads=[ko["iw"]], writes=[iw.k])
                it = 0
                nkg = (nkeys + 511) // 512
                for h in range(8):
                    for kg in range(nkg):
                        k0 = kg * 512
                        w = min(512, nkeys - k0)
                        sk = score.k
                        pb = ps[it % 2]; r = rl[it % 2]; it += 1
                        b0 = 64 if k0 >= H2 else 0
                        kk0 = k0 - (H2 if k0 >= H2 else 0)
                        qsrc = iqT if (h % 2) * 64 == b0 else iqT2
                        S.op("pe", lambda e, pb=pb, h=h, b0=b0, kk0=kk0, w=w, qsrc=qsrc: e.matmul(pb[:, 0:w], lhsT=IDXC(qsrc[b0:b0 + 64, (h // 2) * 128:(h // 2 + 1) * 128]), rhs=IDXC(ikT[b0:b0 + 64, kk0:kk0 + w]), start=True, stop=True), reads=[iqT.k, iqT2.k, ikT.k], writes=[pb.k])
                        S.op("act", lambda e, pb=pb, r=r, w=w: e.activation(out=r[:, 0:w], in_=pb[:, 0:w], func=AF.Relu), reads=[pb.k], writes=[r.k])
                        if h == 0:
                            S.op("dve", lambda e, r=r, k0=k0, w=w: e.tensor_scalar(out=score[:, k0:k0 + w], in0=r[:, 0:w], scalar1=iw[:, 0:1], scalar2=None, op0=ALU.mult), reads=[r.k, iw.k], writes=[sk])
                        else:
                            S.op("dve", lambda e, r=r, k0=k0, w=w, h=h: e.scalar_tensor_tensor(out=score[:, k0:k0 + w], in0=r[:, 0:w], scalar=iw[:, h:h + 1], in1=score[:, k0:k0 + w], op0=ALU.mult, op1=ALU.add), reads=[r.k, iw.k, sk], writes=[sk])
                        yield
                allk = sctok[j % 2][0:nkg]
                S.op("dve", lambda e: e.tensor_copy(out=st[:, 7:8], in_=st[:, 7:8]), reads=allk + [st.k], writes=[score.k, st.k])
                S.op("dve", lambda e: e.tensor_reduce(out=st[:, 5:6], in_=score[:, 0:nkeys], axis=AX.X, op=ALU.max), reads=[score.k], writes=[st.k])
                S.op("dve", lambda e: e.tensor_reduce(out=st[:, 6:7], in_=score[:, 0:nkeys], axis=AX.X, op=ALU.min), reads=[score.k], writes=[st.k])
                S.op("dve", lambda e: e.tensor_scalar(out=st[:, 0:1], in0=st[:, 6:7], scalar1=-1.0, scalar2=None, op0=ALU.add), reads=[st.k], writes=[st.k])
                S.op("dve", lambda e: e.tensor_tensor(out=st[:, 1:2], in0=st[:, 5:6], in1=st[:, 0:1], op=ALU.subtract), reads=[st.k], writes=[st.k])
                S.op("dve", lambda e: e.tensor_tensor(out=score[:, nkeys - 128:nkeys], in0=score[:, nkeys - 128:nkeys], in1=causal[:], op=ALU.add), reads=[score.k, causal.k], writes=[score.k])
                S.op("dve", lambda e: e.tensor_tensor(out=score[:, 0:128], in0=score[:, 0:128], in1=blk0[:], op=ALU.add), reads=[score.k, blk0.k], writes=[score.k])
                yield

            def gB(j):
                nkeys = (2 * j + 2) * 128
                score = scores[j % 2]; st = sts[j % 3]
                for it_ in range(1, NBIS + 1):
                    f = float(2.0 ** -it_)
                    S.op("dve", lambda e, f=f: e.scalar_tensor_tensor(out=st[:, 2:3], in0=st[:, 1:2], scalar=f, in1=st[:, 0:1], op0=ALU.mult, op1=ALU.add), reads=[st.k], writes=[st.k])
                    S.op("dve", lambda e: e.tensor_scalar(out=jk[:, 0:nkeys], in0=score[:, 0:nkeys], scalar1=st[:, 2:3], scalar2=0.0, op0=ALU.is_gt, op1=ALU.add, accum_out=st[:, 3:4]), reads=[score.k, st.k], writes=[jk.k, st.k])
                    S.op("dve", lambda e, f=f: e.tensor_scalar(out=st[:, 4:5], in0=st[:, 3:4], scalar1=KSEL, scalar2=f, op0=ALU.is_gt, op1=ALU.mult), reads=[st.k], writes=[st.k])
                    S.op("dve", lambda e: e.scalar_tensor_tensor(out=st[:, 0:1], in0=st[:, 4:5], scalar=st[:, 1:2], in1=st[:, 0:1], op0=ALU.mult, op1=ALU.add), reads=[st.k], writes=[st.k])
                    yield

            def gA(j):
                p = 2 * j + 1
                nk = p + 1
                nkeys = nk * 128
                ko = self.k_own[j]
                score = scores[j % 2]; st = sts[j % 3]
                S.op("dve", lambda e: e.tensor_scalar(out=msk[:, 0:nkeys], in0=score[:, 0:nkeys], scalar1=st[:, 0:1], scalar2=None, op0=ALU.is_gt), reads=[score.k, st.k], writes=[msk.k])
                S.dma(lambda e: e.dma_start(out=aqT[:], in_=self.s_aqT[j]), reads=[ko["aqT"]], writes=[aqT.k])
                S.dma(lambda e: e.dma_start(out=sgb[:], in_=self.s_sgb[j]), reads=[ko["sgb"]], writes=[sgb.k])
                S.dma(lambda e: e.dma_start(out=mg[:], in_=self.s_mg[j]), reads=[ko["mg"]], writes=[mg.k])
                yield
                units = [(kb, c) for kb in range(nk) for c in range(2)]

                def emit_st(u):
                    kb, c = units[u]
                    pss = ps[2 + u % 2]
                    S.op("pe", lambda e: e.matmul(pss[:], lhsT=akT[:, c, kb * 128:(kb + 1) * 128], rhs=aqT[:, c * 512:(c + 1) * 512], start=True, stop=True), reads=[akT.k, aqT.k], writes=[pss.k])

                def emit_mask(kb):
                    m = mT[(kb // 8) % 2]
                    nb = min(8, nk - kb)

                    def trm(e):
                        for i_ in range(nb):
                            ins = e.transpose(out=psb[:, i_ * 128:(i_ + 1) * 128], in_=msk[:, (kb + i_) * 128:(kb + i_ + 1) * 128], identity=self.identb[:])
                        return ins
                    S.op("pe", trm, reads=[msk.k, self.identb.k], writes=[psb.k])
                    S.op("act", lambda e: e.activation(out=m[:, 0:nb * 128], in_=psb[:, 0:nb * 128], func=AF.Copy), reads=[psb.k], writes=[m.k])

                emit_mask(0)
                emit_st(0)
                yield
                for u, (kb, c) in enumerate(units):
                    pss = ps[2 + u % 2]; pt = pT[u % 2]
                    m = mT[(kb // 8) % 2]
                    if u + 1 < len(units):
                        if units[u + 1][1] == 0 and units[u + 1][0] % 8 == 0:
                            emit_mask(units[u + 1][0])
                        emit_st(u + 1)
                    S.op("act", lambda e, pss=pss, pt=pt: e.activation(out=pt[:], in_=pss[:], func=AF.Exp, bias=negM[:, 0:1], scale=1.0), reads=[pss.k, negM.k], writes=[pt.k])
                    mi = (kb % 8) * 128
                    S.op("pool", lambda e, pt=pt, m=m, mi=mi: e.tensor_tensor(out=pt[:].rearrange("p (h q) -> p h q", h=4), in0=pt[:].rearrange("p (h q) -> p h q", h=4), in1=bc(m[:, mi:mi + 128].unsqueeze(1), [128, 4, 128]), op=ALU.mult), reads=[pt.k, m.k], writes=[pt.k])
                    yield

                    def pv(e, pt=pt, c=c, kb=kb):
                        for hh in range(4):
                            h = c * 4 + hh
                            bk, sl = hb[h]
                            ins = e.matmul(accb[bk][:, sl * 129:(sl + 1) * 129], lhsT=pt[:, hh * 128:(hh + 1) * 128], rhs=vall[:, kb, c * 129:(c + 1) * 129], start=(kb == 0 and sl == 0), stop=(kb == nk - 1), skip_group_check=True)
                        return ins
                    S.op("pe", pv, reads=[pt.k, vall.k], writes=[accb[hb[c * 4][0]].k, accb[hb[c * 4 + 3][0]].k])
                for bk, (h0, n) in enumerate(((0, 3), (3, 3), (6, 2))):
                    a3 = accb[bk][:, 0:n * 129].rearrange("p (h d) -> p h d", h=n)
                    S.op("dve", lambda e, a3=a3, h0=h0, n=n: e.reciprocal(out=rec[:, h0:h0 + n], in_=a3[:, :, 128]), reads=[accb[bk].k], writes=[rec.k])
                    S.op("dve", lambda e, a3=a3, h0=h0, n=n: e.tensor_tensor(out=oatt[:, h0 * 128:(h0 + n) * 128].rearrange("p (h d) -> p h d", h=n), in0=a3[:, :, 0:128], in1=bc(rec[:, h0:h0 + n].unsqueeze(2), [128, n, 128]), op=ALU.mult), reads=[accb[bk].k, rec.k], writes=[oatt.k])
                S.op("pool", lambda e: e.tensor_tensor(out=oatt[:], in0=oatt[:], in1=sgb[:], op=ALU.mult), reads=[oatt.k, sgb.k], writes=[oatt.k])
                S.op("pool", lambda e: e.tensor_tensor(out=mrg[:], in0=oatt[:], in1=mg[:], op=ALU.add), reads=[oatt.k, mg.k], writes=[mrg.k])
                S.dma(lambda e: e.dma_start(out=self.s_mrg[j], in_=mrg[:]), reads=[mrg.k], writes=[ko["x1"]])
                yield

            def count(g):
                n = 0
                for _ in g:
                    n += 1
                return n

            for t in range(NO + 2):
                gens = []
                for mk, jj in ((gA, t - 2), (gB, t - 1), (gI, t)):
                    if 0 <= jj < NO:
                        gens.append(mk(jj))
                sizes = []
                for mk, jj in ((gA, t - 2), (gB, t - 1), (gI, t)):
                    if 0 <= jj < NO:
                        nk = 2 * jj + 2
                        if mk is gA:
                            sizes.append(2 + 2 * nk)
                        elif mk is gB:
                            sizes.append(NBIS)
                        else:
                            sizes.append(((nk * 128 + 511) // 512) * 8 + 1)
                nmax = max(sizes)
                done = [0] * len(gens)
                if 0 <= t - 2 < NO:
                    next(gens[0], None)
                    done[0] = 1
                for k in range(nmax):
                    for gi, g in enumerate(gens):
                        tgt = ((k + 1) * sizes[gi]) // nmax
                        while done[gi] < tgt:
                            next(g, None)
                            done[gi] += 1
                for g in gens:
                    for _ in g:
                        pass
            S.barrier()
            S.emit()

    def phase3(self, G=2):
        S = self.S
        NO = self.NO
        ps, psb = self.ps, self.psb
        G = min(G, NO)
        with ExitStack() as es:
            sb = lambda n, sh, dt: self.sb(es, n, sh, dt)
            wup = sb("wupb", [128, 8, 4096], BF16); wdn = sb("wdnb", [128, 32, 1024], BF16)
            woutb = sb("woutb", [128, 8, 1024], BF16)
            stg = [sb("stg3_%d" % i, [128, 512], F32) for i in range(2)]
            gmlp = sb("gmlp", [128, 8], F32)
            self.load(gmlp, gmlp[:], self.i_gmlp)
            self.load_weight_bf16(es, woutb, self.i_wout, 1024, stg)
            self.load_weight_bf16(es, wup, self.i_wup, 4096, stg)
            self.load_weight_bf16(es, wdn, self.i_wdn, 1024, stg)
            x1 = sb("x1g", [128, G, 1024], F32)
            xb = sb("xb3", [128, 1024], F32); mrg = sb("mrg3", [128, 1024], BF16); mT2 = sb("mT2", [128, 8, 128], BF16)
            ss = sb("ss3", [128, 1], F32); xnb = sb("xnb3", [128, 1024], BF16)
            h2T = sb("h2T", [128, 8, G * 128], BF16)
            hTb = sb("hTb3", [128, 8, 128], BF16)
            hid = sb("hidT", [128, 32, G * 128], BF16)
            rl = [sb("rl3_%d" % i, [128, G * 128], F32) for i in range(2)]
            ot = [sb("ot%d" % i, [128, 1024], F32) for i in range(2)]
            for g0 in range(0, NO, G):
                for b in range(G):
                    j = g0 + b
                    p = 2 * j + 1
                    S.dma(lambda e, j=j: e.dma_start(out=mrg[:], in_=self.s_mrg[j]), reads=[self.k_own[j]["x1"]], writes=[mrg.k])
                    S.dma(lambda e, p=p: e.dma_start(out=xb[:], in_=self.i_x[p * 128:(p + 1) * 128, :]), writes=[xb.k])

                    def trg(e):
                        for kc in range(8):
                            ins = e.transpose(out=psb[:, kc * 128:(kc + 1) * 128], in_=mrg[:, kc * 128:(kc + 1) * 128], identity=self.identb[:])
                        return ins
                    S.op("pe", trg, reads=[mrg.k, self.identb.k], writes=[psb.k])
                    S.op("act", lambda e: e.activation(out=mT2[:].rearrange("p k t -> p (k t)"), in_=psb[:], func=AF.Copy), reads=[psb.k], writes=[mT2.k])
                    for half in range(2):
                        pq = ps[4 + half]
                        self.proj(pq, 512, mT2, woutb, half * 512)
                        S.op("dve", lambda e, pq=pq, half=half, b=b: e.tensor_tensor(out=x1[:, b, half * 512:(half + 1) * 512], in0=xb[:, half * 512:(half + 1) * 512], in1=pq[:], op=ALU.add), reads=[xb.k, pq.k], writes=[x1.k])
                    xbv = TL(x1.t[:, b, :], "x")
                    xbv.k = x1.k
                    self.rms_hT(xbv, ss, xnb, hTb, gmlp, psb)
                    S.op("pool", lambda e, b=b: e.tensor_copy(out=h2T[:, :, b * 128:(b + 1) * 128], in_=hTb[:]), reads=[hTb.k], writes=[h2T.k])
                for f in range(32):
                    pq = ps[f % 2]; r = rl[f % 2]

                    def up(e, pq=pq, f=f):
                        for kc in range(8):
                            ins = e.matmul(pq[:, 0:G * 128], lhsT=wup[:, kc, f * 128:(f + 1) * 128], rhs=h2T[:, kc, :], start=(kc == 0), stop=(kc == 7))
                        return ins
                    S.op("pe", up, reads=[wup.k, h2T.k], writes=[pq.k])
                    S.op("act", lambda e, pq=pq, r=r: e.activation(out=r[:], in_=pq[:, 0:G * 128], func=AF.Relu), reads=[pq.k], writes=[r.k])
                    S.op("dve" if f % 2 else "pool", lambda e, r=r, f=f: e.tensor_tensor(out=hid[:, f, :], in0=r[:], in1=r[:], op=ALU.mult), reads=[r.k], writes=[hid.k])
                for b in range(G):
                    j = g0 + b
                    o = ot[b % 2]
                    for half in range(2):
                        pq = ps[2 + half]

                        def dn(e, pq=pq, b=b, half=half):
                            for f in range(32):
                                ins = e.matmul(pq[:], lhsT=hid[:, f, b * 128:(b + 1) * 128], rhs=wdn[:, f, half * 512:(half + 1) * 512], start=(f == 0), stop=(f == 31))
                            return ins
                        S.op("pe", dn, reads=[hid.k, wdn.k], writes=[pq.k])
                        S.op("dve", lambda e, pq=pq, o=o, b=b, half=half: e.tensor_tensor(out=o[:, half * 512:(half + 1) * 512], in0=x1[:, b, half * 512:(half + 1) * 512], in1=pq[:], op=ALU.add), reads=[x1.k, pq.k], writes=[o.k])
                    S.dma(lambda e, j=j, o=o: e.dma_start(out=self.o_out[j * 128:(j + 1) * 128, :], in_=o[:]), reads=[o.k], writes=[])
            S.barrier()
            S.emit()

    def phase1a(self):
        S = self.S
        P = self.P
        ps, psb = self.ps, self.psb
        with ExitStack() as es:
            sb = lambda n, sh, dt: self.sb(es, n, sh, dt)
            wb = sb("w1a", [128, 8, C1A], BF16)
            stg = [sb("stga%d" % i, [128, 512], F32) for i in range(2)]
            gmix = sb("gmixa", [128, 8], F32); ggdn = sb("ggdn", [128, 128], F32)
            aexp = sb("aexp", [128, 8], F32); dtb = sb("dtb", [128, 8], F32)
            convw = sb("convw", [128, 24, 4], F32)
            uinc = sb("uinc", [128, 128], F32); onesf = sb("onesf", [128, 128], F32); onesb = sb("onesb", [128, 128], BF16)
            mskL4 = sb("mskL4", [128, 4, 128], BF16); mskU4 = sb("mskU4", [128, 4, 128], BF16)
            self.load(gmix, gmix[:], self.i_gmix); self.load(ggdn, ggdn[:], self.i_ggdn)
            self.load(aexp, aexp[:], self.i_alog); self.load(dtb, dtb[:], self.i_dtb)
            self.load(convw, convw[:].rearrange("p t j -> p (t j)"), self.i_convw)
            self.load(uinc, uinc[:], self.i_uinc)
            for q4 in range(4):
                self.load(mskL4, mskL4[:, q4, :], self.i_mskL); self.load(mskU4, mskU4[:, q4, :], self.i_mskU)
            S.op("pool", lambda e: e.memset(onesf[:], 1.0), writes=[onesf.k])
            S.op("pool", lambda e: e.memset(onesb[:], 1.0), writes=[onesb.k])
            S.op("act", lambda e: e.activation(out=aexp[:], in_=aexp[:], func=AF.Exp), reads=[aexp.k], writes=[aexp.k])
            self.load_weight_bf16(es, wb, self.i_w1a, C1A, stg)
            xb = sb("xba", [128, 1024], F32); ss = sb("ssa", [128, 1], F32); xnb = sb("xnba", [128, 1024], BF16); hT = sb("hTa", [128, 8, 128], BF16)
            U = sb("U", [128, 24, 131], F32)
            YgD = sb("YgD", [128, 8, 128], F32); CtD = sb("CtD", [128, 8, 128], F32); YgP = sb("YgP", [128, 8, 128], F32); CtP = sb("CtP", [128, 8, 128], F32)
            sq = sb("sqa", [128, 1024], BF16); rn = sb("rn", [128, 1024], F32)
            qT = sb("qT", [128, 8, 128], BF16); kT2 = [sb("kT%d" % i, [128, 8, 128], BF16) for i in range(2)]; vTb = sb("vTb", [128, 8, 128], BF16)
            ktok = sb("ktok", [128, 8, 128], F32)
            kbg2 = [sb("kbg%d" % i, [128, 8, 128], BF16) for i in range(2)]; kdec2 = [sb("kdec%d" % i, [128, 8, 128], BF16) for i in range(2)]; vb2 = [sb("vb%d" % i, [128, 8, 128], BF16) for i in range(2)]
            smm = [sb("sm_%d" % i, [128, 96], F32) for i in range(2)]
            sm2 = sb("sm2", [128, 16], F32); sm3 = sb("sm3", [128, 8], F32)
            Rall = sb("Rall", [128, 8, 128], F32)
            Dls2 = [sb("Dls%d" % i, [128, 8, 128], F32) for i in range(2)]; DT = sb("DTm", [128, 8, 128], F32); egr = sb("egr", [128, 8, 128], F32)
            qdT2 = [sb("qdT%d" % i, [128, 8, 128], BF16) for i in range(2)]; QKT2 = [sb("QKT%d" % i, [128, 8, 128], BF16) for i in range(2)]
            Am = [sb("Am%d" % i, [128, 8, 128], NDT) for i in range(2)]
            Bm = [sb("Bm%d" % i, [128, 8, 128], NDT) for i in range(2)]
            Pm = sb("Pm", [128, 8, 128], NDT); TTb = sb("TTb", [128, 8, 128], BF16)
            u = sb("u", [128, 8, 128], F32); wT = sb("wT", [128, 8, 128], BF16); vn = sb("vn", [128, 8, 128], BF16)
            S32 = sb("S32", [128, 8, 128], F32); Stmp = sb("Stmp", [128, 8, 128], F32); Sbf = sb("Sbf", [128, 8, 128], BF16)
            o = sb("o", [128, 8, 128], F32); osq = Stmp
            zg2 = [sb("zga%d" % i, [128, 1024], BF16) for i in range(2)]; mgt = sb("mgt", [128, 1024], BF16)
            S.op("pool", lambda e: e.memset(U[:], 0.0), writes=[U.k])
            S.op("pool", lambda e: e.memset(S32[:], 0.0), writes=[S32.k])
            S.op("pool", lambda e: e.memset(Sbf[:], 0.0), writes=[Sbf.k])
            f4 = lambda t, g: t[:, g * 4:(g + 1) * 4, :].rearrange("p h d -> p (h d)")

            def batch_mm(banks, fn_h):
                for g in range(2):
                    def f(e, g=g):
                        for hh in range(4):
                            h = g * 4 + hh
                            ins = fn_h(e, banks[g][:, hh * 128:(hh + 1) * 128], h)
                        return ins
                    yield g, f

            def gF(p):
                own = (p % 2 == 1)
                j = p // 2
                r0 = p * 128
                DB = p % 2
                kT = kT2[DB]; kbg = kbg2[DB]; kdec = kdec2[DB]; vb = vb2[DB]; sm = smm[DB]; Dls = Dls2[DB]
                qdT = qdT2[DB]; QKT = QKT2[DB]; zg = zg2[DB]
                self.load(xb, xb[:], self.i_x[r0:r0 + 128, :])
                if own:
                    S.dma(lambda e, j=j: e.dma_start(out=zg[:], in_=self.s_zg[j]), reads=[self.k_own[j]["zg"]], writes=[zg.k])
                self.rms_hT(xb, ss, xnb, hT, gmix, psb)
                yield
                for grp in range(6):
                    pq = ps[4 + grp % 2]

                    def fm(e, pq=pq, grp=grp):
                        for t4 in range(4):
                            ct = grp * 4 + t4
                            for kc in range(8):
                                ins = e.matmul(pq[:, t4 * 128:(t4 + 1) * 128], lhsT=wb[:, kc, ct * 128:(ct + 1) * 128], rhs=hT[:, kc, :], start=(kc == 0), stop=(kc == 7), skip_group_check=True)
                        return ins
                    S.op("pe", fm, reads=[wb.k, hT.k], writes=[pq.k])
                    yield
                    S.op("act", lambda e, pq=pq, grp=grp: e.activation(out=U[:, grp * 4:(grp + 1) * 4, 3:131], in_=pq[:].rearrange("p (t n) -> p t n", t=4), func=AF.Copy), reads=[pq.k], writes=[U.k])
                    yield
                self.proj(ps[6], 16, hT, wb, 3072)
                yield
                S.op("act", lambda e: e.activation(out=sm[:, 0:16], in_=ps[6][:, 0:16], func=AF.Copy), reads=[ps[6].k], writes=[sm.k])
                yield
                S.op("act", lambda e: e.activation(out=sm[:, 16:24], in_=sm[:, 8:16], func=AF.Sigmoid), reads=[sm.k], writes=[sm.k])
                yield
                S.op("dve", lambda e: e.tensor_tensor(out=sm[:, 24:32], in0=sm[:, 0:8], in1=dtb[:], op=ALU.add), reads=[sm.k, dtb.k], writes=[sm.k])
                yield
                S.op("act", lambda e: e.activation(out=sm[:, 32:40], in_=sm[:, 24:32], func=AF.Abs), reads=[sm.k], writes=[sm.k])
                yield
                S.op("act", lambda e: e.activation(out=sm[:, 32:40], in_=sm[:, 32:40], func=AF.Exp, scale=-1.0), reads=[sm.k], writes=[sm.k])
                yield
                S.op("dve", lambda e: e.tensor_scalar(out=sm[:, 32:40], in0=sm[:, 32:40], scalar1=1.0, scalar2=None, op0=ALU.add), reads=[sm.k], writes=[sm.k])
                yield
                S.op("act", lambda e: e.activation(out=sm[:, 32:40], in_=sm[:, 32:40], func=AF.Ln), reads=[sm.k], writes=[sm.k])
                yield
                S.op("dve", lambda e: e.scalar_tensor_tensor(out=sm[:, 40:48], in0=sm[:, 24:32], scalar=0.0, in1=sm[:, 32:40], op0=ALU.max, op1=ALU.add), reads=[sm.k], writes=[sm.k])
                yield
                S.op("dve", lambda e: e.scalar_tensor_tensor(out=sm[:, 48:56], in0=sm[:, 40:48], scalar=-1.0, in1=aexp[:], op0=ALU.mult, op1=ALU.mult), reads=[sm.k, aexp.k], writes=[sm.k])
                yield

                def cs(e):
                    e.matmul(ps[6][:, 16:24], lhsT=uinc[:], rhs=sm[:, 48:56], start=True, stop=True, skip_group_check=True)
                    return e.matmul(ps[6][:, 24:32], lhsT=onesf[:], rhs=sm[:, 48:56], start=True, stop=True, skip_group_check=True)
                S.op("pe", cs, reads=[uinc.k, onesf.k, sm.k], writes=[ps[6].k])
                yield
                S.op("act", lambda e: e.activation(out=sm[:, 56:72], in_=ps[6][:, 16:32], func=AF.Copy), reads=[ps[6].k], writes=[sm.k])
                yield
                S.op("act", lambda e: e.activation(out=sm[:, 72:80], in_=sm[:, 56:64], func=AF.Exp), reads=[sm.k], writes=[sm.k])
                yield
                S.op("dve", lambda e: e.tensor_tensor(out=sm[:, 80:88], in0=sm[:, 64:72], in1=sm[:, 56:64], op=ALU.subtract), reads=[sm.k], writes=[sm.k])
                yield
                S.op("act", lambda e: e.activation(out=sm[:, 80:96], in_=sm[:, 80:96] if False else sm[:, 80:88], func=AF.Exp) if False else e.activation(out=sm[:, 80:88], in_=sm[:, 80:88], func=AF.Exp), reads=[sm.k], writes=[sm.k])
                yield
                S.op("act", lambda e: e.activation(out=sm[:, 88:96], in_=sm[:, 64:72], func=AF.Exp), reads=[sm.k], writes=[sm.k])
                yield
                S.op("dve", lambda e: e.tensor_tensor(out=sm2[:, 0:8], in0=sm[:, 16:24], in1=sm[:, 72:80], op=ALU.mult), reads=[sm.k], writes=[sm2.k])
                yield
                for grp, dst in ((2, vTb), (0, qT), (1, kT)):
                    if grp == 0 and not own:
                        continue
                    Ug = lambda jj, grp=grp: U[:, grp * 8:(grp + 1) * 8, jj:jj + 128]
                    cw = lambda jj, grp=grp: bc(convw[:, grp * 8:(grp + 1) * 8, jj:jj + 1], [128, 8, 128])
                    ceng = "pool" if grp == 2 else "dve"
                    Yg = YgP if grp == 2 else YgD
                    Ct = CtP if grp == 2 else CtD
                    S.op(ceng, lambda e, Ug=Ug, cw=cw, Yg=Yg: e.tensor_tensor(out=Yg[:], in0=Ug(0), in1=cw(0), op=ALU.mult), reads=[U.k, convw.k], writes=[Yg.k])
                    yield
                    for jj in range(1, 4):
                        S.op(ceng, lambda e, Ug=Ug, cw=cw, jj=jj, Ct=Ct: e.tensor_tensor(out=Ct[:], in0=Ug(jj), in1=cw(jj), op=ALU.mult), reads=[U.k, convw.k], writes=[Ct.k])
                        yield
                        S.op(ceng, lambda e, Yg=Yg, Ct=Ct: e.tensor_tensor(out=Yg[:], in0=Yg[:], in1=Ct[:], op=ALU.add), reads=[Yg.k, Ct.k], writes=[Yg.k])
                        yield
                    Yf = Yg[:].rearrange("p h d -> p (h d)")
                    if grp == 2:
                        S.op("act", lambda e, Yf=Yf: e.activation(out=vTb[:].rearrange("p h d -> p (h d)"), in_=Yf, func=AF.Silu), reads=[Yg.k], writes=[vTb.k])
                        continue
                    S.op("act", lambda e, Yf=Yf: e.activation(out=Yf, in_=Yf, func=AF.Silu), reads=[Yg.k], writes=[Yg.k])
                    yield
                    S.op("dve", lambda e, Yf=Yf: e.tensor_tensor(out=sq[:], in0=Yf, in1=Yf, op=ALU.mult), reads=[Yg.k], writes=[sq.k])
                    yield
                    for g in range(2):
                        pq = ps[4 + g]
                        S.op("pe", lambda e, pq=pq, g=g: e.matmul(pq[:], lhsT=onesb[:], rhs=sq[:, g * 512:(g + 1) * 512], start=True, stop=True), reads=[onesb.k, sq.k], writes=[pq.k])
                        S.op("dve", lambda e, pq=pq, g=g: e.tensor_scalar(out=rn[:, g * 512:(g + 1) * 512], in0=pq[:], scalar1=EPS, scalar2=None, op0=ALU.add), reads=[pq.k], writes=[rn.k])
                    S.op("act", lambda e: e.activation(out=rn[:], in_=rn[:], func=AF.Ln), reads=[rn.k], writes=[rn.k])
                    yield
                    S.op("act", lambda e: e.activation(out=rn[:], in_=rn[:], func=AF.Exp, scale=-0.5), reads=[rn.k], writes=[rn.k])
                    yield
                    sc = float(128.0 ** -0.5) if grp == 0 else 1.0
                    S.op("dve", lambda e, Yf=Yf, dst=dst, sc=sc: e.scalar_tensor_tensor(out=dst[:].rearrange("p h d -> p (h d)"), in0=Yf, scalar=sc, in1=rn[:], op0=ALU.mult, op1=ALU.mult), reads=[Yg.k, rn.k], writes=[dst.k])
                    yield
                S.op("pool", lambda e: e.tensor_copy(out=U[:, :, 0:3], in_=U[:, :, 128:131]), reads=[U.k], writes=[U.k])
                yield
                def trk(e):
                    for h in range(8):
                        ins = e.transpose(out=psb[:, h * 128:(h + 1) * 128], in_=kT[:, h, :], identity=self.identb[:])
                    return ins
                S.op("pe", trk, reads=[kT.k, self.identb.k], writes=[psb.k])
                yield
                S.op("act", lambda e: e.activation(out=ktok[:].rearrange("p h d -> p (h d)"), in_=psb[:], func=AF.Copy), reads=[psb.k], writes=[ktok.k])
                yield
                S.op("dve", lambda e: e.tensor_tensor(out=kbg[:], in0=ktok[:], in1=bc(sm2[:, 0:8].unsqueeze(2), [128, 8, 128]), op=ALU.mult), reads=[ktok.k, sm2.k], writes=[kbg.k])
                yield
                S.op("pool", lambda e: e.tensor_tensor(out=kdec[:], in0=ktok[:], in1=bc(sm[:, 80:88].unsqueeze(2), [128, 8, 128]), op=ALU.mult), reads=[ktok.k, sm.k], writes=[kdec.k])
                yield

                def trv(e):
                    for h in range(8):
                        ins = e.transpose(out=psb[:, h * 128:(h + 1) * 128], in_=vTb[:, h, :], identity=self.identb[:])
                    return ins
                S.op("pe", trv, reads=[vTb.k, self.identb.k], writes=[psb.k])
                yield
                S.op("dve", lambda e: e.tensor_tensor(out=vb[:], in0=psb[:].rearrange("p (h d) -> p h d", h=8), in1=bc(sm[:, 16:24].unsqueeze(2), [128, 8, 128]), op=ALU.mult), reads=[psb.k, sm.k], writes=[vb.k])
                yield
                S.op("dve", lambda e: e.tensor_tensor(out=Rall[:], in0=bc(uinc[:].unsqueeze(1), [128, 8, 128]), in1=bc(sm[:, 48:56].unsqueeze(2), [128, 8, 128]), op=ALU.mult), reads=[uinc.k, sm.k], writes=[Rall.k])
                yield
                for g in range(2):
                    pq = ps[4 + g]

                    def grl(e, pq=pq, g=g):
                        e.matmul(pq[:], lhsT=onesf[:], rhs=f4(Rall, g), start=True, stop=False)
                        return e.matmul(pq[:], lhsT=self.identb[:], rhs=mskL4[:].rearrange("p h d -> p (h d)"), start=False, stop=True)
                    S.op("pe", grl, reads=[onesf.k, Rall.k, self.identb.k, mskL4.k], writes=[pq.k])
                    yield
                    for hh in range(4):
                        h = g * 4 + hh
                        S.op("act", lambda e, pq=pq, h=h, hh=hh: e.activation(out=Dls[:, h, :], in_=pq[:, hh * 128:(hh + 1) * 128], func=AF.Exp, scale=-1.0, bias=sm[:, 56 + h:57 + h]), reads=[pq.k, sm.k], writes=[Dls.k])
                if own:
                    for g in range(2):
                        pq = ps[4 + g]
                        S.op("pe", lambda e, pq=pq, g=g: e.matmul(pq[:], lhsT=onesf[:], rhs=f4(Rall, g), start=True, stop=False), reads=[onesf.k, Rall.k], writes=[pq.k])
                        S.op("act", lambda e, pq=pq, g=g: e.activation(out=f4(egr, g), in_=pq[:], func=AF.Exp), reads=[pq.k], writes=[egr.k])
                        S.op("pe", lambda e, pq=pq: e.matmul(pq[:], lhsT=self.identb[:], rhs=mskU4[:].rearrange("p h d -> p (h d)"), start=False, stop=True), reads=[self.identb.k, mskU4.k], writes=[pq.k])
                        S.op("pool", lambda e, g=g: e.tensor_scalar(out=sm2[:, 8:16], in0=sm[:, 56:64], scalar1=-1.0, scalar2=None, op0=ALU.mult), reads=[sm.k], writes=[sm2.k])
                        for hh in range(4):
                            h = g * 4 + hh
                            S.op("act", lambda e, pq=pq, h=h, hh=hh: e.activation(out=DT[:, h, :], in_=pq[:, hh * 128:(hh + 1) * 128], func=AF.Exp, scale=1.0, bias=sm2[:, 8 + h:9 + h]), reads=[pq.k, sm2.k], writes=[DT.k])
                    S.op("pool", lambda e: e.tensor_tensor(out=qdT[:], in0=qT[:], in1=egr[:], op=ALU.mult), reads=[qT.k, egr.k], writes=[qdT.k])
                    yield
                    for g, f in batch_mm((ps[4], ps[5]), lambda e, out, h: e.matmul(out, lhsT=kT[:, h, :], rhs=qT[:, h, :], start=True, stop=True, skip_group_check=True)):
                        S.op("pe", f, reads=[kT.k, qT.k], writes=[ps[4 + g].k])
                        S.op("dve", lambda e, g=g: e.tensor_tensor(out=f4(QKT, g), in0=ps[4 + g][:], in1=f4(DT, g), op=ALU.mult), reads=[ps[4 + g].k, DT.k], writes=[QKT.k])

                yield

            def gNR(p):
                own = (p % 2 == 1)
                j = p // 2
                r0 = p * 128
                DB = p % 2
                kT = kT2[DB]; kbg = kbg2[DB]; kdec = kdec2[DB]; vb = vb2[DB]; sm = smm[DB]; Dls = Dls2[DB]
                qdT = qdT2[DB]; QKT = QKT2[DB]; zg = zg2[DB]
                S.op("pool", lambda e: e.tensor_tensor(out=Stmp[:], in0=S32[:], in1=bc(sm[:, 88:96].unsqueeze(2), [128, 8, 128]), op=ALU.mult), reads=[S32.k, sm.k], writes=[Stmp.k])
                yield
                A0, B0 = Am[0], Bm[0]
                for g, f in batch_mm((ps[0], ps[1]), lambda e, out, h: e.matmul(out, lhsT=kT[:, h, :], rhs=kT[:, h, :], start=True, stop=True, skip_group_check=True)):
                    S.op("pe", f, reads=[kT.k], writes=[ps[g].k])
                    yield
                    for hh in range(4):
                        h = g * 4 + hh
                        S.op("dve", lambda e, g=g, h=h, hh=hh: e.scalar_tensor_tensor(out=A0[:, h, :], in0=ps[g][:, hh * 128:(hh + 1) * 128], scalar=sm[:, 16 + h:17 + h], in1=Dls[:, h, :], op0=ALU.mult, op1=ALU.mult), reads=[ps[g].k, sm.k, Dls.k], writes=[A0.k])
                for g, f in batch_mm((ps[2], ps[3]), lambda e, out, h: e.transpose(out=out, in_=RF(A0[:, h, :]), identity=self.identf[:])):
                    S.op("pe", f, reads=[A0.k, self.identf.k], writes=[ps[2 + g].k])
                    yield
                    S.op("act", lambda e, g=g: e.activation(out=f4(B0, g), in_=ps[2 + g][:], func=AF.Copy), reads=[ps[2 + g].k], writes=[B0.k])
                    yield
                S.op("pool", lambda e: e.tensor_tensor(out=Pm[:], in0=bc(self.identf[:].unsqueeze(1), [128, 8, 128]), in1=RF(B0[:]), op=ALU.subtract), reads=[self.identf.k, B0.k], writes=[Pm.k])
                yield
                cur = 0
                for lv in range(6):
                    Ac, Bc = Am[cur], Bm[cur]
                    An, Bn_ = Am[1 - cur], Bm[1 - cur]
                    last = (lv == 5)
                    for g, f in batch_mm((ps[0], ps[1]), lambda e, out, h, Ac=Ac, Bc=Bc: e.matmul(out, lhsT=NC_(Bc[:, h, :]), rhs=NC_(Ac[:, h, :]), start=True, stop=True, skip_group_check=True)):
                        S.op("pe", f, reads=[Ac.k, Bc.k], writes=[ps[g].k])
                        S.op("act", lambda e, g=g, An=An: e.activation(out=f4(An, g), in_=ps[g][:], func=AF.Copy), reads=[ps[g].k], writes=[An.k])
                    if not last:
                        for g, f in batch_mm((ps[2], ps[3]), lambda e, out, h, Ac=Ac, Bc=Bc: e.matmul(out, lhsT=NC_(Ac[:, h, :]), rhs=NC_(Bc[:, h, :]), start=True, stop=True, skip_group_check=True)):
                            S.op("pe", f, reads=[Ac.k, Bc.k], writes=[ps[2 + g].k])
                            S.op("dve", lambda e, g=g, Bn_=Bn_: e.tensor_copy(out=f4(Bn_, g), in_=ps[2 + g][:]), reads=[ps[2 + g].k], writes=[Bn_.k])
                    for g, f in batch_mm((ps[0], ps[1]), lambda e, out, h, An=An: e.matmul(out, lhsT=NC_(An[:, h, :]), rhs=NC_(Pm[:, h, :]), start=True, stop=True, skip_group_check=True)):
                        S.op("pe", f, reads=[An.k, Pm.k], writes=[ps[g].k])
                        S.op("dve", lambda e, g=g: e.tensor_tensor(out=f4(Pm, g), in0=RF(f4(Pm, g)), in1=ps[g][:], op=ALU.add), reads=[Pm.k, ps[g].k], writes=[Pm.k])
                    cur = 1 - cur
                S.op("act", lambda e: e.activation(out=TTb[:].rearrange("p h d -> p (h d)"), in_=RF(Pm[:]).rearrange("p h d -> p (h d)"), func=AF.Copy), reads=[Pm.k], writes=[TTb.k])
                yield
                for g, f in batch_mm((ps[0], ps[1]), lambda e, out, h: e.matmul(out, lhsT=TTb[:, h, :], rhs=vb[:, h, :], start=True, stop=True, skip_group_check=True)):
                    S.op("pe", f, reads=[TTb.k, vb.k], writes=[ps[g].k])
                    yield
                    S.op("act", lambda e, g=g: e.activation(out=f4(u, g), in_=ps[g][:], func=AF.Copy), reads=[ps[g].k], writes=[u.k])
                    yield
                for g, f in batch_mm((ps[2], ps[3]), lambda e, out, h: e.matmul(out, lhsT=kbg[:, h, :], rhs=TTb[:, h, :], start=True, stop=True, skip_group_check=True)):
                    S.op("pe", f, reads=[TTb.k, kbg.k], writes=[ps[2 + g].k])
                    yield
                    S.op("act", lambda e, g=g: e.activation(out=f4(wT, g), in_=ps[2 + g][:], func=AF.Copy), reads=[ps[2 + g].k], writes=[wT.k])
                    yield
                for g, f in batch_mm((ps[0], ps[1]), lambda e, out, h: e.matmul(out, lhsT=wT[:, h, :], rhs=Sbf[:, h, :], start=True, stop=True, skip_group_check=True)):
                    S.op("pe", f, reads=[wT.k, Sbf.k], writes=[ps[g].k])
                    yield
                    S.op("dve", lambda e, g=g: e.tensor_tensor(out=f4(vn, g), in0=f4(u, g), in1=ps[g][:], op=ALU.subtract), reads=[u.k, ps[g].k], writes=[vn.k])
                    yield
                if own:
                    for g in range(2):
                        def fo(e, g=g):
                            for hh in range(4):
                                h = g * 4 + hh
                                e.matmul(ps[g][:, hh * 128:(hh + 1) * 128], lhsT=qdT[:, h, :], rhs=Sbf[:, h, :], start=True, stop=False, skip_group_check=True)
                                ins = e.matmul(ps[g][:, hh * 128:(hh + 1) * 128], lhsT=QKT[:, h, :], rhs=vn[:, h, :], start=False, stop=True, skip_group_check=True)
                            return ins
                        S.op("pe", fo, reads=[qdT.k, Sbf.k, QKT.k, vn.k], writes=[ps[g].k])
                        S.op("act", lambda e, g=g: e.activation(out=f4(o, g), in_=ps[g][:], func=AF.Copy), reads=[ps[g].k], writes=[o.k])
                for g, f in batch_mm((ps[2], ps[3]), lambda e, out, h: e.matmul(out, lhsT=kdec[:, h, :], rhs=vn[:, h, :], start=True, stop=True, skip_group_check=True)):
                    S.op("pe", f, reads=[kdec.k, vn.k], writes=[ps[2 + g].k])
                    yield
                    S.op("dve", lambda e, g=g: e.tensor_tensor(out=f4(S32, g), in0=f4(Stmp, g), in1=ps[2 + g][:], op=ALU.add), reads=[Stmp.k, ps[2 + g].k], writes=[S32.k])
                    yield
                S.op("act", lambda e: e.activation(out=Sbf[:].rearrange("p h d -> p (h d)"), in_=S32[:].rearrange("p h d -> p (h d)"), func=AF.Copy), reads=[S32.k], writes=[Sbf.k])
                yield
                if own:
                    S.op("pool", lambda e: e.tensor_tensor(out=osq[:], in0=o[:], in1=o[:], op=ALU.mult), reads=[o.k], writes=[osq.k])
                    yield
                    S.op("dve", lambda e: e.tensor_reduce(out=sm3[:, 0:8], in_=osq[:], axis=AX.X, op=ALU.add), reads=[osq.k], writes=[sm2.k])
                    yield
                    S.op("dve", lambda e: e.tensor_scalar(out=sm3[:, 0:8], in0=sm3[:, 0:8], scalar1=1.0 / 128, scalar2=EPS, op0=ALU.mult, op1=ALU.add), reads=[sm2.k], writes=[sm2.k])
                    yield
                    S.op("act", lambda e: e.activation(out=sm3[:, 0:8], in_=sm3[:, 0:8], func=AF.Ln), reads=[sm2.k], writes=[sm2.k])
                    yield
                    S.op("act", lambda e: e.activation(out=sm3[:, 0:8], in_=sm3[:, 0:8], func=AF.Exp, scale=-0.5), reads=[sm2.k], writes=[sm2.k])
                    yield
                    S.op("dve", lambda e: e.tensor_tensor(out=o[:], in0=o[:], in1=bc(sm3[:, 0:8].unsqueeze(2), [128, 8, 128]), op=ALU.mult), reads=[o.k, sm2.k], writes=[o.k])
                    yield
                    S.op("pool", lambda e: e.tensor_tensor(out=o[:], in0=o[:], in1=bc(ggdn[:].unsqueeze(1), [128, 8, 128]), op=ALU.mult), reads=[o.k, ggdn.k], writes=[o.k])
                    yield
                    S.op("dve", lambda e: e.tensor_tensor(out=mgt[:], in0=o[:].rearrange("p h d -> p (h d)"), in1=zg[:], op=ALU.mult), reads=[o.k, zg.k], writes=[mgt.k])
                    yield
                    S.dma(lambda e, j=j: e.dma_start(out=self.s_mg[j], in_=mgt[:]), reads=[mgt.k], writes=[self.k_own[j]["mg"]])

                yield

            def run_pair(ga, gb, na, nb):
                nmax = max(na, nb, 1)
                da = db = 0
                for k in range(nmax):
                    ta = ((k + 1) * na) // nmax
                    tb = ((k + 1) * nb) // nmax
                    while da < ta:
                        next(ga, None); da += 1
                    while db < tb:
                        next(gb, None); db += 1
                for _ in ga:
                    pass
                for _ in gb:
                    pass

            def count(mk, p):
                S.dry = True
                n = sum(1 for _ in mk(p))
                S.dry = False
                return n
            nF = [count(gF, 0), count(gF, 1)]
            nN = [count(gNR, 0), count(gNR, 1)]
            for _ in gF(0):
                pass
            for p in range(P):
                if p + 1 < P:
                    run_pair(gNR(p), gF(p + 1), nN[p % 2], nF[(p + 1) % 2])
                else:
                    for _ in gNR(p):
                        pass
            S.barrier()
            S.emit()

    def phase1a_stub(self):
        S = self.S
        with ExitStack() as es:
            z = self.sb(es, "zmg", [128, 1024], BF16)
            S.op("pool", lambda e: e.memset(z[:], 0.0), writes=[z.k])
            for j in range(self.NO):
                S.dma(lambda e, j=j: e.dma_start(out=self.s_mg[j], in_=z[:]), reads=[z.k], writes=[self.k_own[j]["mg"]])
            S.barrier()
            S.emit()


def build(S, debug=False, gdn=True):
    B = Builder(S, debug)
    with ExitStack() as es:
        B.setup(es)
        B.phase1b()
        if gdn:
            B.phase1a()
        else:
            B.phase1a_stub()
        B.phase2()
        B.phase3()
    return B.nc


IN_SPLITS = (3072, 1024, 8, 8, 1024, 256, 256, 512, 64, 8, 1024, 1024)


def host_consts():
    bf = ml_dtypes.bfloat16
    i = np.arange(128)
    c = {}
    c["identf"] = np.eye(128, dtype=np.float32)
    c["identb"] = np.eye(128).astype(bf)
    c["uinc"] = (i[:, None] <= i[None, :]).astype(np.float32)
    c["causal"] = np.where(i[None, :] <= i[:, None], 0.0, NEG).astype(np.float32)
    c["mskU"] = np.where(i[None, :] >= i[:, None], 0.0, -30000.0).astype(bf)
    c["mskL"] = np.where(i[None, :] < i[:, None], 0.0, 30000.0).astype(bf)
    return c


def rope_tab(pos, rot):
    inv = (np.float32(500000.0) ** (-np.arange(0, rot, 2, dtype=np.float32) / np.float32(rot))).astype(np.float32)
    ang = pos.astype(np.float32)[:, None] * inv[None, :]
    return np.cos(ang).astype(np.float32), np.sin(ang).astype(np.float32)


def make_in_maps(S, x, norm_mix_g, w_in, conv_w, a_log, dt_bias, gdn_norm_g, q_norm_g, k_norm_g,
                 w_out, norm_mlp_g, w_mlp_up, w_mlp_down):
    Bn = x.shape[0]
    f = np.float32
    w_in = np.asarray(w_in[0], f)
    pts = np.cumsum((0,) + IN_SPLITS)
    seg = {n: w_in[:, pts[i]:pts[i + 1]] for i, n in enumerate(
        ("qkv", "z", "ga", "gb", "aq", "ak", "av", "iq", "ik", "iw", "gatea", "gateb"))}
    w1a = np.ascontiguousarray(np.concatenate([seg["qkv"], seg["ga"], seg["gb"]], 1))
    w1b = np.ascontiguousarray(np.concatenate([seg["ak"], seg["av"], seg["ik"], seg["iw"], seg["aq"], seg["iq"], seg["gateb"], seg["z"], seg["gatea"]], 1))
    col = lambda v: np.ascontiguousarray(np.asarray(v, f).reshape(8, 128).T)
    rep = lambda v: np.ascontiguousarray(np.broadcast_to(np.asarray(v, f)[None, :], (128, len(v))))
    common = host_consts()
    common.update(dict(
        w1a=w1a, w1b=w1b, wout=np.ascontiguousarray(w_out[0], f), wup=np.ascontiguousarray(w_mlp_up[0], f),
        wdn=np.ascontiguousarray(w_mlp_down[0], f),
        gmix=col(norm_mix_g[0]), gmlp=col(norm_mlp_g[0]), gq=rep(q_norm_g[0]), gk=rep(k_norm_g[0]),
        ggdn=rep(gdn_norm_g[0]), alog=rep(a_log[0]), dtb=rep(dt_bias[0]),
        convw=np.ascontiguousarray(np.asarray(conv_w[0], f).reshape(4, 24, 128).transpose(2, 1, 0).reshape(128, 96)),
    ))
    maps = []
    for b in range(Bn):
        for c in range(2):
            m = dict(common)
            xb = np.asarray(x[b], f)
            if c == 0:
                xs = np.concatenate([np.zeros((128, D), f), xb[:S - 128]], 0)
            else:
                xs = xb
            pos = np.maximum(np.arange(S) + (c - 1) * 128, 0)
            m["xseq"] = np.ascontiguousarray(xs)
            m["cosA"], m["sinA"] = rope_tab(pos, 32)
            m["cosI"], m["sinI"] = rope_tab(pos, 16)
            m["blk0"] = np.full((128, 128), NEG if c == 0 else 0.0, f)
            maps.append(m)
    return maps


def assemble(S, results, Bn):
    out = np.zeros((Bn, S, D), np.float32)
    ov = out.reshape(Bn, S // 256, 2, 128, D)
    for b in range(Bn):
        for c in range(2):
            r = results[b * 2 + c]["out"].reshape(S // 256, 128, D)
            ov[b, :, c] = r
    return out


_NC_CACHE = {}


def kernel(**inputs):
    x = np.asarray(inputs["x"])
    Bn, S, _ = x.shape
    if S not in _NC_CACHE:
        _NC_CACHE[S] = build(S)
    nc = _NC_CACHE[S]
    maps = make_in_maps(S, **{k: np.asarray(v) for k, v in inputs.items()})
    res = run_bass_kernel_spmd(nc, maps, core_ids=list(range(2 * Bn)))
    return assemble(S, res.results, Bn)
```

```python
from contextlib import ExitStack
import numpy as np
import ml_dtypes
import concourse.bass as bass
import concourse.mybir as mybir
from concourse.bass_utils import run_bass_kernel_spmd

F32 = mybir.dt.float32
BF16 = mybir.dt.bfloat16
AF = mybir.ActivationFunctionType
ALU = mybir.AluOpType
AX = mybir.AxisListType

D = 1024
EPS = 1e-6
NEG = -1.0e30
SAME_ENGINE_SYNC = True
IDX_F32R = True
IDT = mybir.dt.float32r if IDX_F32R else F32
NEU_F32R = True
NC_ = lambda ap: ap
NDT = mybir.dt.float32r if NEU_F32R else F32
RF = (lambda ap: ap.bitcast(F32)) if NEU_F32R else (lambda ap: ap)


class Tok:
    __slots__ = ("name", "w", "r")

    def __init__(self, name):
        self.name = name
        self.w = None
        self.r = {}


class Sched:
    ENG = ("pe", "act", "dve", "pool", "sp")

    def __init__(self, nc, es, n_dma_sems=24):
        self.nc = nc
        self.sems = {}
        for e in ("pe", "act", "dve", "pool"):
            self.sems[e] = es.enter_context(nc.semaphore("s_" + e))
        self.ndma = n_dma_sems
        for k in range(n_dma_sems):
            self.sems[("d", k)] = es.enter_context(nc.semaphore("s_d%d" % k))
        self.cnt = {e: 0 for e in ("pe", "act", "dve", "pool")}
        self.dtot = [0] * n_dma_sems
        self.rr = 0
        self.waited = {e: {} for e in self.ENG}
        self.ops = {e: [] for e in self.ENG}
        self.nops = 0
        self.dry = False

    def _deps(self, eng, reads, writes):
        deps = {}

        def add(d):
            if d is None:
                return
            sid, val = d
            if sid == eng and (eng == "pe" or not SAME_ENGINE_SYNC):
                return
            if deps.get(sid, 0) < val:
                deps[sid] = val

        for t in reads:
            add(t.w)
        for t in writes:
            add(t.w)
            for sid, val in t.r.items():
                add((sid, val))
        out = []
        wd = self.waited[eng]
        for sid, val in deps.items():
            if wd.get(sid, 0) < val:
                wd[sid] = val
                out.append((sid, val))
        return out

    def _mark(self, me, reads, writes):
        for t in reads:
            if t.r.get(me[0], 0) < me[1]:
                t.r[me[0]] = me[1]
        for t in writes:
            t.w = me
            t.r = {}

    def op(self, eng, fn, reads=(), writes=()):
        if self.dry:
            return
        waits = self._deps(eng, reads, writes)
        self.cnt[eng] += 1
        me = (eng, self.cnt[eng])
        self._mark(me, reads, writes)
        self.ops[eng].append((waits, fn, eng, 1))
        self.nops += 1

    def dma(self, fn, reads=(), writes=()):
        if self.dry:
            return
        k = self.rr
        self.rr = (self.rr + 1) % self.ndma
        sid = ("d", k)
        waits = self._deps("sp", reads, writes)
        wd = self.waited["sp"]
        if wd.get(sid, 0) < self.dtot[k]:
            wd[sid] = self.dtot[k]
            waits.append((sid, self.dtot[k]))
        self.dtot[k] += 16
        me = (sid, self.dtot[k])
        self._mark(me, reads, writes)
        self.ops["sp"].append((waits, fn, sid, 16))
        self.nops += 1

    def barrier(self):
        for e in self.ENG:
            waits = []
            wd = self.waited[e]
            for c in ("pe", "act", "dve", "pool"):
                if c != e and wd.get(c, 0) < self.cnt[c]:
                    wd[c] = self.cnt[c]
                    waits.append((c, self.cnt[c]))
            for k in range(self.ndma):
                sid = ("d", k)
                if wd.get(sid, 0) < self.dtot[k]:
                    wd[sid] = self.dtot[k]
                    waits.append((sid, self.dtot[k]))
            if waits:
                self.ops[e].append((waits, None, None, 0))

    def emit(self):
        nc = self.nc
        ops = self.ops
        self.ops = {e: [] for e in self.ENG}
        sems = self.sems

        def run(engh, lst):
            for waits, fn, sid, inc in lst:
                for s, v in waits:
                    engh.wait_ge(sems[s], v)
                if fn is not None:
                    fn(engh).then_inc(sems[sid], inc)

        with nc.Block() as block:
            @block.tensor
            def _(e):
                run(e, ops["pe"])

            @block.scalar
            def _(e):
                run(e, ops["act"])

            @block.vector
            def _(e):
                run(e, ops["dve"])

            @block.gpsimd
            def _(e):
                run(e, ops["pool"])

            @block.sync
            def _(e):
                run(e, ops["sp"])


class Ctx:
    pass


def bc(ap, shape):
    return ap.to_broadcast(list(shape))


class TL:
    def __init__(self, t, name):
        self.t = t
        self.k = Tok(name)

    def __getitem__(self, key):
        return self.t[key]


C1B = 5192
C1A = 3088


class Builder:
    def __init__(self, S, debug=False):
        self.Sq = S
        self.P = S // 128
        self.NO = self.P // 2
        self.debug = debug
        self.nc = bass.Bass("TRN2", target_bir_lowering=False)

    def dram_in(self, name, shape, dt=F32):
        return self.nc.dram_tensor(name, list(shape), dt, kind="ExternalInput").ap()

    def dram_scr(self, name, shape, dt):
        kind = "ExternalOutput" if self.debug else "Internal"
        return self.nc.dram_tensor(name, list(shape), dt, kind=kind).ap()

    def sb(self, es, name, shape, dt):
        return TL(es.enter_context(self.nc.sbuf_tensor("t_" + name, list(shape), dt)), name)

    def load(self, dst, dst_ap, src_ap):
        self.S.dma(lambda e: e.dma_start(out=dst_ap, in_=src_ap), writes=[dst.k])

    def rsqrt_small(self, v, n):
        S = self.S
        S.op("act", lambda e: e.activation(out=v[:, 0:n], in_=v[:, 0:n], func=AF.Ln), reads=[v.k], writes=[v.k])
        S.op("act", lambda e: e.activation(out=v[:, 0:n], in_=v[:, 0:n], func=AF.Exp, scale=-0.5), reads=[v.k], writes=[v.k])

    def load_weight_bf16(self, es, wb, w_ap, ncols, stg):
        S = self.S
        nk = w_ap.shape[0] // 128
        CH = stg[0].t.shape[1]
        i = 0
        for k in range(nk):
            for c0 in range(0, ncols, CH):
                cw = min(CH, ncols - c0)
                st = stg[i % len(stg)]
                S.dma(lambda e, st=st, k=k, c0=c0, cw=cw: e.dma_start(out=st[:, 0:cw], in_=w_ap[k * 128:(k + 1) * 128, c0:c0 + cw]), writes=[st.k])
                eng = ("pool", "dve", "act")[i % 3] if False else "pool"
                S.op(eng, lambda e, st=st, k=k, c0=c0, cw=cw: e.tensor_copy(out=wb[:, k, c0:c0 + cw], in_=st[:, 0:cw]), reads=[st.k], writes=[wb.k])
                i += 1

    def rms_hT(self, xb, ss, xnb, hT, gcol, psb):
        S = self.S
        junk = self.junk
        idb = self.identb
        S.op("act", lambda e: e.activation(out=junk[:, 0:1024], in_=xb[:], func=AF.Square, accum_out=ss[:, 0:1]), reads=[xb.k], writes=[junk.k, ss.k])
        S.op("dve", lambda e: e.tensor_scalar(out=ss[:, 0:1], in0=ss[:, 0:1], scalar1=1.0 / 1024, scalar2=EPS, op0=ALU.mult, op1=ALU.add), reads=[ss.k], writes=[ss.k])
        self.rsqrt_small(ss, 1)
        S.op("dve", lambda e: e.tensor_scalar(out=xnb[:], in0=xb[:], scalar1=ss[:, 0:1], scalar2=None, op0=ALU.mult), reads=[ss.k, xb.k], writes=[xnb.k])

        def tr(e):
            for kc in range(8):
                i = e.transpose(out=psb[:, kc * 128:(kc + 1) * 128], in_=xnb[:, kc * 128:(kc + 1) * 128], identity=idb[:])
            return i
        S.op("pe", tr, reads=[xnb.k, idb.k], writes=[psb.k])
        S.op("dve", lambda e: e.tensor_tensor(out=hT[:], in0=psb[:].rearrange("p (k t) -> p k t", k=8), in1=bc(gcol[:].unsqueeze(2), [128, 8, 128]), op=ALU.mult), reads=[psb.k, gcol.k], writes=[hT.k])

    def proj(self, ps, ncol, hT, wb, c0, pcol=0, start=True):
        def f(e):
            for kc in range(8):
                i = e.matmul(ps[:, pcol:pcol + ncol], lhsT=hT[:, kc, :], rhs=wb[:, kc, c0:c0 + ncol], start=(kc == 0), stop=(kc == 7))
            return i
        self.S.op("pe", f, reads=[hT.k, wb.k], writes=[ps.k])

    def setup(self, es):
        nc = self.nc
        S_, P, NO = self.Sq, self.P, self.NO
        self.S = Sched(nc, es)
        d = self.dram_in
        self.i_x = d("xseq", [S_, D])
        self.i_w1a = d("w1a", [D, C1A])
        self.i_w1b = d("w1b", [D, C1B])
        self.i_wout = d("wout", [D, D])
        self.i_wup = d("wup", [D, 4 * D])
        self.i_wdn = d("wdn", [4 * D, D])
        self.i_cosA = d("cosA", [S_, 16]); self.i_sinA = d("sinA", [S_, 16])
        self.i_cosI = d("cosI", [S_, 8]); self.i_sinI = d("sinI", [S_, 8])
        self.i_blk0 = d("blk0", [128, 128])
        self.i_identf = d("identf", [128, 128]); self.i_identb = d("identb", [128, 128], BF16)
        self.i_uinc = d("uinc", [128, 128]); self.i_causal = d("causal", [128, 128])
        self.i_mskU = d("mskU", [128, 128], BF16); self.i_mskL = d("mskL", [128, 128], BF16)
        self.i_gmix = d("gmix", [128, 8]); self.i_gmlp = d("gmlp", [128, 8])
        self.i_gq = d("gq", [128, 128]); self.i_gk = d("gk", [128, 128]); self.i_ggdn = d("ggdn", [128, 128])
        self.i_alog = d("alog", [128, 8]); self.i_dtb = d("dtb", [128, 8])
        self.i_convw = d("convw", [128, 24 * 4])
        self.o_out = nc.dram_tensor("out", [NO * 128, D], F32, kind="ExternalOutput").ap()
        s = self.dram_scr
        self.s_akT = s("s_akT", [128, 2 * S_], BF16)
        self.s_v = s("s_v", [S_, 258], BF16)
        self.s_ikT = s("s_ikT", [128, S_], F32)
        self.s_aqT = s("s_aqT", [NO, 128, 1024], BF16)
        self.s_iqT = s("s_iqT", [NO, 128, 512], F32)
        self.s_iw = s("s_iw", [NO, 128, 8], F32)
        self.s_sgb = s("s_sgb", [NO, 128, 1024], BF16)
        self.s_mg = s("s_mg", [NO, 128, 1024], BF16)
        self.s_zg = s("s_zg", [NO, 128, 1024], BF16)
        self.s_mrg = s("s_mrg", [NO, 128, 1024], BF16)
        self.k_akT = Tok("akT_d"); self.k_v = Tok("v_d"); self.k_ikT = Tok("ikT_d")
        self.k_own = [{n: Tok(n + str(j)) for n in ("aqT", "iqT", "iw", "sgb", "mg", "x1", "zg")} for j in range(NO)]
        self.ps = [TL(es.enter_context(nc.psum_tensor("ps%d" % i, [128, 512], F32)), "ps%d" % i) for i in range(7)]
        self.psb = TL(es.enter_context(nc.psum_tensor("psb", [128, 1024], BF16)), "psb")
        sb = lambda n, sh, dt: self.sb(es, n, sh, dt)
        self.identf = sb("identf", [128, 128], F32); self.identb = sb("identb", [128, 128], BF16)
        self.junk = sb("junk", [128, 1024], F32)
        for t, src in ((self.identf, self.i_identf), (self.identb, self.i_identb)):
            self.load(t, t[:], src)

    def norm_rope(self, H, src_ap, src_tok, gain, extra, out, ocol, cs, sn, tmp):
        S = self.S
        tf, sq, ssh, ra, rb = tmp
        n = H * 128
        S.op("act", lambda e: e.activation(out=tf[:, 0:n], in_=src_ap, func=AF.Copy), reads=[src_tok], writes=[tf.k])
        S.op("dve", lambda e: e.tensor_tensor(out=sq[:, 0:n], in0=tf[:, 0:n], in1=tf[:, 0:n], op=ALU.mult), reads=[tf.k], writes=[sq.k])
        S.op("dve", lambda e: e.tensor_reduce(out=ssh[:, 0:H], in_=sq[:, 0:n].rearrange("p (h d) -> p h d", h=H), axis=AX.X, op=ALU.add), reads=[sq.k], writes=[ssh.k])
        S.op("dve", lambda e: e.tensor_scalar(out=ssh[:, 0:H], in0=ssh[:, 0:H], scalar1=1.0 / 128, scalar2=EPS, op0=ALU.mult, op1=ALU.add), reads=[ssh.k], writes=[ssh.k])
        self.rsqrt_small(ssh, H)
        if extra != 1.0:
            S.op("dve", lambda e: e.tensor_scalar(out=ssh[:, 0:H], in0=ssh[:, 0:H], scalar1=extra, scalar2=None, op0=ALU.mult), reads=[ssh.k], writes=[ssh.k])
        tf3 = tf[:, 0:n].rearrange("p (h d) -> p h d", h=H)
        S.op("dve", lambda e: e.tensor_tensor(out=tf3, in0=tf3, in1=bc(ssh[:, 0:H].unsqueeze(2), [128, H, 128]), op=ALU.mult), reads=[tf.k, ssh.k], writes=[tf.k])
        S.op("pool", lambda e: e.tensor_tensor(out=tf3, in0=tf3, in1=bc(gain[:].unsqueeze(1), [128, H, 128]), op=ALU.mult), reads=[tf.k, gain.k], writes=[tf.k])
        o3 = out[:, ocol:ocol + n].rearrange("p (h d) -> p h d", h=H)
        S.op("act", lambda e: e.activation(out=out[:, ocol:ocol + n], in_=tf[:, 0:n], func=AF.Copy), reads=[tf.k], writes=[out.k])
        self.rope(tf3, tf.k, o3, out.k, H, 16, cs, sn, ra, rb)

    def rope(self, t3, ttok, o3, otok, H, hf, cs, sn, ra, rb):
        S = self.S
        x1 = t3[:, :, 0:hf]; x2 = t3[:, :, hf:2 * hf]
        c = bc(cs[:].unsqueeze(1), [128, H, hf]); s_ = bc(sn[:].unsqueeze(1), [128, H, hf])
        a3 = ra[:, 0:H * hf].rearrange("p (h d) -> p h d", h=H)
        b3 = rb[:, 0:H * hf].rearrange("p (h d) -> p h d", h=H)
        S.op("pool", lambda e: e.tensor_tensor(out=a3, in0=x1, in1=c, op=ALU.mult), reads=[ttok, cs.k], writes=[ra.k])
        S.op("pool", lambda e: e.tensor_tensor(out=b3, in0=x2, in1=s_, op=ALU.mult), reads=[ttok, sn.k], writes=[rb.k])
        S.op("dve", lambda e: e.tensor_tensor(out=o3[:, :, 0:hf], in0=a3, in1=b3, op=ALU.subtract), reads=[ra.k, rb.k], writes=[otok])
        S.op("pool", lambda e: e.tensor_tensor(out=a3, in0=x2, in1=c, op=ALU.mult), reads=[ttok, cs.k], writes=[ra.k])
        S.op("pool", lambda e: e.tensor_tensor(out=b3, in0=x1, in1=s_, op=ALU.mult), reads=[ttok, sn.k], writes=[rb.k])
        S.op("dve", lambda e: e.tensor_tensor(out=o3[:, :, hf:2 * hf], in0=a3, in1=b3, op=ALU.add), reads=[ra.k, rb.k], writes=[otok])

    def phase1b(self):
        S = self.S
        P = self.P
        ps, psb = self.ps, self.psb
        with ExitStack() as es:
            sb = lambda n, sh, dt: self.sb(es, n, sh, dt)
            wb = sb("w1b", [128, 8, C1B], BF16)
            stg = [sb("stg%d" % i, [128, 1024], F32) for i in range(2)]
            gmix = sb("gmix", [128, 8], F32); gq = sb("gq", [128, 128], F32); gk = sb("gk", [128, 128], F32)
            self.load(gmix, gmix[:], self.i_gmix); self.load(gq, gq[:], self.i_gq); self.load(gk, gk[:], self.i_gk)
            self.load_weight_bf16(es, wb, self.i_w1b, C1B, stg)
            xb = [sb("xb%d" % i, [128, 1024], F32) for i in range(2)]
            ss = sb("ss", [128, 1], F32); xnb = sb("xnb", [128, 1024], BF16); hT = sb("hT", [128, 8, 128], BF16)
            cA = sb("cA", [128, 16], F32); sA = sb("sA", [128, 16], F32); cI = sb("cI", [128, 8], F32); sI = sb("sI", [128, 8], F32)
            tmp = (sb("tf", [128, 512], F32), sb("sq", [128, 512], F32), sb("ssh", [128, 8], F32), sb("ra", [128, 64], F32), sb("rb", [128, 64], F32))
            ra, rb = tmp[3], tmp[4]
            akb = sb("akb", [128, 256], BF16); akT = sb("akTb", [128, 256], BF16)
            vaug = sb("vaug", [128, 258], BF16)
            ikf = sb("ikf", [128, 72], F32); ik2 = sb("ik2", [128, 128], F32); ikT = sb("ikTb", [128, 128], F32)
            iw = sb("iwb", [128, 8], F32)
            aqb = sb("aqb", [128, 1024], BF16); aqT = sb("aqTb", [128, 1024], BF16)
            iqf = sb("iqf", [128, 512], F32); iqo = sb("iqo", [128, 512], F32); iqT = sb("iqTb", [128, 512], F32)
            sgb = sb("sgbb", [128, 1024], BF16)
            zs = sb("zs", [128, 1024], F32); sga = sb("sga", [128, 1024], F32); zg = sb("zgb", [128, 1024], BF16)
            S.op("pool", lambda e: e.memset(vaug[:], 1.0), writes=[vaug.k])
            for p in range(P):
                own = (p % 2 == 1)
                j = p // 2
                x = xb[p % 2]
                r0 = p * 128
                self.load(x, x[:], self.i_x[r0:r0 + 128, :])
                self.load(cA, cA[:], self.i_cosA[r0:r0 + 128, :]); self.load(sA, sA[:], self.i_sinA[r0:r0 + 128, :])
                self.load(cI, cI[:], self.i_cosI[r0:r0 + 128, :]); self.load(sI, sI[:], self.i_sinI[r0:r0 + 128, :])
                self.rms_hT(x, ss, xnb, hT, gmix, psb)
                self.proj(ps[0], 512, hT, wb, 0)
                self.norm_rope(2, ps[0][:, 0:256], ps[0].k, gk, 1.0, akb, 0, cA, sA, tmp)
                S.op("act", lambda e: e.activation(out=vaug[:].rearrange("p (c d) -> p c d", c=2)[:, :, 0:128], in_=ps[0][:, 256:512].rearrange("p (c d) -> p c d", c=2), func=AF.Copy), reads=[ps[0].k], writes=[vaug.k])
                S.dma(lambda e, r0=r0: e.dma_start(out=self.s_v[r0:r0 + 128, :], in_=vaug[:]), reads=[vaug.k], writes=[self.k_v])

                def trk(e):
                    for c in range(2):
                        i = e.transpose(out=psb[:, c * 128:(c + 1) * 128], in_=akb[:, c * 128:(c + 1) * 128], identity=self.identb[:])
                    return i
                S.op("pe", trk, reads=[akb.k, self.identb.k], writes=[psb.k])
                S.op("act", lambda e: e.activation(out=akT[:], in_=psb[:, 0:256], func=AF.Copy), reads=[psb.k], writes=[akT.k])
                S.dma(lambda e, r0=r0: e.dma_start(out=self.s_akT.rearrange("h (c s) -> h c s", c=2)[:, :, r0:r0 + 128], in_=akT[:].rearrange("h (c s) -> h c s", c=2)), reads=[akT.k], writes=[self.k_akT])
                self.proj(ps[1], 72, hT, wb, 512)
                S.op("act", lambda e: e.activation(out=ikf[:], in_=ps[1][:, 0:72], func=AF.Copy), reads=[ps[1].k], writes=[ikf.k])
                S.op("dve", lambda e: e.tensor_copy(out=ik2[:, 0:64], in_=ikf[:, 0:64]), reads=[ikf.k], writes=[ik2.k])
                self.rope(ikf[:, 0:64].rearrange("p (h d) -> p h d", h=1), ikf.k, ik2[:, 0:64].rearrange("p (h d) -> p h d", h=1), ik2.k, 1, 8, cI, sI, ra, rb)
                S.op("dve", lambda e: e.tensor_copy(out=ik2[:, 64:128], in_=ik2[:, 0:64]), reads=[ik2.k], writes=[ik2.k])
                S.op("pe", lambda e: e.transpose(out=ps[2][:, 0:128], in_=ik2[:], identity=self.identf[:]), reads=[ik2.k, self.identf.k], writes=[ps[2].k])
                S.op("act", lambda e: e.activation(out=ikT[:], in_=ps[2][:, 0:128], func=AF.Copy), reads=[ps[2].k], writes=[ikT.k])
                S.dma(lambda e, r0=r0: e.dma_start(out=self.s_ikT[:, r0:r0 + 128], in_=ikT[:]), reads=[ikT.k], writes=[self.k_ikT])
                if not own:
                    continue
                ko = self.k_own[j]
                S.op("dve", lambda e: e.tensor_scalar(out=iw[:], in0=ikf[:, 64:72], scalar1=float(512.0 ** -0.5), scalar2=None, op0=ALU.mult), reads=[ikf.k], writes=[iw.k])
                S.dma(lambda e, j=j: e.dma_start(out=self.s_iw[j], in_=iw[:]), reads=[iw.k], writes=[ko["iw"]])
                for half in range(2):
                    pq = ps[3 + half]
                    self.proj(pq, 512, hT, wb, 584 + half * 512)
                    self.norm_rope(4, pq[:, 0:512], pq.k, gq, float(128.0 ** -0.5), aqb, half * 512, cA, sA, tmp)

                def trq(e):
                    for h in range(8):
                        i = e.transpose(out=psb[:, h * 128:(h + 1) * 128], in_=aqb[:, h * 128:(h + 1) * 128], identity=self.identb[:])
                    return i
                S.op("pe", trq, reads=[aqb.k, self.identb.k], writes=[psb.k])
                S.op("act", lambda e: e.activation(out=aqT[:], in_=psb[:], func=AF.Copy), reads=[psb.k], writes=[aqT.k])
                S.dma(lambda e, j=j: e.dma_start(out=self.s_aqT[j], in_=aqT[:]), reads=[aqT.k], writes=[ko["aqT"]])
                self.proj(ps[5], 512, hT, wb, 1608)
                S.op("act", lambda e: e.activation(out=iqf[:], in_=ps[5][:], func=AF.Copy), reads=[ps[5].k], writes=[iqf.k])
                S.op("dve", lambda e: e.tensor_copy(out=iqo[:], in_=iqf[:]), reads=[iqf.k], writes=[iqo.k])
                self.rope(iqf[:].rearrange("p (h d) -> p h d", h=8), iqf.k, iqo[:].rearrange("p (h d) -> p h d", h=8), iqo.k, 8, 8, cI, sI, ra, rb)

                def tri(e):
                    for g in range(4):
                        i = e.transpose(out=ps[6][:, g * 128:(g + 1) * 128], in_=iqo[:, g * 128:(g + 1) * 128], identity=self.identf[:])
                    return i
                S.op("pe", tri, reads=[iqo.k, self.identf.k], writes=[ps[6].k])
                S.op("act", lambda e: e.activation(out=iqT[:], in_=ps[6][:], func=AF.Copy), reads=[ps[6].k], writes=[iqT.k])
                S.dma(lambda e, j=j: e.dma_start(out=self.s_iqT[j], in_=iqT[:]), reads=[iqT.k], writes=[ko["iqT"]])
                for half in range(2):
                    pq = ps[half]
                    self.proj(pq, 512, hT, wb, 2120 + half * 512)
                    S.op("act", lambda e, pq=pq, half=half: e.activation(out=sgb[:, half * 512:(half + 1) * 512], in_=pq[:], func=AF.Sigmoid), reads=[pq.k], writes=[sgb.k])
                S.dma(lambda e, j=j: e.dma_start(out=self.s_sgb[j], in_=sgb[:]), reads=[sgb.k], writes=[ko["sgb"]])
                for half in range(2):
                    pq = ps[2 + half]; pg = ps[4 + half]
                    self.proj(pq, 512, hT, wb, 3144 + half * 512)
                    self.proj(pg, 512, hT, wb, 4168 + half * 512)
                    S.op("act", lambda e, pq=pq, half=half: e.activation(out=zs[:, half * 512:(half + 1) * 512], in_=pq[:], func=AF.Silu), reads=[pq.k], writes=[zs.k])
                    S.op("act", lambda e, pg=pg, half=half: e.activation(out=sga[:, half * 512:(half + 1) * 512], in_=pg[:], func=AF.Sigmoid), reads=[pg.k], writes=[sga.k])
                S.op("pool", lambda e: e.tensor_tensor(out=zg[:], in0=zs[:], in1=sga[:], op=ALU.mult), reads=[zs.k, sga.k], writes=[zg.k])
                S.dma(lambda e, j=j: e.dma_start(out=self.s_zg[j], in_=zg[:]), reads=[zg.k], writes=[ko["zg"]])
            S.barrier()
            S.emit()

    def phase2(self, NBIS=20):
        S = self.S
        P, NO, S_ = self.P, self.NO, self.Sq
        ps, psb = self.ps, self.psb
        KSEL = float(min(256, S_ // 4)) - 0.5
        with ExitStack() as es:
            sb = lambda n, sh, dt: self.sb(es, n, sh, dt)
            akT = sb("akT", [128, 2, S_], BF16)
            vall = sb("vall", [128, P, 258], BF16)
            H2 = S_ // 2
            assert H2 % 512 == 0
            ikT = sb("ikT", [128, H2], IDT)
            scores = [sb("score%d" % i, [128, S_], F32) for i in range(2)]
            msk = sb("msk", [128, S_], BF16)
            jk = sb("jk", [128, S_], mybir.dt.uint8)
            gq = sb("gq2", [128, 128], F32); gk = sb("gk2", [128, 128], F32)
            negM = sb("negM", [128, 2], F32)
            causal = sb("causal", [128, 128], F32); blk0 = sb("blk0", [128, 128], F32)
            self.load(gq, gq[:], self.i_gq); self.load(gk, gk[:], self.i_gk)
            self.load(causal, causal[:], self.i_causal); self.load(blk0, blk0[:], self.i_blk0)
            S.dma(lambda e: e.dma_start(out=akT[:], in_=self.s_akT.rearrange("h (c s) -> h c s", c=2)), reads=[self.k_akT], writes=[akT.k])
            S.dma(lambda e: e.dma_start(out=vall[:], in_=self.s_v.rearrange("(p t) n -> t p n", t=128)), reads=[self.k_v], writes=[vall.k])
            stg_ik = scores[1]
            S.dma(lambda e: e.dma_start(out=stg_ik[0:64, 0:H2], in_=self.s_ikT[0:64, 0:H2]), reads=[self.k_ikT], writes=[stg_ik.k])
            S.dma(lambda e: e.dma_start(out=stg_ik[64:128, 0:H2], in_=self.s_ikT[64:128, H2:S_]), reads=[self.k_ikT], writes=[stg_ik.k])
            for c0 in range(0, H2, 512):
                S.op("pool" if (c0 // 512) % 2 else "act", (lambda e, c0=c0: e.tensor_copy(out=ikT[:, c0:c0 + 512], in_=stg_ik[:, c0:c0 + 512])) if (c0 // 512) % 2 else (lambda e, c0=c0: e.activation(out=ikT[:, c0:c0 + 512], in_=stg_ik[:, c0:c0 + 512], func=AF.Copy)), reads=[stg_ik.k], writes=[ikT.k])
            S.op("dve", lambda e: e.tensor_reduce(out=negM[:, 0:1], in_=gq[:], axis=AX.X, op=ALU.max, apply_absolute_value=True), reads=[gq.k], writes=[negM.k])
            S.op("dve", lambda e: e.tensor_reduce(out=negM[:, 1:2], in_=gk[:], axis=AX.X, op=ALU.max, apply_absolute_value=True), reads=[gk.k], writes=[negM.k])
            S.op("dve", lambda e: e.scalar_tensor_tensor(out=negM[:, 0:1], in0=negM[:, 0:1], scalar=float(-(128.0 ** 0.5)), in1=negM[:, 1:2], op0=ALU.mult, op1=ALU.mult), reads=[negM.k], writes=[negM.k])
            aqT = sb("aqT", [128, 1024], BF16); iqT = sb("iqT", [128, 512], IDT); iw = sb("iw", [128, 8], F32)
            iqT2 = sb("iqT2", [128, 512], IDT)
            sgb = sb("sgb", [128, 1024], BF16); mg = sb("mg", [128, 1024], BF16)
            rl = [sb("rl%d" % i, [128, 512], F32) for i in range(2)]
            pT = [sb("pT%d" % i, [128, 512], BF16) for i in range(2)]
            mT = [sb("mT%d" % i, [128, 1024], BF16) for i in range(2)]
            sts = [sb("bst%d" % i, [128, 8], F32) for i in range(3)]
            rec = sb("rec", [128, 8], F32)
            oatt = self.junk
            mrg = sb("mrg", [128, 1024], BF16)
            accb = [ps[4], ps[5], ps[6]]
            hb = [(0, 0), (0, 1), (0, 2), (1, 0), (1, 1), (1, 2), (2, 0), (2, 1)]

            sctok = [[Tok("sc%d_%d" % (i, g)) for g in range(S_ // 512)] for i in range(2)]
            IDXC = lambda ap: ap

            def gI(j):
                p = 2 * j + 1
                nkeys = (p + 1) * 128
                ko = self.k_own[j]
                score = scores[j % 2]; st = sts[j % 3]
                jq = self.junk
                S.dma(lambda e: e.dma_start(out=jq[:, 0:512], in_=self.s_iqT[j]), reads=[ko["iqT"]], writes=[jq.k])
                S.dma(lambda e: e.dma_start(out=jq[0:64, 512:1024], in_=self.s_iqT[j][64:128, :]), reads=[ko["iqT"]], writes=[jq.k])
                S.dma(lambda e: e.dma_start(out=jq[64:128, 512:1024], in_=self.s_iqT[j][0:64, :]), reads=[ko["iqT"]], writes=[jq.k])
                S.op("pool", lambda e: e.tensor_copy(out=iqT[:], in_=jq[:, 0:512]), reads=[jq.k], writes=[iqT.k])
                S.op("pool", lambda e: e.tensor_copy(out=iqT2[:], in_=jq[:, 512:1024]), reads=[jq.k], writes=[iqT2.k])
                S.dma(lambda e: e.dma_start(out=iw[:], in_=self.s_iw[j]), reads=[ko["iw"]], writes=[iw.k])
                it = 0
                nkg = (nkeys + 511) // 512
                for h in range(8):
                    for kg in range(nkg):
                        k0 = kg * 512
                        w = min(512, nkeys - k0)
                        sk = score.k
                        pb = ps[it % 2]; r = rl[it % 2]; it += 1
                        b0 = 64 if k0 >= H2 else 0
                        kk0 = k0 - (H2 if k0 >= H2 else 0)
                        qsrc = iqT if (h % 2) * 64 == b0 else iqT2
                        S.op("pe", lambda e, pb=pb, h=h, b0=b0, kk0=kk0, w=w, qsrc=qsrc: e.matmul(pb[:, 0:w], lhsT=IDXC(qsrc[b0:b0 + 64, (h // 2) * 128:(h // 2 + 1) * 128]), rhs=IDXC(ikT[b0:b0 + 64, kk0:kk0 + w]), start=True, stop=True), reads=[iqT.k, iqT2.k, ikT.k], writes=[pb.k])
                        S.op("act", lambda e, pb=pb, r=r, w=w: e.activation(out=r[:, 0:w], in_=pb[:, 0:w], func=AF.Relu), reads=[pb.k], writes=[r.k])
                        if h == 0:
                            S.op("dve", lambda e, r=r, k0=k0, w=w: e.tensor_scalar(out=score[:, k0:k0 + w], in0=r[:, 0:w], scalar1=iw[:, 0:1], scalar2=None, op0=ALU.mult), reads=[r.k, iw.k], writes=[sk])
                        else:
                            S.op("dve", lambda e, r=r, k0=k0, w=w, h=h: e.scalar_tensor_tensor(out=score[:, k0:k0 + w], in0=r[:, 0:w], scalar=iw[:, h:h + 1], in1=score[:, k0:k0 + w], op0=ALU.mult, op1=ALU.add), reads=[r.k, iw.k, sk], writes=[sk])
                        yield
                allk = sctok[j % 2][0:nkg]
                S.op("dve", lambda e: e.tensor_copy(out=st[:, 7:8], in_=st[:, 7:8]), reads=allk + [st.k], writes=[score.k, st.k])
                S.op("dve", lambda e: e.tensor_reduce(out=st[:, 5:6], in_=score[:, 0:nkeys], axis=AX.X, op=ALU.max), reads=[score.k], writes=[st.k])
                S.op("dve", lambda e: e.tensor_reduce(out=st[:, 6:7], in_=score[:, 0:nkeys], axis=AX.X, op=ALU.min), reads=[score.k], writes=[st.k])
                S.op("dve", lambda e: e.tensor_scalar(out=st[:, 0:1], in0=st[:, 6:7], scalar1=-1.0, scalar2=None, op0=ALU.add), reads=[st.k], writes=[st.k])
                S.op("dve", lambda e: e.tensor_tensor(out=st[:, 1:2], in0=st[:, 5:6], in1=st[:, 0:1], op=ALU.subtract), reads=[st.k], writes=[st.k])
                S.op("dve", lambda e: e.tensor_tensor(out=score[:, nkeys - 128:nkeys], in0=score[:, nkeys - 128:nkeys], in1=causal[:], op=ALU.add), reads=[score.k, causal.k], writes=[score.k])
                S.op("dve", lambda e: e.tensor_tensor(out=score[:, 0:128], in0=score[:, 0:128], in1=blk0[:], op=ALU.add), reads=[score.k, blk0.k], writes=[score.k])
                yield

            def gB(j):
                nkeys = (2 * j + 2) * 128
                score = scores[j % 2]; st = sts[j % 3]
                for it_ in range(1, NBIS + 1):
                    f = float(2.0 ** -it_)
                    S.op("dve", lambda e, f=f: e.scalar_tensor_tensor(out=st[:, 2:3], in0=st[:, 1:2], scalar=f, in1=st[:, 0:1], op0=ALU.mult, op1=ALU.add), reads=[st.k], writes=[st.k])
                    S.op("dve", lambda e: e.tensor_scalar(out=jk[:, 0:nkeys], in0=score[:, 0:nkeys], scalar1=st[:, 2:3], scalar2=0.0, op0=ALU.is_gt, op1=ALU.add, accum_out=st[:, 3:4]), reads=[score.k, st.k], writes=[jk.k, st.k])
                    S.op("dve", lambda e, f=f: e.tensor_scalar(out=st[:, 4:5], in0=st[:, 3:4], scalar1=KSEL, scalar2=f, op0=ALU.is_gt, op1=ALU.mult), reads=[st.k], writes=[st.k])
                    S.op("dve", lambda e: e.scalar_tensor_tensor(out=st[:, 0:1], in0=st[:, 4:5], scalar=st[:, 1:2], in1=st[:, 0:1], op0=ALU.mult, op1=ALU.add), reads=[st.k], writes=[st.k])
                    yield

            def gA(j):
                p = 2 * j + 1
                nk = p + 1
                nkeys = nk * 128
                ko = self.k_own[j]
                score = scores[j % 2]; st = sts[j % 3]
                S.op("dve", lambda e: e.tensor_scalar(out=msk[:, 0:nkeys], in0=score[:, 0:nkeys], scalar1=st[:, 0:1], scalar2=None, op0=ALU.is_gt), reads=[score.k, st.k], writes=[msk.k])
                S.dma(lambda e: e.dma_start(out=aqT[:], in_=self.s_aqT[j]), reads=[ko["aqT"]], writes=[aqT.k])
                S.dma(lambda e: e.dma_start(out=sgb[:], in_=self.s_sgb[j]), reads=[ko["sgb"]], writes=[sgb.k])
                S.dma(lambda e: e.dma_start(out=mg[:], in_=self.s_mg[j]), reads=[ko["mg"]], writes=[mg.k])
                yield
                units = [(kb, c) for kb in range(nk) for c in range(2)]

                def emit_st(u):
                    kb, c = units[u]
                    pss = ps[2 + u % 2]
                    S.op("pe", lambda e: e.matmul(pss[:], lhsT=akT[:, c, kb * 128:(kb + 1) * 128], rhs=aqT[:, c * 512:(c + 1) * 512], start=True, stop=True), reads=[akT.k, aqT.k], writes=[pss.k])

                def emit_mask(kb):
                    m = mT[(kb // 8) % 2]
                    nb = min(8, nk - kb)

                    def trm(e):
                        for i_ in range(nb):
                            ins = e.transpose(out=psb[:, i_ * 128:(i_ + 1) * 128], in_=msk[:, (kb + i_) * 128:(kb + i_ + 1) * 128], identity=self.identb[:])
                        return ins
                    S.op("pe", trm, reads=[msk.k, self.identb.k], writes=[psb.k])
                    S.op("act", lambda e: e.activation(out=m[:, 0:nb * 128], in_=psb[:, 0:nb * 128], func=AF.Copy), reads=[psb.k], writes=[m.k])

                emit_mask(0)
                emit_st(0)
                yield
                for u, (kb, c) in enumerate(units):
                    pss = ps[2 + u % 2]; pt = pT[u % 2]
                    m = mT[(kb // 8) % 2]
                    if u + 1 < len(units):
                        if units[u + 1][1] == 0 and units[u + 1][0] % 8 == 0:
                            emit_mask(units[u + 1][0])
                        emit_st(u + 1)
                    S.op("act", lambda e, pss=pss, pt=pt: e.activation(out=pt[:], in_=pss[:], func=AF.Exp, bias=negM[:, 0:1], scale=1.0), reads=[pss.k, negM.k], writes=[pt.k])
                    mi = (kb % 8) * 128
                    S.op("pool", lambda e, pt=pt, m=m, mi=mi: e.tensor_tensor(out=pt[:].rearrange("p (h q) -> p h q", h=4), in0=pt[:].rearrange("p (h q) -> p h q", h=4), in1=bc(m[:, mi:mi + 128].unsqueeze(1), [128, 4, 128]), op=ALU.mult), reads=[pt.k, m.k], writes=[pt.k])
                    yield

                    def pv(e, pt=pt, c=c, kb=kb):
                        for hh in range(4):
                            h = c * 4 + hh
                            bk, sl = hb[h]
                            ins = e.matmul(accb[bk][:, sl * 129:(sl + 1) * 129], lhsT=pt[:, hh * 128:(hh + 1) * 128], rhs=vall[:, kb, c * 129:(c + 1) * 129], start=(kb == 0 and sl == 0), stop=(kb == nk - 1), skip_group_check=True)
                        return ins
                    S.op("pe", pv, reads=[pt.k, vall.k], writes=[accb[hb[c * 4][0]].k, accb[hb[c * 4 + 3][0]].k])
                for bk, (h0, n) in enumerate(((0, 3), (3, 3), (6, 2))):
                    a3 = accb[bk][:, 0:n * 129].rearrange("p (h d) -> p h d", h=n)
                    S.op("dve", lambda e, a3=a3, h0=h0, n=n: e.reciprocal(out=rec[:, h0:h0 + n], in_=a3[:, :, 128]), reads=[accb[bk].k], writes=[rec.k])
                    S.op("dve", lambda e, a3=a3, h0=h0, n=n: e.tensor_tensor(out=oatt[:, h0 * 128:(h0 + n) * 128].rearrange("p (h d) -> p h d", h=n), in0=a3[:, :, 0:128], in1=bc(rec[:, h0:h0 + n].unsqueeze(2), [128, n, 128]), op=ALU.mult), reads=[accb[bk].k, rec.k], writes=[oatt.k])
                S.op("pool", lambda e: e.tensor_tensor(out=oatt[:], in0=oatt[:], in1=sgb[:], op=ALU.mult), reads=[oatt.k, sgb.k], writes=[oatt.k])
                S.op("pool", lambda e: e.tensor_tensor(out=mrg[:], in0=oatt[:], in1=mg[:], op=ALU.add), reads=[oatt.k, mg.k], writes=[mrg.k])
                S.dma(lambda e: e.dma_start(out=self.s_mrg[j], in_=mrg[:]), reads=[mrg.k], writes=[ko["x1"]])
                yield

            def count(g):
                n = 0
                for _ in g:
                    n += 1
                return n

            for t in range(NO + 2):
                gens = []
                for mk, jj in ((gA, t - 2), (gB, t - 1), (gI, t)):
                    if 0 <= jj < NO:
                        gens.append(mk(jj))
                sizes = []
                for mk, jj in ((gA, t - 2), (gB, t - 1), (gI, t)):
                    if 0 <= jj < NO:
                        nk = 2 * jj + 2
                        if mk is gA:
                            sizes.append(2 + 2 * nk)
                        elif mk is gB:
                            sizes.append(NBIS)
                        else:
                            sizes.append(((nk * 128 + 511) // 512) * 8 + 1)
                nmax = max(sizes)
                done = [0] * len(gens)
                if 0 <= t - 2 < NO:
                    next(gens[0], None)
                    done[0] = 1
                for k in range(nmax):
                    for gi, g in enumerate(gens):
                        tgt = ((k + 1) * sizes[gi]) // nmax
                        while done[gi] < tgt:
                            next(g, None)
                            done[gi] += 1
                for g in gens:
                    for _ in g:
                        pass
            S.barrier()
            S.emit()

    def phase3(self, G=2):
        S = self.S
        NO = self.NO
        ps, psb = self.ps, self.psb
        G = min(G, NO)
        with ExitStack() as es:
            sb = lambda n, sh, dt: self.sb(es, n, sh, dt)
            wup = sb("wupb", [128, 8, 4096], BF16); wdn = sb("wdnb", [128, 32, 1024], BF16)
            woutb = sb("woutb", [128, 8, 1024], BF16)
            stg = [sb("stg3_%d" % i, [128, 512], F32) for i in range(2)]
            gmlp = sb("gmlp", [128, 8], F32)
            self.load(gmlp, gmlp[:], self.i_gmlp)
            self.load_weight_bf16(es, woutb, self.i_wout, 1024, stg)
            self.load_weight_bf16(es, wup, self.i_wup, 4096, stg)
            self.load_weight_bf16(es, wdn, self.i_wdn, 1024, stg)
            x1 = sb("x1g", [128, G, 1024], F32)
            xb = sb("xb3", [128, 1024], F32); mrg = sb("mrg3", [128, 1024], BF16); mT2 = sb("mT2", [128, 8, 128], BF16)
            ss = sb("ss3", [128, 1], F32); xnb = sb("xnb3", [128, 1024], BF16)
            h2T = sb("h2T", [128, 8, G * 128], BF16)
            hTb = sb("hTb3", [128, 8, 128], BF16)
            hid = sb("hidT", [128, 32, G * 128], BF16)
            rl = [sb("rl3_%d" % i, [128, G * 128], F32) for i in range(2)]
            ot = [sb("ot%d" % i, [128, 1024], F32) for i in range(2)]
            for g0 in range(0, NO, G):
                for b in range(G):
                    j = g0 + b
                    p = 2 * j + 1
                    S.dma(lambda e, j=j: e.dma_start(out=mrg[:], in_=self.s_mrg[j]), reads=[self.k_own[j]["x1"]], writes=[mrg.k])
                    S.dma(lambda e, p=p: e.dma_start(out=xb[:], in_=self.i_x[p * 128:(p + 1) * 128, :]), writes=[xb.k])

                    def trg(e):
                        for kc in range(8):
                            ins = e.transpose(out=psb[:, kc * 128:(kc + 1) * 128], in_=mrg[:, kc * 128:(kc + 1) * 128], identity=self.identb[:])
                        return ins
                    S.op("pe", trg, reads=[mrg.k, self.identb.k], writes=[psb.k])
                    S.op("act", lambda e: e.activation(out=mT2[:].rearrange("p k t -> p (k t)"), in_=psb[:], func=AF.Copy), reads=[psb.k], writes=[mT2.k])
                    for half in range(2):
                        pq = ps[4 + half]
                        self.proj(pq, 512, mT2, woutb, half * 512)
                        S.op("dve", lambda e, pq=pq, half=half, b=b: e.tensor_tensor(out=x1[:, b, half * 512:(half + 1) * 512], in0=xb[:, half * 512:(half + 1) * 512], in1=pq[:], op=ALU.add), reads=[xb.k, pq.k], writes=[x1.k])
                    xbv = TL(x1.t[:, b, :], "x")
                    xbv.k = x1.k
                    self.rms_hT(xbv, ss, xnb, hTb, gmlp, psb)
                    S.op("pool", lambda e, b=b: e.tensor_copy(out=h2T[:, :, b * 128:(b + 1) * 128], in_=hTb[:]), reads=[hTb.k], writes=[h2T.k])
                for f in range(32):
                    pq = ps[f % 2]; r = rl[f % 2]

                    def up(e, pq=pq, f=f):
                        for kc in range(8):
                            ins = e.matmul(pq[:, 0:G * 128], lhsT=wup[:, kc, f * 128:(f + 1) * 128], rhs=h2T[:, kc, :], start=(kc == 0), stop=(kc == 7))
                        return ins
                    S.op("pe", up, reads=[wup.k, h2T.k], writes=[pq.k])
                    S.op("act", lambda e, pq=pq, r=r: e.activation(out=r[:], in_=pq[:, 0:G * 128], func=AF.Relu), reads=[pq.k], writes=[r.k])
                    S.op("dve" if f % 2 else "pool", lambda e, r=r, f=f: e.tensor_tensor(out=hid[:, f, :], in0=r[:], in1=r[:], op=ALU.mult), reads=[r.k], writes=[hid.k])
                for b in range(G):
                    j = g0 + b
                    o = ot[b % 2]
                    for half in range(2):
                        pq = ps[2 + half]

                        def dn(e, pq=pq, b=b, half=half):
                            for f in range(32):
                                ins = e.matmul(pq[:], lhsT=hid[:, f, b * 128:(b + 1) * 128], rhs=wdn[:, f, half * 512:(half + 1) * 512], start=(f == 0), stop=(f == 31))
                            return ins
                        S.op("pe", dn, reads=[hid.k, wdn.k], writes=[pq.k])
                        S.op("dve", lambda e, pq=pq, o=o, b=b, half=half: e.tensor_tensor(out=o[:, half * 512:(half + 1) * 512], in0=x1[:, b, half * 512:(half + 1) * 512], in1=pq[:], op=ALU.add), reads=[x1.k, pq.k], writes=[o.k])
                    S.dma(lambda e, j=j, o=o: e.dma_start(out=self.o_out[j * 128:(j + 1) * 128, :], in_=o[:]), reads=[o.k], writes=[])
            S.barrier()
            S.emit()

    def phase1a(self):
        S = self.S
        P = self.P
        ps, psb = self.ps, self.psb
        with ExitStack() as es:
            sb = lambda n, sh, dt: self.sb(es, n, sh, dt)
            wb = sb("w1a", [128, 8, C1A], BF16)
            stg = [sb("stga%d" % i, [128, 512], F32) for i in range(2)]
            gmix = sb("gmixa", [128, 8], F32); ggdn = sb("ggdn", [128, 128], F32)
            aexp = sb("aexp", [128, 8], F32); dtb = sb("dtb", [128, 8], F32)
            convw = sb("convw", [128, 24, 4], F32)
            uinc = sb("uinc", [128, 128], F32); onesf = sb("onesf", [128, 128], F32); onesb = sb("onesb", [128, 128], BF16)
            mskL4 = sb("mskL4", [128, 4, 128], BF16); mskU4 = sb("mskU4", [128, 4, 128], BF16)
            self.load(gmix, gmix[:], self.i_gmix); self.load(ggdn, ggdn[:], self.i_ggdn)
            self.load(aexp, aexp[:], self.i_alog); self.load(dtb, dtb[:], self.i_dtb)
            self.load(convw, convw[:].rearrange("p t j -> p (t j)"), self.i_convw)
            self.load(uinc, uinc[:], self.i_uinc)
            for q4 in range(4):
                self.load(mskL4, mskL4[:, q4, :], self.i_mskL); self.load(mskU4, mskU4[:, q4, :], self.i_mskU)
            S.op("pool", lambda e: e.memset(onesf[:], 1.0), writes=[onesf.k])
            S.op("pool", lambda e: e.memset(onesb[:], 1.0), writes=[onesb.k])
            S.op("act", lambda e: e.activation(out=aexp[:], in_=aexp[:], func=AF.Exp), reads=[aexp.k], writes=[aexp.k])
            self.load_weight_bf16(es, wb, self.i_w1a, C1A, stg)
            xb = sb("xba", [128, 1024], F32); ss = sb("ssa", [128, 1], F32); xnb = sb("xnba", [128, 1024], BF16); hT = sb("hTa", [128, 8, 128], BF16)
            U = sb("U", [128, 24, 131], F32)
            YgD = sb("YgD", [128, 8, 128], F32); CtD = sb("CtD", [128, 8, 128], F32); YgP = sb("YgP", [128, 8, 128], F32); CtP = sb("CtP", [128, 8, 128], F32)
            sq = sb("sqa", [128, 1024], BF16); rn = sb("rn", [128, 1024], F32)
            qT = sb("qT", [128, 8, 128], BF16); kT2 = [sb("kT%d" % i, [128, 8, 128], BF16) for i in range(2)]; vTb = sb("vTb", [128, 8, 128], BF16)
            ktok = sb("ktok", [128, 8, 128], F32)
            kbg2 = [sb("kbg%d" % i, [128, 8, 128], BF16) for i in range(2)]; kdec2 = [sb("kdec%d" % i, [128, 8, 128], BF16) for i in range(2)]; vb2 = [sb("vb%d" % i, [128, 8, 128], BF16) for i in range(2)]
            smm = [sb("sm_%d" % i, [128, 96], F32) for i in range(2)]
            sm2 = sb("sm2", [128, 16], F32); sm3 = sb("sm3", [128, 8], F32)
            Rall = sb("Rall", [128, 8, 128], F32)
            Dls2 = [sb("Dls%d" % i, [128, 8, 128], F32) for i in range(2)]; DT = sb("DTm", [128, 8, 128], F32); egr = sb("egr", [128, 8, 128], F32)
            qdT2 = [sb("qdT%d" % i, [128, 8, 128], BF16) for i in range(2)]; QKT2 = [sb("QKT%d" % i, [128, 8, 128], BF16) for i in range(2)]
            Am = [sb("Am%d" % i, [128, 8, 128], NDT) for i in range(2)]
            Bm = [sb("Bm%d" % i, [128, 8, 128], NDT) for i in range(2)]
            Pm = sb("Pm", [128, 8, 128], NDT); TTb = sb("TTb", [128, 8, 128], BF16)
            u = sb("u", [128, 8, 128], F32); wT = sb("wT", [128, 8, 128], BF16); vn = sb("vn", [128, 8, 128], BF16)
            S32 = sb("S32", [128, 8, 128], F32); Stmp = sb("Stmp", [128, 8, 128], F32); Sbf = sb("Sbf", [128, 8, 128], BF16)
            o = sb("o", [128, 8, 128], F32); osq = Stmp
            zg2 = [sb("zga%d" % i, [128, 1024], BF16) for i in range(2)]; mgt = sb("mgt", [128, 1024], BF16)
            S.op("pool", lambda e: e.memset(U[:], 0.0), writes=[U.k])
            S.op("pool", lambda e: e.memset(S32[:], 0.0), writes=[S32.k])
            S.op("pool", lambda e: e.memset(Sbf[:], 0.0), writes=[Sbf.k])
            f4 = lambda t, g: t[:, g * 4:(g + 1) * 4, :].rearrange("p h d -> p (h d)")

            def batch_mm(banks, fn_h):
                for g in range(2):
                    def f(e, g=g):
                        for hh in range(4):
                            h = g * 4 + hh
                            ins = fn_h(e, banks[g][:, hh * 128:(hh + 1) * 128], h)
                        return ins
                    yield g, f

            def gF(p):
                own = (p % 2 == 1)
                j = p // 2
                r0 = p * 128
                DB = p % 2
                kT = kT2[DB]; kbg = kbg2[DB]; kdec = kdec2[DB]; vb = vb2[DB]; sm = smm[DB]; Dls = Dls2[DB]
                qdT = qdT2[DB]; QKT = QKT2[DB]; zg = zg2[DB]
                self.load(xb, xb[:], self.i_x[r0:r0 + 128, :])
                if own:
                    S.dma(lambda e, j=j: e.dma_start(out=zg[:], in_=self.s_zg[j]), reads=[self.k_own[j]["zg"]], writes=[zg.k])
                self.rms_hT(xb, ss, xnb, hT, gmix, psb)
                yield
                for grp in range(6):
                    pq = ps[4 + grp % 2]

                    def fm(e, pq=pq, grp=grp):
                        for t4 in range(4):
                            ct = grp * 4 + t4
                            for kc in range(8):
                                ins = e.matmul(pq[:, t4 * 128:(t4 + 1) * 128], lhsT=wb[:, kc, ct * 128:(ct + 1) * 128], rhs=hT[:, kc, :], start=(kc == 0), stop=(kc == 7), skip_group_check=True)
                        return ins
                    S.op("pe", fm, reads=[wb.k, hT.k], writes=[pq.k])
                    yield
                    S.op("act", lambda e, pq=pq, grp=grp: e.activation(out=U[:, grp * 4:(grp + 1) * 4, 3:131], in_=pq[:].rearrange("p (t n) -> p t n", t=4), func=AF.Copy), reads=[pq.k], writes=[U.k])
                    yield
                self.proj(ps[6], 16, hT, wb, 3072)
                yield
                S.op("act", lambda e: e.activation(out=sm[:, 0:16], in_=ps[6][:, 0:16], func=AF.Copy), reads=[ps[6].k], writes=[sm.k])
                yield
                S.op("act", lambda e: e.activation(out=sm[:, 16:24], in_=sm[:, 8:16], func=AF.Sigmoid), reads=[sm.k], writes=[sm.k])
                yield
                S.op("dve", lambda e: e.tensor_tensor(out=sm[:, 24:32], in0=sm[:, 0:8], in1=dtb[:], op=ALU.add), reads=[sm.k, dtb.k], writes=[sm.k])
                yield
                S.op("act", lambda e: e.activation(out=sm[:, 32:40], in_=sm[:, 24:32], func=AF.Abs), reads=[sm.k], writes=[sm.k])
                yield
                S.op("act", lambda e: e.activation(out=sm[:, 32:40], in_=sm[:, 32:40], func=AF.Exp, scale=-1.0), reads=[sm.k], writes=[sm.k])
                yield
                S.op("dve", lambda e: e.tensor_scalar(out=sm[:, 32:40], in0=sm[:, 32:40], scalar1=1.0, scalar2=None, op0=ALU.add), reads=[sm.k], writes=[sm.k])
                yield
                S.op("act", lambda e: e.activation(out=sm[:, 32:40], in_=sm[:, 32:40], func=AF.Ln), reads=[sm.k], writes=[sm.k])
                yield
                S.op("dve", lambda e: e.scalar_tensor_tensor(out=sm[:, 40:48], in0=sm[:, 24:32], scalar=0.0, in1=sm[:, 32:40], op0=ALU.max, op1=ALU.add), reads=[sm.k], writes=[sm.k])
                yield
                S.op("dve", lambda e: e.scalar_tensor_tensor(out=sm[:, 48:56], in0=sm[:, 40:48], scalar=-1.0, in1=aexp[:], op0=ALU.mult, op1=ALU.mult), reads=[sm.k, aexp.k], writes=[sm.k])
                yield

                def cs(e):
                    e.matmul(ps[6][:, 16:24], lhsT=uinc[:], rhs=sm[:, 48:56], start=True, stop=True, skip_group_check=True)
                    return e.matmul(ps[6][:, 24:32], lhsT=onesf[:], rhs=sm[:, 48:56], start=True, stop=True, skip_group_check=True)
                S.op("pe", cs, reads=[uinc.k, onesf.k, sm.k], writes=[ps[6].k])
                yield
                S.op("act", lambda e: e.activation(out=sm[:, 56:72], in_=ps[6][:, 16:32], func=AF.Copy), reads=[ps[6].k], writes=[sm.k])
                yield
                S.op("act", lambda e: e.activation(out=sm[:, 72:80], in_=sm[:, 56:64], func=AF.Exp), reads=[sm.k], writes=[sm.k])
                yield
                S.op("dve", lambda e: e.tensor_tensor(out=sm[:, 80:88], in0=sm[:, 64:72], in1=sm[:, 56:64], op=ALU.subtract), reads=[sm.k], writes=[sm.k])
                yield
                S.op("act", lambda e: e.activation(out=sm[:, 80:96], in_=sm[:, 80:96] if False else sm[:, 80:88], func=AF.Exp) if False else e.activation(out=sm[:, 80:88], in_=sm[:, 80:88], func=AF.Exp), reads=[sm.k], writes=[sm.k])
                yield
                S.op("act", lambda e: e.activation(out=sm[:, 88:96], in_=sm[:, 64:72], func=AF.Exp), reads=[sm.k], writes=[sm.k])
                yield
                S.op("dve", lambda e: e.tensor_tensor(out=sm2[:, 0:8], in0=sm[:, 16:24], in1=sm[:, 72:80], op=ALU.mult), reads=[sm.k], writes=[sm2.k])
                yield
                for grp, dst in ((2, vTb), (0, qT), (1, kT)):
                    if grp == 0 and not own:
                        continue
                    Ug = lambda jj, grp=grp: U[:, grp * 8:(grp + 1) * 8, jj:jj + 128]
                    cw = lambda jj, grp=grp: bc(convw[:, grp * 8:(grp + 1) * 8, jj:jj + 1], [128, 8, 128])
                    ceng = "pool" if grp == 2 else "dve"
                    Yg = YgP if grp == 2 else YgD
                    Ct = CtP if grp == 2 else CtD
                    S.op(ceng, lambda e, Ug=Ug, cw=cw, Yg=Yg: e.tensor_tensor(out=Yg[:], in0=Ug(0), in1=cw(0), op=ALU.mult), reads=[U.k, convw.k], writes=[Yg.k])
                    yield
                    for jj in range(1, 4):
                        S.op(ceng, lambda e, Ug=Ug, cw=cw, jj=jj, Ct=Ct: e.tensor_tensor(out=Ct[:], in0=Ug(jj), in1=cw(jj), op=ALU.mult), reads=[U.k, convw.k], writes=[Ct.k])
                        yield
                        S.op(ceng, lambda e, Yg=Yg, Ct=Ct: e.tensor_tensor(out=Yg[:], in0=Yg[:], in1=Ct[:], op=ALU.add), reads=[Yg.k, Ct.k], writes=[Yg.k])
                        yield
                    Yf = Yg[:].rearrange("p h d -> p (h d)")
                    if grp == 2:
                        S.op("act", lambda e, Yf=Yf: e.activation(out=vTb[:].rearrange("p h d -> p (h d)"), in_=Yf, func=AF.Silu), reads=[Yg.k], writes=[vTb.k])
                        continue
                    S.op("act", lambda e, Yf=Yf: e.activation(out=Yf, in_=Yf, func=AF.Silu), reads=[Yg.k], writes=[Yg.k])
                    yield
                    S.op("dve", lambda e, Yf=Yf: e.tensor_tensor(out=sq[:], in0=Yf, in1=Yf, op=ALU.mult), reads=[Yg.k], writes=[sq.k])
                    yield
                    for g in range(2):
                        pq = ps[4 + g]
                        S.op("pe", lambda e, pq=pq, g=g: e.matmul(pq[:], lhsT=onesb[:], rhs=sq[:, g * 512:(g + 1) * 512], start=True, stop=True), reads=[onesb.k, sq.k], writes=[pq.k])
                        S.op("dve", lambda e, pq=pq, g=g: e.tensor_scalar(out=rn[:, g * 512:(g + 1) * 512], in0=pq[:], scalar1=EPS, scalar2=None, op0=ALU.add), reads=[pq.k], writes=[rn.k])
                    S.op("act", lambda e: e.activation(out=rn[:], in_=rn[:], func=AF.Ln), reads=[rn.k], writes=[rn.k])
                    yield
                    S.op("act", lambda e: e.activation(out=rn[:], in_=rn[:], func=AF.Exp, scale=-0.5), reads=[rn.k], writes=[rn.k])
                    yield
                    sc = float(128.0 ** -0.5) if grp == 0 else 1.0
                    S.op("dve", lambda e, Yf=Yf, dst=dst, sc=sc: e.scalar_tensor_tensor(out=dst[:].rearrange("p h d -> p (h d)"), in0=Yf, scalar=sc, in1=rn[:], op0=ALU.mult, op1=ALU.mult), reads=[Yg.k, rn.k], writes=[dst.k])
                    yield
                S.op("pool", lambda e: e.tensor_copy(out=U[:, :, 0:3], in_=U[:, :, 128:131]), reads=[U.k], writes=[U.k])
                yield
                def trk(e):
                    for h in range(8):
                        ins = e.transpose(out=psb[:, h * 128:(h + 1) * 128], in_=kT[:, h, :], identity=self.identb[:])
                    return ins
                S.op("pe", trk, reads=[kT.k, self.identb.k], writes=[psb.k])
                yield
                S.op("act", lambda e: e.activation(out=ktok[:].rearrange("p h d -> p (h d)"), in_=psb[:], func=AF.Copy), reads=[psb.k], writes=[ktok.k])
                yield
                S.op("dve", lambda e: e.tensor_tensor(out=kbg[:], in0=ktok[:], in1=bc(sm2[:, 0:8].unsqueeze(2), [128, 8, 128]), op=ALU.mult), reads=[ktok.k, sm2.k], writes=[kbg.k])
                yield
                S.op("pool", lambda e: e.tensor_tensor(out=kdec[:], in0=ktok[:], in1=bc(sm[:, 80:88].unsqueeze(2), [128, 8, 128]), op=ALU.mult), reads=[ktok.k, sm.k], writes=[kdec.k])
                yield

                def trv(e):
                    for h in range(8):
                        ins = e.transpose(out=psb[:, h * 128:(h + 1) * 128], in_=vTb[:, h, :], identity=self.identb[:])
                    return ins
                S.op("pe", trv, reads=[vTb.k, self.identb.k], writes=[psb.k])
                yield
                S.op("dve", lambda e: e.tensor_tensor(out=vb[:], in0=psb[:].rearrange("p (h d) -> p h d", h=8), in1=bc(sm[:, 16:24].unsqueeze(2), [128, 8, 128]), op=ALU.mult), reads=[psb.k, sm.k], writes=[vb.k])
                yield
                S.op("dve", lambda e: e.tensor_tensor(out=Rall[:], in0=bc(uinc[:].unsqueeze(1), [128, 8, 128]), in1=bc(sm[:, 48:56].unsqueeze(2), [128, 8, 128]), op=ALU.mult), reads=[uinc.k, sm.k], writes=[Rall.k])
                yield
                for g in range(2):
                    pq = ps[4 + g]

                    def grl(e, pq=pq, g=g):
                        e.matmul(pq[:], lhsT=onesf[:], rhs=f4(Rall, g), start=True, stop=False)
                        return e.matmul(pq[:], lhsT=self.identb[:], rhs=mskL4[:].rearrange("p h d -> p (h d)"), start=False, stop=True)
                    S.op("pe", grl, reads=[onesf.k, Rall.k, self.identb.k, mskL4.k], writes=[pq.k])
                    yield
                    for hh in range(4):
                        h = g * 4 + hh
                        S.op("act", lambda e, pq=pq, h=h, hh=hh: e.activation(out=Dls[:, h, :], in_=pq[:, hh * 128:(hh + 1) * 128], func=AF.Exp, scale=-1.0, bias=sm[:, 56 + h:57 + h]), reads=[pq.k, sm.k], writes=[Dls.k])
                if own:
                    for g in range(2):
                        pq = ps[4 + g]
                        S.op("pe", lambda e, pq=pq, g=g: e.matmul(pq[:], lhsT=onesf[:], rhs=f4(Rall, g), start=True, stop=False), reads=[onesf.k, Rall.k], writes=[pq.k])
                        S.op("act", lambda e, pq=pq, g=g: e.activation(out=f4(egr, g), in_=pq[:], func=AF.Exp), reads=[pq.k], writes=[egr.k])
                        S.op("pe", lambda e, pq=pq: e.matmul(pq[:], lhsT=self.identb[:], rhs=mskU4[:].rearrange("p h d -> p (h d)"), start=False, stop=True), reads=[self.identb.k, mskU4.k], writes=[pq.k])
                        S.op("pool", lambda e, g=g: e.tensor_scalar(out=sm2[:, 8:16], in0=sm[:, 56:64], scalar1=-1.0, scalar2=None, op0=ALU.mult), reads=[sm.k], writes=[sm2.k])
                        for hh in range(4):
                            h = g * 4 + hh
                            S.op("act", lambda e, pq=pq, h=h, hh=hh: e.activation(out=DT[:, h, :], in_=pq[:, hh * 128:(hh + 1) * 128], func=AF.Exp, scale=1.0, bias=sm2[:, 8 + h:9 + h]), reads=[pq.k, sm2.k], writes=[DT.k])
                    S.op("pool", lambda e: e.tensor_tensor(out=qdT[:], in0=qT[:], in1=egr[:], op=ALU.mult), reads=[qT.k, egr.k], writes=[qdT.k])
                    yield
                    for g, f in batch_mm((ps[4], ps[5]), lambda e, out, h: e.matmul(out, lhsT=kT[:, h, :], rhs=qT[:, h, :], start=True, stop=True, skip_group_check=True)):
                        S.op("pe", f, reads=[kT.k, qT.k], writes=[ps[4 + g].k])
                        S.op("dve", lambda e, g=g: e.tensor_tensor(out=f4(QKT, g), in0=ps[4 + g][:], in1=f4(DT, g), op=ALU.mult), reads=[ps[4 + g].k, DT.k], writes=[QKT.k])

                yield

            def gNR(p):
                own = (p % 2 == 1)
                j = p // 2
                r0 = p * 128
                DB = p % 2
                kT = kT2[DB]; kbg = kbg2[DB]; kdec = kdec2[DB]; vb = vb2[DB]; sm = smm[DB]; Dls = Dls2[DB]
                qdT = qdT2[DB]; QKT = QKT2[DB]; zg = zg2[DB]
                S.op("pool", lambda e: e.tensor_tensor(out=Stmp[:], in0=S32[:], in1=bc(sm[:, 88:96].unsqueeze(2), [128, 8, 128]), op=ALU.mult), reads=[S32.k, sm.k], writes=[Stmp.k])
                yield
                A0, B0 = Am[0], Bm[0]
                for g, f in batch_mm((ps[0], ps[1]), lambda e, out, h: e.matmul(out, lhsT=kT[:, h, :], rhs=kT[:, h, :], start=True, stop=True, skip_group_check=True)):
                    S.op("pe", f, reads=[kT.k], writes=[ps[g].k])
                    yield
                    for hh in range(4):
                        h = g * 4 + hh
                        S.op("dve", lambda e, g=g, h=h, hh=hh: e.scalar_tensor_tensor(out=A0[:, h, :], in0=ps[g][:, hh * 128:(hh + 1) * 128], scalar=sm[:, 16 + h:17 + h], in1=Dls[:, h, :], op0=ALU.mult, op1=ALU.mult), reads=[ps[g].k, sm.k, Dls.k], writes=[A0.k])
                for g, f in batch_mm((ps[2], ps[3]), lambda e, out, h: e.transpose(out=out, in_=RF(A0[:, h, :]), identity=self.identf[:])):
                    S.op("pe", f, reads=[A0.k, self.identf.k], writes=[ps[2 + g].k])
                    yield
                    S.op("act", lambda e, g=g: e.activation(out=f4(B0, g), in_=ps[2 + g][:], func=AF.Copy), reads=[ps[2 + g].k], writes=[B0.k])
                    yield
                S.op("pool", lambda e: e.tensor_tensor(out=Pm[:], in0=bc(self.identf[:].unsqueeze(1), [128, 8, 128]), in1=RF(B0[:]), op=ALU.subtract), reads=[self.identf.k, B0.k], writes=[Pm.k])
                yield
                cur = 0
                for lv in range(6):
                    Ac, Bc = Am[cur], Bm[cur]
                    An, Bn_ = Am[1 - cur], Bm[1 - cur]
                    last = (lv == 5)
                    for g, f in batch_mm((ps[0], ps[1]), lambda e, out, h, Ac=Ac, Bc=Bc: e.matmul(out, lhsT=NC_(Bc[:, h, :]), rhs=NC_(Ac[:, h, :]), start=True, stop=True, skip_group_check=True)):
                        S.op("pe", f, reads=[Ac.k, Bc.k], writes=[ps[g].k])
                        S.op("act", lambda e, g=g, An=An: e.activation(out=f4(An, g), in_=ps[g][:], func=AF.Copy), reads=[ps[g].k], writes=[An.k])
                    if not last:
                        for g, f in batch_mm((ps[2], ps[3]), lambda e, out, h, Ac=Ac, Bc=Bc: e.matmul(out, lhsT=NC_(Ac[:, h, :]), rhs=NC_(Bc[:, h, :]), start=True, stop=True, skip_group_check=True)):
                            S.op("pe", f, reads=[Ac.k, Bc.k], writes=[ps[2 + g].k])
                            S.op("dve", lambda e, g=g, Bn_=Bn_: e.tensor_copy(out=f4(Bn_, g), in_=ps[2 + g][:]), reads=[ps[2 + g].k], writes=[Bn_.k])
                    for g, f in batch_mm((ps[0], ps[1]), lambda e, out, h, An=An: e.matmul(out, lhsT=NC_(An[:, h, :]), rhs=NC_(Pm[:, h, :]), start=True, stop=True, skip_group_check=True)):
                        S.op("pe", f, reads=[An.k, Pm.k], writes=[ps[g].k])
                        S.op("dve", lambda e, g=g: e.tensor_tensor(out=f4(Pm, g), in0=RF(f4(Pm, g)), in1=ps[g][:], op=ALU.add), reads=[Pm.k, ps[g].k], writes=[Pm.k])
                    cur = 1 - cur
                S.op("act", lambda e: e.activation(out=TTb[:].rearrange("p h d -> p (h d)"), in_=RF(Pm[:]).rearrange("p h d -> p (h d)"), func=AF.Copy), reads=[Pm.k], writes=[TTb.k])
                yield
                for g, f in batch_mm((ps[0], ps[1]), lambda e, out, h: e.matmul(out, lhsT=TTb[:, h, :], rhs=vb[:, h, :], start=True, stop=True, skip_group_check=True)):
                    S.op("pe", f, reads=[TTb.k, vb.k], writes=[ps[g].k])
                    yield
                    S.op("act", lambda e, g=g: e.activation(out=f4(u, g), in_=ps[g][:], func=AF.Copy), reads=[ps[g].k], writes=[u.k])
                    yield
                for g, f in batch_mm((ps[2], ps[3]), lambda e, out, h: e.matmul(out, lhsT=kbg[:, h, :], rhs=TTb[:, h, :], start=True, stop=True, skip_group_check=True)):
                    S.op("pe", f, reads=[TTb.k, kbg.k], writes=[ps[2 + g].k])
                    yield
                    S.op("act", lambda e, g=g: e.activation(out=f4(wT, g), in_=ps[2 + g][:], func=AF.Copy), reads=[ps[2 + g].k], writes=[wT.k])
                    yield
                for g, f in batch_mm((ps[0], ps[1]), lambda e, out, h: e.matmul(out, lhsT=wT[:, h, :], rhs=Sbf[:, h, :], start=True, stop=True, skip_group_check=True)):
                    S.op("pe", f, reads=[wT.k, Sbf.k], writes=[ps[g].k])
                    yield
                    S.op("dve", lambda e, g=g: e.tensor_tensor(out=f4(vn, g), in0=f4(u, g), in1=ps[g][:], op=ALU.subtract), reads=[u.k, ps[g].k], writes=[vn.k])
                    yield
                if own:
                    for g in range(2):
                        def fo(e, g=g):
                            for hh in range(4):
                                h = g * 4 + hh
                                e.matmul(ps[g][:, hh * 128:(hh + 1) * 128], lhsT=qdT[:, h, :], rhs=Sbf[:, h, :], start=True, stop=False, skip_group_check=True)
                                ins = e.matmul(ps[g][:, hh * 128:(hh + 1) * 128], lhsT=QKT[:, h, :], rhs=vn[:, h, :], start=False, stop=True, skip_group_check=True)
                            return ins
                        S.op("pe", fo, reads=[qdT.k, Sbf.k, QKT.k, vn.k], writes=[ps[g].k])
                        S.op("act", lambda e, g=g: e.activation(out=f4(o, g), in_=ps[g][:], func=AF.Copy), reads=[ps[g].k], writes=[o.k])
                for g, f in batch_mm((ps[2], ps[3]), lambda e, out, h: e.matmul(out, lhsT=kdec[:, h, :], rhs=vn[:, h, :], start=True, stop=True, skip_group_check=True)):
                    S.op("pe", f, reads=[kdec.k, vn.k], writes=[ps[2 + g].k])
                    yield
                    S.op("dve", lambda e, g=g: e.tensor_tensor(out=f4(S32, g), in0=f4(Stmp, g), in1=ps[2 + g][:], op=ALU.add), reads=[Stmp.k, ps[2 + g].k], writes=[S32.k])
                    yield
                S.op("act", lambda e: e.activation(out=Sbf[:].rearrange("p h d -> p (h d)"), in_=S32[:].rearrange("p h d -> p (h d)"), func=AF.Copy), reads=[S32.k], writes=[Sbf.k])
                yield
                if own:
                    S.op("pool", lambda e: e.tensor_tensor(out=osq[:], in0=o[:], in1=o[:], op=ALU.mult), reads=[o.k], writes=[osq.k])
                    yield
                    S.op("dve", lambda e: e.tensor_reduce(out=sm3[:, 0:8], in_=osq[:], axis=AX.X, op=ALU.add), reads=[osq.k], writes=[sm2.k])
                    yield
                    S.op("dve", lambda e: e.tensor_scalar(out=sm3[:, 0:8], in0=sm3[:, 0:8], scalar1=1.0 / 128, scalar2=EPS, op0=ALU.mult, op1=ALU.add), reads=[sm2.k], writes=[sm2.k])
                    yield
                    S.op("act", lambda e: e.activation(out=sm3[:, 0:8], in_=sm3[:, 0:8], func=AF.Ln), reads=[sm2.k], writes=[sm2.k])
                    yield
                    S.op("act", lambda e: e.activation(out=sm3[:, 0:8], in_=sm3[:, 0:8], func=AF.Exp, scale=-0.5), reads=[sm2.k], writes=[sm2.k])
                    yield
                    S.op("dve", lambda e: e.tensor_tensor(out=o[:], in0=o[:], in1=bc(sm3[:, 0:8].unsqueeze(2), [128, 8, 128]), op=ALU.mult), reads=[o.k, sm2.k], writes=[o.k])
                    yield
                    S.op("pool", lambda e: e.tensor_tensor(out=o[:], in0=o[:], in1=bc(ggdn[:].unsqueeze(1), [128, 8, 128]), op=ALU.mult), reads=[o.k, ggdn.k], writes=[o.k])
                    yield
                    S.op("dve", lambda e: e.tensor_tensor(out=mgt[:], in0=o[:].rearrange("p h d -> p (h d)"), in1=zg[:], op=ALU.mult), reads=[o.k, zg.k], writes=[mgt.k])
                    yield
                    S.dma(lambda e, j=j: e.dma_start(out=self.s_mg[j], in_=mgt[:]), reads=[mgt.k], writes=[self.k_own[j]["mg"]])

                yield

            def run_pair(ga, gb, na, nb):
                nmax = max(na, nb, 1)
                da = db = 0
                for k in range(nmax):
                    ta = ((k + 1) * na) // nmax
                    tb = ((k + 1) * nb) // nmax
                    while da < ta:
                        next(ga, None); da += 1
                    while db < tb:
                        next(gb, None); db += 1
                for _ in ga:
                    pass
                for _ in gb:
                    pass

            def count(mk, p):
                S.dry = True
                n = sum(1 for _ in mk(p))
                S.dry = False
                return n
            nF = [count(gF, 0), count(gF, 1)]
            nN = [count(gNR, 0), count(gNR, 1)]
            for _ in gF(0):
                pass
            for p in range(P):
                if p + 1 < P:
                    run_pair(gNR(p), gF(p + 1), nN[p % 2], nF[(p + 1) % 2])
                else:
                    for _ in gNR(p):
                        pass
            S.barrier()
            S.emit()

    def phase1a_stub(self):
        S = self.S
        with ExitStack() as es:
            z = self.sb(es, "zmg", [128, 1024], BF16)
            S.op("pool", lambda e: e.memset(z[:], 0.0), writes=[z.k])
            for j in range(self.NO):
                S.dma(lambda e, j=j: e.dma_start(out=self.s_mg[j], in_=z[:]), reads=[z.k], writes=[self.k_own[j]["mg"]])
            S.barrier()
            S.emit()


def build(S, debug=False, gdn=True):
    B = Builder(S, debug)
    with ExitStack() as es:
        B.setup(es)
        B.phase1b()
        if gdn:
            B.phase1a()
        else:
            B.phase1a_stub()
        B.phase2()
        B.phase3()
    return B.nc


IN_SPLITS = (3072, 1024, 8, 8, 1024, 256, 256, 512, 64, 8, 1024, 1024)


def host_consts():
    bf = ml_dtypes.bfloat16
    i = np.arange(128)
    c = {}
    c["identf"] = np.eye(128, dtype=np.float32)
    c["identb"] = np.eye(128).astype(bf)
    c["uinc"] = (i[:, None] <= i[None, :]).astype(np.float32)
    c["causal"] = np.where(i[None, :] <= i[:, None], 0.0, NEG).astype(np.float32)
    c["mskU"] = np.where(i[None, :] >= i[:, None], 0.0, -30000.0).astype(bf)
    c["mskL"] = np.where(i[None, :] < i[:, None], 0.0, 30000.0).astype(bf)
    return c


def rope_tab(pos, rot):
    inv = (np.float32(500000.0) ** (-np.arange(0, rot, 2, dtype=np.float32) / np.float32(rot))).astype(np.float32)
    ang = pos.astype(np.float32)[:, None] * inv[None, :]
    return np.cos(ang).astype(np.float32), np.sin(ang).astype(np.float32)


def make_in_maps(S, x, norm_mix_g, w_in, conv_w, a_log, dt_bias, gdn_norm_g, q_norm_g, k_norm_g,
                 w_out, norm_mlp_g, w_mlp_up, w_mlp_down):
    Bn = x.shape[0]
    f = np.float32
    w_in = np.asarray(w_in[0], f)
    pts = np.cumsum((0,) + IN_SPLITS)
    seg = {n: w_in[:, pts[i]:pts[i + 1]] for i, n in enumerate(
        ("qkv", "z", "ga", "gb", "aq", "ak", "av", "iq", "ik", "iw", "gatea", "gateb"))}
    w1a = np.ascontiguousarray(np.concatenate([seg["qkv"], seg["ga"], seg["gb"]], 1))
    w1b = np.ascontiguousarray(np.concatenate([seg["ak"], seg["av"], seg["ik"], seg["iw"], seg["aq"], seg["iq"], seg["gateb"], seg["z"], seg["gatea"]], 1))
    col = lambda v: np.ascontiguousarray(np.asarray(v, f).reshape(8, 128).T)
    rep = lambda v: np.ascontiguousarray(np.broadcast_to(np.asarray(v, f)[None, :], (128, len(v))))
    common = host_consts()
    common.update(dict(
        w1a=w1a, w1b=w1b, wout=np.ascontiguousarray(w_out[0], f), wup=np.ascontiguousarray(w_mlp_up[0], f),
        wdn=np.ascontiguousarray(w_mlp_down[0], f),
        gmix=col(norm_mix_g[0]), gmlp=col(norm_mlp_g[0]), gq=rep(q_norm_g[0]), gk=rep(k_norm_g[0]),
        ggdn=rep(gdn_norm_g[0]), alog=rep(a_log[0]), dtb=rep(dt_bias[0]),
        convw=np.ascontiguousarray(np.asarray(conv_w[0], f).reshape(4, 24, 128).transpose(2, 1, 0).reshape(128, 96)),
    ))
    maps = []
    for b in range(Bn):
        for c in range(2):
            m = dict(common)
            xb = np.asarray(x[b], f)
            if c == 0:
                xs = np.concatenate([np.zeros((128, D), f), xb[:S - 128]], 0)
            else:
                xs = xb
            pos = np.maximum(np.arange(S) + (c - 1) * 128, 0)
            m["xseq"] = np.ascontiguousarray(xs)
            m["cosA"], m["sinA"] = rope_tab(pos, 32)
            m["cosI"], m["sinI"] = rope_tab(pos, 16)
            m["blk0"] = np.full((128, 128), NEG if c == 0 else 0.0, f)
            maps.append(m)
    return maps


def assemble(S, results, Bn):
    out = np.zeros((Bn, S, D), np.float32)
    ov = out.reshape(Bn, S // 256, 2, 128, D)
    for b in range(Bn):
        for c in range(2):
            r = results[b * 2 + c]["out"].reshape(S // 256, 128, D)
            ov[b, :, c] = r
    return out


_NC_CACHE = {}


def kernel(**inputs):
    x = np.asarray(inputs["x"])
    Bn, S, _ = x.shape
    if S not in _NC_CACHE:
        _NC_CACHE[S] = build(S)
    nc = _NC_CACHE[S]
    maps = make_in_maps(S, **{k: np.asarray(v) for k, v in inputs.items()})
    res = run_bass_kernel_spmd(nc, maps, core_ids=list(range(2 * Bn)))
    return assemble(S, res.results, Bn)
```

```python
from contextlib import ExitStack
import numpy as np
import ml_dtypes
import concourse.bass as bass
import concourse.mybir as mybir
from concourse.bass_utils import run_bass_kernel_spmd

F32 = mybir.dt.float32
BF16 = mybir.dt.bfloat16
AF = mybir.ActivationFunctionType
ALU = mybir.AluOpType
AX = mybir.AxisListType

D = 1024
EPS = 1e-6
NEG = -1.0e30
SAME_ENGINE_SYNC = True
IDX_F32R = True
IDT = mybir.dt.float32r if IDX_F32R else F32
NEU_F32R = True
NC_ = lambda ap: ap
NDT = mybir.dt.float32r if NEU_F32R else F32
RF = (lambda ap: ap.bitcast(F32)) if NEU_F32R else (lambda ap: ap)


class Tok:
    __slots__ = ("name", "w", "r")

    def __init__(self, name):
        self.name = name
        self.w = None
        self.r = {}


class Sched:
    ENG = ("pe", "act", "dve", "pool", "sp")

    def __init__(self, nc, es, n_dma_sems=24):
        self.nc = nc
        self.sems = {}
        for e in ("pe", "act", "dve", "pool"):
            self.sems[e] = es.enter_context(nc.semaphore("s_" + e))
        self.ndma = n_dma_sems
        for k in range(n_dma_sems):
            self.sems[("d", k)] = es.enter_context(nc.semaphore("s_d%d" % k))
        self.cnt = {e: 0 for e in ("pe", "act", "dve", "pool")}
        self.dtot = [0] * n_dma_sems
        self.rr = 0
        self.waited = {e: {} for e in self.ENG}
        self.ops = {e: [] for e in self.ENG}
        self.nops = 0
        self.dry = False

    def _deps(self, eng, reads, writes):
        deps = {}

        def add(d):
            if d is None:
                return
            sid, val = d
            if sid == eng and (eng == "pe" or not SAME_ENGINE_SYNC):
                return
            if deps.get(sid, 0) < val:
                deps[sid] = val

        for t in reads:
            add(t.w)
        for t in writes:
            add(t.w)
            for sid, val in t.r.items():
                add((sid, val))
        out = []
        wd = self.waited[eng]
        for sid, val in deps.items():
            if wd.get(sid, 0) < val:
                wd[sid] = val
                out.append((sid, val))
        return out

    def _mark(self, me, reads, writes):
        for t in reads:
            if t.r.get(me[0], 0) < me[1]:
                t.r[me[0]] = me[1]
        for t in writes:
            t.w = me
            t.r = {}

    def op(self, eng, fn, reads=(), writes=()):
        if self.dry:
            return
        waits = self._deps(eng, reads, writes)
        self.cnt[eng] += 1
        me = (eng, self.cnt[eng])
        self._mark(me, reads, writes)
        self.ops[eng].append((waits, fn, eng, 1))
        self.nops += 1

    def dma(self, fn, reads=(), writes=()):
        if self.dry:
            return
        k = self.rr
        self.rr = (self.rr + 1) % self.ndma
        sid = ("d", k)
        waits = self._deps("sp", reads, writes)
        wd = self.waited["sp"]
        if wd.get(sid, 0) < self.dtot[k]:
            wd[sid] = self.dtot[k]
            waits.append((sid, self.dtot[k]))
        self.dtot[k] += 16
        me = (sid, self.dtot[k])
        self._mark(me, reads, writes)
        self.ops["sp"].append((waits, fn, sid, 16))
        self.nops += 1

    def barrier(self):
        for e in self.ENG:
            waits = []
            wd = self.waited[e]
            for c in ("pe", "act", "dve", "pool"):
                if c != e and wd.get(c, 0) < self.cnt[c]:
                    wd[c] = self.cnt[c]
                    waits.append((c, self.cnt[c]))
            for k in range(self.ndma):
                sid = ("d", k)
                if wd.get(sid, 0) < self.dtot[k]:
                    wd[sid] = self.dtot[k]
                    waits.append((sid, self.dtot[k]))
            if waits:
                self.ops[e].append((waits, None, None, 0))

    def emit(self):
        nc = self.nc
        ops = self.ops
        self.ops = {e: [] for e in self.ENG}
        sems = self.sems

        def run(engh, lst):
            for waits, fn, sid, inc in lst:
                for s, v in waits:
                    engh.wait_ge(sems[s], v)
                if fn is not None:
                    fn(engh).then_inc(sems[sid], inc)

        with nc.Block() as block:
            @block.tensor
            def _(e):
                run(e, ops["pe"])

            @block.scalar
            def _(e):
                run(e, ops["act"])

            @block.vector
            def _(e):
                run(e, ops["dve"])

            @block.gpsimd
            def _(e):
                run(e, ops["pool"])

            @block.sync
            def _(e):
                run(e, ops["sp"])


class Ctx:
    pass


def bc(ap, shape):
    return ap.to_broadcast(list(shape))


class TL:
    def __init__(self, t, name):
        self.t = t
        self.k = Tok(name)

    def __getitem__(self, key):
        return self.t[key]


C1B = 5192
C1A = 3088


class Builder:
    def __init__(self, S, debug=False):
        self.Sq = S
        self.P = S // 128
        self.NO = self.P // 2
        self.debug = debug
        self.nc = bass.Bass("TRN2", target_bir_lowering=False)

    def dram_in(self, name, shape, dt=F32):
        return self.nc.dram_tensor(name, list(shape), dt, kind="ExternalInput").ap()

    def dram_scr(self, name, shape, dt):
        kind = "ExternalOutput" if self.debug else "Internal"
        return self.nc.dram_tensor(name, list(shape), dt, kind=kind).ap()

    def sb(self, es, name, shape, dt):
        return TL(es.enter_context(self.nc.sbuf_tensor("t_" + name, list(shape), dt)), name)

    def load(self, dst, dst_ap, src_ap):
        self.S.dma(lambda e: e.dma_start(out=dst_ap, in_=src_ap), writes=[dst.k])

    def rsqrt_small(self, v, n):
        S = self.S
        S.op("act", lambda e: e.activation(out=v[:, 0:n], in_=v[:, 0:n], func=AF.Ln), reads=[v.k], writes=[v.k])
        S.op("act", lambda e: e.activation(out=v[:, 0:n], in_=v[:, 0:n], func=AF.Exp, scale=-0.5), reads=[v.k], writes=[v.k])

    def load_weight_bf16(self, es, wb, w_ap, ncols, stg):
        S = self.S
        nk = w_ap.shape[0] // 128
        CH = stg[0].t.shape[1]
        i = 0
        for k in range(nk):
            for c0 in range(0, ncols, CH):
                cw = min(CH, ncols - c0)
                st = stg[i % len(stg)]
                S.dma(lambda e, st=st, k=k, c0=c0, cw=cw: e.dma_start(out=st[:, 0:cw], in_=w_ap[k * 128:(k + 1) * 128, c0:c0 + cw]), writes=[st.k])
                eng = ("pool", "dve", "act")[i % 3] if False else "pool"
                S.op(eng, lambda e, st=st, k=k, c0=c0, cw=cw: e.tensor_copy(out=wb[:, k, c0:c0 + cw], in_=st[:, 0:cw]), reads=[st.k], writes=[wb.k])
                i += 1

    def rms_hT(self, xb, ss, xnb, hT, gcol, psb):
        S = self.S
        junk = self.junk
        idb = self.identb
        S.op("act", lambda e: e.activation(out=junk[:, 0:1024], in_=xb[:], func=AF.Square, accum_out=ss[:, 0:1]), reads=[xb.k], writes=[junk.k, ss.k])
        S.op("dve", lambda e: e.tensor_scalar(out=ss[:, 0:1], in0=ss[:, 0:1], scalar1=1.0 / 1024, scalar2=EPS, op0=ALU.mult, op1=ALU.add), reads=[ss.k], writes=[ss.k])
        self.rsqrt_small(ss, 1)
        S.op("dve", lambda e: e.tensor_scalar(out=xnb[:], in0=xb[:], scalar1=ss[:, 0:1], scalar2=None, op0=ALU.mult), reads=[ss.k, xb.k], writes=[xnb.k])

        def tr(e):
            for kc in range(8):
                i = e.transpose(out=psb[:, kc * 128:(kc + 1) * 128], in_=xnb[:, kc * 128:(kc + 1) * 128], identity=idb[:])
            return i
        S.op("pe", tr, reads=[xnb.k, idb.k], writes=[psb.k])
        S.op("dve", lambda e: e.tensor_tensor(out=hT[:], in0=psb[:].rearrange("p (k t) -> p k t", k=8), in1=bc(gcol[:].unsqueeze(2), [128, 8, 128]), op=ALU.mult), reads=[psb.k, gcol.k], writes=[hT.k])

    def proj(self, ps, ncol, hT, wb, c0, pcol=0, start=True):
        def f(e):
            for kc in range(8):
                i = e.matmul(ps[:, pcol:pcol + ncol], lhsT=hT[:, kc, :], rhs=wb[:, kc, c0:c0 + ncol], start=(kc == 0), stop=(kc == 7))
            return i
        self.S.op("pe", f, reads=[hT.k, wb.k], writes=[ps.k])

    def setup(self, es):
        nc = self.nc
        S_, P, NO = self.Sq, self.P, self.NO
        self.S = Sched(nc, es)
        d = self.dram_in
        self.i_x = d("xseq", [S_, D])
        self.i_w1a = d("w1a", [D, C1A])
        self.i_w1b = d("w1b", [D, C1B])
        self.i_wout = d("wout", [D, D])
        self.i_wup = d("wup", [D, 4 * D])
        self.i_wdn = d("wdn", [4 * D, D])
        self.i_cosA = d("cosA", [S_, 16]); self.i_sinA = d("sinA", [S_, 16])
        self.i_cosI = d("cosI", [S_, 8]); self.i_sinI = d("sinI", [S_, 8])
        self.i_blk0 = d("blk0", [128, 128])
        self.i_identf = d("identf", [128, 128]); self.i_identb = d("identb", [128, 128], BF16)
        self.i_uinc = d("uinc", [128, 128]); self.i_causal = d("causal", [128, 128])
        self.i_mskU = d("mskU", [128, 128], BF16); self.i_mskL = d("mskL", [128, 128], BF16)
        self.i_gmix = d("gmix", [128, 8]); self.i_gmlp = d("gmlp", [128, 8])
        self.i_gq = d("gq", [128, 128]); self.i_gk = d("gk", [128, 128]); self.i_ggdn = d("ggdn", [128, 128])
        self.i_alog = d("alog", [128, 8]); self.i_dtb = d("dtb", [128, 8])
        self.i_convw = d("convw", [128, 24 * 4])
        self.o_out = nc.dram_tensor("out", [NO * 128, D], F32, kind="ExternalOutput").ap()
        s = self.dram_scr
        self.s_akT = s("s_akT", [128, 2 * S_], BF16)
        self.s_v = s("s_v", [S_, 258], BF16)
        self.s_ikT = s("s_ikT", [128, S_], F32)
        self.s_aqT = s("s_aqT", [NO, 128, 1024], BF16)
        self.s_iqT = s("s_iqT", [NO, 128, 512], F32)
        self.s_iw = s("s_iw", [NO, 128, 8], F32)
        self.s_sgb = s("s_sgb", [NO, 128, 1024], BF16)
        self.s_mg = s("s_mg", [NO, 128, 1024], BF16)
        self.s_zg = s("s_zg", [NO, 128, 1024], BF16)
        self.s_mrg = s("s_mrg", [NO, 128, 1024], BF16)
        self.k_akT = Tok("akT_d"); self.k_v = Tok("v_d"); self.k_ikT = Tok("ikT_d")
        self.k_own = [{n: Tok(n + str(j)) for n in ("aqT", "iqT", "iw", "sgb", "mg", "x1", "zg")} for j in range(NO)]
        self.ps = [TL(es.enter_context(nc.psum_tensor("ps%d" % i, [128, 512], F32)), "ps%d" % i) for i in range(7)]
        self.psb = TL(es.enter_context(nc.psum_tensor("psb", [128, 1024], BF16)), "psb")
        sb = lambda n, sh, dt: self.sb(es, n, sh, dt)
        self.identf = sb("identf", [128, 128], F32); self.identb = sb("identb", [128, 128], BF16)
        self.junk = sb("junk", [128, 1024], F32)
        for t, src in ((self.identf, self.i_identf), (self.identb, self.i_identb)):
            self.load(t, t[:], src)

    def norm_rope(self, H, src_ap, src_tok, gain, extra, out, ocol, cs, sn, tmp):
        S = self.S
        tf, sq, ssh, ra, rb = tmp
        n = H * 128
        S.op("act", lambda e: e.activation(out=tf[:, 0:n], in_=src_ap, func=AF.Copy), reads=[src_tok], writes=[tf.k])
        S.op("dve", lambda e: e.tensor_tensor(out=sq[:, 0:n], in0=tf[:, 0:n], in1=tf[:, 0:n], op=ALU.mult), reads=[tf.k], writes=[sq.k])
        S.op("dve", lambda e: e.tensor_reduce(out=ssh[:, 0:H], in_=sq[:, 0:n].rearrange("p (h d) -> p h d", h=H), axis=AX.X, op=ALU.add), reads=[sq.k], writes=[ssh.k])
        S.op("dve", lambda e: e.tensor_scalar(out=ssh[:, 0:H], in0=ssh[:, 0:H], scalar1=1.0 / 128, scalar2=EPS, op0=ALU.mult, op1=ALU.add), reads=[ssh.k], writes=[ssh.k])
        self.rsqrt_small(ssh, H)
        if extra != 1.0:
            S.op("dve", lambda e: e.tensor_scalar(out=ssh[:, 0:H], in0=ssh[:, 0:H], scalar1=extra, scalar2=None, op0=ALU.mult), reads=[ssh.k], writes=[ssh.k])
        tf3 = tf[:, 0:n].rearrange("p (h d) -> p h d", h=H)
        S.op("dve", lambda e: e.tensor_tensor(out=tf3, in0=tf3, in1=bc(ssh[:, 0:H].unsqueeze(2), [128, H, 128]), op=ALU.mult), reads=[tf.k, ssh.k], writes=[tf.k])
        S.op("pool", lambda e: e.tensor_tensor(out=tf3, in0=tf3, in1=bc(gain[:].unsqueeze(1), [128, H, 128]), op=ALU.mult), reads=[tf.k, gain.k], writes=[tf.k])
        o3 = out[:, ocol:ocol + n].rearrange("p (h d) -> p h d", h=H)
        S.op("act", lambda e: e.activation(out=out[:, ocol:ocol + n], in_=tf[:, 0:n], func=AF.Copy), reads=[tf.k], writes=[out.k])
        self.rope(tf3, tf.k, o3, out.k, H, 16, cs, sn, ra, rb)

    def rope(self, t3, ttok, o3, otok, H, hf, cs, sn, ra, rb):
        S = self.S
        x1 = t3[:, :, 0:hf]; x2 = t3[:, :, hf:2 * hf]
        c = bc(cs[:].unsqueeze(1), [128, H, hf]); s_ = bc(sn[:].unsqueeze(1), [128, H, hf])
        a3 = ra[:, 0:H * hf].rearrange("p (h d) -> p h d", h=H)
        b3 = rb[:, 0:H * hf].rearrange("p (h d) -> p h d", h=H)
        S.op("pool", lambda e: e.tensor_tensor(out=a3, in0=x1, in1=c, op=ALU.mult), reads=[ttok, cs.k], writes=[ra.k])
        S.op("pool", lambda e: e.tensor_tensor(out=b3, in0=x2, in1=s_, op=ALU.mult), reads=[ttok, sn.k], writes=[rb.k])
        S.op("dve", lambda e: e.tensor_tensor(out=o3[:, :, 0:hf], in0=a3, in1=b3, op=ALU.subtract), reads=[ra.k, rb.k], writes=[otok])
        S.op("pool", lambda e: e.tensor_tensor(out=a3, in0=x2, in1=c, op=ALU.mult), reads=[ttok, cs.k], writes=[ra.k])
        S.op("pool", lambda e: e.tensor_tensor(out=b3, in0=x1, in1=s_, op=ALU.mult), reads=[ttok, sn.k], writes=[rb.k])
        S.op("dve", lambda e: e.tensor_tensor(out=o3[:, :, hf:2 * hf], in0=a3, in1=b3, op=ALU.add), reads=[ra.k, rb.k], writes=[otok])

    def phase1b(self):
        S = self.S
        P = self.P
        ps, psb = self.ps, self.psb
        with ExitStack() as es:
            sb = lambda n, sh, dt: self.sb(es, n, sh, dt)
            wb = sb("w1b", [128, 8, C1B], BF16)
            stg = [sb("stg%d" % i, [128, 1024], F32) for i in range(2)]
            gmix = sb("gmix", [128, 8], F32); gq = sb("gq", [128, 128], F32); gk = sb("gk", [128, 128], F32)
            self.load(gmix, gmix[:], self.i_gmix); self.load(gq, gq[:], self.i_gq); self.load(gk, gk[:], self.i_gk)
            self.load_weight_bf16(es, wb, self.i_w1b, C1B, stg)
            xb = [sb("xb%d" % i, [128, 1024], F32) for i in range(2)]
            ss = sb("ss", [128, 1], F32); xnb = sb("xnb", [128, 1024], BF16); hT = sb("hT", [128, 8, 128], BF16)
            cA = sb("cA", [128, 16], F32); sA = sb("sA", [128, 16], F32); cI = sb("cI", [128, 8], F32); sI = sb("sI", [128, 8], F32)
            tmp = (sb("tf", [128, 512], F32), sb("sq", [128, 512], F32), sb("ssh", [128, 8], F32), sb("ra", [128, 64], F32), sb("rb", [128, 64], F32))
            ra, rb = tmp[3], tmp[4]
            akb = sb("akb", [128, 256], BF16); akT = sb("akTb", [128, 256], BF16)
            vaug = sb("vaug", [128, 258], BF16)
            ikf = sb("ikf", [128, 72], F32); ik2 = sb("ik2", [128, 128], F32); ikT = sb("ikTb", [128, 128], F32)
            iw = sb("iwb", [128, 8], F32)
            aqb = sb("aqb", [128, 1024], BF16); aqT = sb("aqTb", [128, 1024], BF16)
            iqf = sb("iqf", [128, 512], F32); iqo = sb("iqo", [128, 512], F32); iqT = sb("iqTb", [128, 512], F32)
            sgb = sb("sgbb", [128, 1024], BF16)
            zs = sb("zs", [128, 1024], F32); sga = sb("sga", [128, 1024], F32); zg = sb("zgb", [128, 1024], BF16)
            S.op("pool", lambda e: e.memset(vaug[:], 1.0), writes=[vaug.k])
            for p in range(P):
                own = (p % 2 == 1)
                j = p // 2
                x = xb[p % 2]
                r0 = p * 128
                self.load(x, x[:], self.i_x[r0:r0 + 128, :])
                self.load(cA, cA[:], self.i_cosA[r0:r0 + 128, :]); self.load(sA, sA[:], self.i_sinA[r0:r0 + 128, :])
                self.load(cI, cI[:], self.i_cosI[r0:r0 + 128, :]); self.load(sI, sI[:], self.i_sinI[r0:r0 + 128, :])
                self.rms_hT(x, ss, xnb, hT, gmix, psb)
                self.proj(ps[0], 512, hT, wb, 0)
                self.norm_rope(2, ps[0][:, 0:256], ps[0].k, gk, 1.0, akb, 0, cA, sA, tmp)
                S.op("act", lambda e: e.activation(out=vaug[:].rearrange("p (c d) -> p c d", c=2)[:, :, 0:128], in_=ps[0][:, 256:512].rearrange("p (c d) -> p c d", c=2), func=AF.Copy), reads=[ps[0].k], writes=[vaug.k])
                S.dma(lambda e, r0=r0: e.dma_start(out=self.s_v[r0:r0 + 128, :], in_=vaug[:]), reads=[vaug.k], writes=[self.k_v])

                def trk(e):
                    for c in range(2):
                        i = e.transpose(out=psb[:, c * 128:(c + 1) * 128], in_=akb[:, c * 128:(c + 1) * 128], identity=self.identb[:])
                    return i
                S.op("pe", trk, reads=[akb.k, self.identb.k], writes=[psb.k])
                S.op("act", lambda e: e.activation(out=akT[:], in_=psb[:, 0:256], func=AF.Copy), reads=[psb.k], writes=[akT.k])
                S.dma(lambda e, r0=r0: e.dma_start(out=self.s_akT.rearrange("h (c s) -> h c s", c=2)[:, :, r0:r0 + 128], in_=akT[:].rearrange("h (c s) -> h c s", c=2)), reads=[akT.k], writes=[self.k_akT])
                self.proj(ps[1], 72, hT, wb, 512)
                S.op("act", lambda e: e.activation(out=ikf[:], in_=ps[1][:, 0:72], func=AF.Copy), reads=[ps[1].k], writes=[ikf.k])
                S.op("dve", lambda e: e.tensor_copy(out=ik2[:, 0:64], in_=ikf[:, 0:64]), reads=[ikf.k], writes=[ik2.k])
                self.rope(ikf[:, 0:64].rearrange("p (h d) -> p h d", h=1), ikf.k, ik2[:, 0:64].rearrange("p (h d) -> p h d", h=1), ik2.k, 1, 8, cI, sI, ra, rb)
                S.op("dve", lambda e: e.tensor_copy(out=ik2[:, 64:128], in_=ik2[:, 0:64]), reads=[ik2.k], writes=[ik2.k])
                S.op("pe", lambda e: e.transpose(out=ps[2][:, 0:128], in_=ik2[:], identity=self.identf[:]), reads=[ik2.k, self.identf.k], writes=[ps[2].k])
                S.op("act", lambda e: e.activation(out=ikT[:], in_=ps[2][:, 0:128], func=AF.Copy), reads=[ps[2].k], writes=[ikT.k])
                S.dma(lambda e, r0=r0: e.dma_start(out=self.s_ikT[:, r0:r0 + 128], in_=ikT[:]), reads=[ikT.k], writes=[self.k_ikT])
                if not own:
                    continue
                ko = self.k_own[j]
                S.op("dve", lambda e: e.tensor_scalar(out=iw[:], in0=ikf[:, 64:72], scalar1=float(512.0 ** -0.5), scalar2=None, op0=ALU.mult), reads=[ikf.k], writes=[iw.k])
                S.dma(lambda e, j=j: e.dma_start(out=self.s_iw[j], in_=iw[:]), reads=[iw.k], writes=[ko["iw"]])
                for half in range(2):
                    pq = ps[3 + half]
                    self.proj(pq, 512, hT, wb, 584 + half * 512)
                    self.norm_rope(4, pq[:, 0:512], pq.k, gq, float(128.0 ** -0.5), aqb, half * 512, cA, sA, tmp)

                def trq(e):
                    for h in range(8):
                        i = e.transpose(out=psb[:, h * 128:(h + 1) * 128], in_=aqb[:, h * 128:(h + 1) * 128], identity=self.identb[:])
                    return i
                S.op("pe", trq, reads=[aqb.k, self.identb.k], writes=[psb.k])
                S.op("act", lambda e: e.activation(out=aqT[:], in_=psb[:], func=AF.Copy), reads=[psb.k], writes=[aqT.k])
                S.dma(lambda e, j=j: e.dma_start(out=self.s_aqT[j], in_=aqT[:]), reads=[aqT.k], writes=[ko["aqT"]])
                self.proj(ps[5], 512, hT, wb, 1608)
                S.op("act", lambda e: e.activation(out=iqf[:], in_=ps[5][:], func=AF.Copy), reads=[ps[5].k], writes=[iqf.k])
                S.op("dve", lambda e: e.tensor_copy(out=iqo[:], in_=iqf[:]), reads=[iqf.k], writes=[iqo.k])
                self.rope(iqf[:].rearrange("p (h d) -> p h d", h=8), iqf.k, iqo[:].rearrange("p (h d) -> p h d", h=8), iqo.k, 8, 8, cI, sI, ra, rb)

                def tri(e):
                    for g in range(4):
                        i = e.transpose(out=ps[6][:, g * 128:(g + 1) * 128], in_=iqo[:, g * 128:(g + 1) * 128], identity=self.identf[:])
                    return i
                S.op("pe", tri, reads=[iqo.k, self.identf.k], writes=[ps[6].k])
                S.op("act", lambda e: e.activation(out=iqT[:], in_=ps[6][:], func=AF.Copy), reads=[ps[6].k], writes=[iqT.k])
                S.dma(lambda e, j=j: e.dma_start(out=self.s_iqT[j], in_=iqT[:]), reads=[iqT.k], writes=[ko["iqT"]])
                for half in range(2):
                    pq = ps[half]
                    self.proj(pq, 512, hT, wb, 2120 + half * 512)
                    S.op("act", lambda e, pq=pq, half=half: e.activation(out=sgb[:, half * 512:(half + 1) * 512], in_=pq[:], func=AF.Sigmoid), reads=[pq.k], writes=[sgb.k])
                S.dma(lambda e, j=j: e.dma_start(out=self.s_sgb[j], in_=sgb[:]), reads=[sgb.k], writes=[ko["sgb"]])
                for half in range(2):
                    pq = ps[2 + half]; pg = ps[4 + half]
                    self.proj(pq, 512, hT, wb, 3144 + half * 512)
                    self.proj(pg, 512, hT, wb, 4168 + half * 512)
                    S.op("act", lambda e, pq=pq, half=half: e.activation(out=zs[:, half * 512:(half + 1) * 512], in_=pq[:], func=AF.Silu), reads=[pq.k], writes=[zs.k])
                    S.op("act", lambda e, pg=pg, half=half: e.activation(out=sga[:, half * 512:(half + 1) * 512], in_=pg[:], func=AF.Sigmoid), reads=[pg.k], writes=[sga.k])
                S.op("pool", lambda e: e.tensor_tensor(out=zg[:], in0=zs[:], in1=sga[:], op=ALU.mult), reads=[zs.k, sga.k], writes=[zg.k])
                S.dma(lambda e, j=j: e.dma_start(out=self.s_zg[j], in_=zg[:]), reads=[zg.k], writes=[ko["zg"]])
            S.barrier()
            S.emit()

    def phase2(self, NBIS=20):
        S = self.S
        P, NO, S_ = self.P, self.NO, self.Sq
        ps, psb = self.ps, self.psb
        KSEL = float(min(256, S_ // 4)) - 0.5
        with ExitStack() as es:
            sb = lambda n, sh, dt: self.sb(es, n, sh, dt)
            akT = sb("akT", [128, 2, S_], BF16)
            vall = sb("vall", [128, P, 258], BF16)
            H2 = S_ // 2
            assert H2 % 512 == 0
            ikT = sb("ikT", [128, H2], IDT)
            scores = [sb("score%d" % i, [128, S_], F32) for i in range(2)]
            msk = sb("msk", [128, S_], BF16)
            jk = sb("jk", [128, S_], mybir.dt.uint8)
            gq = sb("gq2", [128, 128], F32); gk = sb("gk2", [128, 128], F32)
            negM = sb("negM", [128, 2], F32)
            causal = sb("causal", [128, 128], F32); blk0 = sb("blk0", [128, 128], F32)
            self.load(gq, gq[:], self.i_gq); self.load(gk, gk[:], self.i_gk)
            self.load(causal, causal[:], self.i_causal); self.load(blk0, blk0[:], self.i_blk0)
            S.dma(lambda e: e.dma_start(out=akT[:], in_=self.s_akT.rearrange("h (c s) -> h c s", c=2)), reads=[self.k_akT], writes=[akT.k])
            S.dma(lambda e: e.dma_start(out=vall[:], in_=self.s_v.rearrange("(p t) n -> t p n", t=128)), reads=[self.k_v], writes=[vall.k])
            stg_ik = scores[1]
            S.dma(lambda e: e.dma_start(out=stg_ik[0:64, 0:H2], in_=self.s_ikT[0:64, 0:H2]), reads=[self.k_ikT], writes=[stg_ik.k])
            S.dma(lambda e: e.dma_start(out=stg_ik[64:128, 0:H2], in_=self.s_ikT[64:128, H2:S_]), reads=[self.k_ikT], writes=[stg_ik.k])
            for c0 in range(0, H2, 512):
                S.op("pool" if (c0 // 512) % 2 else "act", (lambda e, c0=c0: e.tensor_copy(out=ikT[:, c0:c0 + 512], in_=stg_ik[:, c0:c0 + 512])) if (c0 // 512) % 2 else (lambda e, c0=c0: e.activation(out=ikT[:, c0:c0 + 512], in_=stg_ik[:, c0:c0 + 512], func=AF.Copy)), reads=[stg_ik.k], writes=[ikT.k])
            S.op("dve", lambda e: e.tensor_reduce(out=negM[:, 0:1], in_=gq[:], axis=AX.X, op=ALU.max, apply_absolute_value=True), reads=[gq.k], writes=[negM.k])
            S.op("dve", lambda e: e.tensor_reduce(out=negM[:, 1:2], in_=gk[:], axis=AX.X, op=ALU.max, apply_absolute_value=True), reads=[gk.k], writes=[negM.k])
            S.op("dve", lambda e: e.scalar_tensor_tensor(out=negM[:, 0:1], in0=negM[:, 0:1], scalar=float(-(128.0 ** 0.5)), in1=negM[:, 1:2], op0=ALU.mult, op1=ALU.mult), reads=[negM.k], writes=[negM.k])
            aqT = sb("aqT", [128, 1024], BF16); iqT = sb("iqT", [128, 512], IDT); iw = sb("iw", [128, 8], F32)
            iqT2 = sb("iqT2", [128, 512], IDT)
            sgb = sb("sgb", [128, 1024], BF16); mg = sb("mg", [128, 1024], BF16)
            rl = [sb("rl%d" % i, [128, 512], F32) for i in range(2)]
            pT = [sb("pT%d" % i, [128, 512], BF16) for i in range(2)]
            mT = [sb("mT%d" % i, [128, 1024], BF16) for i in range(2)]
            sts = [sb("bst%d" % i, [128, 8], F32) for i in range(3)]
            rec = sb("rec", [128, 8], F32)
            oatt = self.junk
            mrg = sb("mrg", [128, 1024], BF16)
            accb = [ps[4], ps[5], ps[6]]
            hb = [(0, 0), (0, 1), (0, 2), (1, 0), (1, 1), (1, 2), (2, 0), (2, 1)]

            sctok = [[Tok("sc%d_%d" % (i, g)) for g in range(S_ // 512)] for i in range(2)]
            IDXC = lambda ap: ap

            def gI(j):
                p = 2 * j + 1
                nkeys = (p + 1) * 128
                ko = self.k_own[j]
                score = scores[j % 2]; st = sts[j % 3]
                jq = self.junk
                S.dma(lambda e: e.dma_start(out=jq[:, 0:512], in_=self.s_iqT[j]), reads=[ko["iqT"]], writes=[jq.k])
                S.dma(lambda e: e.dma_start(out=jq[0:64, 512:1024], in_=self.s_iqT[j][64:128, :]), reads=[ko["iqT"]], writes=[jq.k])
                S.dma(lambda e: e.dma_start(out=jq[64:128, 512:1024], in_=self.s_iqT[j][0:64, :]), reads=[ko["iqT"]], writes=[jq.k])
                S.op("pool", lambda e: e.tensor_copy(out=iqT[:], in_=jq[:, 0:512]), reads=[jq.k], writes=[iqT.k])
                S.op("pool", lambda e: e.tensor_copy(out=iqT2[:], in_=jq[:, 512:1024]), reads=[jq.k], writes=[iqT2.k])
                S.dma(lambda e: e.dma_start(out=iw[:], in_=self.s_iw[j]), reads=[ko["iw"]], writes=[iw.k])
                it = 0
                nkg = (nkeys + 511) // 512
                for h in range(8):
                    for kg in range(nkg):
                        k0 = kg * 512
                        w = min(512, nkeys - k0)
                        sk = score.k
                        pb = ps[it % 2]; r = rl[it % 2]; it += 1
                        b0 = 64 if k0 >= H2 else 0
                        kk0 = k0 - (H2 if k0 >= H2 else 0)
                        qsrc = iqT if (h % 2) * 64 == b0 else iqT2
                        S.op("pe", lambda e, pb=pb, h=h, b0=b0, kk0=kk0, w=w, qsrc=qsrc: e.matmul(pb[:, 0:w], lhsT=IDXC(qsrc[b0:b0 + 64, (h // 2) * 128:(h // 2 + 1) * 128]), rhs=IDXC(ikT[b0:b0 + 64, kk0:kk0 + w]), start=True, stop=True), reads=[iqT.k, iqT2.k, ikT.k], writes=[pb.k])
                        S.op("act", lambda e, pb=pb, r=r, w=w: e.activation(out=r[:, 0:w], in_=pb[:, 0:w], func=AF.Relu), reads=[pb.k], writes=[r.k])
                        if h == 0:
                            S.op("dve", lambda e, r=r, k0=k0, w=w: e.tensor_scalar(out=score[:, k0:k0 + w], in0=r[:, 0:w], scalar1=iw[:, 0:1], scalar2=None, op0=ALU.mult), reads=[r.k, iw.k], writes=[sk])
                        else:
                            S.op("dve", lambda e, r=r, k0=k0, w=w, h=h: e.scalar_tensor_tensor(out=score[:, k0:k0 + w], in0=r[:, 0:w], scalar=iw[:, h:h + 1], in1=score[:, k0:k0 + w], op0=ALU.mult, op1=ALU.add), reads=[r.k, iw.k, sk], writes=[sk])
                        yield
                allk = sctok[j % 2][0:nkg]
                S.op("dve", lambda e: e.tensor_copy(out=st[:, 7:8], in_=st[:, 7:8]), reads=allk + [st.k], writes=[score.k, st.k])
                S.op("dve", lambda e: e.tensor_reduce(out=st[:, 5:6], in_=score[:, 0:nkeys], axis=AX.X, op=ALU.max), reads=[score.k], writes=[st.k])
                S.op("dve", lambda e: e.tensor_reduce(out=st[:, 6:7], in_=score[:, 0:nkeys], axis=AX.X, op=ALU.min), reads=[score.k], writes=[st.k])
                S.op("dve", lambda e: e.tensor_scalar(out=st[:, 0:1], in0=st[:, 6:7], scalar1=-1.0, scalar2=None, op0=ALU.add), reads=[st.k], writes=[st.k])
                S.op("dve", lambda e: e.tensor_tensor(out=st[:, 1:2], in0=st[:, 5:6], in1=st[:, 0:1], op=ALU.subtract), reads=[st.k], writes=[st.k])
                S.op("dve", lambda e: e.tensor_tensor(out=score[:, nkeys - 128:nkeys], in0=score[:, nkeys - 128:nkeys], in1=causal[:], op=ALU.add), reads=[score.k, causal.k], writes=[score.k])
                S.op("dve", lambda e: e.tensor_tensor(out=score[:, 0:128], in0=score[:, 0:128], in1=blk0[:], op=ALU.add), reads=[score.k, blk0.k], writes=[score.k])
                yield

            def gB(j):
                nkeys = (2 * j + 2) * 128
                score = scores[j % 2]; st = sts[j % 3]
                for it_ in range(1, NBIS + 1):
                    f = float(2.0 ** -it_)
                    S.op("dve", lambda e, f=f: e.scalar_tensor_tensor(out=st[:, 2:3], in0=st[:, 1:2], scalar=f, in1=st[:, 0:1], op0=ALU.mult, op1=ALU.add), reads=[st.k], writes=[st.k])
                    S.op("dve", lambda e: e.tensor_scalar(out=jk[:, 0:nkeys], in0=score[:, 0:nkeys], scalar1=st[:, 2:3], scalar2=0.0, op0=ALU.is_gt, op1=ALU.add, accum_out=st[:, 3:4]), reads=[score.k, st.k], writes=[jk.k, st.k])
                    S.op("dve", lambda e, f=f: e.tensor_scalar(out=st[:, 4:5], in0=st[:, 3:4], scalar1=KSEL, scalar2=f, op0=ALU.is_gt, op1=ALU.mult), reads=[st.k], writes=[st.k])
                    S.op("dve", lambda e: e.scalar_tensor_tensor(out=st[:, 0:1], in0=st[:, 4:5], scalar=st[:, 1:2], in1=st[:, 0:1], op0=ALU.mult, op1=ALU.add), reads=[st.k], writes=[st.k])
                    yield

            def gA(j):
                p = 2 * j + 1
                nk = p + 1
                nkeys = nk * 128
                ko = self.k_own[j]
                score = scores[j % 2]; st = sts[j % 3]
                S.op("dve", lambda e: e.tensor_scalar(out=msk[:, 0:nkeys], in0=score[:, 0:nkeys], scalar1=st[:, 0:1], scalar2=None, op0=ALU.is_gt), reads=[score.k, st.k], writes=[msk.k])
                S.dma(lambda e: e.dma_start(out=aqT[:], in_=self.s_aqT[j]), reads=[ko["aqT"]], writes=[aqT.k])
                S.dma(lambda e: e.dma_start(out=sgb[:], in_=self.s_sgb[j]), reads=[ko["sgb"]], writes=[sgb.k])
                S.dma(lambda e: e.dma_start(out=mg[:], in_=self.s_mg[j]), reads=[ko["mg"]], writes=[mg.k])
                yield
                units = [(kb, c) for kb in range(nk) for c in range(2)]

                def emit_st(u):
                    kb, c = units[u]
                    pss = ps[2 + u % 2]
                    S.op("pe", lambda e: e.matmul(pss[:], lhsT=akT[:, c, kb * 128:(kb + 1) * 128], rhs=aqT[:, c * 512:(c + 1) * 512], start=True, stop=True), reads=[akT.k, aqT.k], writes=[pss.k])

                def emit_mask(kb):
                    m = mT[(kb // 8) % 2]
                    nb = min(8, nk - kb)

                    def trm(e):
                        for i_ in range(nb):
                            ins = e.transpose(out=psb[:, i_ * 128:(i_ + 1) * 128], in_=msk[:, (kb + i_) * 128:(kb + i_ + 1) * 128], identity=self.identb[:])
                        return ins
                    S.op("pe", trm, reads=[msk.k, self.identb.k], writes=[psb.k])
                    S.op("act", lambda e: e.activation(out=m[:, 0:nb * 128], in_=psb[:, 0:nb * 128], func=AF.Copy), reads=[psb.k], writes=[m.k])

                emit_mask(0)
                emit_st(0)
                yield
                for u, (kb, c) in enumerate(units):
                    pss = ps[2 + u % 2]; pt = pT[u % 2]
                    m = mT[(kb // 8) % 2]
                    if u + 1 < len(units):
                        if units[u + 1][1] == 0 and units[u + 1][0] % 8 == 0:
                            emit_mask(units[u + 1][0])
                        emit_st(u + 1)
                    S.op("act", lambda e, pss=pss, pt=pt: e.activation(out=pt[:], in_=pss[:], func=AF.Exp, bias=negM[:, 0:1], scale=1.0), reads=[pss.k, negM.k], writes=[pt.k])
                    mi = (kb % 8) * 128
                    S.op("pool", lambda e, pt=pt, m=m, mi=mi: e.tensor_tensor(out=pt[:].rearrange("p (h q) -> p h q", h=4), in0=pt[:].rearrange("p (h q) -> p h q", h=4), in1=bc(m[:, mi:mi + 128].unsqueeze(1), [128, 4, 128]), op=ALU.mult), reads=[pt.k, m.k], writes=[pt.k])
                    yield

                    def pv(e, pt=pt, c=c, kb=kb):
                        for hh in range(4):
                            h = c * 4 + hh
                            bk, sl = hb[h]
                            ins = e.matmul(accb[bk][:, sl * 129:(sl + 1) * 129], lhsT=pt[:, hh * 128:(hh + 1) * 128], rhs=vall[:, kb, c * 129:(c + 1) * 129], start=(kb == 0 and sl == 0), stop=(kb == nk - 1), skip_group_check=True)
                        return ins
                    S.op("pe", pv, reads=[pt.k, vall.k], writes=[accb[hb[c * 4][0]].k, accb[hb[c * 4 + 3][0]].k])
                for bk, (h0, n) in enumerate(((0, 3), (3, 3), (6, 2))):
                    a3 = accb[bk][:, 0:n * 129].rearrange("p (h d) -> p h d", h=n)
                    S.op("dve", lambda e, a3=a3, h0=h0, n=n: e.reciprocal(out=rec[:, h0:h0 + n], in_=a3[:, :, 128]), reads=[accb[bk].k], writes=[rec.k])
                    S.op("dve", lambda e, a3=a3, h0=h0, n=n: e.tensor_tensor(out=oatt[:, h0 * 128:(h0 + n) * 128].rearrange("p (h d) -> p h d", h=n), in0=a3[:, :, 0:128], in1=bc(rec[:, h0:h0 + n].unsqueeze(2), [128, n, 128]), op=ALU.mult), reads=[accb[bk].k, rec.k], writes=[oatt.k])
                S.op("pool", lambda e: e.tensor_tensor(out=oatt[:], in0=oatt[:], in1=sgb[:], op=ALU.mult), reads=[oatt.k, sgb.k], writes=[oatt.k])
                S.op("pool", lambda e: e.tensor_tensor(out=mrg[:], in0=oatt[:], in1=mg[:], op=ALU.add), reads=[oatt.k, mg.k], writes=[mrg.k])
                S.dma(lambda e: e.dma_start(out=self.s_mrg[j], in_=mrg[:]), reads=[mrg.k], writes=[ko["x1"]])
                yield

            def count(g):
                n = 0
                for _ in g:
                    n += 1
                return n

            for t in range(NO + 2):
                gens = []
                for mk, jj in ((gA, t - 2), (gB, t - 1), (gI, t)):
                    if 0 <= jj < NO:
                        gens.append(mk(jj))
                sizes = []
                for mk, jj in ((gA, t - 2), (gB, t - 1), (gI, t)):
                    if 0 <= jj < NO:
                        nk = 2 * jj + 2
                        if mk is gA:
                            sizes.append(2 + 2 * nk)
                        elif mk is gB:
                            sizes.append(NBIS)
                        else:
                            sizes.append(((nk * 128 + 511) // 512) * 8 + 1)
                nmax = max(sizes)
                done = [0] * len(gens)
                if 0 <= t - 2 < NO:
                    next(gens[0], None)
                    done[0] = 1
                for k in range(nmax):
                    for gi, g in enumerate(gens):
                        tgt = ((k + 1) * sizes[gi]) // nmax
                        while done[gi] < tgt:
                            next(g, None)
                            done[gi] += 1
                for g in gens:
                    for _ in g:
                        pass
            S.barrier()
            S.emit()

    def phase3(self, G=2):
        S = self.S
        NO = self.NO
        ps, psb = self.ps, self.psb
        G = min(G, NO)
        with ExitStack() as es:
            sb = lambda n, sh, dt: self.sb(es, n, sh, dt)
            wup = sb("wupb", [128, 8, 4096], BF16); wdn = sb("wdnb", [128, 32, 1024], BF16)
            woutb = sb("woutb", [128, 8, 1024], BF16)
            stg = [sb("stg3_%d" % i, [128, 512], F32) for i in range(2)]
            gmlp = sb("gmlp", [128, 8], F32)
            self.load(gmlp, gmlp[:], self.i_gmlp)
            self.load_weight_bf16(es, woutb, self.i_wout, 1024, stg)
            self.load_weight_bf16(es, wup, self.i_wup, 4096, stg)
            self.load_weight_bf16(es, wdn, self.i_wdn, 1024, stg)
            x1 = sb("x1g", [128, G, 1024], F32)
            xb = sb("xb3", [128, 1024], F32); mrg = sb("mrg3", [128, 1024], BF16); mT2 = sb("mT2", [128, 8, 128], BF16)
            ss = sb("ss3", [128, 1], F32); xnb = sb("xnb3", [128, 1024], BF16)
            h2T = sb("h2T", [128, 8, G * 128], BF16)
            hTb = sb("hTb3", [128, 8, 128], BF16)
            hid = sb("hidT", [128, 32, G * 128], BF16)
            rl = [sb("rl3_%d" % i, [128, G * 128], F32) for i in range(2)]
            ot = [sb("ot%d" % i, [128, 1024], F32) for i in range(2)]
            for g0 in range(0, NO, G):
                for b in range(G):
                    j = g0 + b
                    p = 2 * j + 1
                    S.dma(lambda e, j=j: e.dma_start(out=mrg[:], in_=self.s_mrg[j]), reads=[self.k_own[j]["x1"]], writes=[mrg.k])
                    S.dma(lambda e, p=p: e.dma_start(out=xb[:], in_=self.i_x[p * 128:(p + 1) * 128, :]), writes=[xb.k])

                    def trg(e):
                        for kc in range(8):
                            ins = e.transpose(out=psb[:, kc * 128:(kc + 1) * 128], in_=mrg[:, kc * 128:(kc + 1) * 128], identity=self.identb[:])
                        return ins
                    S.op("pe", trg, reads=[mrg.k, self.identb.k], writes=[psb.k])
                    S.op("act", lambda e: e.activation(out=mT2[:].rearrange("p k t -> p (k t)"), in_=psb[:], func=AF.Copy), reads=[psb.k], writes=[mT2.k])
                    for half in range(2):
                        pq = ps[4 + half]
                        self.proj(pq, 512, mT2, woutb, half * 512)
                        S.op("dve", lambda e, pq=pq, half=half, b=b: e.tensor_tensor(out=x1[:, b, half * 512:(half + 1) * 512], in0=xb[:, half * 512:(half + 1) * 512], in1=pq[:], op=ALU.add), reads=[xb.k, pq.k], writes=[x1.k])
                    xbv = TL(x1.t[:, b, :], "x")
                    xbv.k = x1.k
                    self.rms_hT(xbv, ss, xnb, hTb, gmlp, psb)
                    S.op("pool", lambda e, b=b: e.tensor_copy(out=h2T[:, :, b * 128:(b + 1) * 128], in_=hTb[:]), reads=[hTb.k], writes=[h2T.k])
                for f in range(32):
                    pq = ps[f % 2]; r = rl[f % 2]

                    def up(e, pq=pq, f=f):
                        for kc in range(8):
                            ins = e.matmul(pq[:, 0:G * 128], lhsT=wup[:, kc, f * 128:(f + 1) * 128], rhs=h2T[:, kc, :], start=(kc == 0), stop=(kc == 7))
                        return ins
                    S.op("pe", up, reads=[wup.k, h2T.k], writes=[pq.k])
                    S.op("act", lambda e, pq=pq, r=r: e.activation(out=r[:], in_=pq[:, 0:G * 128], func=AF.Relu), reads=[pq.k], writes=[r.k])
                    S.op("dve" if f % 2 else "pool", lambda e, r=r, f=f: e.tensor_tensor(out=hid[:, f, :], in0=r[:], in1=r[:], op=ALU.mult), reads=[r.k], writes=[hid.k])
                for b in range(G):
                    j = g0 + b
                    o = ot[b % 2]
                    for half in range(2):
                        pq = ps[2 + half]

                        def dn(e, pq=pq, b=b, half=half):
                            for f in range(32):
                                ins = e.matmul(pq[:], lhsT=hid[:, f, b * 128:(b + 1) * 128], rhs=wdn[:, f, half * 512:(half + 1) * 512], start=(f == 0), stop=(f == 31))
                            return ins
                        S.op("pe", dn, reads=[hid.k, wdn.k], writes=[pq.k])
                        S.op("dve", lambda e, pq=pq, o=o, b=b, half=half: e.tensor_tensor(out=o[:, half * 512:(half + 1) * 512], in0=x1[:, b, half * 512:(half + 1) * 512], in1=pq[:], op=ALU.add), reads=[x1.k, pq.k], writes=[o.k])
                    S.dma(lambda e, j=j, o=o: e.dma_start(out=self.o_out[j * 128:(j + 1) * 128, :], in_=o[:]), reads=[o.k], writes=[])
            S.barrier()
            S.emit()

    def phase1a(self):
        S = self.S
        P = self.P
        ps, psb = self.ps, self.psb
        with ExitStack() as es:
            sb = lambda n, sh, dt: self.sb(es, n, sh, dt)
            wb = sb("w1a", [128, 8, C1A], BF16)
            stg = [sb("stga%d" % i, [128, 512], F32) for i in range(2)]
            gmix = sb("gmixa", [128, 8], F32); ggdn = sb("ggdn", [128, 128], F32)
            aexp = sb("aexp", [128, 8], F32); dtb = sb("dtb", [128, 8], F32)
            convw = sb("convw", [128, 24, 4], F32)
            uinc = sb("uinc", [128, 128], F32); onesf = sb("onesf", [128, 128], F32); onesb = sb("onesb", [128, 128], BF16)
            mskL4 = sb("mskL4", [128, 4, 128], BF16); mskU4 = sb("mskU4", [128, 4, 128], BF16)
            self.load(gmix, gmix[:], self.i_gmix); self.load(ggdn, ggdn[:], self.i_ggdn)
            self.load(aexp, aexp[:], self.i_alog); self.load(dtb, dtb[:], self.i_dtb)
            self.load(convw, convw[:].rearrange("p t j -> p (t j)"), self.i_convw)
            self.load(uinc, uinc[:], self.i_uinc)
            for q4 in range(4):
                self.load(mskL4, mskL4[:, q4, :], self.i_mskL); self.load(mskU4, mskU4[:, q4, :], self.i_mskU)
            S.op("pool", lambda e: e.memset(onesf[:], 1.0), writes=[onesf.k])
            S.op("pool", lambda e: e.memset(onesb[:], 1.0), writes=[onesb.k])
            S.op("act", lambda e: e.activation(out=aexp[:], in_=aexp[:], func=AF.Exp), reads=[aexp.k], writes=[aexp.k])
            self.load_weight_bf16(es, wb, self.i_w1a, C1A, stg)
            xb = sb("xba", [128, 1024], F32); ss = sb("ssa", [128, 1], F32); xnb = sb("xnba", [128, 1024], BF16); hT = sb("hTa", [128, 8, 128], BF16)
            U = sb("U", [128, 24, 131], F32)
            YgD = sb("YgD", [128, 8, 128], F32); CtD = sb("CtD", [128, 8, 128], F32); YgP = sb("YgP", [128, 8, 128], F32); CtP = sb("CtP", [128, 8, 128], F32)
            sq = sb("sqa", [128, 1024], BF16); rn = sb("rn", [128, 1024], F32)
            qT = sb("qT", [128, 8, 128], BF16); kT2 = [sb("kT%d" % i, [128, 8, 128], BF16) for i in range(2)]; vTb = sb("vTb", [128, 8, 128], BF16)
            ktok = sb("ktok", [128, 8, 128], F32)
            kbg2 = [sb("kbg%d" % i, [128, 8, 128], BF16) for i in range(2)]; kdec2 = [sb("kdec%d" % i, [128, 8, 128], BF16) for i in range(2)]; vb2 = [sb("vb%d" % i, [128, 8, 128], BF16) for i in range(2)]
            smm = [sb("sm_%d" % i, [128, 96], F32) for i in range(2)]
            sm2 = sb("sm2", [128, 16], F32); sm3 = sb("sm3", [128, 8], F32)
            Rall = sb("Rall", [128, 8, 128], F32)
            Dls2 = [sb("Dls%d" % i, [128, 8, 128], F32) for i in range(2)]; DT = sb("DTm", [128, 8, 128], F32); egr = sb("egr", [128, 8, 128], F32)
            qdT2 = [sb("qdT%d" % i, [128, 8, 128], BF16) for i in range(2)]; QKT2 = [sb("QKT%d" % i, [128, 8, 128], BF16) for i in range(2)]
            Am = [sb("Am%d" % i, [128, 8, 128], NDT) for i in range(2)]
            Bm = [sb("Bm%d" % i, [128, 8, 128], NDT) for i in range(2)]
            Pm = sb("Pm", [128, 8, 128], NDT); TTb = sb("TTb", [128, 8, 128], BF16)
            u = sb("u", [128, 8, 128], F32); wT = sb("wT", [128, 8, 128], BF16); vn = sb("vn", [128, 8, 128], BF16)
            S32 = sb("S32", [128, 8, 128], F32); Stmp = sb("Stmp", [128, 8, 128], F32); Sbf = sb("Sbf", [128, 8, 128], BF16)
            o = sb("o", [128, 8, 128], F32); osq = Stmp
            zg2 = [sb("zga%d" % i, [128, 1024], BF16) for i in range(2)]; mgt = sb("mgt", [128, 1024], BF16)
            S.op("pool", lambda e: e.memset(U[:], 0.0), writes=[U.k])
            S.op("pool", lambda e: e.memset(S32[:], 0.0), writes=[S32.k])
            S.op("pool", lambda e: e.memset(Sbf[:], 0.0), writes=[Sbf.k])
            f4 = lambda t, g: t[:, g * 4:(g + 1) * 4, :].rearrange("p h d -> p (h d)")

            def batch_mm(banks, fn_h):
                for g in range(2):
                    def f(e, g=g):
                        for hh in range(4):
                            h = g * 4 + hh
                            ins = fn_h(e, banks[g][:, hh * 128:(hh + 1) * 128], h)
                        return ins
                    yield g, f

            def gF(p):
                own = (p % 2 == 1)
                j = p // 2
                r0 = p * 128
                DB = p % 2
                kT = kT2[DB]; kbg = kbg2[DB]; kdec = kdec2[DB]; vb = vb2[DB]; sm = smm[DB]; Dls = Dls2[DB]
                qdT = qdT2[DB]; QKT = QKT2[DB]; zg = zg2[DB]
                self.load(xb, xb[:], self.i_x[r0:r0 + 128, :])
                if own:
                    S.dma(lambda e, j=j: e.dma_start(out=zg[:], in_=self.s_zg[j]), reads=[self.k_own[j]["zg"]], writes=[zg.k])
                self.rms_hT(xb, ss, xnb, hT, gmix, psb)
                yield
                for grp in range(6):
                    pq = ps[4 + grp % 2]

                    def fm(e, pq=pq, grp=grp):
                        for t4 in range(4):
                            ct = grp * 4 + t4
                            for kc in range(8):
                                ins = e.matmul(pq[:, t4 * 128:(t4 + 1) * 128], lhsT=wb[:, kc, ct * 128:(ct + 1) * 128], rhs=hT[:, kc, :], start=(kc == 0), stop=(kc == 7), skip_group_check=True)
                        return ins
                    S.op("pe", fm, reads=[wb.k, hT.k], writes=[pq.k])
                    yield
                    S.op("act", lambda e, pq=pq, grp=grp: e.activation(out=U[:, grp * 4:(grp + 1) * 4, 3:131], in_=pq[:].rearrange("p (t n) -> p t n", t=4), func=AF.Copy), reads=[pq.k], writes=[U.k])
                    yield
                self.proj(ps[6], 16, hT, wb, 3072)
                yield
                S.op("act", lambda e: e.activation(out=sm[:, 0:16], in_=ps[6][:, 0:16], func=AF.Copy), reads=[ps[6].k], writes=[sm.k])
                yield
                S.op("act", lambda e: e.activation(out=sm[:, 16:24], in_=sm[:, 8:16], func=AF.Sigmoid), reads=[sm.k], writes=[sm.k])
                yield
                S.op("dve", lambda e: e.tensor_tensor(out=sm[:, 24:32], in0=sm[:, 0:8], in1=dtb[:], op=ALU.add), reads=[sm.k, dtb.k], writes=[sm.k])
                yield
                S.op("act", lambda e: e.activation(out=sm[:, 32:40], in_=sm[:, 24:32], func=AF.Abs), reads=[sm.k], writes=[sm.k])
                yield
                S.op("act", lambda e: e.activation(out=sm[:, 32:40], in_=sm[:, 32:40], func=AF.Exp, scale=-1.0), reads=[sm.k], writes=[sm.k])
                yield
                S.op("dve", lambda e: e.tensor_scalar(out=sm[:, 32:40], in0=sm[:, 32:40], scalar1=1.0, scalar2=None, op0=ALU.add), reads=[sm.k], writes=[sm.k])
                yield
                S.op("act", lambda e: e.activation(out=sm[:, 32:40], in_=sm[:, 32:40], func=AF.Ln), reads=[sm.k], writes=[sm.k])
                yield
                S.op("dve", lambda e: e.scalar_tensor_tensor(out=sm[:, 40:48], in0=sm[:, 24:32], scalar=0.0, in1=sm[:, 32:40], op0=ALU.max, op1=ALU.add), reads=[sm.k], writes=[sm.k])
                yield
                S.op("dve", lambda e: e.scalar_tensor_tensor(out=sm[:, 48:56], in0=sm[:, 40:48], scalar=-1.0, in1=aexp[:], op0=ALU.mult, op1=ALU.mult), reads=[sm.k, aexp.k], writes=[sm.k])
                yield

                def cs(e):
                    e.matmul(ps[6][:, 16:24], lhsT=uinc[:], rhs=sm[:, 48:56], start=True, stop=True, skip_group_check=True)
                    return e.matmul(ps[6][:, 24:32], lhsT=onesf[:], rhs=sm[:, 48:56], start=True, stop=True, skip_group_check=True)
                S.op("pe", cs, reads=[uinc.k, onesf.k, sm.k], writes=[ps[6].k])
                yield
                S.op("act", lambda e: e.activation(out=sm[:, 56:72], in_=ps[6][:, 16:32], func=AF.Copy), reads=[ps[6].k], writes=[sm.k])
                yield
                S.op("act", lambda e: e.activation(out=sm[:, 72:80], in_=sm[:, 56:64], func=AF.Exp), reads=[sm.k], writes=[sm.k])
                yield
                S.op("dve", lambda e: e.tensor_tensor(out=sm[:, 80:88], in0=sm[:, 64:72], in1=sm[:, 56:64], op=ALU.subtract), reads=[sm.k], writes=[sm.k])
                yield
                S.op("act", lambda e: e.activation(out=sm[:, 80:96], in_=sm[:, 80:96] if False else sm[:, 80:88], func=AF.Exp) if False else e.activation(out=sm[:, 80:88], in_=sm[:, 80:88], func=AF.Exp), reads=[sm.k], writes=[sm.k])
                yield
                S.op("act", lambda e: e.activation(out=sm[:, 88:96], in_=sm[:, 64:72], func=AF.Exp), reads=[sm.k], writes=[sm.k])
                yield
                S.op("dve", lambda e: e.tensor_tensor(out=sm2[:, 0:8], in0=sm[:, 16:24], in1=sm[:, 72:80], op=ALU.mult), reads=[sm.k], writes=[sm2.k])
                yield
                for grp, dst in ((2, vTb), (0, qT), (1, kT)):
                    if grp == 0 and not own:
                        continue
                    Ug = lambda jj, grp=grp: U[:, grp * 8:(grp + 1) * 8, jj:jj + 128]
                    cw = lambda jj, grp=grp: bc(convw[:, grp * 8:(grp + 1) * 8, jj:jj + 1], [128, 8, 128])
                    ceng = "pool" if grp == 2 else "dve"
                    Yg = YgP if grp == 2 else YgD
                    Ct = CtP if grp == 2 else CtD
                    S.op(ceng, lambda e, Ug=Ug, cw=cw, Yg=Yg: e.tensor_tensor(out=Yg[:], in0=Ug(0), in1=cw(0), op=ALU.mult), reads=[U.k, convw.k], writes=[Yg.k])
                    yield
                    for jj in range(1, 4):
                        S.op(ceng, lambda e, Ug=Ug, cw=cw, jj=jj, Ct=Ct: e.tensor_tensor(out=Ct[:], in0=Ug(jj), in1=cw(jj), op=ALU.mult), reads=[U.k, convw.k], writes=[Ct.k])
                        yield
                        S.op(ceng, lambda e, Yg=Yg, Ct=Ct: e.tensor_tensor(out=Yg[:], in0=Yg[:], in1=Ct[:], op=ALU.add), reads=[Yg.k, Ct.k], writes=[Yg.k])
                        yield
                    Yf = Yg[:].rearrange("p h d -> p (h d)")
                    if grp == 2:
                        S.op("act", lambda e, Yf=Yf: e.activation(out=vTb[:].rearrange("p h d -> p (h d)"), in_=Yf, func=AF.Silu), reads=[Yg.k], writes=[vTb.k])
                        continue
                    S.op("act", lambda e, Yf=Yf: e.activation(out=Yf, in_=Yf, func=AF.Silu), reads=[Yg.k], writes=[Yg.k])
                    yield
                    S.op("dve", lambda e, Yf=Yf: e.tensor_tensor(out=sq[:], in0=Yf, in1=Yf, op=ALU.mult), reads=[Yg.k], writes=[sq.k])
                    yield
                    for g in range(2):
                        pq = ps[4 + g]
                        S.op("pe", lambda e, pq=pq, g=g: e.matmul(pq[:], lhsT=onesb[:], rhs=sq[:, g * 512:(g + 1) * 512], start=True, stop=True), reads=[onesb.k, sq.k], writes=[pq.k])
                        S.op("dve", lambda e, pq=pq, g=g: e.tensor_scalar(out=rn[:, g * 512:(g + 1) * 512], in0=pq[:], scalar1=EPS, scalar2=None, op0=ALU.add), reads=[pq.k], writes=[rn.k])
                    S.op("act", lambda e: e.activation(out=rn[:], in_=rn[:], func=AF.Ln), reads=[rn.k], writes=[rn.k])
                    yield
                    S.op("act", lambda e: e.activation(out=rn[:], in_=rn[:], func=AF.Exp, scale=-0.5), reads=[rn.k], writes=[rn.k])
                    yield
                    sc = float(128.0 ** -0.5) if grp == 0 else 1.0
                    S.op("dve", lambda e, Yf=Yf, dst=dst, sc=sc: e.scalar_tensor_tensor(out=dst[:].rearrange("p h d -> p (h d)"), in0=Yf, scalar=sc, in1=rn[:], op0=ALU.mult, op1=ALU.mult), reads=[Yg.k, rn.k], writes=[dst.k])
                    yield
                S.op("pool", lambda e: e.tensor_copy(out=U[:, :, 0:3], in_=U[:, :, 128:131]), reads=[U.k], writes=[U.k])
                yield
                def trk(e):
                    for h in range(8):
                        ins = e.transpose(out=psb[:, h * 128:(h + 1) * 128], in_=kT[:, h, :], identity=self.identb[:])
                    return ins
                S.op("pe", trk, reads=[kT.k, self.identb.k], writes=[psb.k])
                yield
                S.op("act", lambda e: e.activation(out=ktok[:].rearrange("p h d -> p (h d)"), in_=psb[:], func=AF.Copy), reads=[psb.k], writes=[ktok.k])
                yield
                S.op("dve", lambda e: e.tensor_tensor(out=kbg[:], in0=ktok[:], in1=bc(sm2[:, 0:8].unsqueeze(2), [128, 8, 128]), op=ALU.mult), reads=[ktok.k, sm2.k], writes=[kbg.k])
                yield
                S.op("pool", lambda e: e.tensor_tensor(out=kdec[:], in0=ktok[:], in1=bc(sm[:, 80:88].unsqueeze(2), [128, 8, 128]), op=ALU.mult), reads=[ktok.k, sm.k], writes=[kdec.k])
                yield

                def trv(e):
                    for h in range(8):
                        ins = e.transpose(out=psb[:, h * 128:(h + 1) * 128], in_=vTb[:, h, :], identity=self.identb[:])
                    return ins
                S.op("pe", trv, reads=[vTb.k, self.identb.k], writes=[psb.k])
                yield
                S.op("dve", lambda e: e.tensor_tensor(out=vb[:], in0=psb[:].rearrange("p (h d) -> p h d", h=8), in1=bc(sm[:, 16:24].unsqueeze(2), [128, 8, 128]), op=ALU.mult), reads=[psb.k, sm.k], writes=[vb.k])
                yield
                S.op("dve", lambda e: e.tensor_tensor(out=Rall[:], in0=bc(uinc[:].unsqueeze(1), [128, 8, 128]), in1=bc(sm[:, 48:56].unsqueeze(2), [128, 8, 128]), op=ALU.mult), reads=[uinc.k, sm.k], writes=[Rall.k])
                yield
                for g in range(2):
                    pq = ps[4 + g]

                    def grl(e, pq=pq, g=g):
                        e.matmul(pq[:], lhsT=onesf[:], rhs=f4(Rall, g), start=True, stop=False)
                        return e.matmul(pq[:], lhsT=self.identb[:], rhs=mskL4[:].rearrange("p h d -> p (h d)"), start=False, stop=True)
                    S.op("pe", grl, reads=[onesf.k, Rall.k, self.identb.k, mskL4.k], writes=[pq.k])
                    yield
                    for hh in range(4):
                        h = g * 4 + hh
                        S.op("act", lambda e, pq=pq, h=h, hh=hh: e.activation(out=Dls[:, h, :], in_=pq[:, hh * 128:(hh + 1) * 128], func=AF.Exp, scale=-1.0, bias=sm[:, 56 + h:57 + h]), reads=[pq.k, sm.k], writes=[Dls.k])
                if own:
                    for g in range(2):
                        pq = ps[4 + g]
                        S.op("pe", lambda e, pq=pq, g=g: e.matmul(pq[:], lhsT=onesf[:], rhs=f4(Rall, g), start=True, stop=True), reads=[onesf.k, Rall.k], writes=[pq.k])
                        S.op("act", lambda e, pq=pq, g=g: e.activation(out=f4(egr, g), in_=pq[:], func=AF.Exp), reads=[pq.k], writes=[egr.k])
                        def gru2(e, pq=pq, g=g):
                            e.matmul(pq[:], lhsT=onesf[:], rhs=f4(Rall, g), start=True, stop=False)
                            return e.matmul(pq[:], lhsT=self.identb[:], rhs=mskU4[:].rearrange("p h d -> p (h d)"), start=False, stop=True)
                        S.op("pe", gru2, reads=[onesf.k, Rall.k, self.identb.k, mskU4.k], writes=[pq.k])
                        S.op("pool", lambda e, g=g: e.tensor_scalar(out=sm2[:, 8:16], in0=sm[:, 56:64], scalar1=-1.0, scalar2=None, op0=ALU.mult), reads=[sm.k], writes=[sm2.k])
                        for hh in range(4):
                            h = g * 4 + hh
                            S.op("act", lambda e, pq=pq, h=h, hh=hh: e.activation(out=DT[:, h, :], in_=pq[:, hh * 128:(hh + 1) * 128], func=AF.Exp, scale=1.0, bias=sm2[:, 8 + h:9 + h]), reads=[pq.k, sm2.k], writes=[DT.k])
                    S.op("pool", lambda e: e.tensor_tensor(out=qdT[:], in0=qT[:], in1=egr[:], op=ALU.mult), reads=[qT.k, egr.k], writes=[qdT.k])
                    yield
                    for g, f in batch_mm((ps[4], ps[5]), lambda e, out, h: e.matmul(out, lhsT=kT[:, h, :], rhs=qT[:, h, :], start=True, stop=True, skip_group_check=True)):
                        S.op("pe", f, reads=[kT.k, qT.k], writes=[ps[4 + g].k])
                        S.op("dve", lambda e, g=g: e.tensor_tensor(out=f4(QKT, g), in0=ps[4 + g][:], in1=f4(DT, g), op=ALU.mult), reads=[ps[4 + g].k, DT.k], writes=[QKT.k])

                yield

            def gNR(p):
                own = (p % 2 == 1)
                j = p // 2
                r0 = p * 128
                DB = p % 2
                kT = kT2[DB]; kbg = kbg2[DB]; kdec = kdec2[DB]; vb = vb2[DB]; sm = smm[DB]; Dls = Dls2[DB]
                qdT = qdT2[DB]; QKT = QKT2[DB]; zg = zg2[DB]
                S.op("pool", lambda e: e.tensor_tensor(out=Stmp[:], in0=S32[:], in1=bc(sm[:, 88:96].unsqueeze(2), [128, 8, 128]), op=ALU.mult), reads=[S32.k, sm.k], writes=[Stmp.k])
                yield
                A0, B0 = Am[0], Bm[0]
                for g, f in batch_mm((ps[0], ps[1]), lambda e, out, h: e.matmul(out, lhsT=kT[:, h, :], rhs=kT[:, h, :], start=True, stop=True, skip_group_check=True)):
                    S.op("pe", f, reads=[kT.k], writes=[ps[g].k])
                    yield
                    for hh in range(4):
                        h = g * 4 + hh
                        S.op("dve", lambda e, g=g, h=h, hh=hh: e.scalar_tensor_tensor(out=A0[:, h, :], in0=ps[g][:, hh * 128:(hh + 1) * 128], scalar=sm[:, 16 + h:17 + h], in1=Dls[:, h, :], op0=ALU.mult, op1=ALU.mult), reads=[ps[g].k, sm.k, Dls.k], writes=[A0.k])
                for g, f in batch_mm((ps[2], ps[3]), lambda e, out, h: e.transpose(out=out, in_=RF(A0[:, h, :]), identity=self.identf[:])):
                    S.op("pe", f, reads=[A0.k, self.identf.k], writes=[ps[2 + g].k])
                    yield
                    S.op("act", lambda e, g=g: e.activation(out=f4(B0, g), in_=ps[2 + g][:], func=AF.Copy), reads=[ps[2 + g].k], writes=[B0.k])
                    yield
                S.op("pool", lambda e: e.tensor_tensor(out=Pm[:], in0=bc(self.identf[:].unsqueeze(1), [128, 8, 128]), in1=RF(B0[:]), op=ALU.subtract), reads=[self.identf.k, B0.k], writes=[Pm.k])
                yield
                cur = 0
                for lv in range(6):
                    Ac, Bc = Am[cur], Bm[cur]
                    An, Bn_ = Am[1 - cur], Bm[1 - cur]
                    last = (lv == 5)
                    for g, f in batch_mm((ps[0], ps[1]), lambda e, out, h, Ac=Ac, Bc=Bc: e.matmul(out, lhsT=NC_(Bc[:, h, :]), rhs=NC_(Ac[:, h, :]), start=True, stop=True, skip_group_check=True)):
                        S.op("pe", f, reads=[Ac.k, Bc.k], writes=[ps[g].k])
                        S.op("act", lambda e, g=g, An=An: e.activation(out=f4(An, g), in_=ps[g][:], func=AF.Copy), reads=[ps[g].k], writes=[An.k])
                    if not last:
                        for g, f in batch_mm((ps[2], ps[3]), lambda e, out, h, Ac=Ac, Bc=Bc: e.matmul(out, lhsT=NC_(Ac[:, h, :]), rhs=NC_(Bc[:, h, :]), start=True, stop=True, skip_group_check=True)):
                            S.op("pe", f, reads=[Ac.k, Bc.k], writes=[ps[2 + g].k])
                            S.op("dve", lambda e, g=g, Bn_=Bn_: e.tensor_copy(out=f4(Bn_, g), in_=ps[2 + g][:]), reads=[ps[2 + g].k], writes=[Bn_.k])
                    for g, f in batch_mm((ps[0], ps[1]), lambda e, out, h, An=An: e.matmul(out, lhsT=NC_(An[:, h, :]), rhs=NC_(Pm[:, h, :]), start=True, stop=True, skip_group_check=True)):
                        S.op("pe", f, reads=[An.k, Pm.k], writes=[ps[g].k])
                        S.op("dve", lambda e, g=g: e.tensor_tensor(out=f4(Pm, g), in0=RF(f4(Pm, g)), in1=ps[g][:], op=ALU.add), reads=[Pm.k, ps[g].k], writes=[Pm.k])
                    cur = 1 - cur
                S.op("act", lambda e: e.activation(out=TTb[:].rearrange("p h d -> p (h d)"), in_=RF(Pm[:]).rearrange("p h d -> p (h d)"), func=AF.Copy), reads=[Pm.k], writes=[TTb.k])
                yield
                for g, f in batch_mm((ps[0], ps[1]), lambda e, out, h: e.matmul(out, lhsT=TTb[:, h, :], rhs=vb[:, h, :], start=True, stop=True, skip_group_check=True)):
                    S.op("pe", f, reads=[TTb.k, vb.k], writes=[ps[g].k])
                    yield
                    S.op("act", lambda e, g=g: e.activation(out=f4(u, g), in_=ps[g][:], func=AF.Copy), reads=[ps[g].k], writes=[u.k])
                    yield
                for g, f in batch_mm((ps[2], ps[3]), lambda e, out, h: e.matmul(out, lhsT=kbg[:, h, :], rhs=TTb[:, h, :], start=True, stop=True, skip_group_check=True)):
                    S.op("pe", f, reads=[TTb.k, kbg.k], writes=[ps[2 + g].k])
                    yield
                    S.op("act", lambda e, g=g: e.activation(out=f4(wT, g), in_=ps[2 + g][:], func=AF.Copy), reads=[ps[2 + g].k], writes=[wT.k])
                    yield
                for g, f in batch_mm((ps[0], ps[1]), lambda e, out, h: e.matmul(out, lhsT=wT[:, h, :], rhs=Sbf[:, h, :], start=True, stop=True, skip_group_check=True)):
                    S.op("pe", f, reads=[wT.k, Sbf.k], writes=[ps[g].k])
                    yield
                    S.op("dve", lambda e, g=g: e.tensor_tensor(out=f4(vn, g), in0=f4(u, g), in1=ps[g][:], op=ALU.subtract), reads=[u.k, ps[g].k], writes=[vn.k])
                    yield
                if own:
                    for g in range(2):
                        def fo(e, g=g):
                            for hh in range(4):
                                h = g * 4 + hh
                                e.matmul(ps[g][:, hh * 128:(hh + 1) * 128], lhsT=qdT[:, h, :], rhs=Sbf[:, h, :], start=True, stop=False, skip_group_check=True)
                                ins = e.matmul(ps[g][:, hh * 128:(hh + 1) * 128], lhsT=QKT[:, h, :], rhs=vn[:, h, :], start=False, stop=True, skip_group_check=True)
                            return ins
                        S.op("pe", fo, reads=[qdT.k, Sbf.k, QKT.k, vn.k], writes=[ps[g].k])
                        S.op("act", lambda e, g=g: e.activation(out=f4(o, g), in_=ps[g][:], func=AF.Copy), reads=[ps[g].k], writes=[o.k])
                for g, f in batch_mm((ps[2], ps[3]), lambda e, out, h: e.matmul(out, lhsT=kdec[:, h, :], rhs=vn[:, h, :], start=True, stop=True, skip_group_check=True)):
                    S.op("pe", f, reads=[kdec.k, vn.k], writes=[ps[2 + g].k])
                    yield
                    S.op("dve", lambda e, g=g: e.tensor_tensor(out=f4(S32, g), in0=f4(Stmp, g), in1=ps[2 + g][:], op=ALU.add), reads=[Stmp.k, ps[2 + g].k], writes=[S32.k])
                    yield
                S.op("act", lambda e: e.activation(out=Sbf[:].rearrange("p h d -> p (h d)"), in_=S32[:].rearrange("p h d -> p (h d)"), func=AF.Copy), reads=[S32.k], writes=[Sbf.k])
                yield
                if own:
                    S.op("pool", lambda e: e.tensor_tensor(out=osq[:], in0=o[:], in1=o[:], op=ALU.mult), reads=[o.k], writes=[osq.k])
                    yield
                    S.op("dve", lambda e: e.tensor_reduce(out=sm3[:, 0:8], in_=osq[:], axis=AX.X, op=ALU.add), reads=[osq.k], writes=[sm2.k])
                    yield
                    S.op("dve", lambda e: e.tensor_scalar(out=sm3[:, 0:8], in0=sm3[:, 0:8], scalar1=1.0 / 128, scalar2=EPS, op0=ALU.mult, op1=ALU.add), reads=[sm2.k], writes=[sm2.k])
                    yield
                    S.op("act", lambda e: e.activation(out=sm3[:, 0:8], in_=sm3[:, 0:8], func=AF.Ln), reads=[sm2.k], writes=[sm2.k])
                    yield
                    S.op("act", lambda e: e.activation(out=sm3[:, 0:8], in_=sm3[:, 0:8], func=AF.Exp, scale=-0.5), reads=[sm2.k], writes=[sm2.k])
                    yield
                    S.op("dve", lambda e: e.tensor_tensor(out=o[:], in0=o[:], in1=bc(sm3[:, 0:8].unsqueeze(2), [128, 8, 128]), op=ALU.mult), reads=[o.k, sm2.k], writes=[o.k])
                    yield
                    S.op("pool", lambda e: e.tensor_tensor(out=o[:], in0=o[:], in1=bc(ggdn[:].unsqueeze(1), [128, 8, 128]), op=ALU.mult), reads=[o.k, ggdn.k], writes=[o.k])
                    yield
                    S.op("dve", lambda e: e.tensor_tensor(out=mgt[:], in0=o[:].rearrange("p h d -> p (h d)"), in1=zg[:], op=ALU.mult), reads=[o.k, zg.k], writes=[mgt.k])
                    yield
                    S.dma(lambda e, j=j: e.dma_start(out=self.s_mg[j], in_=mgt[:]), reads=[mgt.k], writes=[self.k_own[j]["mg"]])

                yield

            def run_pair(ga, gb, na, nb):
                nmax = max(na, nb, 1)
                da = db = 0
                for k in range(nmax):
                    ta = ((k + 1) * na) // nmax
                    tb = ((k + 1) * nb) // nmax
                    while da < ta:
                        next(ga, None); da += 1
                    while db < tb:
                        next(gb, None); db += 1
                for _ in ga:
                    pass
                for _ in gb:
                    pass

            def count(mk, p):
                S.dry = True
                n = sum(1 for _ in mk(p))
                S.dry = False
                return n
            nF = [count(gF, 0), count(gF, 1)]
            nN = [count(gNR, 0), count(gNR, 1)]
            for _ in gF(0):
                pass
            for p in range(P):
                if p + 1 < P:
                    run_pair(gNR(p), gF(p + 1), nN[p % 2], nF[(p + 1) % 2])
                else:
                    for _ in gNR(p):
                        pass
            S.barrier()
            S.emit()

    def phase1a_stub(self):
        S = self.S
        with ExitStack() as es:
            z = self.sb(es, "zmg", [128, 1024], BF16)
            S.op("pool", lambda e: e.memset(z[:], 0.0), writes=[z.k])
            for j in range(self.NO):
                S.dma(lambda e, j=j: e.dma_start(out=self.s_mg[j], in_=z[:]), reads=[z.k], writes=[self.k_own[j]["mg"]])
            S.barrier()
            S.emit()


def build(S, debug=False, gdn=True):
    B = Builder(S, debug)
    with ExitStack() as es:
        B.setup(es)
        B.phase1b()
        if gdn:
            B.phase1a()
        else:
            B.phase1a_stub()
        B.phase2()
        B.phase3()
    return B.nc


IN_SPLITS = (3072, 1024, 8, 8, 1024, 256, 256, 512, 64, 8, 1024, 1024)


def host_consts():
    bf = ml_dtypes.bfloat16
    i = np.arange(128)
    c = {}
    c["identf"] = np.eye(128, dtype=np.float32)
    c["identb"] = np.eye(128).astype(bf)
    c["uinc"] = (i[:, None] <= i[None, :]).astype(np.float32)
    c["causal"] = np.where(i[None, :] <= i[:, None], 0.0, NEG).astype(np.float32)
    c["mskU"] = np.where(i[None, :] >= i[:, None], 0.0, -30000.0).astype(bf)
    c["mskL"] = np.where(i[None, :] < i[:, None], 0.0, 30000.0).astype(bf)
    return c


def rope_tab(pos, rot):
    inv = (np.float32(500000.0) ** (-np.arange(0, rot, 2, dtype=np.float32) / np.float32(rot))).astype(np.float32)
    ang = pos.astype(np.float32)[:, None] * inv[None, :]
    return np.cos(ang).astype(np.float32), np.sin(ang).astype(np.float32)


def make_in_maps(S, x, norm_mix_g, w_in, conv_w, a_log, dt_bias, gdn_norm_g, q_norm_g, k_norm_g,
                 w_out, norm_mlp_g, w_mlp_up, w_mlp_down):
    Bn = x.shape[0]
    f = np.float32
    w_in = np.asarray(w_in[0], f)
    pts = np.cumsum((0,) + IN_SPLITS)
    seg = {n: w_in[:, pts[i]:pts[i + 1]] for i, n in enumerate(
        ("qkv", "z", "ga", "gb", "aq", "ak", "av", "iq", "ik", "iw", "gatea", "gateb"))}
    w1a = np.ascontiguousarray(np.concatenate([seg["qkv"], seg["ga"], seg["gb"]], 1))
    w1b = np.ascontiguousarray(np.concatenate([seg["ak"], seg["av"], seg["ik"], seg["iw"], seg["aq"], seg["iq"], seg["gateb"], seg["z"], seg["gatea"]], 1))
    col = lambda v: np.ascontiguousarray(np.asarray(v, f).reshape(8, 128).T)
    rep = lambda v: np.ascontiguousarray(np.broadcast_to(np.asarray(v, f)[None, :], (128, len(v))))
    common = host_consts()
    common.update(dict(
        w1a=w1a, w1b=w1b, wout=np.ascontiguousarray(w_out[0], f), wup=np.ascontiguousarray(w_mlp_up[0], f),
        wdn=np.ascontiguousarray(w_mlp_down[0], f),
        gmix=col(norm_mix_g[0]), gmlp=col(norm_mlp_g[0]), gq=rep(q_norm_g[0]), gk=rep(k_norm_g[0]),
        ggdn=rep(gdn_norm_g[0]), alog=rep(a_log[0]), dtb=rep(dt_bias[0]),
        convw=np.ascontiguousarray(np.asarray(conv_w[0], f).reshape(4, 24, 128).transpose(2, 1, 0).reshape(128, 96)),
    ))
    maps = []
    for b in range(Bn):
        for c in range(2):
            m = dict(common)
            xb = np.asarray(x[b], f)
            if c == 0:
                xs = np.concatenate([np.zeros((128, D), f), xb[:S - 128]], 0)
            else:
                xs = xb
            pos = np.maximum(np.arange(S) + (c - 1) * 128, 0)
            m["xseq"] = np.ascontiguousarray(xs)
            m["cosA"], m["sinA"] = rope_tab(pos, 32)
            m["cosI"], m["sinI"] = rope_tab(pos, 16)
            m["blk0"] = np.full((128, 128), NEG if c == 0 else 0.0, f)
            maps.append(m)
    return maps


def assemble(S, results, Bn):
    out = np.zeros((Bn, S, D), np.float32)
    ov = out.reshape(Bn, S // 256, 2, 128, D)
    for b in range(Bn):
        for c in range(2):
            r = results[b * 2 + c]["out"].reshape(S // 256, 128, D)
            ov[b, :, c] = r
    return out


_NC_CACHE = {}


def kernel(**inputs):
    x = np.asarray(inputs["x"])
    Bn, S, _ = x.shape
    if S not in _NC_CACHE:
        _NC_CACHE[S] = build(S)
    nc = _NC_CACHE[S]
    maps = make_in_maps(S, **{k: np.asarray(v) for k, v in inputs.items()})
    res = run_bass_kernel_spmd(nc, maps, core_ids=list(range(2 * Bn)))
    return assemble(S, res.results, Bn)
```

```python
from contextlib import ExitStack
import numpy as np
import ml_dtypes
import concourse.bass as bass
import concourse.mybir as mybir
from concourse.bass_utils import run_bass_kernel_spmd

F32 = mybir.dt.float32
BF16 = mybir.dt.bfloat16
AF = mybir.ActivationFunctionType
ALU = mybir.AluOpType
AX = mybir.AxisListType

D = 1024
EPS = 1e-6
NEG = -1.0e30
SAME_ENGINE_SYNC = True
IDX_F32R = True
IDT = mybir.dt.float32r if IDX_F32R else F32
NEU_F32R = True
NC_ = lambda ap: ap
NDT = mybir.dt.float32r if NEU_F32R else F32
RF = (lambda ap: ap.bitcast(F32)) if NEU_F32R else (lambda ap: ap)


class Tok:
    __slots__ = ("name", "w", "r")

    def __init__(self, name):
        self.name = name
        self.w = None
        self.r = {}


class Sched:
    ENG = ("pe", "act", "dve", "pool", "sp")

    def __init__(self, nc, es, n_dma_sems=24):
        self.nc = nc
        self.sems = {}
        for e in ("pe", "act", "dve", "pool"):
            self.sems[e] = es.enter_context(nc.semaphore("s_" + e))
        self.ndma = n_dma_sems
        for k in range(n_dma_sems):
            self.sems[("d", k)] = es.enter_context(nc.semaphore("s_d%d" % k))
        self.cnt = {e: 0 for e in ("pe", "act", "dve", "pool")}
        self.dtot = [0] * n_dma_sems
        self.rr = 0
        self.waited = {e: {} for e in self.ENG}
        self.ops = {e: [] for e in self.ENG}
        self.nops = 0
        self.dry = False

    def _deps(self, eng, reads, writes):
        deps = {}

        def add(d):
            if d is None:
                return
            sid, val = d
            if sid == eng and (eng == "pe" or not SAME_ENGINE_SYNC):
                return
            if deps.get(sid, 0) < val:
                deps[sid] = val

        for t in reads:
            add(t.w)
        for t in writes:
            add(t.w)
            for sid, val in t.r.items():
                add((sid, val))
        out = []
        wd = self.waited[eng]
        for sid, val in deps.items():
            if wd.get(sid, 0) < val:
                wd[sid] = val
                out.append((sid, val))
        return out

    def _mark(self, me, reads, writes):
        for t in reads:
            if t.r.get(me[0], 0) < me[1]:
                t.r[me[0]] = me[1]
        for t in writes:
            t.w = me
            t.r = {}

    def op(self, eng, fn, reads=(), writes=()):
        if self.dry:
            return
        waits = self._deps(eng, reads, writes)
        self.cnt[eng] += 1
        me = (eng, self.cnt[eng])
        self._mark(me, reads, writes)
        self.ops[eng].append((waits, fn, eng, 1))
        self.nops += 1

    def dma(self, fn, reads=(), writes=()):
        if self.dry:
            return
        k = self.rr
        self.rr = (self.rr + 1) % self.ndma
        sid = ("d", k)
        waits = self._deps("sp", reads, writes)
        wd = self.waited["sp"]
        if wd.get(sid, 0) < self.dtot[k]:
            wd[sid] = self.dtot[k]
            waits.append((sid, self.dtot[k]))
        self.dtot[k] += 16
        me = (sid, self.dtot[k])
        self._mark(me, reads, writes)
        self.ops["sp"].append((waits, fn, sid, 16))
        self.nops += 1

    def barrier(self):
        for e in self.ENG:
            waits = []
            wd = self.waited[e]
            for c in ("pe", "act", "dve", "pool"):
                if c != e and wd.get(c, 0) < self.cnt[c]:
                    wd[c] = self.cnt[c]
                    waits.append((c, self.cnt[c]))
            for k in range(self.ndma):
                sid = ("d", k)
                if wd.get(sid, 0) < self.dtot[k]:
                    wd[sid] = self.dtot[k]
                    waits.append((sid, self.dtot[k]))
            if waits:
                self.ops[e].append((waits, None, None, 0))

    def emit(self):
        nc = self.nc
        ops = self.ops
        self.ops = {e: [] for e in self.ENG}
        sems = self.sems

        def run(engh, lst):
            for waits, fn, sid, inc in lst:
                for s, v in waits:
                    engh.wait_ge(sems[s], v)
                if fn is not None:
                    fn(engh).then_inc(sems[sid], inc)

        with nc.Block() as block:
            @block.tensor
            def _(e):
                run(e, ops["pe"])

            @block.scalar
            def _(e):
                run(e, ops["act"])

            @block.vector
            def _(e):
                run(e, ops["dve"])

            @block.gpsimd
            def _(e):
                run(e, ops["pool"])

            @block.sync
            def _(e):
                run(e, ops["sp"])


class Ctx:
    pass


def bc(ap, shape):
    return ap.to_broadcast(list(shape))


class TL:
    def __init__(self, t, name):
        self.t = t
        self.k = Tok(name)

    def __getitem__(self, key):
        return self.t[key]


C1B = 5192
C1A = 3088


class Builder:
    def __init__(self, S, debug=False):
        self.Sq = S
        self.P = S // 128
        self.NO = self.P // 2
        self.debug = debug
        self.nc = bass.Bass("TRN2", target_bir_lowering=False)

    def dram_in(self, name, shape, dt=F32):
        return self.nc.dram_tensor(name, list(shape), dt, kind="ExternalInput").ap()

    def dram_scr(self, name, shape, dt):
        kind = "ExternalOutput" if self.debug else "Internal"
        return self.nc.dram_tensor(name, list(shape), dt, kind=kind).ap()

    def sb(self, es, name, shape, dt):
        return TL(es.enter_context(self.nc.sbuf_tensor("t_" + name, list(shape), dt)), name)

    def load(self, dst, dst_ap, src_ap):
        self.S.dma(lambda e: e.dma_start(out=dst_ap, in_=src_ap), writes=[dst.k])

    def rsqrt_small(self, v, n):
        S = self.S
        S.op("act", lambda e: e.activation(out=v[:, 0:n], in_=v[:, 0:n], func=AF.Ln), reads=[v.k], writes=[v.k])
        S.op("act", lambda e: e.activation(out=v[:, 0:n], in_=v[:, 0:n], func=AF.Exp, scale=-0.5), reads=[v.k], writes=[v.k])

    def load_weight_bf16(self, es, wb, w_ap, ncols, stg):
        S = self.S
        nk = w_ap.shape[0] // 128
        CH = stg[0].t.shape[1]
        i = 0
        for k in range(nk):
            for c0 in range(0, ncols, CH):
                cw = min(CH, ncols - c0)
                st = stg[i % len(stg)]
                S.dma(lambda e, st=st, k=k, c0=c0, cw=cw: e.dma_start(out=st[:, 0:cw], in_=w_ap[k * 128:(k + 1) * 128, c0:c0 + cw]), writes=[st.k])
                eng = ("pool", "dve", "act")[i % 3] if False else "pool"
                S.op(eng, lambda e, st=st, k=k, c0=c0, cw=cw: e.tensor_copy(out=wb[:, k, c0:c0 + cw], in_=st[:, 0:cw]), reads=[st.k], writes=[wb.k])
                i += 1

    def rms_hT(self, xb, ss, xnb, hT, gcol, psb):
        S = self.S
        junk = self.junk
        idb = self.identb
        S.op("act", lambda e: e.activation(out=junk[:, 0:1024], in_=xb[:], func=AF.Square, accum_out=ss[:, 0:1]), reads=[xb.k], writes=[junk.k, ss.k])
        S.op("dve", lambda e: e.tensor_scalar(out=ss[:, 0:1], in0=ss[:, 0:1], scalar1=1.0 / 1024, scalar2=EPS, op0=ALU.mult, op1=ALU.add), reads=[ss.k], writes=[ss.k])
        self.rsqrt_small(ss, 1)
        S.op("dve", lambda e: e.tensor_scalar(out=xnb[:], in0=xb[:], scalar1=ss[:, 0:1], scalar2=None, op0=ALU.mult), reads=[ss.k, xb.k], writes=[xnb.k])

        def tr(e):
            for kc in range(8):
                i = e.transpose(out=psb[:, kc * 128:(kc + 1) * 128], in_=xnb[:, kc * 128:(kc + 1) * 128], identity=idb[:])
            return i
        S.op("pe", tr, reads=[xnb.k, idb.k], writes=[psb.k])
        S.op("dve", lambda e: e.tensor_tensor(out=hT[:], in0=psb[:].rearrange("p (k t) -> p k t", k=8), in1=bc(gcol[:].unsqueeze(2), [128, 8, 128]), op=ALU.mult), reads=[psb.k, gcol.k], writes=[hT.k])

    def proj(self, ps, ncol, hT, wb, c0, pcol=0, start=True):
        def f(e):
            for kc in range(8):
                i = e.matmul(ps[:, pcol:pcol + ncol], lhsT=hT[:, kc, :], rhs=wb[:, kc, c0:c0 + ncol], start=(kc == 0), stop=(kc == 7))
            return i
        self.S.op("pe", f, reads=[hT.k, wb.k], writes=[ps.k])

    def setup(self, es):
        nc = self.nc
        S_, P, NO = self.Sq, self.P, self.NO
        self.S = Sched(nc, es)
        d = self.dram_in
        self.i_x = d("xseq", [S_, D])
        self.i_w1a = d("w1a", [D, C1A])
        self.i_w1b = d("w1b", [D, C1B])
        self.i_wout = d("wout", [D, D])
        self.i_wup = d("wup", [D, 4 * D])
        self.i_wdn = d("wdn", [4 * D, D])
        self.i_cosA = d("cosA", [S_, 16]); self.i_sinA = d("sinA", [S_, 16])
        self.i_cosI = d("cosI", [S_, 8]); self.i_sinI = d("sinI", [S_, 8])
        self.i_blk0 = d("blk0", [128, 128])
        self.i_identf = d("identf", [128, 128]); self.i_identb = d("identb", [128, 128], BF16)
        self.i_uinc = d("uinc", [128, 128]); self.i_causal = d("causal", [128, 128])
        self.i_mskU = d("mskU", [128, 128], BF16); self.i_mskL = d("mskL", [128, 128], BF16)
        self.i_gmix = d("gmix", [128, 8]); self.i_gmlp = d("gmlp", [128, 8])
        self.i_gq = d("gq", [128, 128]); self.i_gk = d("gk", [128, 128]); self.i_ggdn = d("ggdn", [128, 128])
        self.i_alog = d("alog", [128, 8]); self.i_dtb = d("dtb", [128, 8])
        self.i_convw = d("convw", [128, 24 * 4])
        self.o_out = nc.dram_tensor("out", [NO * 128, D], F32, kind="ExternalOutput").ap()
        s = self.dram_scr
        self.s_akT = s("s_akT", [128, 2 * S_], BF16)
        self.s_v = s("s_v", [S_, 258], BF16)
        self.s_ikT = s("s_ikT", [128, S_], F32)
        self.s_aqT = s("s_aqT", [NO, 128, 1024], BF16)
        self.s_iqT = s("s_iqT", [NO, 128, 512], F32)
        self.s_iw = s("s_iw", [NO, 128, 8], F32)
        self.s_sgb = s("s_sgb", [NO, 128, 1024], BF16)
        self.s_mg = s("s_mg", [NO, 128, 1024], BF16)
        self.s_zg = s("s_zg", [NO, 128, 1024], BF16)
        self.s_mrg = s("s_mrg", [NO, 128, 1024], BF16)
        self.k_akT = Tok("akT_d"); self.k_v = Tok("v_d"); self.k_ikT = Tok("ikT_d")
        self.k_own = [{n: Tok(n + str(j)) for n in ("aqT", "iqT", "iw", "sgb", "mg", "x1", "zg")} for j in range(NO)]
        self.ps = [TL(es.enter_context(nc.psum_tensor("ps%d" % i, [128, 512], F32)), "ps%d" % i) for i in range(7)]
        self.psb = TL(es.enter_context(nc.psum_tensor("psb", [128, 1024], BF16)), "psb")
        sb = lambda n, sh, dt: self.sb(es, n, sh, dt)
        self.identf = sb("identf", [128, 128], F32); self.identb = sb("identb", [128, 128], BF16)
        self.junk = sb("junk", [128, 1024], F32)
        for t, src in ((self.identf, self.i_identf), (self.identb, self.i_identb)):
            self.load(t, t[:], src)

    def norm_rope(self, H, src_ap, src_tok, gain, extra, out, ocol, cs, sn, tmp):
        S = self.S
        tf, sq, ssh, ra, rb = tmp
        n = H * 128
        S.op("act", lambda e: e.activation(out=tf[:, 0:n], in_=src_ap, func=AF.Copy), reads=[src_tok], writes=[tf.k])
        S.op("dve", lambda e: e.tensor_tensor(out=sq[:, 0:n], in0=tf[:, 0:n], in1=tf[:, 0:n], op=ALU.mult), reads=[tf.k], writes=[sq.k])
        S.op("dve", lambda e: e.tensor_reduce(out=ssh[:, 0:H], in_=sq[:, 0:n].rearrange("p (h d) -> p h d", h=H), axis=AX.X, op=ALU.add), reads=[sq.k], writes=[ssh.k])
        S.op("dve", lambda e: e.tensor_scalar(out=ssh[:, 0:H], in0=ssh[:, 0:H], scalar1=1.0 / 128, scalar2=EPS, op0=ALU.mult, op1=ALU.add), reads=[ssh.k], writes=[ssh.k])
        self.rsqrt_small(ssh, H)
        if extra != 1.0:
            S.op("dve", lambda e: e.tensor_scalar(out=ssh[:, 0:H], in0=ssh[:, 0:H], scalar1=extra, scalar2=None, op0=ALU.mult), reads=[ssh.k], writes=[ssh.k])
        tf3 = tf[:, 0:n].rearrange("p (h d) -> p h d", h=H)
        S.op("dve", lambda e: e.tensor_tensor(out=tf3, in0=tf3, in1=bc(ssh[:, 0:H].unsqueeze(2), [128, H, 128]), op=ALU.mult), reads=[tf.k, ssh.k], writes=[tf.k])
        S.op("pool", lambda e: e.tensor_tensor(out=tf3, in0=tf3, in1=bc(gain[:].unsqueeze(1), [128, H, 128]), op=ALU.mult), reads=[tf.k, gain.k], writes=[tf.k])
        o3 = out[:, ocol:ocol + n].rearrange("p (h d) -> p h d", h=H)
        S.op("act", lambda e: e.activation(out=out[:, ocol:ocol + n], in_=tf[:, 0:n], func=AF.Copy), reads=[tf.k], writes=[out.k])
        self.rope(tf3, tf.k, o3, out.k, H, 16, cs, sn, ra, rb)

    def rope(self, t3, ttok, o3, otok, H, hf, cs, sn, ra, rb):
        S = self.S
        x1 = t3[:, :, 0:hf]; x2 = t3[:, :, hf:2 * hf]
        c = bc(cs[:].unsqueeze(1), [128, H, hf]); s_ = bc(sn[:].unsqueeze(1), [128, H, hf])
        a3 = ra[:, 0:H * hf].rearrange("p (h d) -> p h d", h=H)
        b3 = rb[:, 0:H * hf].rearrange("p (h d) -> p h d", h=H)
        S.op("pool", lambda e: e.tensor_tensor(out=a3, in0=x1, in1=c, op=ALU.mult), reads=[ttok, cs.k], writes=[ra.k])
        S.op("pool", lambda e: e.tensor_tensor(out=b3, in0=x2, in1=s_, op=ALU.mult), reads=[ttok, sn.k], writes=[rb.k])
        S.op("dve", lambda e: e.tensor_tensor(out=o3[:, :, 0:hf], in0=a3, in1=b3, op=ALU.subtract), reads=[ra.k, rb.k], writes=[otok])
        S.op("pool", lambda e: e.tensor_tensor(out=a3, in0=x2, in1=c, op=ALU.mult), reads=[ttok, cs.k], writes=[ra.k])
        S.op("pool", lambda e: e.tensor_tensor(out=b3, in0=x1, in1=s_, op=ALU.mult), reads=[ttok, sn.k], writes=[rb.k])
        S.op("dve", lambda e: e.tensor_tensor(out=o3[:, :, hf:2 * hf], in0=a3, in1=b3, op=ALU.add), reads=[ra.k, rb.k], writes=[otok])

    def phase1b(self):
        S = self.S
        P = self.P
        ps, psb = self.ps, self.psb
        with ExitStack() as es:
            sb = lambda n, sh, dt: self.sb(es, n, sh, dt)
            wb = sb("w1b", [128, 8, C1B], BF16)
            stg = [sb("stg%d" % i, [128, 1024], F32) for i in range(2)]
            gmix = sb("gmix", [128, 8], F32); gq = sb("gq", [128, 128], F32); gk = sb("gk", [128, 128], F32)
            self.load(gmix, gmix[:], self.i_gmix); self.load(gq, gq[:], self.i_gq); self.load(gk, gk[:], self.i_gk)
            self.load_weight_bf16(es, wb, self.i_w1b, C1B, stg)
            xb = [sb("xb%d" % i, [128, 1024], F32) for i in range(2)]
            ss2 = [sb("ss%d" % i, [128, 1], F32) for i in range(2)]; xnb2 = [sb("xnb%d" % i, [128, 1024], BF16) for i in range(2)]; hT2 = [sb("hT%d" % i, [128, 8, 128], BF16) for i in range(2)]
            cA2 = [sb("cA%d" % i, [128, 16], F32) for i in range(2)]; sA2 = [sb("sA%d" % i, [128, 16], F32) for i in range(2)]
            cI2 = [sb("cI%d" % i, [128, 8], F32) for i in range(2)]; sI2 = [sb("sI%d" % i, [128, 8], F32) for i in range(2)]
            mktmp = lambda n: (sb("tf" + n, [128, 512], F32), sb("sq" + n, [128, 512], F32), sb("ssh" + n, [128, 8], F32), sb("ra" + n, [128, 64], F32), sb("rb" + n, [128, 64], F32))
            tmpK = mktmp("K"); tmpQ = [mktmp("Q0"), mktmp("Q1")]; tmpI = mktmp("I")
            ra, rb = tmpI[3], tmpI[4]
            akb = sb("akb", [128, 256], BF16); akT = sb("akTb", [128, 256], BF16)
            vaug = sb("vaug", [128, 258], BF16)
            ikf = sb("ikf", [128, 72], F32); ik2 = sb("ik2", [128, 128], F32); ikT = sb("ikTb", [128, 128], F32)
            iw = sb("iwb", [128, 8], F32)
            aqb = sb("aqb", [128, 1024], BF16); aqT = sb("aqTb", [128, 1024], BF16)
            iqf = sb("iqf", [128, 512], F32); iqo = sb("iqo", [128, 512], F32); iqT = sb("iqTb", [128, 512], F32)
            sgb = sb("sgbb", [128, 1024], BF16)
            zs = sb("zs", [128, 1024], F32); sga = sb("sga", [128, 1024], F32); zg = sb("zgb", [128, 1024], BF16)
            S.op("pool", lambda e: e.memset(vaug[:], 1.0), writes=[vaug.k])
            for p in range(P):
                own = (p % 2 == 1)
                j = p // 2
                x = xb[p % 2]
                r0 = p * 128
                ss = ss2[p % 2]; xnb = xnb2[p % 2]; hT = hT2[p % 2]
                cA = cA2[p % 2]; sA = sA2[p % 2]; cI = cI2[p % 2]; sI = sI2[p % 2]
                self.load(x, x[:], self.i_x[r0:r0 + 128, :])
                self.load(cA, cA[:], self.i_cosA[r0:r0 + 128, :]); self.load(sA, sA[:], self.i_sinA[r0:r0 + 128, :])
                self.load(cI, cI[:], self.i_cosI[r0:r0 + 128, :]); self.load(sI, sI[:], self.i_sinI[r0:r0 + 128, :])
                self.rms_hT(x, ss, xnb, hT, gmix, psb)
                self.proj(ps[0], 512, hT, wb, 0)
                self.norm_rope(2, ps[0][:, 0:256], ps[0].k, gk, 1.0, akb, 0, cA, sA, tmpK)
                S.op("act", lambda e: e.activation(out=vaug[:].rearrange("p (c d) -> p c d", c=2)[:, :, 0:128], in_=ps[0][:, 256:512].rearrange("p (c d) -> p c d", c=2), func=AF.Copy), reads=[ps[0].k], writes=[vaug.k])
                S.dma(lambda e, r0=r0: e.dma_start(out=self.s_v[r0:r0 + 128, :], in_=vaug[:]), reads=[vaug.k], writes=[self.k_v])

                def trk(e):
                    for c in range(2):
                        i = e.transpose(out=psb[:, c * 128:(c + 1) * 128], in_=akb[:, c * 128:(c + 1) * 128], identity=self.identb[:])
                    return i
                S.op("pe", trk, reads=[akb.k, self.identb.k], writes=[psb.k])
                S.op("act", lambda e: e.activation(out=akT[:], in_=psb[:, 0:256], func=AF.Copy), reads=[psb.k], writes=[akT.k])
                S.dma(lambda e, r0=r0: e.dma_start(out=self.s_akT.rearrange("h (c s) -> h c s", c=2)[:, :, r0:r0 + 128], in_=akT[:].rearrange("h (c s) -> h c s", c=2)), reads=[akT.k], writes=[self.k_akT])
                self.proj(ps[1], 72, hT, wb, 512)
                S.op("act", lambda e: e.activation(out=ikf[:], in_=ps[1][:, 0:72], func=AF.Copy), reads=[ps[1].k], writes=[ikf.k])
                S.op("dve", lambda e: e.tensor_copy(out=ik2[:, 0:64], in_=ikf[:, 0:64]), reads=[ikf.k], writes=[ik2.k])
                self.rope(ikf[:, 0:64].rearrange("p (h d) -> p h d", h=1), ikf.k, ik2[:, 0:64].rearrange("p (h d) -> p h d", h=1), ik2.k, 1, 8, cI, sI, ra, rb)
                S.op("dve", lambda e: e.tensor_copy(out=ik2[:, 64:128], in_=ik2[:, 0:64]), reads=[ik2.k], writes=[ik2.k])
                S.op("pe", lambda e: e.transpose(out=ps[2][:, 0:128], in_=ik2[:], identity=self.identf[:]), reads=[ik2.k, self.identf.k], writes=[ps[2].k])
                S.op("act", lambda e: e.activation(out=ikT[:], in_=ps[2][:, 0:128], func=AF.Copy), reads=[ps[2].k], writes=[ikT.k])
                S.dma(lambda e, r0=r0: e.dma_start(out=self.s_ikT[:, r0:r0 + 128], in_=ikT[:]), reads=[ikT.k], writes=[self.k_ikT])
                if not own:
                    continue
                ko = self.k_own[j]
                S.op("dve", lambda e: e.tensor_scalar(out=iw[:], in0=ikf[:, 64:72], scalar1=float(512.0 ** -0.5), scalar2=None, op0=ALU.mult), reads=[ikf.k], writes=[iw.k])
                S.dma(lambda e, j=j: e.dma_start(out=self.s_iw[j], in_=iw[:]), reads=[iw.k], writes=[ko["iw"]])
                for half in range(2):
                    pq = ps[3 + half]
                    self.proj(pq, 512, hT, wb, 584 + half * 512)
                    self.norm_rope(4, pq[:, 0:512], pq.k, gq, float(128.0 ** -0.5), aqb, half * 512, cA, sA, tmpQ[half])

                def trq(e):
                    for h in range(8):
                        i = e.transpose(out=psb[:, h * 128:(h + 1) * 128], in_=aqb[:, h * 128:(h + 1) * 128], identity=self.identb[:])
                    return i
                S.op("pe", trq, reads=[aqb.k, self.identb.k], writes=[psb.k])
                S.op("act", lambda e: e.activation(out=aqT[:], in_=psb[:], func=AF.Copy), reads=[psb.k], writes=[aqT.k])
                S.dma(lambda e, j=j: e.dma_start(out=self.s_aqT[j], in_=aqT[:]), reads=[aqT.k], writes=[ko["aqT"]])
                self.proj(ps[5], 512, hT, wb, 1608)
                S.op("act", lambda e: e.activation(out=iqf[:], in_=ps[5][:], func=AF.Copy), reads=[ps[5].k], writes=[iqf.k])
                S.op("dve", lambda e: e.tensor_copy(out=iqo[:], in_=iqf[:]), reads=[iqf.k], writes=[iqo.k])
                self.rope(iqf[:].rearrange("p (h d) -> p h d", h=8), iqf.k, iqo[:].rearrange("p (h d) -> p h d", h=8), iqo.k, 8, 8, cI, sI, ra, rb)

                def tri(e):
                    for g in range(4):
                        i = e.transpose(out=ps[6][:, g * 128:(g + 1) * 128], in_=iqo[:, g * 128:(g + 1) * 128], identity=self.identf[:])
                    return i
                S.op("pe", tri, reads=[iqo.k, self.identf.k], writes=[ps[6].k])
                S.op("act", lambda e: e.activation(out=iqT[:], in_=ps[6][:], func=AF.Copy), reads=[ps[6].k], writes=[iqT.k])
                S.dma(lambda e, j=j: e.dma_start(out=self.s_iqT[j], in_=iqT[:]), reads=[iqT.k], writes=[ko["iqT"]])
                for half in range(2):
                    pq = ps[half]
                    self.proj(pq, 512, hT, wb, 2120 + half * 512)
                    S.op("act", lambda e, pq=pq, half=half: e.activation(out=sgb[:, half * 512:(half + 1) * 512], in_=pq[:], func=AF.Sigmoid), reads=[pq.k], writes=[sgb.k])
                S.dma(lambda e, j=j: e.dma_start(out=self.s_sgb[j], in_=sgb[:]), reads=[sgb.k], writes=[ko["sgb"]])
                for half in range(2):
                    pq = ps[2 + half]; pg = ps[4 + half]
                    self.proj(pq, 512, hT, wb, 3144 + half * 512)
                    self.proj(pg, 512, hT, wb, 4168 + half * 512)
                    S.op("act", lambda e, pq=pq, half=half: e.activation(out=zs[:, half * 512:(half + 1) * 512], in_=pq[:], func=AF.Silu), reads=[pq.k], writes=[zs.k])
                    S.op("act", lambda e, pg=pg, half=half: e.activation(out=sga[:, half * 512:(half + 1) * 512], in_=pg[:], func=AF.Sigmoid), reads=[pg.k], writes=[sga.k])
                S.op("pool", lambda e: e.tensor_tensor(out=zg[:], in0=zs[:], in1=sga[:], op=ALU.mult), reads=[zs.k, sga.k], writes=[zg.k])
                S.dma(lambda e, j=j: e.dma_start(out=self.s_zg[j], in_=zg[:]), reads=[zg.k], writes=[ko["zg"]])
            S.barrier()
            S.emit()

    def phase2(self, NBIS=18):
        S = self.S
        P, NO, S_ = self.P, self.NO, self.Sq
        ps, psb = self.ps, self.psb
        KSEL = float(min(256, S_ // 4)) - 0.5
        with ExitStack() as es:
            sb = lambda n, sh, dt: self.sb(es, n, sh, dt)
            akT = sb("akT", [128, 2, S_], BF16)
            vall = sb("vall", [128, P, 258], BF16)
            H2 = S_ // 2
            assert H2 % 512 == 0
            ikT = sb("ikT", [128, H2], IDT)
            scores = [sb("score%d" % i, [128, S_], F32) for i in range(2)]
            msk = sb("msk", [128, S_], BF16)
            jk = sb("jk", [128, S_], mybir.dt.uint8)
            gq = sb("gq2", [128, 128], F32); gk = sb("gk2", [128, 128], F32)
            negM = sb("negM", [128, 2], F32)
            causal = sb("causal", [128, 128], F32); blk0 = sb("blk0", [128, 128], F32)
            self.load(gq, gq[:], self.i_gq); self.load(gk, gk[:], self.i_gk)
            self.load(causal, causal[:], self.i_causal); self.load(blk0, blk0[:], self.i_blk0)
            S.dma(lambda e: e.dma_start(out=akT[:], in_=self.s_akT.rearrange("h (c s) -> h c s", c=2)), reads=[self.k_akT], writes=[akT.k])
            S.dma(lambda e: e.dma_start(out=vall[:], in_=self.s_v.rearrange("(p t) n -> t p n", t=128)), reads=[self.k_v], writes=[vall.k])
            stg_ik = scores[1]
            S.dma(lambda e: e.dma_start(out=stg_ik[0:64, 0:H2], in_=self.s_ikT[0:64, 0:H2]), reads=[self.k_ikT], writes=[stg_ik.k])
            S.dma(lambda e: e.dma_start(out=stg_ik[64:128, 0:H2], in_=self.s_ikT[64:128, H2:S_]), reads=[self.k_ikT], writes=[stg_ik.k])
            for c0 in range(0, H2, 512):
                S.op("pool" if (c0 // 512) % 2 else "act", (lambda e, c0=c0: e.tensor_copy(out=ikT[:, c0:c0 + 512], in_=stg_ik[:, c0:c0 + 512])) if (c0 // 512) % 2 else (lambda e, c0=c0: e.activation(out=ikT[:, c0:c0 + 512], in_=stg_ik[:, c0:c0 + 512], func=AF.Copy)), reads=[stg_ik.k], writes=[ikT.k])
            S.op("dve", lambda e: e.tensor_reduce(out=negM[:, 0:1], in_=gq[:], axis=AX.X, op=ALU.max, apply_absolute_value=True), reads=[gq.k], writes=[negM.k])
            S.op("dve", lambda e: e.tensor_reduce(out=negM[:, 1:2], in_=gk[:], axis=AX.X, op=ALU.max, apply_absolute_value=True), reads=[gk.k], writes=[negM.k])
            S.op("dve", lambda e: e.scalar_tensor_tensor(out=negM[:, 0:1], in0=negM[:, 0:1], scalar=float(-(128.0 ** 0.5)), in1=negM[:, 1:2], op0=ALU.mult, op1=ALU.mult), reads=[negM.k], writes=[negM.k])
            aqT = sb("aqT", [128, 1024], BF16); iqT = sb("iqT", [128, 512], IDT); iw = sb("iw", [128, 8], F32)
            iqT2 = sb("iqT2", [128, 512], IDT)
            sgb = sb("sgb", [128, 1024], BF16); mg = sb("mg", [128, 1024], BF16)
            rl = [sb("rl%d" % i, [128, 512], F32) for i in range(2)]
            pT = [sb("pT%d" % i, [128, 512], BF16) for i in range(2)]
            mT = [sb("mT%d" % i, [128, 1024], BF16) for i in range(2)]
            sts = [sb("bst%d" % i, [128, 8], F32) for i in range(3)]
            rec = sb("rec", [128, 8], F32)
            oatt = self.junk
            mrg = sb("mrg", [128, 1024], BF16)
            accb = [ps[4], ps[5], ps[6]]
            hb = [(0, 0), (0, 1), (0, 2), (1, 0), (1, 1), (1, 2), (2, 0), (2, 1)]

            sctok = [[Tok("sc%d_%d" % (i, g)) for g in range(S_ // 512)] for i in range(2)]
            IDXC = lambda ap: ap

            def gI(j):
                p = 2 * j + 1
                nkeys = (p + 1) * 128
                ko = self.k_own[j]
                score = scores[j % 2]; st = sts[j % 3]
                jq = self.junk
                S.dma(lambda e: e.dma_start(out=jq[:, 0:512], in_=self.s_iqT[j]), reads=[ko["iqT"]], writes=[jq.k])
                S.dma(lambda e: e.dma_start(out=jq[0:64, 512:1024], in_=self.s_iqT[j][64:128, :]), reads=[ko["iqT"]], writes=[jq.k])
                S.dma(lambda e: e.dma_start(out=jq[64:128, 512:1024], in_=self.s_iqT[j][0:64, :]), reads=[ko["iqT"]], writes=[jq.k])
                S.op("pool", lambda e: e.tensor_copy(out=iqT[:], in_=jq[:, 0:512]), reads=[jq.k], writes=[iqT.k])
                S.op("pool", lambda e: e.tensor_copy(out=iqT2[:], in_=jq[:, 512:1024]), reads=[jq.k], writes=[iqT2.k])
                S.dma(lambda e: e.dma_start(out=iw[:], in_=self.s_iw[j]), reads=[ko["iw"]], writes=[iw.k])
                it = 0
                nkg = (nkeys + 511) // 512
                for h in range(8):
                    for kg in range(nkg):
                        k0 = kg * 512
                        w = min(512, nkeys - k0)
                        sk = score.k
                        pb = ps[it % 2]; r = rl[it % 2]; it += 1
                        b0 = 64 if k0 >= H2 else 0
                        kk0 = k0 - (H2 if k0 >= H2 else 0)
                        qsrc = iqT if (h % 2) * 64 == b0 else iqT2
                        S.op("pe", lambda e, pb=pb, h=h, b0=b0, kk0=kk0, w=w, qsrc=qsrc: e.matmul(pb[:, 0:w], lhsT=IDXC(qsrc[b0:b0 + 64, (h // 2) * 128:(h // 2 + 1) * 128]), rhs=IDXC(ikT[b0:b0 + 64, kk0:kk0 + w]), start=True, stop=True), reads=[iqT.k, iqT2.k, ikT.k], writes=[pb.k])
                        S.op("act", lambda e, pb=pb, r=r, w=w: e.activation(out=r[:, 0:w], in_=pb[:, 0:w], func=AF.Relu), reads=[pb.k], writes=[r.k])
                        if h == 0:
                            S.op("dve", lambda e, r=r, k0=k0, w=w: e.tensor_scalar(out=score[:, k0:k0 + w], in0=r[:, 0:w], scalar1=iw[:, 0:1], scalar2=None, op0=ALU.mult), reads=[r.k, iw.k], writes=[sk])
                        else:
                            S.op("dve", lambda e, r=r, k0=k0, w=w, h=h: e.scalar_tensor_tensor(out=score[:, k0:k0 + w], in0=r[:, 0:w], scalar=iw[:, h:h + 1], in1=score[:, k0:k0 + w], op0=ALU.mult, op1=ALU.add), reads=[r.k, iw.k, sk], writes=[sk])
                        yield
                allk = sctok[j % 2][0:nkg]
                S.op("dve", lambda e: e.tensor_copy(out=st[:, 7:8], in_=st[:, 7:8]), reads=allk + [st.k], writes=[score.k, st.k])
                S.op("dve", lambda e: e.tensor_reduce(out=st[:, 5:6], in_=score[:, 0:nkeys], axis=AX.X, op=ALU.max), reads=[score.k], writes=[st.k])
                S.op("dve", lambda e: e.tensor_reduce(out=st[:, 6:7], in_=score[:, 0:nkeys], axis=AX.X, op=ALU.min), reads=[score.k], writes=[st.k])
                S.op("dve", lambda e: e.tensor_scalar(out=st[:, 0:1], in0=st[:, 6:7], scalar1=-1.0, scalar2=None, op0=ALU.add), reads=[st.k], writes=[st.k])
                S.op("dve", lambda e: e.tensor_tensor(out=st[:, 1:2], in0=st[:, 5:6], in1=st[:, 0:1], op=ALU.subtract), reads=[st.k], writes=[st.k])
                S.op("dve", lambda e: e.tensor_tensor(out=score[:, nkeys - 128:nkeys], in0=score[:, nkeys - 128:nkeys], in1=causal[:], op=ALU.add), reads=[score.k, causal.k], writes=[score.k])
                S.op("dve", lambda e: e.tensor_tensor(out=score[:, 0:128], in0=score[:, 0:128], in1=blk0[:], op=ALU.add), reads=[score.k, blk0.k], writes=[score.k])
                yield

            def gB(j):
                nkeys = (2 * j + 2) * 128
                score = scores[j % 2]; st = sts[j % 3]
                for it_ in range(1, NBIS + 1):
                    f = float(2.0 ** -it_)
                    S.op("dve", lambda e, f=f: e.scalar_tensor_tensor(out=st[:, 2:3], in0=st[:, 1:2], scalar=f, in1=st[:, 0:1], op0=ALU.mult, op1=ALU.add), reads=[st.k], writes=[st.k])
                    S.op("dve", lambda e: e.tensor_scalar(out=jk[:, 0:nkeys], in0=score[:, 0:nkeys], scalar1=st[:, 2:3], scalar2=0.0, op0=ALU.is_gt, op1=ALU.add, accum_out=st[:, 3:4]), reads=[score.k, st.k], writes=[jk.k, st.k])
                    S.op("dve", lambda e, f=f: e.tensor_scalar(out=st[:, 4:5], in0=st[:, 3:4], scalar1=KSEL, scalar2=f, op0=ALU.is_gt, op1=ALU.mult), reads=[st.k], writes=[st.k])
                    S.op("dve", lambda e: e.scalar_tensor_tensor(out=st[:, 0:1], in0=st[:, 4:5], scalar=st[:, 1:2], in1=st[:, 0:1], op0=ALU.mult, op1=ALU.add), reads=[st.k], writes=[st.k])
                    yield

            def gA(j):
                p = 2 * j + 1
                nk = p + 1
                nkeys = nk * 128
                ko = self.k_own[j]
                score = scores[j % 2]; st = sts[j % 3]
                S.op("dve", lambda e: e.tensor_scalar(out=msk[:, 0:nkeys], in0=score[:, 0:nkeys], scalar1=st[:, 0:1], scalar2=None, op0=ALU.is_gt), reads=[score.k, st.k], writes=[msk.k])
                S.dma(lambda e: e.dma_start(out=aqT[:], in_=self.s_aqT[j]), reads=[ko["aqT"]], writes=[aqT.k])
                S.dma(lambda e: e.dma_start(out=sgb[:], in_=self.s_sgb[j]), reads=[ko["sgb"]], writes=[sgb.k])
                S.dma(lambda e: e.dma_start(out=mg[:], in_=self.s_mg[j]), reads=[ko["mg"]], writes=[mg.k])
                yield
                units = [(kb, c) for kb in range(nk) for c in range(2)]

                def emit_st(u):
                    kb, c = units[u]
                    pss = ps[2 + u % 2]
                    S.op("pe", lambda e: e.matmul(pss[:], lhsT=akT[:, c, kb * 128:(kb + 1) * 128], rhs=aqT[:, c * 512:(c + 1) * 512], start=True, stop=True), reads=[akT.k, aqT.k], writes=[pss.k])

                def emit_mask(kb):
                    m = mT[(kb // 8) % 2]
                    nb = min(8, nk - kb)

                    def trm(e):
                        for i_ in range(nb):
                            ins = e.transpose(out=psb[:, i_ * 128:(i_ + 1) * 128], in_=msk[:, (kb + i_) * 128:(kb + i_ + 1) * 128], identity=self.identb[:])
                        return ins
                    S.op("pe", trm, reads=[msk.k, self.identb.k], writes=[psb.k])
                    S.op("act", lambda e: e.activation(out=m[:, 0:nb * 128], in_=psb[:, 0:nb * 128], func=AF.Copy), reads=[psb.k], writes=[m.k])

                emit_mask(0)
                emit_st(0)
                yield
                for u, (kb, c) in enumerate(units):
                    pss = ps[2 + u % 2]; pt = pT[u % 2]
                    m = mT[(kb // 8) % 2]
                    if u + 1 < len(units):
                        if units[u + 1][1] == 0 and units[u + 1][0] % 8 == 0:
                            emit_mask(units[u + 1][0])
                        emit_st(u + 1)
                    S.op("act", lambda e, pss=pss, pt=pt: e.activation(out=pt[:], in_=pss[:], func=AF.Exp, bias=negM[:, 0:1], scale=1.0), reads=[pss.k, negM.k], writes=[pt.k])
                    mi = (kb % 8) * 128
                    S.op("pool", lambda e, pt=pt, m=m, mi=mi: e.tensor_tensor(out=pt[:].rearrange("p (h q) -> p h q", h=4), in0=pt[:].rearrange("p (h q) -> p h q", h=4), in1=bc(m[:, mi:mi + 128].unsqueeze(1), [128, 4, 128]), op=ALU.mult), reads=[pt.k, m.k], writes=[pt.k])
                    yield

                    def pv(e, pt=pt, c=c, kb=kb):
                        for hh in range(4):
                            h = c * 4 + hh
                            bk, sl = hb[h]
                            ins = e.matmul(accb[bk][:, sl * 129:(sl + 1) * 129], lhsT=pt[:, hh * 128:(hh + 1) * 128], rhs=vall[:, kb, c * 129:(c + 1) * 129], start=(kb == 0 and sl == 0), stop=(kb == nk - 1), skip_group_check=True)
                        return ins
                    S.op("pe", pv, reads=[pt.k, vall.k], writes=[accb[hb[c * 4][0]].k, accb[hb[c * 4 + 3][0]].k])
                for bk, (h0, n) in enumerate(((0, 3), (3, 3), (6, 2))):
                    a3 = accb[bk][:, 0:n * 129].rearrange("p (h d) -> p h d", h=n)
                    S.op("dve", lambda e, a3=a3, h0=h0, n=n: e.reciprocal(out=rec[:, h0:h0 + n], in_=a3[:, :, 128]), reads=[accb[bk].k], writes=[rec.k])
                    S.op("dve", lambda e, a3=a3, h0=h0, n=n: e.tensor_tensor(out=oatt[:, h0 * 128:(h0 + n) * 128].rearrange("p (h d) -> p h d", h=n), in0=a3[:, :, 0:128], in1=bc(rec[:, h0:h0 + n].unsqueeze(2), [128, n, 128]), op=ALU.mult), reads=[accb[bk].k, rec.k], writes=[oatt.k])
                S.op("pool", lambda e: e.tensor_tensor(out=oatt[:], in0=oatt[:], in1=sgb[:], op=ALU.mult), reads=[oatt.k, sgb.k], writes=[oatt.k])
                S.op("pool", lambda e: e.tensor_tensor(out=mrg[:], in0=oatt[:], in1=mg[:], op=ALU.add), reads=[oatt.k, mg.k], writes=[mrg.k])
                S.dma(lambda e: e.dma_start(out=self.s_mrg[j], in_=mrg[:]), reads=[mrg.k], writes=[ko["x1"]])
                yield

            def count(g):
                n = 0
                for _ in g:
                    n += 1
                return n

            for t in range(NO + 2):
                gens = []
                for mk, jj in ((gA, t - 2), (gB, t - 1), (gI, t)):
                    if 0 <= jj < NO:
                        gens.append(mk(jj))
                sizes = []
                for mk, jj in ((gA, t - 2), (gB, t - 1), (gI, t)):
                    if 0 <= jj < NO:
                        nk = 2 * jj + 2
                        if mk is gA:
                            sizes.append(2 + 2 * nk)
                        elif mk is gB:
                            sizes.append(NBIS)
                        else:
                            sizes.append(((nk * 128 + 511) // 512) * 8 + 1)
                nmax = max(sizes)
                done = [0] * len(gens)
                if 0 <= t - 2 < NO:
                    next(gens[0], None)
                    done[0] = 1
                for k in range(nmax):
                    for gi, g in enumerate(gens):
                        tgt = ((k + 1) * sizes[gi]) // nmax
                        while done[gi] < tgt:
                            next(g, None)
                            done[gi] += 1
                for g in gens:
                    for _ in g:
                        pass
            S.barrier()
            S.emit()

    def phase3(self, G=2):
        S = self.S
        NO = self.NO
        ps, psb = self.ps, self.psb
        G = min(G, NO)
        with ExitStack() as es:
            sb = lambda n, sh, dt: self.sb(es, n, sh, dt)
            wup = sb("wupb", [128, 8, 4096], BF16); wdn = sb("wdnb", [128, 32, 1024], BF16)
            woutb = sb("woutb", [128, 8, 1024], BF16)
            stg = [sb("stg3_%d" % i, [128, 512], F32) for i in range(2)]
            gmlp = sb("gmlp", [128, 8], F32)
            self.load(gmlp, gmlp[:], self.i_gmlp)
            self.load_weight_bf16(es, woutb, self.i_wout, 1024, stg)
            self.load_weight_bf16(es, wup, self.i_wup, 4096, stg)
            self.load_weight_bf16(es, wdn, self.i_wdn, 1024, stg)
            x1 = sb("x1g", [128, G, 1024], F32)
            xb = sb("xb3", [128, 1024], F32); mrg = sb("mrg3", [128, 1024], BF16); mT2 = sb("mT2", [128, 8, 128], BF16)
            ss = sb("ss3", [128, 1], F32); xnb = sb("xnb3", [128, 1024], BF16)
            h2T = sb("h2T", [128, 8, G * 128], BF16)
            hTb = sb("hTb3", [128, 8, 128], BF16)
            hid = sb("hidT", [128, 32, G * 128], BF16)
            rl = [sb("rl3_%d" % i, [128, G * 128], F32) for i in range(2)]
            ot = [sb("ot%d" % i, [128, 1024], F32) for i in range(2)]
            for g0 in range(0, NO, G):
                for b in range(G):
                    j = g0 + b
                    p = 2 * j + 1
                    S.dma(lambda e, j=j: e.dma_start(out=mrg[:], in_=self.s_mrg[j]), reads=[self.k_own[j]["x1"]], writes=[mrg.k])
                    S.dma(lambda e, p=p: e.dma_start(out=xb[:], in_=self.i_x[p * 128:(p + 1) * 128, :]), writes=[xb.k])

                    def trg(e):
                        for kc in range(8):
                            ins = e.transpose(out=psb[:, kc * 128:(kc + 1) * 128], in_=mrg[:, kc * 128:(kc + 1) * 128], identity=self.identb[:])
                        return ins
                    S.op("pe", trg, reads=[mrg.k, self.identb.k], writes=[psb.k])
                    S.op("act", lambda e: e.activation(out=mT2[:].rearrange("p k t -> p (k t)"), in_=psb[:], func=AF.Copy), reads=[psb.k], writes=[mT2.k])
                    for half in range(2):
                        pq = ps[4 + half]
                        self.proj(pq, 512, mT2, woutb, half * 512)
                        S.op("dve", lambda e, pq=pq, half=half, b=b: e.tensor_tensor(out=x1[:, b, half * 512:(half + 1) * 512], in0=xb[:, half * 512:(half + 1) * 512], in1=pq[:], op=ALU.add), reads=[xb.k, pq.k], writes=[x1.k])
                    xbv = TL(x1.t[:, b, :], "x")
                    xbv.k = x1.k
                    self.rms_hT(xbv, ss, xnb, hTb, gmlp, psb)
                    S.op("pool", lambda e, b=b: e.tensor_copy(out=h2T[:, :, b * 128:(b + 1) * 128], in_=hTb[:]), reads=[hTb.k], writes=[h2T.k])
                for f in range(32):
                    pq = ps[f % 2]; r = rl[f % 2]

                    def up(e, pq=pq, f=f):
                        for kc in range(8):
                            ins = e.matmul(pq[:, 0:G * 128], lhsT=wup[:, kc, f * 128:(f + 1) * 128], rhs=h2T[:, kc, :], start=(kc == 0), stop=(kc == 7))
                        return ins
                    S.op("pe", up, reads=[wup.k, h2T.k], writes=[pq.k])
                    S.op("act", lambda e, pq=pq, r=r: e.activation(out=r[:], in_=pq[:, 0:G * 128], func=AF.Relu), reads=[pq.k], writes=[r.k])
                    S.op("dve" if f % 2 else "pool", lambda e, r=r, f=f: e.tensor_tensor(out=hid[:, f, :], in0=r[:], in1=r[:], op=ALU.mult), reads=[r.k], writes=[hid.k])
                for b in range(G):
                    j = g0 + b
                    o = ot[b % 2]
                    for half in range(2):
                        pq = ps[2 + half]

                        def dn(e, pq=pq, b=b, half=half):
                            for f in range(32):
                                ins = e.matmul(pq[:], lhsT=hid[:, f, b * 128:(b + 1) * 128], rhs=wdn[:, f, half * 512:(half + 1) * 512], start=(f == 0), stop=(f == 31))
                            return ins
                        S.op("pe", dn, reads=[hid.k, wdn.k], writes=[pq.k])
                        S.op("dve", lambda e, pq=pq, o=o, b=b, half=half: e.tensor_tensor(out=o[:, half * 512:(half + 1) * 512], in0=x1[:, b, half * 512:(half + 1) * 512], in1=pq[:], op=ALU.add), reads=[x1.k, pq.k], writes=[o.k])
                    S.dma(lambda e, j=j, o=o: e.dma_start(out=self.o_out[j * 128:(j + 1) * 128, :], in_=o[:]), reads=[o.k], writes=[])
            S.barrier()
            S.emit()

    def phase1a(self):
        S = self.S
        P = self.P
        ps, psb = self.ps, self.psb
        with ExitStack() as es:
            sb = lambda n, sh, dt: self.sb(es, n, sh, dt)
            wb = sb("w1a", [128, 8, C1A], BF16)
            stg = [sb("stga%d" % i, [128, 512], F32) for i in range(2)]
            gmix = sb("gmixa", [128, 8], F32); ggdn = sb("ggdn", [128, 128], F32)
            aexp = sb("aexp", [128, 8], F32); dtb = sb("dtb", [128, 8], F32)
            convw = sb("convw", [128, 24, 4], F32)
            uinc = sb("uinc", [128, 128], F32); onesf = sb("onesf", [128, 128], F32); onesb = sb("onesb", [128, 128], BF16)
            mskL4 = sb("mskL4", [128, 4, 128], BF16); mskU4 = sb("mskU4", [128, 4, 128], BF16)
            self.load(gmix, gmix[:], self.i_gmix); self.load(ggdn, ggdn[:], self.i_ggdn)
            self.load(aexp, aexp[:], self.i_alog); self.load(dtb, dtb[:], self.i_dtb)
            self.load(convw, convw[:].rearrange("p t j -> p (t j)"), self.i_convw)
            self.load(uinc, uinc[:], self.i_uinc)
            for q4 in range(4):
                self.load(mskL4, mskL4[:, q4, :], self.i_mskL); self.load(mskU4, mskU4[:, q4, :], self.i_mskU)
            S.op("pool", lambda e: e.memset(onesf[:], 1.0), writes=[onesf.k])
            S.op("pool", lambda e: e.memset(onesb[:], 1.0), writes=[onesb.k])
            S.op("act", lambda e: e.activation(out=aexp[:], in_=aexp[:], func=AF.Exp), reads=[aexp.k], writes=[aexp.k])
            self.load_weight_bf16(es, wb, self.i_w1a, C1A, stg)
            xb = sb("xba", [128, 1024], F32); ss = sb("ssa", [128, 1], F32); xnb = sb("xnba", [128, 1024], BF16); hT = sb("hTa", [128, 8, 128], BF16)
            U = sb("U", [128, 24, 131], F32)
            YgD = sb("YgD", [128, 8, 128], F32); CtD = sb("CtD", [128, 8, 128], F32); YgP = sb("YgP", [128, 8, 128], F32); CtP = sb("CtP", [128, 8, 128], F32)
            sq = sb("sqa", [128, 1024], BF16); rn = sb("rn", [128, 1024], F32)
            qT = sb("qT", [128, 8, 128], BF16); kT2 = [sb("kT%d" % i, [128, 8, 128], BF16) for i in range(2)]; vTb = sb("vTb", [128, 8, 128], BF16)
            ktok = sb("ktok", [128, 8, 128], F32)
            kbg2 = [sb("kbg%d" % i, [128, 8, 128], BF16) for i in range(2)]; kdec2 = [sb("kdec%d" % i, [128, 8, 128], BF16) for i in range(2)]; vb2 = [sb("vb%d" % i, [128, 8, 128], BF16) for i in range(2)]
            smm = [sb("sm_%d" % i, [128, 96], F32) for i in range(2)]
            sm2 = sb("sm2", [128, 16], F32); sm3 = sb("sm3", [128, 8], F32)
            Rall = sb("Rall", [128, 8, 128], F32)
            Dls2 = [sb("Dls%d" % i, [128, 8, 128], F32) for i in range(2)]; DT = sb("DTm", [128, 8, 128], F32); egr = sb("egr", [128, 8, 128], F32)
            qdT2 = [sb("qdT%d" % i, [128, 8, 128], BF16) for i in range(2)]; QKT2 = [sb("QKT%d" % i, [128, 8, 128], BF16) for i in range(2)]
            Am = [sb("Am%d" % i, [128, 8, 128], NDT) for i in range(2)]
            Bm = [sb("Bm%d" % i, [128, 8, 128], NDT) for i in range(2)]
            Pm = sb("Pm", [128, 8, 128], NDT); TTb = sb("TTb", [128, 8, 128], BF16)
            u = sb("u", [128, 8, 128], F32); wT = sb("wT", [128, 8, 128], BF16); vn = sb("vn", [128, 8, 128], BF16)
            S32 = sb("S32", [128, 8, 128], F32); Stmp = sb("Stmp", [128, 8, 128], F32); Sbf = sb("Sbf", [128, 8, 128], BF16)
            o = sb("o", [128, 8, 128], F32); osq = Stmp
            zg2 = [sb("zga%d" % i, [128, 1024], BF16) for i in range(2)]; mgt = sb("mgt", [128, 1024], BF16)
            S.op("pool", lambda e: e.memset(U[:], 0.0), writes=[U.k])
            S.op("pool", lambda e: e.memset(S32[:], 0.0), writes=[S32.k])
            S.op("pool", lambda e: e.memset(Sbf[:], 0.0), writes=[Sbf.k])
            f4 = lambda t, g: t[:, g * 4:(g + 1) * 4, :].rearrange("p h d -> p (h d)")

            def batch_mm(banks, fn_h):
                for g in range(2):
                    def f(e, g=g):
                        for hh in range(4):
                            h = g * 4 + hh
                            ins = fn_h(e, banks[g][:, hh * 128:(hh + 1) * 128], h)
                        return ins
                    yield g, f

            def gF(p):
                own = (p % 2 == 1)
                j = p // 2
                r0 = p * 128
                DB = p % 2
                kT = kT2[DB]; kbg = kbg2[DB]; kdec = kdec2[DB]; vb = vb2[DB]; sm = smm[DB]; Dls = Dls2[DB]
                qdT = qdT2[DB]; QKT = QKT2[DB]; zg = zg2[DB]
                self.load(xb, xb[:], self.i_x[r0:r0 + 128, :])
                if own:
                    S.dma(lambda e, j=j: e.dma_start(out=zg[:], in_=self.s_zg[j]), reads=[self.k_own[j]["zg"]], writes=[zg.k])
                self.rms_hT(xb, ss, xnb, hT, gmix, psb)
                yield
                for grp in range(6):
                    pq = ps[4 + grp % 2]

                    def fm(e, pq=pq, grp=grp):
                        for t4 in range(4):
                            ct = grp * 4 + t4
                            for kc in range(8):
                                ins = e.matmul(pq[:, t4 * 128:(t4 + 1) * 128], lhsT=wb[:, kc, ct * 128:(ct + 1) * 128], rhs=hT[:, kc, :], start=(kc == 0), stop=(kc == 7), skip_group_check=True)
                        return ins
                    S.op("pe", fm, reads=[wb.k, hT.k], writes=[pq.k])
                    yield
                    S.op("act", lambda e, pq=pq, grp=grp: e.activation(out=U[:, grp * 4:(grp + 1) * 4, 3:131], in_=pq[:].rearrange("p (t n) -> p t n", t=4), func=AF.Copy), reads=[pq.k], writes=[U.k])
                    yield
                self.proj(ps[6], 16, hT, wb, 3072)
                yield
                S.op("act", lambda e: e.activation(out=sm[:, 0:16], in_=ps[6][:, 0:16], func=AF.Copy), reads=[ps[6].k], writes=[sm.k])
                yield
                S.op("act", lambda e: e.activation(out=sm[:, 16:24], in_=sm[:, 8:16], func=AF.Sigmoid), reads=[sm.k], writes=[sm.k])
                yield
                S.op("dve", lambda e: e.tensor_tensor(out=sm[:, 24:32], in0=sm[:, 0:8], in1=dtb[:], op=ALU.add), reads=[sm.k, dtb.k], writes=[sm.k])
                yield
                S.op("act", lambda e: e.activation(out=sm[:, 32:40], in_=sm[:, 24:32], func=AF.Abs), reads=[sm.k], writes=[sm.k])
                yield
                S.op("act", lambda e: e.activation(out=sm[:, 32:40], in_=sm[:, 32:40], func=AF.Exp, scale=-1.0), reads=[sm.k], writes=[sm.k])
                yield
                S.op("dve", lambda e: e.tensor_scalar(out=sm[:, 32:40], in0=sm[:, 32:40], scalar1=1.0, scalar2=None, op0=ALU.add), reads=[sm.k], writes=[sm.k])
                yield
                S.op("act", lambda e: e.activation(out=sm[:, 32:40], in_=sm[:, 32:40], func=AF.Ln), reads=[sm.k], writes=[sm.k])
                yield
                S.op("dve", lambda e: e.scalar_tensor_tensor(out=sm[:, 40:48], in0=sm[:, 24:32], scalar=0.0, in1=sm[:, 32:40], op0=ALU.max, op1=ALU.add), reads=[sm.k], writes=[sm.k])
                yield
                S.op("dve", lambda e: e.scalar_tensor_tensor(out=sm[:, 48:56], in0=sm[:, 40:48], scalar=-1.0, in1=aexp[:], op0=ALU.mult, op1=ALU.mult), reads=[sm.k, aexp.k], writes=[sm.k])
                yield

                def cs(e):
                    e.matmul(ps[6][:, 16:24], lhsT=uinc[:], rhs=sm[:, 48:56], start=True, stop=True, skip_group_check=True)
                    return e.matmul(ps[6][:, 24:32], lhsT=onesf[:], rhs=sm[:, 48:56], start=True, stop=True, skip_group_check=True)
                S.op("pe", cs, reads=[uinc.k, onesf.k, sm.k], writes=[ps[6].k])
                yield
                S.op("act", lambda e: e.activation(out=sm[:, 56:72], in_=ps[6][:, 16:32], func=AF.Copy), reads=[ps[6].k], writes=[sm.k])
                yield
                S.op("act", lambda e: e.activation(out=sm[:, 72:80], in_=sm[:, 56:64], func=AF.Exp), reads=[sm.k], writes=[sm.k])
                yield
                S.op("dve", lambda e: e.tensor_tensor(out=sm[:, 80:88], in0=sm[:, 64:72], in1=sm[:, 56:64], op=ALU.subtract), reads=[sm.k], writes=[sm.k])
                yield
                S.op("act", lambda e: e.activation(out=sm[:, 80:96], in_=sm[:, 80:96] if False else sm[:, 80:88], func=AF.Exp) if False else e.activation(out=sm[:, 80:88], in_=sm[:, 80:88], func=AF.Exp), reads=[sm.k], writes=[sm.k])
                yield
                S.op("act", lambda e: e.activation(out=sm[:, 88:96], in_=sm[:, 64:72], func=AF.Exp), reads=[sm.k], writes=[sm.k])
                yield
                S.op("dve", lambda e: e.tensor_tensor(out=sm2[:, 0:8], in0=sm[:, 16:24], in1=sm[:, 72:80], op=ALU.mult), reads=[sm.k], writes=[sm2.k])
                yield
                for grp, dst in ((2, vTb), (0, qT), (1, kT)):
                    if grp == 0 and not own:
                        continue
                    Ug = lambda jj, grp=grp: U[:, grp * 8:(grp + 1) * 8, jj:jj + 128]
                    cw = lambda jj, grp=grp: bc(convw[:, grp * 8:(grp + 1) * 8, jj:jj + 1], [128, 8, 128])
                    ceng = "pool" if grp == 2 else "dve"
                    Yg = YgP if grp == 2 else YgD
                    Ct = CtP if grp == 2 else CtD
                    S.op(ceng, lambda e, Ug=Ug, cw=cw, Yg=Yg: e.tensor_tensor(out=Yg[:], in0=Ug(0), in1=cw(0), op=ALU.mult), reads=[U.k, convw.k], writes=[Yg.k])
                    yield
                    for jj in range(1, 4):
                        S.op(ceng, lambda e, Ug=Ug, cw=cw, jj=jj, Ct=Ct: e.tensor_tensor(out=Ct[:], in0=Ug(jj), in1=cw(jj), op=ALU.mult), reads=[U.k, convw.k], writes=[Ct.k])
                        yield
                        S.op(ceng, lambda e, Yg=Yg, Ct=Ct: e.tensor_tensor(out=Yg[:], in0=Yg[:], in1=Ct[:], op=ALU.add), reads=[Yg.k, Ct.k], writes=[Yg.k])
                        yield
                    Yf = Yg[:].rearrange("p h d -> p (h d)")
                    if grp == 2:
                        S.op("act", lambda e, Yf=Yf: e.activation(out=vTb[:].rearrange("p h d -> p (h d)"), in_=Yf, func=AF.Silu), reads=[Yg.k], writes=[vTb.k])
                        continue
                    S.op("act", lambda e, Yf=Yf: e.activation(out=Yf, in_=Yf, func=AF.Silu), reads=[Yg.k], writes=[Yg.k])
                    yield
                    S.op("dve", lambda e, Yf=Yf: e.tensor_tensor(out=sq[:], in0=Yf, in1=Yf, op=ALU.mult), reads=[Yg.k], writes=[sq.k])
                    yield
                    for g in range(2):
                        pq = ps[4 + g]
                        S.op("pe", lambda e, pq=pq, g=g: e.matmul(pq[:], lhsT=onesb[:], rhs=sq[:, g * 512:(g + 1) * 512], start=True, stop=True), reads=[onesb.k, sq.k], writes=[pq.k])
                        S.op("dve", lambda e, pq=pq, g=g: e.tensor_scalar(out=rn[:, g * 512:(g + 1) * 512], in0=pq[:], scalar1=EPS, scalar2=None, op0=ALU.add), reads=[pq.k], writes=[rn.k])
                    S.op("act", lambda e: e.activation(out=rn[:], in_=rn[:], func=AF.Ln), reads=[rn.k], writes=[rn.k])
                    yield
                    S.op("act", lambda e: e.activation(out=rn[:], in_=rn[:], func=AF.Exp, scale=-0.5), reads=[rn.k], writes=[rn.k])
                    yield
                    sc = float(128.0 ** -0.5) if grp == 0 else 1.0
                    S.op("dve", lambda e, Yf=Yf, dst=dst, sc=sc: e.scalar_tensor_tensor(out=dst[:].rearrange("p h d -> p (h d)"), in0=Yf, scalar=sc, in1=rn[:], op0=ALU.mult, op1=ALU.mult), reads=[Yg.k, rn.k], writes=[dst.k])
                    yield
                S.op("pool", lambda e: e.tensor_copy(out=U[:, :, 0:3], in_=U[:, :, 128:131]), reads=[U.k], writes=[U.k])
                yield
                def trk(e):
                    for h in range(8):
                        ins = e.transpose(out=psb[:, h * 128:(h + 1) * 128], in_=kT[:, h, :], identity=self.identb[:])
                    return ins
                S.op("pe", trk, reads=[kT.k, self.identb.k], writes=[psb.k])
                yield
                S.op("act", lambda e: e.activation(out=ktok[:].rearrange("p h d -> p (h d)"), in_=psb[:], func=AF.Copy), reads=[psb.k], writes=[ktok.k])
                yield
                S.op("dve", lambda e: e.tensor_tensor(out=kbg[:], in0=ktok[:], in1=bc(sm2[:, 0:8].unsqueeze(2), [128, 8, 128]), op=ALU.mult), reads=[ktok.k, sm2.k], writes=[kbg.k])
                yield
                S.op("pool", lambda e: e.tensor_tensor(out=kdec[:], in0=ktok[:], in1=bc(sm[:, 80:88].unsqueeze(2), [128, 8, 128]), op=ALU.mult), reads=[ktok.k, sm.k], writes=[kdec.k])
                yield

                def trv(e):
                    for h in range(8):
                        ins = e.transpose(out=psb[:, h * 128:(h + 1) * 128], in_=vTb[:, h, :], identity=self.identb[:])
                    return ins
                S.op("pe", trv, reads=[vTb.k, self.identb.k], writes=[psb.k])
                yield
                S.op("dve", lambda e: e.tensor_tensor(out=vb[:], in0=psb[:].rearrange("p (h d) -> p h d", h=8), in1=bc(sm[:, 16:24].unsqueeze(2), [128, 8, 128]), op=ALU.mult), reads=[psb.k, sm.k], writes=[vb.k])
                yield
                S.op("dve", lambda e: e.tensor_tensor(out=Rall[:], in0=bc(uinc[:].unsqueeze(1), [128, 8, 128]), in1=bc(sm[:, 48:56].unsqueeze(2), [128, 8, 128]), op=ALU.mult), reads=[uinc.k, sm.k], writes=[Rall.k])
                yield
                for g in range(2):
                    pq = ps[4 + g]

                    def grl(e, pq=pq, g=g):
                        e.matmul(pq[:], lhsT=onesf[:], rhs=f4(Rall, g), start=True, stop=False)
                        return e.matmul(pq[:], lhsT=self.identb[:], rhs=mskL4[:].rearrange("p h d -> p (h d)"), start=False, stop=True)
                    S.op("pe", grl, reads=[onesf.k, Rall.k, self.identb.k, mskL4.k], writes=[pq.k])
                    yield
                    for hh in range(4):
                        h = g * 4 + hh
                        S.op("act", lambda e, pq=pq, h=h, hh=hh: e.activation(out=Dls[:, h, :], in_=pq[:, hh * 128:(hh + 1) * 128], func=AF.Exp, scale=-1.0, bias=sm[:, 56 + h:57 + h]), reads=[pq.k, sm.k], writes=[Dls.k])
                if own:
                    for g in range(2):
                        pq = ps[4 + g]
                        S.op("pe", lambda e, pq=pq, g=g: e.matmul(pq[:], lhsT=onesf[:], rhs=f4(Rall, g), start=True, stop=True), reads=[onesf.k, Rall.k], writes=[pq.k])
                        S.op("act", lambda e, pq=pq, g=g: e.activation(out=f4(egr, g), in_=pq[:], func=AF.Exp), reads=[pq.k], writes=[egr.k])
                        def gru2(e, pq=pq, g=g):
                            e.matmul(pq[:], lhsT=onesf[:], rhs=f4(Rall, g), start=True, stop=False)
                            return e.matmul(pq[:], lhsT=self.identb[:], rhs=mskU4[:].rearrange("p h d -> p (h d)"), start=False, stop=True)
                        S.op("pe", gru2, reads=[onesf.k, Rall.k, self.identb.k, mskU4.k], writes=[pq.k])
                        S.op("pool", lambda e, g=g: e.tensor_scalar(out=sm2[:, 8:16], in0=sm[:, 56:64], scalar1=-1.0, scalar2=None, op0=ALU.mult), reads=[sm.k], writes=[sm2.k])
                        for hh in range(4):
                            h = g * 4 + hh
                            S.op("act", lambda e, pq=pq, h=h, hh=hh: e.activation(out=DT[:, h, :], in_=pq[:, hh * 128:(hh + 1) * 128], func=AF.Exp, scale=1.0, bias=sm2[:, 8 + h:9 + h]), reads=[pq.k, sm2.k], writes=[DT.k])
                    S.op("pool", lambda e: e.tensor_tensor(out=qdT[:], in0=qT[:], in1=egr[:], op=ALU.mult), reads=[qT.k, egr.k], writes=[qdT.k])
                    yield
                    for g, f in batch_mm((ps[4], ps[5]), lambda e, out, h: e.matmul(out, lhsT=kT[:, h, :], rhs=qT[:, h, :], start=True, stop=True, skip_group_check=True)):
                        S.op("pe", f, reads=[kT.k, qT.k], writes=[ps[4 + g].k])
                        S.op("dve", lambda e, g=g: e.tensor_tensor(out=f4(QKT, g), in0=ps[4 + g][:], in1=f4(DT, g), op=ALU.mult), reads=[ps[4 + g].k, DT.k], writes=[QKT.k])

                yield

            def gNR(p):
                own = (p % 2 == 1)
                j = p // 2
                r0 = p * 128
                DB = p % 2
                kT = kT2[DB]; kbg = kbg2[DB]; kdec = kdec2[DB]; vb = vb2[DB]; sm = smm[DB]; Dls = Dls2[DB]
                qdT = qdT2[DB]; QKT = QKT2[DB]; zg = zg2[DB]
                S.op("pool", lambda e: e.tensor_tensor(out=Stmp[:], in0=S32[:], in1=bc(sm[:, 88:96].unsqueeze(2), [128, 8, 128]), op=ALU.mult), reads=[S32.k, sm.k], writes=[Stmp.k])
                yield
                A0, B0 = Am[0], Bm[0]
                for g, f in batch_mm((ps[0], ps[1]), lambda e, out, h: e.matmul(out, lhsT=kT[:, h, :], rhs=kT[:, h, :], start=True, stop=True, skip_group_check=True)):
                    S.op("pe", f, reads=[kT.k], writes=[ps[g].k])
                    yield
                    for hh in range(4):
                        h = g * 4 + hh
                        S.op("dve", lambda e, g=g, h=h, hh=hh: e.scalar_tensor_tensor(out=A0[:, h, :], in0=ps[g][:, hh * 128:(hh + 1) * 128], scalar=sm[:, 16 + h:17 + h], in1=Dls[:, h, :], op0=ALU.mult, op1=ALU.mult), reads=[ps[g].k, sm.k, Dls.k], writes=[A0.k])
                for g, f in batch_mm((ps[2], ps[3]), lambda e, out, h: e.transpose(out=out, in_=RF(A0[:, h, :]), identity=self.identf[:])):
                    S.op("pe", f, reads=[A0.k, self.identf.k], writes=[ps[2 + g].k])
                    yield
                    S.op("act", lambda e, g=g: e.activation(out=f4(B0, g), in_=ps[2 + g][:], func=AF.Copy), reads=[ps[2 + g].k], writes=[B0.k])
                    yield
                S.op("pool", lambda e: e.tensor_tensor(out=Pm[:], in0=bc(self.identf[:].unsqueeze(1), [128, 8, 128]), in1=RF(B0[:]), op=ALU.subtract), reads=[self.identf.k, B0.k], writes=[Pm.k])
                yield
                cur = 0
                for lv in range(6):
                    Ac, Bc = Am[cur], Bm[cur]
                    An, Bn_ = Am[1 - cur], Bm[1 - cur]
                    last = (lv == 5)
                    for g, f in batch_mm((ps[0], ps[1]), lambda e, out, h, Ac=Ac, Bc=Bc: e.matmul(out, lhsT=NC_(Bc[:, h, :]), rhs=NC_(Ac[:, h, :]), start=True, stop=True, skip_group_check=True)):
                        S.op("pe", f, reads=[Ac.k, Bc.k], writes=[ps[g].k])
                        S.op("act", lambda e, g=g, An=An: e.activation(out=f4(An, g), in_=ps[g][:], func=AF.Copy), reads=[ps[g].k], writes=[An.k])
                    if not last:
                        for g, f in batch_mm((ps[2], ps[3]), lambda e, out, h, Ac=Ac, Bc=Bc: e.matmul(out, lhsT=NC_(Ac[:, h, :]), rhs=NC_(Bc[:, h, :]), start=True, stop=True, skip_group_check=True)):
                            S.op("pe", f, reads=[Ac.k, Bc.k], writes=[ps[2 + g].k])
                            S.op("dve", lambda e, g=g, Bn_=Bn_: e.tensor_copy(out=f4(Bn_, g), in_=ps[2 + g][:]), reads=[ps[2 + g].k], writes=[Bn_.k])
                    for g, f in batch_mm((ps[0], ps[1]), lambda e, out, h, An=An: e.matmul(out, lhsT=NC_(An[:, h, :]), rhs=NC_(Pm[:, h, :]), start=True, stop=True, skip_group_check=True)):
                        S.op("pe", f, reads=[An.k, Pm.k], writes=[ps[g].k])
                        S.op("dve", lambda e, g=g: e.tensor_tensor(out=f4(Pm, g), in0=RF(f4(Pm, g)), in1=ps[g][:], op=ALU.add), reads=[Pm.k, ps[g].k], writes=[Pm.k])
                    cur = 1 - cur
                S.op("act", lambda e: e.activation(out=TTb[:].rearrange("p h d -> p (h d)"), in_=RF(Pm[:]).rearrange("p h d -> p (h d)"), func=AF.Copy), reads=[Pm.k], writes=[TTb.k])
                yield
                for g, f in batch_mm((ps[0], ps[1]), lambda e, out, h: e.matmul(out, lhsT=TTb[:, h, :], rhs=vb[:, h, :], start=True, stop=True, skip_group_check=True)):
                    S.op("pe", f, reads=[TTb.k, vb.k], writes=[ps[g].k])
                    yield
                    S.op("act", lambda e, g=g: e.activation(out=f4(u, g), in_=ps[g][:], func=AF.Copy), reads=[ps[g].k], writes=[u.k])
                    yield
                for g, f in batch_mm((ps[2], ps[3]), lambda e, out, h: e.matmul(out, lhsT=kbg[:, h, :], rhs=TTb[:, h, :], start=True, stop=True, skip_group_check=True)):
                    S.op("pe", f, reads=[TTb.k, kbg.k], writes=[ps[2 + g].k])
                    yield
                    S.op("act", lambda e, g=g: e.activation(out=f4(wT, g), in_=ps[2 + g][:], func=AF.Copy), reads=[ps[2 + g].k], writes=[wT.k])
                    yield
                for g, f in batch_mm((ps[0], ps[1]), lambda e, out, h: e.matmul(out, lhsT=wT[:, h, :], rhs=Sbf[:, h, :], start=True, stop=True, skip_group_check=True)):
                    S.op("pe", f, reads=[wT.k, Sbf.k], writes=[ps[g].k])
                    yield
                    S.op("dve", lambda e, g=g: e.tensor_tensor(out=f4(vn, g), in0=f4(u, g), in1=ps[g][:], op=ALU.subtract), reads=[u.k, ps[g].k], writes=[vn.k])
                    yield
                if own:
                    for g in range(2):
                        def fo(e, g=g):
                            for hh in range(4):
                                h = g * 4 + hh
                                e.matmul(ps[g][:, hh * 128:(hh + 1) * 128], lhsT=qdT[:, h, :], rhs=Sbf[:, h, :], start=True, stop=False, skip_group_check=True)
                                ins = e.matmul(ps[g][:, hh * 128:(hh + 1) * 128], lhsT=QKT[:, h, :], rhs=vn[:, h, :], start=False, stop=True, skip_group_check=True)
                            return ins
                        S.op("pe", fo, reads=[qdT.k, Sbf.k, QKT.k, vn.k], writes=[ps[g].k])
                        S.op("act", lambda e, g=g: e.activation(out=f4(o, g), in_=ps[g][:], func=AF.Copy), reads=[ps[g].k], writes=[o.k])
                for g, f in batch_mm((ps[2], ps[3]), lambda e, out, h: e.matmul(out, lhsT=kdec[:, h, :], rhs=vn[:, h, :], start=True, stop=True, skip_group_check=True)):
                    S.op("pe", f, reads=[kdec.k, vn.k], writes=[ps[2 + g].k])
                    yield
                    S.op("dve", lambda e, g=g: e.tensor_tensor(out=f4(S32, g), in0=f4(Stmp, g), in1=ps[2 + g][:], op=ALU.add), reads=[Stmp.k, ps[2 + g].k], writes=[S32.k])
                    yield
                S.op("act", lambda e: e.activation(out=Sbf[:].rearrange("p h d -> p (h d)"), in_=S32[:].rearrange("p h d -> p (h d)"), func=AF.Copy), reads=[S32.k], writes=[Sbf.k])
                yield
                if own:
                    S.op("pool", lambda e: e.tensor_tensor(out=osq[:], in0=o[:], in1=o[:], op=ALU.mult), reads=[o.k], writes=[osq.k])
                    yield
                    S.op("dve", lambda e: e.tensor_reduce(out=sm3[:, 0:8], in_=osq[:], axis=AX.X, op=ALU.add), reads=[osq.k], writes=[sm2.k])
                    yield
                    S.op("dve", lambda e: e.tensor_scalar(out=sm3[:, 0:8], in0=sm3[:, 0:8], scalar1=1.0 / 128, scalar2=EPS, op0=ALU.mult, op1=ALU.add), reads=[sm2.k], writes=[sm2.k])
                    yield
                    S.op("act", lambda e: e.activation(out=sm3[:, 0:8], in_=sm3[:, 0:8], func=AF.Ln), reads=[sm2.k], writes=[sm2.k])
                    yield
                    S.op("act", lambda e: e.activation(out=sm3[:, 0:8], in_=sm3[:, 0:8], func=AF.Exp, scale=-0.5), reads=[sm2.k], writes=[sm2.k])
                    yield
                    S.op("dve", lambda e: e.tensor_tensor(out=o[:], in0=o[:], in1=bc(sm3[:, 0:8].unsqueeze(2), [128, 8, 128]), op=ALU.mult), reads=[o.k, sm2.k], writes=[o.k])
                    yield
                    S.op("pool", lambda e: e.tensor_tensor(out=o[:], in0=o[:], in1=bc(ggdn[:].unsqueeze(1), [128, 8, 128]), op=ALU.mult), reads=[o.k, ggdn.k], writes=[o.k])
                    yield
                    S.op("dve", lambda e: e.tensor_tensor(out=mgt[:], in0=o[:].rearrange("p h d -> p (h d)"), in1=zg[:], op=ALU.mult), reads=[o.k, zg.k], writes=[mgt.k])
                    yield
                    S.dma(lambda e, j=j: e.dma_start(out=self.s_mg[j], in_=mgt[:]), reads=[mgt.k], writes=[self.k_own[j]["mg"]])

                yield

            def run_pair(ga, gb, na, nb):
                nmax = max(na, nb, 1)
                da = db = 0
                for k in range(nmax):
                    ta = ((k + 1) * na) // nmax
                    tb = ((k + 1) * nb) // nmax
                    while da < ta:
                        next(ga, None); da += 1
                    while db < tb:
                        next(gb, None); db += 1
                for _ in ga:
                    pass
                for _ in gb:
                    pass

            def count(mk, p):
                S.dry = True
                n = sum(1 for _ in mk(p))
                S.dry = False
                return n
            nF = [count(gF, 0), count(gF, 1)]
            nN = [count(gNR, 0), count(gNR, 1)]
            for _ in gF(0):
                pass
            for p in range(P):
                if p + 1 < P:
                    run_pair(gNR(p), gF(p + 1), nN[p % 2], nF[(p + 1) % 2])
                else:
                    for _ in gNR(p):
                        pass
            S.barrier()
            S.emit()

    def phase1a_stub(self):
        S = self.S
        with ExitStack() as es:
            z = self.sb(es, "zmg", [128, 1024], BF16)
            S.op("pool", lambda e: e.memset(z[:], 0.0), writes=[z.k])
            for j in range(self.NO):
                S.dma(lambda e, j=j: e.dma_start(out=self.s_mg[j], in_=z[:]), reads=[z.k], writes=[self.k_own[j]["mg"]])
            S.barrier()
            S.emit()


def build(S, debug=False, gdn=True):
    B = Builder(S, debug)
    with ExitStack() as es:
        B.setup(es)
        B.phase1b()
        if gdn:
            B.phase1a()
        else:
            B.phase1a_stub()
        B.phase2()
        B.phase3()
    return B.nc


IN_SPLITS = (3072, 1024, 8, 8, 1024, 256, 256, 512, 64, 8, 1024, 1024)


def host_consts():
    bf = ml_dtypes.bfloat16
    i = np.arange(128)
    c = {}
    c["identf"] = np.eye(128, dtype=np.float32)
    c["identb"] = np.eye(128).astype(bf)
    c["uinc"] = (i[:, None] <= i[None, :]).astype(np.float32)
    c["causal"] = np.where(i[None, :] <= i[:, None], 0.0, NEG).astype(np.float32)
    c["mskU"] = np.where(i[None, :] >= i[:, None], 0.0, -30000.0).astype(bf)
    c["mskL"] = np.where(i[None, :] < i[:, None], 0.0, 30000.0).astype(bf)
    return c


def rope_tab(pos, rot):
    inv = (np.float32(500000.0) ** (-np.arange(0, rot, 2, dtype=np.float32) / np.float32(rot))).astype(np.float32)
    ang = pos.astype(np.float32)[:, None] * inv[None, :]
    return np.cos(ang).astype(np.float32), np.sin(ang).astype(np.float32)


def make_in_maps(S, x, norm_mix_g, w_in, conv_w, a_log, dt_bias, gdn_norm_g, q_norm_g, k_norm_g,
                 w_out, norm_mlp_g, w_mlp_up, w_mlp_down):
    Bn = x.shape[0]
    f = np.float32
    w_in = np.asarray(w_in[0], f)
    pts = np.cumsum((0,) + IN_SPLITS)
    seg = {n: w_in[:, pts[i]:pts[i + 1]] for i, n in enumerate(
        ("qkv", "z", "ga", "gb", "aq", "ak", "av", "iq", "ik", "iw", "gatea", "gateb"))}
    w1a = np.ascontiguousarray(np.concatenate([seg["qkv"], seg["ga"], seg["gb"]], 1))
    w1b = np.ascontiguousarray(np.concatenate([seg["ak"], seg["av"], seg["ik"], seg["iw"], seg["aq"], seg["iq"], seg["gateb"], seg["z"], seg["gatea"]], 1))
    col = lambda v: np.ascontiguousarray(np.asarray(v, f).reshape(8, 128).T)
    rep = lambda v: np.ascontiguousarray(np.broadcast_to(np.asarray(v, f)[None, :], (128, len(v))))
    common = host_consts()
    common.update(dict(
        w1a=w1a, w1b=w1b, wout=np.ascontiguousarray(w_out[0], f), wup=np.ascontiguousarray(w_mlp_up[0], f),
        wdn=np.ascontiguousarray(w_mlp_down[0], f),
        gmix=col(norm_mix_g[0]), gmlp=col(norm_mlp_g[0]), gq=rep(q_norm_g[0]), gk=rep(k_norm_g[0]),
        ggdn=rep(gdn_norm_g[0]), alog=rep(a_log[0]), dtb=rep(dt_bias[0]),
        convw=np.ascontiguousarray(np.asarray(conv_w[0], f).reshape(4, 24, 128).transpose(2, 1, 0).reshape(128, 96)),
    ))
    maps = []
    for b in range(Bn):
        for c in range(2):
            m = dict(common)
            xb = np.asarray(x[b], f)
            if c == 0:
                xs = np.concatenate([np.zeros((128, D), f), xb[:S - 128]], 0)
            else:
                xs = xb
            pos = np.maximum(np.arange(S) + (c - 1) * 128, 0)
            m["xseq"] = np.ascontiguousarray(xs)
            m["cosA"], m["sinA"] = rope_tab(pos, 32)
            m["cosI"], m["sinI"] = rope_tab(pos, 16)
            m["blk0"] = np.full((128, 128), NEG if c == 0 else 0.0, f)
            maps.append(m)
    return maps


def assemble(S, results, Bn):
    out = np.zeros((Bn, S, D), np.float32)
    ov = out.reshape(Bn, S // 256, 2, 128, D)
    for b in range(Bn):
        for c in range(2):
            r = results[b * 2 + c]["out"].reshape(S // 256, 128, D)
            ov[b, :, c] = r
    return out


_NC_CACHE = {}


def kernel(**inputs):
    x = np.asarray(inputs["x"])
    Bn, S, _ = x.shape
    if S not in _NC_CACHE:
        _NC_CACHE[S] = build(S)
    nc = _NC_CACHE[S]
    maps = make_in_maps(S, **{k: np.asarray(v) for k, v in inputs.items()})
    res = run_bass_kernel_spmd(nc, maps, core_ids=list(range(2 * Bn)))
    return assemble(S, res.results, Bn)
```
